# Optimizing a Trainium2 kernel written in Bass

```python
import math
import jax
import jax.numpy as jnp
from jax import lax
import numpy as np

D_MODEL = 1024
BATCH = 32
SEQ = 256
DEPTH = 2
DEC_BATCH = 4
DEC_SEQ = 2048
PAST_LEN = 256

GRID_W = 64
ROPE_THETA = 10000.0
EPS = 1e-6
Q_BLOCK = 128
N_MOD = 9
D_FF = 2816

DA_HEADS = 6
DA_QK = 32
DA_V = 2 * DA_QK
DA_WIDTH = DA_HEADS * DA_V
MLA_HEADS = 6
MLA_Q_RANK = 256
MLA_KV_RANK = 128
MLA_NOPE = 64
MLA_ROPE = 32
MLA_V = 64
MLA_WIDTH = MLA_HEADS * MLA_V
S5_WIDTH = D_MODEL - DA_WIDTH - MLA_WIDTH
S5_CH = 16
S5_GROUPS = S5_WIDTH // S5_CH
S5_STATE = 64

MIX_WIDTH = DA_WIDTH + MLA_WIDTH + S5_WIDTH
IN_SIZES = (DA_HEADS * 2 * DA_QK, DA_HEADS * 2 * DA_QK, DA_WIDTH, MLA_Q_RANK, MLA_KV_RANK, MLA_ROPE, S5_WIDTH)
IN_COLS = 3 * DA_WIDTH + MLA_Q_RANK + MLA_KV_RANK + MLA_ROPE + S5_WIDTH

kernel_name = 'hybrid_dit_diffattn_mla_s5_step'

F32 = jnp.float32


def rms_norm(x, w):
    xf = x.astype(F32)
    y = xf * lax.rsqrt(jnp.mean(xf * xf, axis=-1, keepdims=True) + EPS)
    return (y * w.astype(F32)).astype(x.dtype)


def modulate(h, shift, scale):
    return h * (1.0 + scale) + shift


def swiglu(h, w_in, w_out):
    a, g = jnp.split(h @ w_in, 2, axis=-1)
    return (jax.nn.silu(g) * a) @ w_out


def adaln(cvec, l, p):
    m = jax.nn.silu(cvec) @ p['w_ada'][l] + p['b_ada'][l]
    return m.reshape(cvec.shape[0], N_MOD, D_MODEL)


def grid_positions(n):
    rows = n // GRID_W
    row = jnp.repeat(jnp.arange(rows, dtype=jnp.int32), GRID_W)
    col = jnp.tile(jnp.arange(GRID_W, dtype=jnp.int32), rows)
    return row, col


def _rope_axis(x, pos):
    n = x.shape[-1] // 2
    inv = ROPE_THETA ** (-jnp.arange(n, dtype=F32) / n)
    ang = pos.astype(F32)[:, None] * inv[None, :]
    shape = (1, ang.shape[0]) + (1,) * (x.ndim - 3) + (n,)
    cos = jnp.cos(ang).reshape(shape).astype(x.dtype)
    sin = jnp.sin(ang).reshape(shape).astype(x.dtype)
    x1, x2 = x[..., :n], x[..., n:]
    return jnp.concatenate([x1 * cos - x2 * sin, x1 * sin + x2 * cos], axis=-1)


def rope_2d(x, pos):
    row, col = pos
    h = x.shape[-1] // 2
    return jnp.concatenate([_rope_axis(x[..., :h], row), _rope_axis(x[..., h:], col)], axis=-1)


def sweep_queries(fn, q):
    b, s = q.shape[:2]
    qb = jnp.moveaxis(q.reshape((b, s // Q_BLOCK, Q_BLOCK) + q.shape[2:]), 1, 0)
    out = jnp.moveaxis(lax.map(fn, qb), 0, 1)
    return out.reshape((b, s) + out.shape[3:])


def diff_attention(q, k, v, lam):
    scale = DA_QK ** -0.5
    def block(qb):
        s = jnp.einsum('bqhcd,bkhcd->bhcqk', qb, k).astype(F32) * scale
        pr = jax.nn.softmax(s, axis=-1)
        w = pr[:, :, 0] - lam * pr[:, :, 1]
        return jnp.einsum('bhqk,bkhd->bqhd', w.astype(v.dtype), v)
    return sweep_queries(block, q)


def softmax_attention(q, k, v, scale):
    def block(qb):
        s = jnp.einsum('bqhd,bkhd->bhqk', qb, k).astype(F32) * scale
        pr = jax.nn.softmax(s, axis=-1)
        return jnp.einsum('bhqk,bkhd->bqhd', pr.astype(v.dtype), v)
    return sweep_queries(block, q)


def diff_mixer(q, k, v, l, p, pos, ctx_k, ctx_v):
    b, s = q.shape[:2]
    q = q.reshape(b, s, DA_HEADS, 2, DA_QK)
    k = k.reshape(b, s, DA_HEADS, 2, DA_QK)
    v = v.reshape(b, s, DA_HEADS, DA_V)
    new = (k.reshape(b, s, DA_HEADS, 2 * DA_QK), v)
    if pos is None:
        k_all, v_all = k, v
    else:
        q = rope_2d(q, pos)
        kc = ctx_k.reshape(ctx_k.shape[0], ctx_k.shape[1], DA_HEADS, 2, DA_QK)
        k_all = jnp.concatenate([rope_2d(k, pos), kc.astype(k.dtype)], axis=1)
        v_all = jnp.concatenate([v, ctx_v.astype(v.dtype)], axis=1)
    lam_init = 0.8 - 0.6 * math.exp(-0.3 * l)
    lp = p['diff_lambda'][l].astype(F32)
    lam = jnp.exp(jnp.sum(lp[0] * lp[1])) - jnp.exp(jnp.sum(lp[2] * lp[3])) + lam_init
    o = diff_attention(q, k_all, v_all, lam)
    o = rms_norm(o, p['diff_subln_w'][l]) * (1.0 - lam_init)
    return o.reshape(b, s, DA_WIDTH), new


def mla_expand(ckv, kpe, l, p):
    b, s = ckv.shape[:2]
    kv = (ckv @ p['mla_w_kv_up'][l]).reshape(b, s, MLA_HEADS, MLA_NOPE + MLA_V)
    k_pe = jnp.broadcast_to(kpe[:, :, None, :].astype(kv.dtype), (b, s, MLA_HEADS, MLA_ROPE))
    return jnp.concatenate([kv[..., :MLA_NOPE], k_pe], axis=-1), kv[..., MLA_NOPE:]


def mla_mixer(cq, ckv, kpe, l, p, pos, ctx_ckv, ctx_kpe):
    b, s = cq.shape[:2]
    q = (rms_norm(cq, p['mla_q_norm_w'][l]) @ p['mla_w_q_up'][l]).reshape(b, s, MLA_HEADS, MLA_NOPE + MLA_ROPE)
    ckv = rms_norm(ckv, p['mla_kv_norm_w'][l])
    new = (ckv, kpe)
    if pos is None:
        k, v = mla_expand(ckv, kpe, l, p)
    else:
        q = jnp.concatenate([q[..., :MLA_NOPE], rope_2d(q[..., MLA_NOPE:], pos)], axis=-1)
        kpe_r = rope_2d(kpe[:, :, None, :], pos)[:, :, 0]
        k_l, v_l = mla_expand(ckv, kpe_r, l, p)
        k_c, v_c = mla_expand(ctx_ckv.astype(ckv.dtype), ctx_kpe, l, p)
        k = jnp.concatenate([k_l, k_c], axis=1)
        v = jnp.concatenate([v_l, v_c], axis=1)
    o = softmax_attention(q, k, v, (MLA_NOPE + MLA_ROPE) ** -0.5)
    return o.reshape(b, s, MLA_WIDTH), new


def _ssm_combine(e1, e2):
    a1, b1 = e1
    a2, b2 = e2
    return a2 * a1, a2 * b1 + b2


def s5_mixer(u, l, p, h0_re, h0_im):
    b, n = u.shape[:2]
    uf = u.astype(F32).reshape(b, n, S5_GROUPS, S5_CH)
    uc = uf.astype(jnp.complex64)
    lam = lax.complex(p['s5_a_re'][l].astype(F32), p['s5_a_im'][l].astype(F32))
    step = jnp.exp(p['s5_log_step'][l].astype(F32))[..., None]
    lam_bar = jnp.exp(lam * step)
    b_mat = lax.complex(p['s5_b_re'][l].astype(F32), p['s5_b_im'][l].astype(F32))
    b_bar = ((lam_bar - 1.0) / lam)[..., None] * b_mat
    c_mat = lax.complex(p['s5_c_re'][l].astype(F32), p['s5_c_im'][l].astype(F32))
    y = p['s5_d'][l].astype(F32) * uf
    finals = []
    for d, rev in ((0, False), (1, True)):
        bu = jnp.einsum('gpc,blgc->blgp', b_bar[d], uc)
        if h0_re is not None:
            h_init = lax.complex(h0_re[:, d].astype(F32), h0_im[:, d].astype(F32))
            edge = n - 1 if rev else 0
            bu = bu.at[:, edge].add(lam_bar[d] * h_init)
        a = jnp.broadcast_to(lam_bar[d], bu.shape)
        _, h = lax.associative_scan(_ssm_combine, (a, bu), axis=1, reverse=rev)
        y = y + jnp.real(jnp.einsum('gcp,blgp->blgc', c_mat[d], h))
        finals.append(h[:, 0] if rev else h[:, -1])
    y = jax.nn.gelu(y.reshape(b, n, S5_WIDTH))
    out = y * jax.nn.sigmoid(y @ p['s5_w_glu'][l].astype(F32) + p['s5_b_glu'][l].astype(F32))
    fin = jnp.stack(finals, axis=1)
    return out.astype(u.dtype), jnp.real(fin), jnp.imag(fin)


def mixer(h, l, p, pos, ctx):
    offs = np.cumsum(np.array(IN_SIZES))[:-1].tolist()
    qa, ka, va, cq, ckv, kpe, u = jnp.split(h @ p['w_in'][l], offs, axis=-1)
    if ctx is None:
        ctx = (None,) * 6
    o_a, new_a = diff_mixer(qa, ka, va, l, p, pos, ctx[0], ctx[1])
    o_b, new_b = mla_mixer(cq, ckv, kpe, l, p, pos, ctx[2], ctx[3])
    o_c, s_re, s_im = s5_mixer(u, l, p, ctx[4], ctx[5])
    out = jnp.concatenate([o_a, o_b, o_c], axis=-1) @ p['w_out'][l]
    return out, new_a + new_b + (s_re, s_im)


def trunk_layer(x, mod, l, p, pos, ctx):
    sh1, sc1, g1, sh2, sc2, g2, sh3, sc3, g3 = [mod[:, i, None, :] for i in range(N_MOD)]
    nw = p['norm_w'][l]
    x = x + 0.5 * g1 * swiglu(modulate(rms_norm(x, nw[0]), sh1, sc1), p['ffn_w_in'][l, 0], p['ffn_w_out'][l, 0])
    m, new_ctx = mixer(modulate(rms_norm(x, nw[1]), sh2, sc2), l, p, pos, ctx)
    x = x + g2 * m
    x = x + 0.5 * g3 * swiglu(modulate(rms_norm(x, nw[2]), sh3, sc3), p['ffn_w_in'][l, 1], p['ffn_w_out'][l, 1])
    return x, new_ctx


def setup_inputs(seed: int = 0) -> dict:
    key = jax.random.key(seed)
    ks = iter(jax.random.split(key, 48))
    def nrm(shape, s):
        return jax.random.normal(next(ks), shape, F32) * s
    def gain(shape):
        return 1.0 + nrm(shape, 0.01)
    a_im0 = jnp.pi * jnp.arange(S5_STATE, dtype=F32)
    return {
        'x_prompt': nrm((BATCH, SEQ, D_MODEL), 1.0),
        'x_sample': nrm((DEC_BATCH, DEC_SEQ, D_MODEL), 1.0),
        'cache_diff_k': nrm((DEC_BATCH, DEPTH, PAST_LEN, DA_HEADS, 2 * DA_QK), 1.0),
        'cache_diff_v': nrm((DEC_BATCH, DEPTH, PAST_LEN, DA_HEADS, DA_V), 1.0),
        'cache_mla_ckv': nrm((DEC_BATCH, DEPTH, PAST_LEN, MLA_KV_RANK), 1.0),
        'cache_mla_kpe': nrm((DEC_BATCH, DEPTH, PAST_LEN, MLA_ROPE), 1.0),
        'state_s5_re': nrm((DEC_BATCH, DEPTH, 2, S5_GROUPS, S5_STATE), 1.0),
        'state_s5_im': nrm((DEC_BATCH, DEPTH, 2, S5_GROUPS, S5_STATE), 1.0),
        'c': nrm((DEC_BATCH, D_MODEL), 1.0),
        'c_ctx': nrm((D_MODEL,), 1.0),
        'w_ada': nrm((DEPTH, D_MODEL, N_MOD * D_MODEL), 0.5 * D_MODEL ** -0.5),
        'b_ada': nrm((DEPTH, N_MOD * D_MODEL), 0.01),
        'norm_w': gain((DEPTH, 3, D_MODEL)),
        'ffn_w_in': nrm((DEPTH, 2, D_MODEL, 2 * D_FF), D_MODEL ** -0.5),
        'ffn_w_out': nrm((DEPTH, 2, D_FF, D_MODEL), D_FF ** -0.5),
        'w_in': nrm((DEPTH, D_MODEL, IN_COLS), D_MODEL ** -0.5),
        'diff_lambda': nrm((DEPTH, 4, DA_QK), 0.1),
        'diff_subln_w': gain((DEPTH, DA_V)),
        'mla_q_norm_w': gain((DEPTH, MLA_Q_RANK)),
        'mla_w_q_up': nrm((DEPTH, MLA_Q_RANK, MLA_HEADS * (MLA_NOPE + MLA_ROPE)), MLA_Q_RANK ** -0.5),
        'mla_kv_norm_w': gain((DEPTH, MLA_KV_RANK)),
        'mla_w_kv_up': nrm((DEPTH, MLA_KV_RANK, MLA_HEADS * (MLA_NOPE + MLA_V)), MLA_KV_RANK ** -0.5),
        's5_a_re': -0.5 + nrm((DEPTH, 2, S5_GROUPS, S5_STATE), 0.01),
        's5_a_im': a_im0 + nrm((DEPTH, 2, S5_GROUPS, S5_STATE), 0.01),
        's5_log_step': jax.random.uniform(next(ks), (DEPTH, 2, S5_GROUPS), F32, math.log(1e-3), math.log(1e-1)),
        's5_b_re': nrm((DEPTH, 2, S5_GROUPS, S5_STATE, S5_CH), (2 * S5_CH) ** -0.5),
        's5_b_im': nrm((DEPTH, 2, S5_GROUPS, S5_STATE, S5_CH), (2 * S5_CH) ** -0.5),
        's5_c_re': nrm((DEPTH, 2, S5_GROUPS, S5_CH, S5_STATE), (2 * S5_STATE) ** -0.5),
        's5_c_im': nrm((DEPTH, 2, S5_GROUPS, S5_CH, S5_STATE), (2 * S5_STATE) ** -0.5),
        's5_d': nrm((DEPTH, S5_GROUPS, S5_CH), 1.0),
        's5_w_glu': nrm((DEPTH, S5_WIDTH, S5_WIDTH), S5_WIDTH ** -0.5),
        's5_b_glu': nrm((DEPTH, S5_WIDTH), 0.01),
        'w_out': nrm((DEPTH, MIX_WIDTH, D_MODEL), MIX_WIDTH ** -0.5),
        'final_norm_w': gain((D_MODEL,)),
    }


def reference(x_prompt, x_sample, cache_diff_k, cache_diff_v, cache_mla_ckv, cache_mla_kpe, state_s5_re, state_s5_im, c, c_ctx, w_ada, b_ada, norm_w, ffn_w_in, ffn_w_out, w_in, diff_lambda, diff_subln_w, mla_q_norm_w, mla_w_q_up, mla_kv_norm_w, mla_w_kv_up, s5_a_re, s5_a_im, s5_log_step, s5_b_re, s5_b_im, s5_c_re, s5_c_im, s5_d, s5_w_glu, s5_b_glu, w_out, final_norm_w):
    p = dict(w_ada=w_ada, b_ada=b_ada, norm_w=norm_w, ffn_w_in=ffn_w_in, ffn_w_out=ffn_w_out, w_in=w_in,
             diff_lambda=diff_lambda, diff_subln_w=diff_subln_w, mla_q_norm_w=mla_q_norm_w,
             mla_w_q_up=mla_w_q_up, mla_kv_norm_w=mla_kv_norm_w, mla_w_kv_up=mla_w_kv_up,
             s5_a_re=s5_a_re, s5_a_im=s5_a_im, s5_log_step=s5_log_step, s5_b_re=s5_b_re, s5_b_im=s5_b_im,
             s5_c_re=s5_c_re, s5_c_im=s5_c_im, s5_d=s5_d, s5_w_glu=s5_w_glu, s5_b_glu=s5_b_glu, w_out=w_out)

    x = x_prompt
    ctx_out = []
    for l in range(DEPTH):
        x, new_ctx = trunk_layer(x, adaln(c_ctx[None, :], l, p), l, p, None, None)
        ctx_out.append(new_ctx)
    y_prompt = rms_norm(x, final_norm_w)
    new_diff_k = jnp.stack([t[0] for t in ctx_out], axis=1)
    new_diff_v = jnp.stack([t[1] for t in ctx_out], axis=1)
    new_mla_ckv = jnp.stack([t[2] for t in ctx_out], axis=1)
    new_mla_kpe = jnp.stack([t[3] for t in ctx_out], axis=1)
    new_s5_re = jnp.stack([t[4] for t in ctx_out], axis=1)
    new_s5_im = jnp.stack([t[5] for t in ctx_out], axis=1)

    pos = grid_positions(x_sample.shape[1])
    x = x_sample
    for l in range(DEPTH):
        ctx = (cache_diff_k[:, l], cache_diff_v[:, l], cache_mla_ckv[:, l], cache_mla_kpe[:, l],
               state_s5_re[:, l], state_s5_im[:, l])
        x, _ = trunk_layer(x, adaln(c, l, p), l, p, pos, ctx)
    y_sample = rms_norm(x, final_norm_w)

    return (y_prompt, y_sample, new_diff_k, new_diff_v, new_mla_ckv, new_mla_kpe, new_s5_re, new_s5_im)
```

```python
import contextlib
import math
import numpy as np
import concourse.bass as bass
import concourse.mybir as mybir
from concourse.bass_utils import run_bass_kernel_spmd

F32 = mybir.dt.float32
BF16 = mybir.dt.bfloat16
AF = mybir.ActivationFunctionType
ALU = mybir.AluOpType

D = 1024
NT = 2048
DEPTH = 2
DFF = 2816
NHT = 22
EPS = 1e-6
INC = 1824
STAGE = 99


class Sched:
    def __init__(self, nc, ndma=8):
        self.nc = nc
        self.engs = ['pe', 'act', 'dve', 'pool', 'sp']
        self.streams = {e: [] for e in self.engs}
        self.cnt = {e: 0 for e in self.engs}
        self.seen = {e: {} for e in self.engs}
        self.res = {}
        self.ndma = ndma
        self.dma_issued = {'sp': 0, 'pool': 0, 'act': 0}
        self.dma_last = {}
        self.final_tokens = []

    def _deps(self, eng, reads, writes):
        toks = {}

        def add(t):
            if t is None:
                return
            k, v = t
            if toks.get(k, 0) < v:
                toks[k] = v
        for r in reads:
            st = self.res.get(r)
            if st:
                add(st['w'])
        for w in writes:
            st = self.res.get(w)
            if st:
                add(st['w'])
                for t in st['r']:
                    add(t)
        out = []
        for k, v in toks.items():
            if eng == 'pe' and k == ('c', 'pe'):
                continue
            if self.seen[eng].get(k, 0) >= v:
                continue
            self.seen[eng][k] = v
            out.append((k, v))
        return out

    def _mark(self, tok, reads, writes):
        for r in reads:
            st = self.res.setdefault(r, {'w': None, 'r': []})
            st['r'].append(tok)
            if len(st['r']) > 64:
                mx = {}
                for k, v in st['r']:
                    if mx.get(k, 0) < v:
                        mx[k] = v
                st['r'] = list(mx.items())
        for w in writes:
            self.res[w] = {'w': tok, 'r': []}

    PSUM_NAMES = {'lxps', 'ad_pst', 'ad_psm', 'nm_pss', 'f_pag', 'f_pso', 'fin_ps', 'sa_ps', 'sb_psu', 'sb_pst', 'sc_ps',
                  'sc_psF', 'sd_ps', 'se_pst', 'se_psg', 'a_psp', 'a_pst', 'a_pss', 'a_pso'}

    def _excl(self, reads, writes):
        rd, wr = [], list(writes)
        for r in reads:
            nm = r if isinstance(r, str) else r[0]
            if nm in self.PSUM_NAMES:
                if r not in wr:
                    wr.append(r)
            else:
                rd.append(r)
        return rd, wr

    def op(self, eng, fn, reads=(), writes=()):
        reads, writes = self._excl(reads, writes)
        waits = self._deps(eng, reads, writes)
        self.cnt[eng] += 1
        tok = (('c', eng), self.cnt[eng])
        self.streams[eng].append((waits, fn, tok))
        self._mark(tok, reads, writes)
        return tok

    def dma(self, eng, fn, reads=(), writes=(), final=False):
        k = self.dma_issued[eng]
        self.dma_issued[eng] += 1
        slot = k % self.ndma
        val = 16 * (k // self.ndma + 1)
        key = ('d', eng, slot)
        waits = self._deps(eng, reads, writes)
        if val > 16 and self.seen[eng].get(key, 0) < val - 16:
            self.seen[eng][key] = val - 16
            waits.append((key, val - 16))
        tok = (key, val)
        self.dma_last[key] = val
        self.streams[eng].append((waits, fn, tok))
        self._mark(tok, reads, writes)
        if final:
            self.final_tokens.append(tok)
        return tok

    def barrier(self):
        allt = [(('c', e), self.cnt[e]) for e in ['pe', 'act', 'dve', 'pool'] if self.cnt[e]]
        allt += list(self.dma_last.items())
        for e in self.engs:
            waits = []
            for k, v in allt:
                if k == ('c', e):
                    continue
                if self.seen[e].get(k, 0) >= v:
                    continue
                self.seen[e][k] = v
                waits.append((k, v))
            if waits:
                self.streams[e].append((waits, None, None))

    def emit(self):
        nc = self.nc
        with contextlib.ExitStack() as es:
            sems = {}
            for e in ['pe', 'act', 'dve', 'pool']:
                sems[('c', e)] = es.enter_context(nc.semaphore('c_' + e))
            for e in ['sp', 'pool', 'act']:
                if self.dma_issued[e]:
                    for s in range(self.ndma):
                        sems[('d', e, s)] = es.enter_context(nc.semaphore('d_%s_%d' % (e, s)))
            block = es.enter_context(nc.Block())

            def run(engname, engobj):
                for waits, fn, tok in self.streams[engname]:
                    for k, v in waits:
                        engobj.wait_ge(sems[k], v)
                    if fn is None:
                        continue
                    ins = fn(engobj)
                    k, v = tok
                    ins.then_inc(sems[k], 16 if k[0] == 'd' else 1)
                if engname == 'sp':
                    for k, v in self.final_tokens:
                        engobj.wait_ge(sems[k], v)

            @block.sync
            def _(e):
                run('sp', e)

            @block.tensor
            def _(e):
                run('pe', e)

            @block.scalar
            def _(e):
                run('act', e)

            @block.vector
            def _(e):
                run('dve', e)

            @block.gpsimd
            def _(e):
                run('pool', e)


class Builder:
    def __init__(self, stage=99):
        self.stage = stage
        self.nc = bass.Bass("TRN2", target_bir_lowering=False)
        self.S = Sched(self.nc)
        self.es = contextlib.ExitStack()
        self.uid = 0
        self.rr = 0
        self._dr_cache = {}

    def din(self, name, shape):
        ap = self.nc.dram_tensor(name, list(shape), F32, kind="ExternalInput").ap()
        self._dr_cache[name] = ap
        return ap

    def _ap(self, name):
        return self._dr_cache[name]

    def dout(self, name, shape):
        return self.nc.dram_tensor(name, list(shape), F32, kind="ExternalOutput").ap()

    def sb(self, name, shape, dt, es=None):
        self.uid += 1
        return (es or self.es).enter_context(self.nc.sbuf_tensor("%s_u%d" % (name, self.uid), list(shape), dt))

    def psum(self, name, shape, dt, es):
        self.uid += 1
        return es.enter_context(self.nc.psum_tensor("%s_u%d" % (name, self.uid), list(shape), dt))

    def evac_eng(self):
        self.rr += 1
        return 'act' if self.rr % 2 else 'dve'

    def copy(self, eng, out, in_, reads, writes):
        if eng == 'act':
            self.S.op('act', lambda e: e.copy(out=out, in_=in_), reads, writes)
        else:
            self.S.op(eng, lambda e: e.tensor_copy(out=out, in_=in_), reads, writes)

    def tt(self, eng, out, a, b, op, reads, writes):
        self.S.op(eng, lambda e: e.tensor_tensor(out=out, in0=a, in1=b, op=op), reads, writes)

    def ts(self, eng, out, a, s1, s2, op0, op1, reads, writes):
        if op1 is None:
            self.S.op(eng, lambda e: e.tensor_scalar(out=out, in0=a, scalar1=s1, scalar2=None, op0=op0), reads, writes)
        else:
            self.S.op(eng, lambda e: e.tensor_scalar(out=out, in0=a, scalar1=s1, scalar2=s2, op0=op0, op1=op1), reads, writes)

    def stt(self, eng, out, a, s, b, op0, op1, reads, writes):
        self.S.op(eng, lambda e: e.scalar_tensor_tensor(out=out, in0=a, scalar=s, in1=b, op0=op0, op1=op1), reads, writes)

    def act(self, out, in_, func, reads, writes, bias=None, scale=None, accum=None):
        kw = {}
        if bias is not None:
            kw['bias'] = bias
        if scale is not None:
            kw['scale'] = scale
        if accum is not None:
            kw['accum_out'] = accum
        self.S.op('act', lambda e: e.activation(out=out, in_=in_, func=func, **kw), reads, writes)

    def mm(self, out, lhsT, rhs, start, stop, reads, writes):
        self.S.op('pe', lambda e: e.matmul(out, lhsT=lhsT, rhs=rhs, start=start, stop=stop), reads, writes)

    def tr(self, out, in_, ident, reads, writes):
        self.S.op('pe', lambda e: e.transpose(out=out, in_=in_, identity=ident), reads, writes)

    def ld(self, out, in_, writes, reads=(), eng='sp', **kw):
        self.S.dma(eng, lambda e: e.dma_start(out=out, in_=in_, **kw), reads, writes)

    def st(self, out, in_, reads, eng='sp', **kw):
        self.S.dma(eng, lambda e: e.dma_start(out=out, in_=in_, **kw), reads, (), final=True)

    def build(self):
        nc, S = self.nc, self.S
        din, dout, sb = self.din, self.dout, self.sb
        xin = din("xin", [NT, D])
        cvec = din("cvec", [8, 128])
        w_ada = din("w_ada", [DEPTH, D, 9 * D])
        b_ada = din("b_ada", [DEPTH * 72, 128])
        norm_w = din("norm_w", [DEPTH * 3 * 8, 128])
        fnw = din("final_norm_w", [8, 128])
        ffn_w_in = din("ffn_w_in", [DEPTH, 2, D, 2 * DFF])
        ffn_w_out = din("ffn_w_out", [DEPTH, 2, DFF, D])
        self.dr = dict(
            w_in=din("w_in", [DEPTH, D, INC]), w_out=din("w_out", [DEPTH, D, D]),
            ctx_dk=din("ctx_dk", [DEPTH, 256, 384]), ctx_dv=din("ctx_dv", [DEPTH, 256, 384]),
            ctx_ckv=din("ctx_ckv", [DEPTH, 256, 128]), ctx_kpe=din("ctx_kpe", [DEPTH, 256, 32]),
            h0re=din("h0re", [DEPTH, 2, 2, 8, 64]), h0im=din("h0im", [DEPTH, 2, 2, 8, 64]),
            ropec=din("ropec", [NT, 16]), ropes=din("ropes", [NT, 16]),
            augk=din("augk", [NT + 256, 9]), augq=din("augq", [NT, 9]),
            flag=din("flag", [128, 1]),
            diff_lambda=din("diff_lambda", [DEPTH, 128]), diff_subln_w=din("diff_subln_w", [DEPTH, 64]),
            mla_q_norm_w=din("mla_q_norm_w", [DEPTH, 256]), mla_w_q_up=din("mla_w_q_up", [DEPTH, 256, 576]),
            mla_kv_norm_w=din("mla_kv_norm_w", [DEPTH, 128]), mla_w_kv_up=din("mla_w_kv_up", [DEPTH, 128, 768]),
            s5_a_re=din("s5_a_re", [DEPTH, 2, 2, 8, 64]), s5_a_im=din("s5_a_im", [DEPTH, 2, 2, 8, 64]),
            s5_log_step=din("s5_log_step", [DEPTH, 2, 2, 8]),
            s5_b_re=din("s5_b_re", [DEPTH, 2, 2, 8, 64, 16]), s5_b_im=din("s5_b_im", [DEPTH, 2, 2, 8, 64, 16]),
            s5_c_re=din("s5_c_re", [DEPTH, 2, 2, 128, 64]), s5_c_im=din("s5_c_im", [DEPTH, 2, 2, 128, 64]),
            s5_d=din("s5_d", [DEPTH, 16, 16]), s5_w_glu=din("s5_w_glu", [DEPTH, 256, 256]),
            s5_b_glu=din("s5_b_glu", [DEPTH, 2, 128]),
            cmask_f=din("cmask_f", [128, 128]), cmask_b=din("cmask_b", [128, 128]),
        )
        self.y = dout("y", [NT, D])
        self.o = dict(
            ndk=dout("ndk", [DEPTH, NT, 384]), ndv=dout("ndv", [DEPTH, NT, 384]),
            nckv=dout("nckv", [DEPTH, NT, 128]), nkpe=dout("nkpe", [DEPTH, NT, 32]),
            ns5re=dout("ns5re", [DEPTH, 2, 8, 1024]), ns5im=dout("ns5im", [DEPTH, 2, 8, 1024]),
        )
        self.xT = sb("xT", [128, 8, NT], F32)
        self.identb = sb("identb", [128, 128], BF16)
        self.identf = sb("identf", [128, 128], F32)
        self.onesb = sb("onesb", [128, 128], BF16)
        self.epsc = sb("epsc", [128, 1], F32)
        self.mod = sb("mod", [128, DEPTH, 72], F32)
        self.cA = sb("cA", [128, DEPTH, 3, 8], F32)
        self.cB = sb("cB", [128, DEPTH, 3, 8], F32)
        self.cG = sb("cG", [128, DEPTH, 3, 8], F32)
        self.cF = sb("cF", [128, 8], F32)
        self.rs = sb("rs", [128, 2, 512], F32)
        self.wi_n = 0
        self.wo_n = 0

        self.setup_consts()
        self.load_x()
        self.adaln()
        S.barrier()
        for l in range(DEPTH):
            if self.stage >= 1:
                self.ffn(l, 0)
                S.barrier()
            if self.stage >= 3:
                self.mixer(l)
                S.barrier()
            if self.stage >= 2:
                self.ffn(l, 1)
                S.barrier()
            if self.stage < 4:
                break
        self.final()
        S.emit()
        return nc

    def setup_consts(self):
        S = self.S
        ib, iff, ob = self.identb, self.identf, self.onesb
        S.op('pool', lambda e: e.memset(ib[:], 0.0), (), ['identb'])
        S.op('pool', lambda e: e.affine_select(out=ib[:], in_=ib[:], compare_op=ALU.not_equal, fill=1.0, base=0,
                                               pattern=[[-1, 128]], channel_multiplier=1), ['identb'], ['identb'])
        S.op('pool', lambda e: e.memset(iff[:], 0.0), (), ['identf'])
        S.op('pool', lambda e: e.affine_select(out=iff[:], in_=iff[:], compare_op=ALU.not_equal, fill=1.0, base=0,
                                               pattern=[[-1, 128]], channel_multiplier=1), ['identf'], ['identf'])
        S.op('pool', lambda e: e.memset(ob[:], 1.0), (), ['onesb'])
        ep = self.epsc
        S.op('pool', lambda e: e.memset(ep[:], EPS), (), ['epsc'])

    def load_x(self):
        with contextlib.ExitStack() as es:
            xt = [self.sb("xtok%d" % i, [128, D], F32, es) for i in range(2)]
            ps = [self.psum("lxps%d" % i, [128, 512], F32, es) for i in range(4)]
            src = self._ap("xin").rearrange("(b p t) d -> b t p d", b=2, t=8)
            n = 0
            for b in range(2):
                for t in range(8):
                    buf = xt[n % 2]
                    bk = ('xtok', n % 2)
                    self.ld(buf[:], src[b, t], [bk])
                    pos = 1024 * b + 128 * t
                    for h in range(2):
                        pb = ps[(2 * n + h) % 4]
                        pk = ('lxps', (2 * n + h) % 4)
                        for jj in range(4):
                            j = 4 * h + jj
                            self.tr(pb[:, 128 * jj:128 * jj + 128], buf[:, 128 * j:128 * j + 128], self.identf[:],
                                    [bk, 'identf'], [pk])
                        dst = self.xT[:, 4 * h:4 * h + 4, pos:pos + 128]
                        srcp = pb[:].rearrange("p (j c) -> p j c", j=4)
                        self.copy(self.evac_eng(), dst, srcp, [pk],
                                  [('xT', j, pos // 512) for j in range(4 * h, 4 * h + 4)])
                    n += 1
        self.S.barrier()

    def adaln(self):
        S = self.S
        with contextlib.ExitStack() as es:
            rows = self.sb("ad_rows", [128, 3, 128], F32, es)
            rT = self.sb("ad_rT", [128, 3, 128], F32, es)
            scv = self.sb("ad_scv", [128, 8], F32, es)
            wa = [self.sb("ad_w%d" % i, [128, 8, 512], F32, es) for i in range(2)]
            pst = self.psum("ad_pst", [128, 512], F32, es)
            psm = self.psum("ad_psm", [128, 512], F32, es)
            S.op('dve', lambda e: e.memset(rows[:], 0.0), (), ['ad_rows'])
            self.ld(rows[0:8, 0, :], self._dr_cache['cvec'], ['ad_rows'], ['ad_rows'])
            self.ld(rows[8:16, 0, :], self._dr_cache['final_norm_w'], ['ad_rows'], ['ad_rows'])
            self.ld(rows[16:64, 0, :], self._dr_cache['norm_w'], ['ad_rows'], ['ad_rows'])
            self.ld(rows[:, 1, :], self._dr_cache['b_ada'][0:128, :], ['ad_rows'], ['ad_rows'])
            self.ld(rows[0:16, 2, :], self._dr_cache['b_ada'][128:144, :], ['ad_rows'], ['ad_rows'])
            for i in range(3):
                self.tr(pst[:, 128 * i:128 * i + 128], rows[:, i, :], self.identf[:], ['ad_rows', 'identf'], ['ad_pst'])
            self.copy('dve', rT[:].rearrange("p a b -> p (a b)"), pst[:, 0:384], ['ad_pst'], ['ad_rT'])
            self.act(scv[:], rT[:, 0, 0:8], AF.Silu, ['ad_rT'], ['ad_scv'])
            badaT = rT[:].rearrange("p a b -> p (a b)")[:, 128:128 + 144]
            n = 0
            for l in range(DEPTH):
                wv = self._dr_cache['w_ada'][l].rearrange("(kc p) f -> p kc f", p=128)
                for pc in range(18):
                    buf = wa[n % 2]
                    bk = ('ad_w', n % 2)
                    self.ld(buf[:], wv[:, :, 512 * pc:512 * pc + 512], [bk])
                    for ii in range(4):
                        i = 4 * pc + ii
                        for k in range(8):
                            self.mm(psm[:, l * 72 + i:l * 72 + i + 1], buf[:, k, 128 * ii:128 * ii + 128], scv[:, k:k + 1],
                                    k == 0, k == 7, [bk, 'ad_scv'], ['ad_psm'])
                    n += 1
            self.tt('dve', self.mod[:].rearrange("p l i -> p (l i)"), psm[:, 0:144], badaT, ALU.add,
                    ['ad_psm', 'ad_rT'], ['mod'])
            for l in range(DEPTH):
                for n3 in range(3):
                    nw = rT[:, 0, 16 + (l * 3 + n3) * 8:16 + (l * 3 + n3) * 8 + 8]
                    sh = self.mod[:, l, (3 * n3) * 8:(3 * n3) * 8 + 8]
                    sc = self.mod[:, l, (3 * n3 + 1) * 8:(3 * n3 + 1) * 8 + 8]
                    g = self.mod[:, l, (3 * n3 + 2) * 8:(3 * n3 + 2) * 8 + 8]
                    self.stt('dve', self.cA[:, l, n3, :], sc, 1.0, nw, ALU.add, ALU.mult, ['mod', 'ad_rT'], ['cA'])
                    self.copy('dve', self.cB[:, l, n3, :], sh, ['mod'], ['cB'])
                    self.ts('dve', self.cG[:, l, n3, :], g, (1.0 if n3 == 1 else 0.5), None, ALU.mult, None, ['mod'], ['cG'])
            self.copy('dve', self.cF[:], rT[:, 0, 8:16], ['ad_rT'], ['cF'])
            S.barrier()

    def norm_mod(self, blk, A, Bv, hm, hmcol, es_t, pss, hmkey):
        c0 = 512 * blk
        sq, tmp = es_t['sq'], es_t['tmp']
        xk = [('xT', j, blk) for j in range(8)]
        self.act(sq[:], self.xT[:, :, c0:c0 + 512], AF.Square, xk, ['nm_sq'])
        for j in range(8):
            self.mm(pss[:], self.onesb[:], sq[:, j, :], j == 0, j == 7, ['nm_sq', 'onesb'], ['nm_pss'])
        rb = blk % 2
        self.act(self.rs[:, rb, :], pss[:], AF.Sqrt, ['nm_pss'], [('rs', rb)], bias=self.epsc[:, 0:1], scale=1.0 / D)
        self.S.op('dve', lambda e: e.reciprocal(out=self.rs[:, rb, :], in_=self.rs[:, rb, :]), [('rs', rb)], [('rs', rb)])
        for j in range(8):
            tb = tmp[j % 2]
            self.tt('dve', tb[:], self.xT[:, j, c0:c0 + 512], self.rs[:, rb, :], ALU.mult,
                    [('xT', j, blk), ('rs', rb)], [('nm_tmp', j % 2)])
            if Bv is None:
                self.act(hm[:, j, hmcol:hmcol + 512], tb[:], AF.Identity, [('nm_tmp', j % 2)], [hmkey(j)], scale=A[:, j:j + 1])
            else:
                self.act(hm[:, j, hmcol:hmcol + 512], tb[:], AF.Identity, [('nm_tmp', j % 2)], [hmkey(j)],
                         scale=A[:, j:j + 1], bias=Bv[:, j:j + 1])

    def ffn(self, l, n):
        n3 = 0 if n == 0 else 2
        A, Bv, G = self.cA[:, l, n3, :], self.cB[:, l, n3, :], self.cG[:, l, n3, :]
        w_in = self._dr_cache['ffn_w_in'][l, n].rearrange("(kc p) f -> p kc f", p=128)
        w_out = self._dr_cache['ffn_w_out'][l, n].rearrange("(i p) d -> p i d", p=128)
        with contextlib.ExitStack() as es:
            hm = self.sb("f_hm", [128, 8, 1024], BF16, es)
            self.wi = [self.sb("wi%d" % i, [128, 8, 2, 256], BF16, es) for i in range(3)]
            self.wo = [self.sb("wo%d" % i, [128, NHT, 128], BF16, es) for i in range(3)]
            actb = self.sb("f_act", [128, NHT, 1024], BF16, es)
            sq = self.sb("f_sq", [128, 8, 512], BF16, es)
            tmp = [self.sb("f_tmp%d" % i, [128, 512], F32, es) for i in range(2)]
            sg = [self.sb("f_sg%d" % i, [128, 512], F32, es) for i in range(2)]
            pss = self.psum("f_pss", [128, 512], F32, es)
            pag = [self.psum("f_pag%d" % i, [128, 512], F32, es) for i in range(4)]
            pso = [self.psum("f_pso%d" % i, [128, 512], F32, es) for i in range(2)]
            est = {'sq': sq, 'tmp': tmp}
            npag = 0
            npso = 0
            for half in range(2):
                for bb in range(2):
                    blk = 2 * half + bb
                    self.norm_mod(blk, A, Bv, hm, 512 * bb, est, pss, lambda j, bb=bb: ('f_hm', j, bb))
                for pc in range(11):
                    slot = self.wi_n % 3
                    self.wi_n += 1
                    wb = self.wi[slot]
                    wk = ('wi', slot)
                    for ag in range(2):
                        cb = ag * DFF + 256 * pc
                        self.ld(wb[:, :, ag, :], w_in[:, :, cb:cb + 256], [wk], eng='pool')
                    for ii in range(2):
                        i = 2 * pc + ii
                        for bb in range(2):
                            pa = pag[npag % 4]
                            ka = ('f_pag', npag % 4)
                            pg = pag[(npag + 1) % 4]
                            kg = ('f_pag', (npag + 1) % 4)
                            npag += 2
                            for k in range(8):
                                self.mm(pa[:], wb[:, k, 0, 128 * ii:128 * ii + 128], hm[:, k, 512 * bb:512 * bb + 512],
                                        k == 0, k == 7, [wk, ('f_hm', k, bb)], [ka])
                            for k in range(8):
                                self.mm(pg[:], wb[:, k, 1, 128 * ii:128 * ii + 128], hm[:, k, 512 * bb:512 * bb + 512],
                                        k == 0, k == 7, [wk, ('f_hm', k, bb)], [kg])
                            sgi = (npag // 2) % 2
                            self.act(sg[sgi][:], pg[:], AF.Silu, [kg], [('f_sg', sgi)])
                            self.tt('dve', actb[:, i, 512 * bb:512 * bb + 512], sg[sgi][:], pa[:], ALU.mult,
                                    [('f_sg', sgi), ka], [('f_act', i, bb)])
                for jo in range(8):
                    slot = self.wo_n % 3
                    self.wo_n += 1
                    wb = self.wo[slot]
                    wk = ('wo', slot)
                    self.ld(wb[:], w_out[:, :, 128 * jo:128 * jo + 128], [wk], eng='pool')
                    for bb in range(2):
                        blk = 2 * half + bb
                        po = pso[npso % 2]
                        ko = ('f_pso', npso % 2)
                        npso += 1
                        for i in range(NHT):
                            self.mm(po[:], wb[:, i, :], actb[:, i, 512 * bb:512 * bb + 512], i == 0, i == NHT - 1,
                                    [wk, ('f_act', i, bb)], [ko])
                        xs = self.xT[:, jo, 512 * blk:512 * blk + 512]
                        self.stt('dve', xs, po[:], G[:, jo:jo + 1], xs, ALU.mult, ALU.add,
                                 [ko, ('xT', jo, blk)], [('xT', jo, blk)])

    def final(self):
        with contextlib.ExitStack() as es:
            yb = [self.sb("fin_y%d" % i, [128, 8, 512], F32, es) for i in range(1)]
            sq = self.sb("fin_sq", [128, 8, 512], BF16, es)
            tmp = [self.sb("fin_tmp%d" % i, [128, 512], F32, es) for i in range(2)]
            yt = [self.sb("fin_yt%d" % i, [128, D], F32, es) for i in range(2)]
            pss = self.psum("fin_pss", [128, 512], F32, es)
            ps = [self.psum("fin_ps%d" % i, [128, 512], F32, es) for i in range(4)]
            est = {'sq': sq, 'tmp': tmp}
            dst = self.y.rearrange("(b p t) d -> b t p d", b=2, t=8)
            n = 0
            for blk in range(4):
                self.norm_mod(blk, self.cF, None, yb[0], 0, est, pss, lambda j: ('fin_y', j))
                for tt4 in range(4):
                    tile = 4 * blk + tt4
                    b, t = tile // 8, tile % 8
                    ytb = yt[n % 2]
                    yk = ('fin_yt', n % 2)
                    for h in range(2):
                        pb = ps[(2 * n + h) % 4]
                        pk = ('fin_ps', (2 * n + h) % 4)
                        for jj in range(4):
                            j = 4 * h + jj
                            self.tr(pb[:, 128 * jj:128 * jj + 128], yb[0][:, j, 128 * tt4:128 * tt4 + 128], self.identf[:],
                                    [('fin_y', j), 'identf'], [pk])
                        self.copy(self.evac_eng(), ytb[:, 512 * h:512 * h + 512], pb[:], [pk], [yk])
                    self.st(dst[b, t], ytb[:], [yk])
                    n += 1

    def mixer(self, l):
        S = self.S
        A, Bv, G = self.cA[:, l, 1, :], self.cB[:, l, 1, :], self.cG[:, l, 1, :]
        with contextlib.ExitStack() as es:
            hm = self.sb("m_hm", [128, 8, NT], BF16, es)
            mo = None
            self.G2 = G
            with contextlib.ExitStack() as es2:
                sq = self.sb("m_sq", [128, 8, 512], BF16, es2)
                tmp = [self.sb("m_tmp%d" % i, [128, 512], F32, es2) for i in range(2)]
                pss = self.psum("m_pss", [128, 512], F32, es2)
                for blk in range(4):
                    self.norm_mod(blk, A, Bv, hm, 512 * blk, {'sq': sq, 'tmp': tmp}, pss,
                                  lambda j, blk=blk: ('m_hm', j, blk))
            import os
            mix = int(os.environ.get('MIX', '3'))
            S.barrier()
            if mix & 1:
                self.s5(l, hm, mo)
            S.barrier()
            if mix & 2:
                self.attn(l, hm, mo)

    def wout_part(self, l, ktiles, srcs, wbufs, psl, pskeys):
        wv = self._ap('w_out')[l].rearrange("(k p) d -> p k d", p=128)
        G = self.G2
        for i, kt in enumerate(ktiles):
            wb, wk = wbufs[i]
            self.ld(wb, wv[:, kt, :], [wk], eng='pool')
        n = 0
        for jo in range(8):
            for blk in range(4):
                po, pk = psl[n % len(psl)], pskeys[n % len(psl)]
                n += 1
                for i, kt in enumerate(ktiles):
                    wb, wk = wbufs[i]
                    src, kf = srcs[i]
                    self.mm(po[:], wb[:, 128 * jo:128 * jo + 128], src[:, 512 * blk:512 * blk + 512], i == 0, i == len(ktiles) - 1,
                            [wk, kf(blk)], [pk])
                xs = self.xT[:, jo, 512 * blk:512 * blk + 512]
                self.stt('dve', xs, po[:], G[:, jo:jo + 1], xs, ALU.mult, ALU.add, [pk, ('xT', jo, blk)], [('xT', jo, blk)])

    def cmul(self, outr, outi, ar, ai, br, bi, t1, t2, key_r, key_w, neg_im=False, eng='dve'):
        rd = list(key_r)
        self.tt(eng, t1, ar, br, ALU.mult, rd, ['cm_t1'])
        self.tt(eng, t2, ai, bi, ALU.mult, rd, ['cm_t2'])
        self.tt(eng, outr, t1, t2, ALU.subtract, ['cm_t1', 'cm_t2'] + rd, list(key_w))
        self.tt(eng, t1, ar, bi, ALU.mult, rd + list(key_w), ['cm_t1'])
        self.tt(eng, t2, ai, br, ALU.mult, rd + list(key_w), ['cm_t2'])
        if neg_im:
            self.stt(eng, outi, t1, -1.0, t2, ALU.mult, ALU.subtract, ['cm_t1', 'cm_t2'], list(key_w))
        else:
            self.tt(eng, outi, t1, t2, ALU.add, ['cm_t1', 'cm_t2'], list(key_w))

    def s5(self, l, hm, mo):
        S = self.S
        dr = self._dr_cache
        PI = math.pi
        with contextlib.ExitStack() as es:
            ToepT = [self.sb("s_toep%d" % d, [128, 16, 128], BF16, es) for d in range(2)]
            BSm = [self.sb("s_bsm%d" % d, [128, 16, 2, 64], BF16, es) for d in range(2)]
            CCm = [self.sb("s_ccm%d" % d, [128, 8, 2, 128], BF16, es) for d in range(2)]
            PWr = [self.sb("s_pwr%d" % d, [128, 8, 33], F32, es) for d in range(2)]
            PWi = [self.sb("s_pwi%d" % d, [128, 8, 33], F32, es) for d in range(2)]
            PWin = [self.sb("s_pwin%d" % d, [128, 8, 33], F32, es) for d in range(2)]
            h0r = [self.sb("s_h0r%d" % d, [128, 8], F32, es) for d in range(2)]
            h0i = [self.sb("s_h0i%d" % d, [128, 8], F32, es) for d in range(2)]
            U = self.sb("s_U", [128, 16, 256], BF16, es)
            flag = self.sb("s_flag", [128, 1], F32, es)
            self.ld(flag[:], dr['flag'], ['s_flag'])
            with contextlib.ExitStack() as ea:
                def t4(name):
                    return self.sb(name, [128, 8, 8, 16], F32, ea)
                cm1, cm2 = t4("sa_cm1"), t4("sa_cm2")
                BLr, BLi, CLr, CLi = t4("sa_blr"), t4("sa_bli"), t4("sa_clr"), t4("sa_cli")
                BSr, BSi, CCr, CCi = t4("sa_bsr"), t4("sa_bsi"), t4("sa_ccr"), t4("sa_cci")
                sm = self.sb("sa_sm", [128, 40, 8], F32, ea)
                bre = self.sb("sa_bre", [128, 8, 16], F32, ea)
                bim = self.sb("sa_bim", [128, 8, 16], F32, ea)
                bbr = self.sb("sa_bbr", [128, 8, 16], F32, ea)
                bbi = self.sb("sa_bbi", [128, 8, 16], F32, ea)
                cre = self.sb("sa_cre", [128, 8, 16], F32, ea)
                cim = self.sb("sa_cim", [128, 8, 16], F32, ea)
                crow = self.sb("sa_crow", [128, 2, 2, 64], F32, ea)
                Pr = self.sb("sa_Pr", [128, 8, 9], F32, ea)
                Pi_ = self.sb("sa_Pi", [128, 8, 9], F32, ea)
                Nr = self.sb("sa_Nr", [128, 8, 8], F32, ea)
                Ni = self.sb("sa_Ni", [128, 8, 8], F32, ea)
                Dcol = self.sb("sa_Dcol", [128, 16], F32, ea)
                cmk = [self.sb("sa_cmk%d" % d, [128, 128], F32, ea) for d in range(2)]
                tT = self.sb("sa_tT", [128, 128], F32, ea)
                cst = self.sb("sa_cst", [128, 2], F32, ea)
                psA = [self.psum("sa_ps%d" % i, [128, 512], F32, ea) for i in range(4)]
                S.op('dve', lambda e: e.memset(cst[:, 0:1], -PI), (), ['sa_cst'])
                self.ld(cmk[0][:], dr['cmask_f'], ['sa_cmk'])
                self.ld(cmk[1][:], dr['cmask_b'], ['sa_cmk'])
                for t in range(8):
                    self.ld(Dcol[16 * t:16 * t + 16, :], dr['s5_d'][l].rearrange("g c -> c g"), ['sa_Dcol'],
                            allow_slow_non_contiguous=True)
                npsA = 0
                for d in range(2):
                    K = ['sa']
                    are, aim, lst = sm[:, 0, :], sm[:, 1, :], sm[:, 2, :]
                    for gl in range(2):
                        ps_ = slice(64 * gl, 64 * gl + 64)
                        self.ld(sm[ps_, 0, :], dr['s5_a_re'][l, d, gl].rearrange("g p -> p g"), K, K, allow_slow_non_contiguous=True)
                        self.ld(sm[ps_, 1, :], dr['s5_a_im'][l, d, gl].rearrange("g p -> p g"), K, K, allow_slow_non_contiguous=True)
                        self.ld(sm[ps_, 2, :], dr['s5_log_step'][l, d, gl].partition_broadcast(64), K, K)
                        self.ld(h0r[d][ps_, :], dr['h0re'][l, d, gl].rearrange("g p -> p g"), K, K, allow_slow_non_contiguous=True)
                        self.ld(h0i[d][ps_, :], dr['h0im'][l, d, gl].rearrange("g p -> p g"), K, K, allow_slow_non_contiguous=True)
                        self.ld(bre[ps_, :, :], dr['s5_b_re'][l, d, gl].rearrange("g p c -> p g c"), K, K)
                        self.ld(bim[ps_, :, :], dr['s5_b_im'][l, d, gl].rearrange("g p c -> p g c"), K, K)
                        self.ld(crow[:, gl, 0, :], dr['s5_c_re'][l, d, gl], K, K)
                        self.ld(crow[:, gl, 1, :], dr['s5_c_im'][l, d, gl], K, K)
                    pc, pck = psA[npsA % 4], ('sa_ps', npsA % 4)
                    npsA += 1
                    for gl in range(2):
                        for ri in range(2):
                            self.mm(pc[64 * gl:64 * gl + 64, 128 * ri:128 * ri + 128], crow[:, gl, ri, :], self.identf[:], True, True, K + ['identf'], [pck])
                    self.copy('dve', cre[:].rearrange("p g c -> p (g c)"), pc[:, 0:128], [pck], K)
                    self.copy('dve', cim[:].rearrange("p g c -> p (g c)"), pc[:, 128:256], [pck], K)

                    import os
                    s5a = int(os.environ.get('S5A', '9'))
                    if s5a <= 1:
                        S.barrier()
                        return

                    def sop(fn):
                        S.op('dve', fn, K, K)
                    sl = lambda i: sm[:, i, :]
                    self.act(sl(3), lst, AF.Exp, K, K)
                    self.tt('dve', sl(4), are, sl(3), ALU.mult, K, K)
                    self.tt('dve', sl(5), aim, sl(3), ALU.mult, K, K)
                    self.act(sl(6), sl(4), AF.Exp, K, K)
                    self.act(sl(7), sl(4), AF.Exp, K, K, scale=-2.0)
                    MAGIC = 12582912.0
                    for (dst_i, off) in ((9, 0.0), (10, 0.25)):
                        self.ts('dve', sl(8), sl(5), 1.0 / (2 * PI), None, ALU.mult, None, K, K)
                        if off:
                            self.ts('dve', sl(8), sl(8), off, None, ALU.add, None, K, K)
                        self.ts('dve', sl(22), sl(8), MAGIC, None, ALU.add, None, K, K)
                        self.ts('dve', sl(22), sl(22), -MAGIC, None, ALU.add, None, K, K)
                        self.tt('dve', sl(8), sl(8), sl(22), ALU.subtract, K, K)
                        self.act(sl(dst_i), sl(8), AF.Sin, K, K, scale=2 * PI)
                    lr, li = sl(11), sl(12)
                    self.tt('dve', lr, sl(6), sl(10), ALU.mult, K, K)
                    self.tt('dve', li, sl(6), sl(9), ALU.mult, K, K)
                    ilr, ili = sl(13), sl(14)
                    self.tt('dve', ilr, lr, sl(7), ALU.mult, K, K)
                    self.stt('dve', ili, li, -1.0, sl(7), ALU.mult, ALU.mult, K, K)
                    self.tt('dve', sl(15), are, are, ALU.mult, K, K)
                    self.tt('dve', sl(16), aim, aim, ALU.mult, K, K)
                    self.tt('dve', sl(15), sl(15), sl(16), ALU.add, K, K)
                    sop(lambda e: e.reciprocal(out=sl(15), in_=sl(15)))
                    self.ts('dve', sl(16), lr, -1.0, None, ALU.add, None, K, K)
                    self.tt('dve', sl(17), sl(16), are, ALU.mult, K, K)
                    self.tt('dve', sl(18), li, aim, ALU.mult, K, K)
                    self.tt('dve', sl(17), sl(17), sl(18), ALU.add, K, K)
                    self.tt('dve', sl(17), sl(17), sl(15), ALU.mult, K, K)
                    self.tt('dve', sl(18), li, are, ALU.mult, K, K)
                    self.tt('dve', sl(19), sl(16), aim, ALU.mult, K, K)
                    self.tt('dve', sl(18), sl(18), sl(19), ALU.subtract, K, K)
                    self.tt('dve', sl(18), sl(18), sl(15), ALU.mult, K, K)
                    kb = lambda i: sm[:, i, :].unsqueeze(2).to_broadcast([128, 8, 16])
                    self.cmul(bbr[:], bbi[:], kb(17), kb(18), bre[:], bim[:], cm1[:, :, 0, :], cm2[:, :, 0, :], K, K)
                    sop(lambda e: e.memset(Pr[:, :, 0:1], 1.0))
                    sop(lambda e: e.memset(Pi_[:, :, 0:1], 0.0))
                    sop(lambda e: e.memset(Nr[:, :, 0:1], 1.0))
                    sop(lambda e: e.memset(Ni[:, :, 0:1], 0.0))
                    for k in range(1, 9):
                        self.cmul(Pr[:, :, k], Pi_[:, :, k], Pr[:, :, k - 1], Pi_[:, :, k - 1], lr, li, sl(20), sl(21), K, K)
                    for k in range(1, 8):
                        self.cmul(Nr[:, :, k], Ni[:, :, k], Nr[:, :, k - 1], Ni[:, :, k - 1], ilr, ili, sl(20), sl(21), K, K)
                    pr, pi = PWr[d], PWi[d]
                    self.copy('dve', pr[:, :, 1], Pr[:, :, 8], K, K)
                    self.copy('dve', pi[:, :, 1], Pi_[:, :, 8], K, K)
                    n = 1
                    while n < 32:
                        bshape = [128, 8, n]
                        self.cmul(pr[:, :, n + 1:2 * n + 1], pi[:, :, n + 1:2 * n + 1], pr[:, :, 1:n + 1], pi[:, :, 1:n + 1],
                                  pr[:, :, n:n + 1].to_broadcast(bshape), pi[:, :, n:n + 1].to_broadcast(bshape),
                                  cm1[:].rearrange("p a b c -> p a (b c)")[:, :, 0:n], cm2[:].rearrange("p a b c -> p a (b c)")[:, :, 0:n], K, K)
                        n *= 2
                    self.ts('dve', PWin[d][:], pi[:], -1.0, None, ALU.mult, None, K, K)
                    bsh = [128, 8, 8, 16]
                    bbR = bbr[:].unsqueeze(2).to_broadcast(bsh)
                    bbI = bbi[:].unsqueeze(2).to_broadcast(bsh)
                    cR = cre[:].unsqueeze(2).to_broadcast(bsh)
                    cI = cim[:].unsqueeze(2).to_broadcast(bsh)
                    pw = lambda T, sl_: T[:, :, sl_].unsqueeze(3).to_broadcast(bsh)
                    if d == 0:
                        self.cmul(BLr[:], BLi[:], bbR, bbI, pw(Nr, slice(0, 8)), pw(Ni, slice(0, 8)), cm1[:], cm2[:], K, K)
                        self.cmul(CLr[:], CLi[:], cR, cI, pw(Pr, slice(0, 8)), pw(Pi_, slice(0, 8)), cm1[:], cm2[:], K, K, neg_im=True)
                        self.cmul(BSr[:], BSi[:], bbR, bbI, pw(Pr, slice(7, None, -1)), pw(Pi_, slice(7, None, -1)), cm1[:], cm2[:], K, K)
                        self.cmul(CCr[:], CCi[:], cR, cI, pw(Pr, slice(1, 9)), pw(Pi_, slice(1, 9)), cm1[:], cm2[:], K, K, neg_im=True)
                    else:
                        self.cmul(BLr[:], BLi[:], bbR, bbI, pw(Pr, slice(0, 8)), pw(Pi_, slice(0, 8)), cm1[:], cm2[:], K, K)
                        self.cmul(CLr[:], CLi[:], cR, cI, pw(Nr, slice(0, 8)), pw(Ni, slice(0, 8)), cm1[:], cm2[:], K, K, neg_im=True)
                        self.copy('dve', BSr[:], BLr[:], K, K)
                        self.copy('dve', BSi[:], BLi[:], K, K)
                        self.cmul(CCr[:], CCi[:], cR, cI, pw(Pr, slice(8, 0, -1)), pw(Pi_, slice(8, 0, -1)), cm1[:], cm2[:], K, K, neg_im=True)
                    f2 = lambda T: T[:].rearrange("p a b c -> p a (b c)")
                    if s5a <= 2:
                        S.barrier()
                        return
                    for g in range(16):
                        gl, gh = g % 2, g // 2
                        ps_ = slice(64 * gl, 64 * gl + 64)
                        pt, ptk = psA[npsA % 4], ('sa_ps', npsA % 4)
                        npsA += 1
                        self.mm(pt[:, 0:128], f2(BLr)[ps_, gh, :], f2(CLr)[ps_, gh, :], True, False, K, [ptk])
                        self.mm(pt[:, 0:128], f2(BLi)[ps_, gh, :], f2(CLi)[ps_, gh, :], False, True, K, [ptk])
                        if d == 0:
                            self.tt('dve', tT[:], pt[:, 0:128], cmk[0][:], ALU.mult, [ptk, 'sa_cmk'], ['sa_tT'])
                            self.stt('dve', ToepT[0][:, g, :], self.identf[:], Dcol[:, g:g + 1], tT[:], ALU.mult, ALU.add,
                                     ['sa_tT', 'sa_Dcol', 'identf'], [('s_toep', 0)])
                        else:
                            self.tt('dve', ToepT[1][:, g, :], pt[:, 0:128], cmk[1][:], ALU.mult, [ptk, 'sa_cmk'], [('s_toep', 1)])
                    if s5a <= 3:
                        S.barrier()
                        return
                    for ri, T in enumerate((BSr, BSi)):
                        for g8 in range(2):
                            pt, ptk = psA[npsA % 4], ('sa_ps', npsA % 4)
                            npsA += 1
                            for hh in range(4):
                                gh = 4 * g8 + hh
                                self.tr(pt[:, 128 * hh:128 * hh + 128], f2(T)[:, gh, :], self.identf[:], K + ['identf'], [ptk])
                            self.copy('act', BSm[d][:, 8 * g8:8 * g8 + 8, ri, :], pt[:].rearrange("p (g q) -> p g q", g=8), [ptk], [('s_bsm', d)])
                    if s5a <= 4:
                        S.barrier()
                        return
                    self.copy('dve', CCm[d][:, :, 0, :], f2(CCr), K, [('s_ccm', d)])
                    self.copy('dve', CCm[d][:, :, 1, :], f2(CCi), K, [('s_ccm', d)])
                    if s5a <= 5:
                        S.barrier()
                        return
            S.barrier()
            import os
            s5stop = os.environ.get('S5STOP', 'Z')
            if s5stop == 'A':
                return
            with contextlib.ExitStack() as eb:
                wu = self.sb("sb_wu", [128, 8, 256], BF16, eb)
                ub = self.sb("sb_ub", [128, 2, 16, 8, 16], BF16, eb)
                psu = [self.psum("sb_psu%d" % i, [128, 512], F32, eb) for i in range(2)]
                pst = [self.psum("sb_pst%d" % i, [128, 1024], BF16, eb) for i in range(2)]
                self.ld(wu[:], self._ap('w_in')[l].rearrange("(k p) f -> p k f", p=128)[:, :, 1568:1824], ['sb_wu'], eng='pool')
                n = 0
                for b in range(2):
                    for t in range(8):
                        pos = 1024 * b + 128 * t
                        pu, puk = psu[n % 2], ('sb_psu', n % 2)
                        n += 1
                        for k in range(8):
                            self.mm(pu[:, 0:256], hm[:, k, pos:pos + 128], wu[:, k, :], k == 0, k == 7,
                                    ['sb_wu', ('m_hm', k, pos // 512)], [puk])
                        self.copy(self.evac_eng(), ub[:, b, :, t, :], pu[:, 0:256].rearrange("p (g c) -> p g c", g=16), [puk], [('sb_ub', b)])
                n = 0
                for b in range(2):
                    for q in range(4):
                        pt, ptk = pst[n % 2], ('sb_pst', n % 2)
                        n += 1
                        for gg in range(4):
                            g = 4 * q + gg
                            self.tr(pt[:, 128 * gg:128 * gg + 128], ub[:, b, g, :, :].rearrange("p t c -> p (t c)"), self.identb[:],
                                    [('sb_ub', b), 'identb'], [ptk])
                        self.copy(self.evac_eng(), U[:, 4 * q:4 * q + 4, 128 * b:128 * b + 128],
                                  pt[:, 0:512].rearrange("p (g c) -> p g c", g=4), [ptk], ['s_U'])
            S.barrier()
            if s5stop == 'B':
                return
            with contextlib.ExitStack() as ec:
                Hp = [[self.sb("sc_hp%d%d" % (d, ri), [128, 8, 256], BF16, ec) for ri in range(2)] for d in range(2)]
                with contextlib.ExitStack() as ec2:
                    La = [self.sb("sc_la%d" % ri, [128, 8, 256], F32, ec2) for ri in range(2)]
                    Lb = [self.sb("sc_lb%d" % ri, [128, 8, 256], F32, ec2) for ri in range(2)]
                    Cy = [self.sb("sc_cy%d" % ri, [128, 8, 8], F32, ec2) for ri in range(2)]
                    sm2 = self.sb("sc_sm", [128, 4, 8], F32, ec2)
                    Fsb = self.sb("sc_F", [128, 2, 64], F32, ec2)
                    FT = self.sb("sc_FT", [64, 2, 128], F32, ec2)
                    psS = [[self.psum("sc_ps%d%d" % (ri, q), [128, 512], F32, ec2) for q in range(3)] for ri in range(2)]
                    psF = self.psum("sc_psF", [128, 512], F32, ec2)
                    for d in range(2):
                        KL = ['sc_L']
                        for ri in range(2):
                            for q4 in range(4):
                                pb, pbk = psS[ri][q4 % 3], ('sc_ps', ri, q4 % 3)
                                for hh in range(2):
                                    gh = 2 * q4 + hh
                                    for gl in range(2):
                                        g = 2 * gh + gl
                                        self.mm(pb[64 * gl:64 * gl + 64, 256 * hh:256 * hh + 256], BSm[d][:, g, ri, :], U[:, g, :], True, True,
                                                [('s_bsm', d), 's_U'], [pbk])
                                self.copy(self.evac_eng(), La[ri][:, 2 * q4:2 * q4 + 2, :], pb[:].rearrange("p (a c) -> p a c", a=2), [pbk], KL)
                        cur, nxt = La, Lb
                        v5 = lambda T: T[:].rearrange("p a (s k) -> p a s k", k=32)
                        for dd in (1, 2, 4, 8, 16):
                            for ri in range(2):
                                if d == 0:
                                    self.copy('act', v5(nxt[ri])[:, :, :, 0:dd], v5(cur[ri])[:, :, :, 0:dd], KL, KL)
                                else:
                                    self.copy('act', v5(nxt[ri])[:, :, :, 32 - dd:32], v5(cur[ri])[:, :, :, 32 - dd:32], KL, KL)
                            for gh in range(8):
                                vv = lambda T: T[:, gh, :].rearrange("p (s k) -> p s k", k=32)
                                if d == 0:
                                    dst = slice(dd, 32)
                                    src = slice(0, 32 - dd)
                                else:
                                    dst = slice(0, 32 - dd)
                                    src = slice(dd, 32)
                                lr_ = PWr[d][:, gh, dd:dd + 1]
                                li_ = PWi[d][:, gh, dd:dd + 1]
                                lin_ = PWin[d][:, gh, dd:dd + 1]
                                self.stt('dve', vv(nxt[0])[:, :, dst], vv(cur[0])[:, :, src], lr_, vv(cur[0])[:, :, dst], ALU.mult, ALU.add, KL, KL)
                                self.stt('dve', vv(nxt[0])[:, :, dst], vv(cur[1])[:, :, src], lin_, vv(nxt[0])[:, :, dst], ALU.mult, ALU.add, KL, KL)
                                self.stt('dve', vv(nxt[1])[:, :, dst], vv(cur[0])[:, :, src], li_, vv(cur[1])[:, :, dst], ALU.mult, ALU.add, KL, KL)
                                self.stt('dve', vv(nxt[1])[:, :, dst], vv(cur[1])[:, :, src], lr_, vv(nxt[1])[:, :, dst], ALU.mult, ALU.add, KL, KL)
                            cur, nxt = nxt, cur
                        L = cur
                        E = [v5(L[ri])[:, :, :, 31 if d == 0 else 0] for ri in range(2)]
                        l32r, l32i = PWr[d][:, :, 32], PWi[d][:, :, 32]
                        order = list(range(8)) if d == 0 else list(range(7, -1, -1))
                        s0 = order[0]
                        self.copy('dve', Cy[0][:, :, s0], h0r[d][:], KL, KL)
                        self.copy('dve', Cy[1][:, :, s0], h0i[d][:], KL, KL)
                        for idx in range(1, 8):
                            s, sp_ = order[idx], order[idx - 1]
                            self.cmul(sm2[:, 0, :], sm2[:, 1, :], l32r, l32i, Cy[0][:, :, sp_], Cy[1][:, :, sp_], sm2[:, 2, :], sm2[:, 3, :], KL, KL)
                            for ri in range(2):
                                self.tt('dve', sm2[:, ri, :], sm2[:, ri, :], E[ri][:, :, sp_], ALU.add, KL, KL)
                                self.ts('dve', Cy[ri][:, :, s], sm2[:, ri, :], flag[:, 0:1], None, ALU.mult, None, KL + ['s_flag'], KL)
                        sh4 = [128, 8, 8, 32]
                        if d == 0:
                            pwv = lambda T: T[:, :, 1:33].unsqueeze(2).to_broadcast(sh4)
                        else:
                            pwv = lambda T: T[:, :, 32:0:-1].unsqueeze(2).to_broadcast(sh4)
                        cyv = lambda ri: Cy[ri][:].unsqueeze(3).to_broadcast(sh4)
                        t1v = v5(nxt[0])
                        for (ri, a, b_, op) in ((0, PWr[d], 0, ALU.add), (0, PWi[d], 1, ALU.subtract), (1, PWr[d], 1, ALU.add), (1, PWi[d], 0, ALU.add)):
                            self.tt('dve', t1v, pwv(a), cyv(b_), ALU.mult, KL, ['sc_t1'])
                            self.tt('dve', v5(L[ri]), v5(L[ri]), t1v, op, KL + ['sc_t1'], KL)
                        for ri in range(2):
                            hv = v5(Hp[d][ri])
                            if d == 0:
                                self.copy('act', hv[:, :, :, 1:32], v5(L[ri])[:, :, :, 0:31], KL, [('sc_hp', d)])
                                self.copy('dve', hv[:, :, :, 0], Cy[ri][:], KL, [('sc_hp', d)])
                            else:
                                self.copy('act', hv[:, :, :, 0:31], v5(L[ri])[:, :, :, 1:32], KL, [('sc_hp', d)])
                                self.copy('dve', hv[:, :, :, 31], Cy[ri][:], KL, [('sc_hp', d)])
                        for ri in range(2):
                            self.copy('dve', Fsb[:, ri, :].rearrange("p (a s) -> p a s", a=8), E[ri], KL, ['sc_F'])
                            self.tr(psF[0:64, 128 * ri:128 * ri + 128], Fsb[:, ri, :], self.identf[:], ['sc_F', 'identf'], ['sc_psF'])
                        self.copy('dve', FT[:].rearrange("p a b -> p (a b)"), psF[0:64, 0:256], ['sc_psF'], ['sc_FT'])
                        for ri, nm in enumerate(('ns5re', 'ns5im')):
                            for gh in range(8):
                                self.st(self.o[nm][l, d][:, 128 * gh:128 * gh + 128], FT[8 * gh:8 * gh + 8, ri, :], ['sc_FT'])
                S.barrier()
                if s5stop == 'C':
                    return
                with contextlib.ExitStack() as ed:
                    Ysb = self.sb("sd_Y", [128, 16, 256], BF16, ed)
                    g1 = [self.sb("sd_g%d" % i, [128, 512], F32, ed) for i in range(2)]
                    psY = [self.psum("sd_ps%d" % i, [128, 512], F32, ed) for i in range(3)]
                    for q in range(8):
                        py, pyk = psY[q % 3], ('sd_ps', q % 3)
                        for hh in range(2):
                            g = 2 * q + hh
                            gl, gh = g % 2, g // 2
                            ps_ = slice(64 * gl, 64 * gl + 64)
                            o = py[:, 256 * hh:256 * hh + 256]
                            for d in range(2):
                                self.mm(o, ToepT[d][:, g, :], U[:, g, :], d == 0, False, [('s_toep', d), 's_U'], [pyk])
                                self.mm(o, CCm[d][ps_, gh, 0, :], Hp[d][0][ps_, gh, :], False, False, [('s_ccm', d), ('sc_hp', d)], [pyk])
                                self.mm(o, CCm[d][ps_, gh, 1, :], Hp[d][1][ps_, gh, :], False, d == 1, [('s_ccm', d), ('sc_hp', d)], [pyk])
                        gb, gk = g1[q % 2], ('sd_g', q % 2)
                        self.act(gb[:], py[:], AF.Square, [pyk], [gk])
                        self.ts('dve', gb[:], gb[:], 0.044715, None, ALU.mult, None, [gk], [gk])
                        self.ts('dve', gb[:], gb[:], 1.0, None, ALU.add, None, [gk], [gk])
                        self.tt('dve', gb[:], gb[:], py[:], ALU.mult, [gk, pyk], [gk])
                        self.act(gb[:], gb[:], AF.Sigmoid, [gk], [gk], scale=1.5957691216)
                        self.tt('dve', Ysb[:, 2 * q:2 * q + 2, :], gb[:].rearrange("p (a c) -> p a c", a=2),
                                py[:].rearrange("p (a c) -> p a c", a=2), ALU.mult, [gk, pyk], ['sd_Y'])
                    S.barrier()
                    ytok = self.sb("se_ytok", [128, 2, 8, 256], BF16, ed)
                    ygT = self.sb("se_ygT", [128, 2, NT], BF16, ed)
                    mos = self.sb("se_mo", [128, 2, NT], BF16, ed)
                    wob = [self.sb("se_wo%d" % i, [128, D], BF16, ed) for i in range(2)]
                    wg = self.sb("se_wg", [128, 2, 256], BF16, ed)
                    bg = self.sb("se_bg", [128, 2], F32, ed)
                    sg = [self.sb("se_sg%d" % i, [128, 512], F32, ed) for i in range(2)]
                    pst = [self.psum("se_pst%d" % i, [128, 1024], BF16, ed) for i in range(2)]
                    psg = [self.psum("se_psg%d" % i, [128, 512], F32, ed) for i in range(2)]
                    self.ld(wg[:], dr['s5_w_glu'][l].rearrange("(k p) f -> p k f", p=128), ['se_wg'], eng='pool')
                    self.ld(bg[:], dr['s5_b_glu'][l].rearrange("a p -> p a"), ['se_bg'], allow_slow_non_contiguous=True)
                    n = 0
                    for b in range(2):
                        for q in range(4):
                            pt, ptk = pst[n % 2], ('se_pst', n % 2)
                            n += 1
                            for gg in range(4):
                                g = 4 * q + gg
                                self.tr(pt[:, 128 * gg:128 * gg + 128], Ysb[:, g, 128 * b:128 * b + 128], self.identb[:], ['sd_Y', 'identb'], [ptk])
                            dst = ytok[:, b].rearrange("p t (g c) -> p g t c", c=16)[:, 4 * q:4 * q + 4]
                            self.copy(self.evac_eng(), dst, pt[:, 0:512].rearrange("p (g t c) -> p g t c", g=4, t=8), [ptk], [('se_ytok', b)])
                    for b in range(2):
                        for f in range(2):
                            for h4 in range(2):
                                pt, ptk = pst[n % 2], ('se_pst', n % 2)
                                n += 1
                                for tt_ in range(4):
                                    t = 4 * h4 + tt_
                                    self.tr(pt[:, 128 * tt_:128 * tt_ + 128], ytok[:, b, t, 128 * f:128 * f + 128], self.identb[:],
                                            [('se_ytok', b), 'identb'], [ptk])
                                blk = 2 * b + h4
                                self.copy(self.evac_eng(), ygT[:, f, 512 * blk:512 * blk + 512], pt[:, 0:512], [ptk], [('se_ygT', f, blk)])
                    n = 0
                    for fo in range(2):
                        for blk in range(4):
                            pg, pgk = psg[n % 2], ('se_psg', n % 2)
                            sgb, sgk = sg[n % 2], ('se_sg', n % 2)
                            n += 1
                            for k in range(2):
                                self.mm(pg[:], wg[:, k, 128 * fo:128 * fo + 128], ygT[:, k, 512 * blk:512 * blk + 512], k == 0, k == 1,
                                        ['se_wg', ('se_ygT', k, blk)], [pgk])
                            self.act(sgb[:], pg[:], AF.Sigmoid, [pgk, 'se_bg'], [sgk], bias=bg[:, fo:fo + 1])
                            self.tt('dve', mos[:, fo, 512 * blk:512 * blk + 512], sgb[:], ygT[:, fo, 512 * blk:512 * blk + 512], ALU.mult,
                                    [sgk, ('se_ygT', fo, blk)], [('se_mo', fo, blk)])
                    self.wout_part(l, [6, 7], [(mos[:, fo, :], (lambda blk, fo=fo: ('se_mo', fo, blk))) for fo in range(2)],
                                   [(wob[i][:], ('se_wo', i)) for i in range(2)], psg, [('se_psg', 0), ('se_psg', 1)])

    def rope(self, src5, dsts, tt, nb, tmps, rkey, wkeys):
        b, t = tt // 8, tt % 8
        tk = ['rp_t']
        if nb == 1:
            cosb = self.rc[:, b, t, :].rearrange("p (a f) -> p a f", a=2)
            sinb = self.rsn[:, b, t, :].rearrange("p (a f) -> p a f", a=2)
            x1, x2 = src5[:, 0, :, 0, :], src5[:, 0, :, 1, :]
            t1, t2, t3, t4 = [T[:, 0:16].rearrange("p (a f) -> p a f", a=2) for T in tmps]
        else:
            sh = [128, nb, 2, 8]
            cosb = self.rc[:, b, t, :].rearrange("p (a f) -> p a f", a=2).unsqueeze(1).to_broadcast(sh)
            sinb = self.rsn[:, b, t, :].rearrange("p (a f) -> p a f", a=2).unsqueeze(1).to_broadcast(sh)
            x1, x2 = src5[:, :, :, 0, :], src5[:, :, :, 1, :]
            t1, t2, t3, t4 = [T[:, 0:nb * 16].rearrange("p (n a f) -> p n a f", a=2, f=8) for T in tmps]
        self.tt('dve', t1, x1, cosb, ALU.mult, rkey + ['rope_tab'], tk)
        self.tt('dve', t2, x2, sinb, ALU.mult, rkey + ['rope_tab'], tk)
        self.tt('dve', t3, x1, sinb, ALU.mult, rkey + ['rope_tab'], tk)
        self.tt('dve', t4, x2, cosb, ALU.mult, rkey + ['rope_tab'], tk)
        for (bs, dst5), wk in zip(dsts, wkeys):
            if nb == 1:
                self.tt('dve', dst5[:, 0, :, 0, :], t1, t2, ALU.subtract, tk, [wk])
                self.tt('dve', dst5[:, 0, :, 1, :], t3, t4, ALU.add, tk, [wk])
            else:
                self.tt('dve', dst5[:, :, :, 0, :], t1[:, bs], t2[:, bs], ALU.subtract, tk, [wk])
                self.tt('dve', dst5[:, :, :, 1, :], t3[:, bs], t4[:, bs], ALU.add, tk, [wk])

    def attn(self, l, hm, mo):
        S = self.S
        dr = self._dr_cache
        lam_init = 0.8 - 0.6 * math.exp(-0.3 * l)
        with contextlib.ExitStack() as es:
            sb = lambda n, s, d: self.sb(n, s, d, es)
            self.rc = sb("a_rc", [128, 2, 8, 16], F32)
            self.rsn = sb("a_rs", [128, 2, 8, 16], F32)
            aq = sb("a_aq", [128, 16, 9], F32)
            ak = sb("a_ak", [128, 18, 9], F32)
            QS = sb("a_QS", [128, 16, 128], BF16)
            KS = sb("a_KS", [128, 18, 128], BF16)
            QT = [sb("a_QT%d" % i, [128, NT], BF16) for i in range(2)]
            KT = [sb("a_KT%d" % i, [128, NT + 256], BF16) for i in range(2)]
            Vh = [sb("a_V%d" % i, [128, 18, 72], BF16) for i in range(2)]
            PT = [sb("a_PT%d" % i, [128, 512], BF16) for i in range(4)]
            moh = [sb("a_moh%d" % i, [128, NT], BF16) for i in range(2)]
            wob = [sb("a_wo%d" % i, [128, D], BF16) for i in range(2)]
            rt = [sb("a_rt%d" % i, [128, 64], F32) for i in range(4)]
            o0 = sb("a_o0", [128, 4, 64], F32)
            o1 = sb("a_o1", [128, 4, 64], F32)
            osq = sb("a_osq", [128, 4, 64], F32)
            ost = sb("a_ost", [128, 4, 64], BF16)
            sml = sb("a_sml", [128, 16], F32)
            dl = sb("a_dl", [128, 128], F32)
            subw = sb("a_subw", [128, 64], F32)
            lamt = sb("a_lam", [128, 4], F32)
            oTs = [sb("a_oTs%d" % i, [128, 512], F32) for i in range(2)]
            edf = contextlib.ExitStack()
            wh = [self.sb("a_wh%d" % i, [128, 8, 192], BF16, edf) for i in range(2)]
            cdk = self.sb("a_cdk", [128, 2, 384], F32, edf)
            cdv = self.sb("a_cdv", [128, 2, 384], F32, edf)
            kvo = [self.sb("a_kvo%d" % i, [128, 128], F32, edf) for i in range(2)]
            psp = [self.psum("a_psp%d" % i, [128, 512], F32, es) for i in range(2)]
            pstr = [self.psum("a_pst%d" % i, [128, 1024], BF16, es) for i in range(1)]
            NPSS = 3
            pss = [self.psum("a_pss%d" % i, [128, 512], F32, es) for i in range(NPSS)]
            pso = [self.psum("a_pso%d" % i, [128, 512], F32, es) for i in range(2)]
            cnt = {'psp': 0, 'pst': 0, 'pss': 0, 'PT': 0, 'kvo': 0}

            self.ld(self.rc[:], dr['ropec'].rearrange("(b p t) f -> p b t f", b=2, t=8), ['rope_tab'])
            self.ld(self.rsn[:], dr['ropes'].rearrange("(b p t) f -> p b t f", b=2, t=8), ['rope_tab'])
            self.ld(aq[:].rearrange("p (b t) f -> p b t f", b=2), dr['augq'].rearrange("(b p t) f -> p b t f", b=2, t=8), ['a_aq'])
            self.ld(ak[:, 0:16, :].rearrange("p (b t) f -> p b t f", b=2), dr['augk'][0:NT].rearrange("(b p t) f -> p b t f", b=2, t=8), ['a_ak'])
            self.ld(ak[:, 16:18, :], dr['augk'][NT:NT + 256].rearrange("(i p) f -> p i f", p=128), ['a_ak'])
            self.ld(cdk[:], dr['ctx_dk'][l].rearrange("(i p) f -> p i f", p=128), ['a_cdk'])
            self.ld(cdv[:], dr['ctx_dv'][l].rearrange("(i p) f -> p i f", p=128), ['a_cdv'])
            self.ld(dl[:], dr['diff_lambda'][l].partition_broadcast(128), ['a_dl'])
            self.ld(subw[:], dr['diff_subln_w'][l].partition_broadcast(128), ['a_subw'])
            LK = ['a_lam']
            self.tt('dve', o0[:, 0, :].rearrange("p (a f) -> p a f", a=2), dl[:].rearrange("p (a b f) -> p a b f", a=2, b=2)[:, :, 0, :],
                    dl[:].rearrange("p (a b f) -> p a b f", a=2, b=2)[:, :, 1, :], ALU.mult, ['a_dl'], LK)
            S.op('dve', lambda e: e.tensor_reduce(out=lamt[:, 1:3], in_=o0[:, 0, :].rearrange("p (a f) -> p a f", a=2),
                                                  axis=mybir.AxisListType.X, op=ALU.add), LK, LK)
            self.act(lamt[:, 1:3], lamt[:, 1:3], AF.Exp, LK, LK)
            self.tt('dve', lamt[:, 3:4], lamt[:, 2:3], lamt[:, 1:2], ALU.subtract, LK, LK)
            self.ts('dve', lamt[:, 0:1], lamt[:, 3:4], -lam_init, None, ALU.add, None, LK, LK)
            self.ts('dve', subw[:], subw[:], 1.0 - lam_init, None, ALU.mult, None, ['a_subw'], ['a_subw'])
            S.op('dve', lambda e: e.memset(self.epsc[:, 0:1], EPS), (), ['epsc'])

            def init_staging(qcols, kcols):
                S.op('dve', lambda e: e.memset(QS[:], 0.0), ['a_QS'], ['a_QS'])
                S.op('dve', lambda e: e.memset(KS[:], 0.0), ['a_KS'], ['a_KS'])
                for c0 in qcols:
                    self.copy('dve', QS[:, :, c0:c0 + 9], aq[:], ['a_aq'], ['a_QS'])
                for c0 in kcols:
                    self.copy('dve', KS[:, :, c0:c0 + 9], ak[:], ['a_ak'], ['a_KS'])
            for i in range(2):
                S.op('dve', lambda e, i=i: e.memset(Vh[i][:, :, 64:65], 1.0), [('a_V', i)], [('a_V', i)])

            def transposes(src, ntile, dstT, skey, dkey):
                for q in range((ntile + 3) // 4):
                    k = cnt['pst']
                    cnt['pst'] += 1
                    pt, ptk = pstr[0], ('a_pst', 0)
                    m = min(4, ntile - 4 * q)
                    for i in range(m):
                        self.tr(pt[:, 128 * i:128 * i + 128], src[:, 4 * q + i, :], self.identb[:], [skey, 'identb'], [ptk])
                    self.copy(self.evac_eng(), dstT[:, 512 * q:512 * q + 128 * m], pt[:, 0:128 * m], [ptk], [dkey])

            def core(hs, comps, scale, post):
                qt_, kt_, vh_ = QT[hs], KT[hs], Vh[hs]
                steps = [(qb, ci, kt) for qb in range(4) for ci in range(len(comps)) for kt in range(18)]
                slots = {}

                def score(i):
                    qb, ci, kt = steps[i]
                    r0, nr = comps[ci]
                    k = cnt['pss']
                    cnt['pss'] += 1
                    ps_, psk = pss[k % NPSS], ('a_pss', k % NPSS)
                    slots[i] = (ps_, psk)
                    self.mm(ps_[:], kt_[r0:r0 + nr, 128 * kt:128 * kt + 128], qt_[r0:r0 + nr, 512 * qb:512 * qb + 512], True, True,
                            [('a_KT', hs), ('a_QT', hs)], [psk])
                LA = 2
                for j in range(LA):
                    score(j)
                for i in range(len(steps)):
                    qb, ci, kt = steps[i]
                    if i + LA < len(steps):
                        score(i + LA)
                    ps_, psk = slots.pop(i)
                    po, pok = pso[ci], ('a_pso', ci)
                    k2 = cnt['PT']
                    cnt['PT'] += 1
                    pt, ptk = PT[k2 % 4], ('a_PT', k2 % 4)
                    self.act(pt[:], ps_[:], AF.Exp, [psk], [ptk], scale=scale)
                    self.mm(po[0:65, :], vh_[:, kt, 0:65], pt[:], kt == 0, kt == 17, [ptk, ('a_V', hs)], [pok])
                    if kt == 17:
                        self.copy('dve', oTs[ci][0:65, :], po[0:65, :], [pok], [('a_oTs', ci)])
                        for qt in range(4):
                            self.tr(psp[ci][:, 128 * qt:128 * qt + 65], oTs[ci][0:65, 128 * qt:128 * qt + 128], self.identf[0:65, 0:65],
                                    [('a_oTs', ci), 'identf'], [('a_psp', ci)])
                        if ci == len(comps) - 1:
                            post(qb)

            def normalize(ci, dst):
                po = psp[ci][:].rearrange("p (q f) -> p q f", f=128)
                S.op('dve', lambda e: e.reciprocal(out=sml[:, 4 * ci:4 * ci + 4], in_=po[:, :, 64]), [('a_psp', ci)], ['a_sml'])
                self.tt('dve', dst, po[:, :, 0:64], sml[:, 4 * ci:4 * ci + 4].unsqueeze(2).to_broadcast([128, 4, 64]), ALU.mult,
                        [('a_psp', ci), 'a_sml'], ['a_o'])

            def out_transposes(qb, jt, roff):
                k = cnt['psp']
                cnt['psp'] += 1
                pt, ptk = psp[k % 2], ('a_psp', k % 2)
                for qt in range(4):
                    self.mm(pt[roff:roff + 64, 128 * qt:128 * qt + 128], ost[:, qt, :], self.identb[:], True, True, ['a_ost', 'identb'], [ptk])
                self.copy(self.evac_eng(), moh[jt % 2][roff:roff + 64, 512 * qb:512 * qb + 512], pt[roff:roff + 64, :], [ptk], [('a_moh', jt % 2, qb)])

            def pair_wout(jt):
                self.wout_part(l, [jt], [(moh[jt % 2][:], (lambda blk, jt=jt: ('a_moh', jt % 2, blk)))],
                               [(wob[jt % 2][:], ('a_wo', jt % 2))], psp, [('a_psp', 0), ('a_psp', 1)])

            w_in_v = self._ap('w_in')[l].rearrange("(k p) f -> p k f", p=128)
            import os
            att = int(os.environ.get('ATT', '99'))
            init_staging((32, 96), (32, 96))
            if att <= 1:
                S.barrier()
                edf.close()
                return
            for h in range(6):
                hs = h % 2
                whb, whk = wh[hs], ('a_wh', hs)
                for i3 in range(3):
                    self.ld(whb[:, :, 64 * i3:64 * i3 + 64], w_in_v[:, :, 384 * i3 + 64 * h:384 * i3 + 64 * h + 64], [whk], eng='pool')
                for tt in range(16):
                    pos = 128 * tt
                    b, t = tt // 8, tt % 8
                    k = cnt['psp']
                    cnt['psp'] += 1
                    pp, ppk = psp[k % 2], ('a_psp', k % 2)
                    for kk in range(8):
                        self.mm(pp[:, 0:192], hm[:, kk, pos:pos + 128], whb[:, kk, :], kk == 0, kk == 7, [whk, ('m_hm', kk, pos // 512)], [ppk])
                    src5 = pp[:, 0:128].rearrange("p (n a h f) -> p n a h f", n=4, a=2, h=2)
                    qd = QS[:, tt, :].rearrange("p (c x) -> p c x", c=2)[:, :, 0:32].rearrange("p c (a h f) -> p c a h f", a=2, h=2)
                    kd = KS[:, tt, :].rearrange("p (c x) -> p c x", c=2)[:, :, 0:32].rearrange("p c (a h f) -> p c a h f", a=2, h=2)
                    self.rope(src5, [(slice(0, 2), qd), (slice(2, 4), kd)], tt, 4, [r[:] for r in rt], [ppk], ['a_QS', 'a_KS'])
                    kv = cnt['kvo']
                    cnt['kvo'] += 1
                    kvb, kvk = kvo[kv % 2], ('a_kvo', kv % 2)
                    self.copy('act', kvb[:], pp[:, 64:192], [ppk], [kvk])
                    rows_k = self.o['ndk'][l].rearrange("(b p t) f -> b t p f", b=2, t=8)[b, t]
                    rows_v = self.o['ndv'][l].rearrange("(b p t) f -> b t p f", b=2, t=8)[b, t]
                    self.st(rows_k[:, 64 * h:64 * h + 64], kvb[:, 0:64], [kvk])
                    self.st(rows_v[:, 64 * h:64 * h + 64], kvb[:, 64:128], [kvk])
                    self.copy('act', Vh[hs][:, tt, 0:64], pp[:, 128:192], [ppk], [('a_V', hs)])
                for i in range(2):
                    kd = KS[:, 16 + i, :].rearrange("p (c x) -> p c x", c=2)[:, :, 0:32]
                    self.copy('dve', kd, cdk[:, i, 64 * h:64 * h + 64].rearrange("p (c x) -> p c x", c=2), ['a_cdk'], ['a_KS'])
                    self.copy('dve', Vh[hs][:, 16 + i, 0:64], cdv[:, i, 64 * h:64 * h + 64], ['a_cdv'], [('a_V', hs)])
                if att <= 2:
                    S.barrier()
                    edf.close()
                    return
                transposes(QS, 16, QT[hs], 'a_QS', ('a_QT', hs))
                transposes(KS, 18, KT[hs], 'a_KS', ('a_KT', hs))
                if att <= 3:
                    S.barrier()
                    edf.close()
                    return

                def post(qb, h=h):
                    normalize(0, o0[:])
                    normalize(1, o1[:])
                    self.stt('dve', o0[:], o1[:], lamt[:, 0:1], o0[:], ALU.mult, ALU.add, ['a_o', 'a_lam'], ['a_o'])
                    self.tt('dve', osq[:], o0[:], o0[:], ALU.mult, ['a_o'], ['a_osq'])
                    S.op('dve', lambda e: e.tensor_reduce(out=sml[:, 8:12], in_=osq[:], axis=mybir.AxisListType.X, op=ALU.add),
                         ['a_osq'], ['a_sml'])
                    self.act(sml[:, 8:12], sml[:, 8:12], AF.Sqrt, ['a_sml', 'epsc'], ['a_sml'], bias=self.epsc[:, 0:1], scale=1.0 / 64)
                    S.op('dve', lambda e: e.reciprocal(out=sml[:, 8:12], in_=sml[:, 8:12]), ['a_sml'], ['a_sml'])
                    self.tt('dve', o0[:], o0[:], sml[:, 8:12].unsqueeze(2).to_broadcast([128, 4, 64]), ALU.mult, ['a_o', 'a_sml'], ['a_o'])
                    self.tt('dve', ost[:], o0[:], subw[:].unsqueeze(1).to_broadcast([128, 4, 64]), ALU.mult, ['a_o', 'a_subw'], ['a_ost'])
                    out_transposes(qb, h // 2, 64 * (h % 2))
                core(hs, [(0, 64), (64, 64)], 32 ** -0.5, post)
                if att <= 4:
                    S.barrier()
                    edf.close()
                    return
                if h % 2 == 1:
                    pair_wout(h // 2)
            S.barrier()
            edf.close()
            if att <= 5:
                return
            with contextlib.ExitStack() as em:
                sbm = lambda n, s, d: self.sb(n, s, d, em)
                wm = sbm("a_wm", [128, 8, 416], BF16)
                cqnT = sbm("a_cqnT", [128, 2, NT], BF16)
                ckvT = sbm("a_ckvT", [128, NT + 256], BF16)
                cqs = sbm("a_cqs", [128, 2, 256], BF16)
                cks = sbm("a_cks", [128, 2, 128], BF16)
                qnw = sbm("a_qnw", [128, 256], F32)
                kvnw = sbm("a_kvnw", [128, 128], F32)
                cckv = sbm("a_cckv", [128, 2, 128], F32)
                ckpe = sbm("a_ckpe", [128, 2, 32], F32)
                tq = sbm("a_tq", [128, 256], F32)
                tk_ = [sbm("a_tk%d" % i, [128, 160], F32) for i in range(2)]
                wq = [sbm("a_wq%d" % i, [128, 2, 96], BF16) for i in range(2)]
                wkv = [sbm("a_wkv%d" % i, [128, 128], BF16) for i in range(2)]
                init_staging((96,), (96,))
                self.ld(wm[:], w_in_v[:, :, 1152:1568], ['a_wm'], eng='pool')
                self.ld(qnw[:], dr['mla_q_norm_w'][l].partition_broadcast(128), ['a_qnw'])
                self.ld(kvnw[:], dr['mla_kv_norm_w'][l].partition_broadcast(128), ['a_kvnw'])
                self.ld(cckv[:], dr['ctx_ckv'][l].rearrange("(i p) f -> p i f", p=128), ['a_cckv'])
                self.ld(ckpe[:], dr['ctx_kpe'][l].rearrange("(i p) f -> p i f", p=128), ['a_ckpe'])
                mla = int(os.environ.get('MLA', '99'))
                if mla <= 1:
                    S.barrier()
                    return
                for tt in range(16):
                    pos = 128 * tt
                    b, t = tt // 8, tt % 8
                    k = cnt['psp']
                    cnt['psp'] += 1
                    pp, ppk = psp[k % 2], ('a_psp', k % 2)
                    for kk in range(8):
                        self.mm(pp[:, 0:416], hm[:, kk, pos:pos + 128], wm[:, kk, :], kk == 0, kk == 7, ['a_wm', ('m_hm', kk, pos // 512)], [ppk])
                    self.act(tq[:], pp[:, 0:256], AF.Square, [ppk], ['a_tq'], accum=sml[:, 12:13])
                    self.act(sml[:, 12:13], sml[:, 12:13], AF.Sqrt, ['a_tq'], ['a_sml2'], bias=self.epsc[:, 0:1], scale=1.0 / 256)
                    S.op('dve', lambda e: e.reciprocal(out=sml[:, 12:13], in_=sml[:, 12:13]), ['a_sml2'], ['a_sml2'])
                    cb = tt % 2
                    self.stt('dve', cqs[:, cb, :], pp[:, 0:256], sml[:, 12:13], qnw[:], ALU.mult, ALU.mult, [ppk, 'a_sml2', 'a_qnw'], [('a_cqs', cb)])
                    kq = cnt['pst']
                    cnt['pst'] += 1
                    ptq, ptqk = pstr[0], ('a_pst', 0)
                    for f in range(2):
                        self.tr(ptq[:, 128 * f:128 * f + 128], cqs[:, cb, 128 * f:128 * f + 128], self.identb[:], [('a_cqs', cb), 'identb'], [ptqk])
                    self.copy(self.evac_eng(), cqnT[:, :, pos:pos + 128], ptq[:, 0:256].rearrange("p (f c) -> p f c", f=2), [ptqk], ['a_cqnT'])
                    mlap = int(os.environ.get('MLAP', '99'))
                    if mlap <= 1:
                        continue
                    kv = cnt['kvo']
                    cnt['kvo'] += 1
                    tkb, tkk = tk_[kv % 2], ('a_tk', kv % 2)
                    self.act(tq[:, 0:128], pp[:, 256:384], AF.Square, [ppk, 'a_tq'], ['a_tq'], accum=sml[:, 13:14])
                    self.act(sml[:, 13:14], sml[:, 13:14], AF.Sqrt, ['a_tq'], ['a_sml3'], bias=self.epsc[:, 0:1], scale=1.0 / 128)
                    S.op('dve', lambda e: e.reciprocal(out=sml[:, 13:14], in_=sml[:, 13:14]), ['a_sml3'], ['a_sml3'])
                    self.stt('dve', tkb[:, 0:128], pp[:, 256:384], sml[:, 13:14], kvnw[:], ALU.mult, ALU.mult, [ppk, 'a_sml3', 'a_kvnw'], [tkk])
                    self.copy('act', tkb[:, 128:160], pp[:, 384:416], [ppk], [tkk])
                    self.copy('act', cks[:, cb, :], tkb[:, 0:128], [tkk], [('a_cks', cb)])
                    self.tr(ptq[:, 256:384], cks[:, cb, :], self.identb[:], [('a_cks', cb), 'identb'], [ptqk])
                    self.copy(self.evac_eng(), ckvT[:, pos:pos + 128], ptq[:, 256:384], [ptqk], ['a_ckvT'])
                    if mlap <= 2:
                        continue
                    rows_c = self.o['nckv'][l].rearrange("(b p t) f -> b t p f", b=2, t=8)[b, t]
                    rows_p = self.o['nkpe'][l].rearrange("(b p t) f -> b t p f", b=2, t=8)[b, t]
                    self.st(rows_c, tkb[:, 0:128], [tkk])
                    self.st(rows_p, tkb[:, 128:160], [tkk])
                    if mlap <= 3:
                        continue
                    src5 = pp[:, 384:416].rearrange("p (n a h f) -> p n a h f", n=1, a=2, h=2)
                    kd = KS[:, tt, 64:96].rearrange("p (n a h f) -> p n a h f", n=1, a=2, h=2)
                    self.rope(src5, [(slice(0, 1), kd)], tt, 1, [r[:] for r in rt], [ppk], ['a_KS'])
                if mla <= 2:
                    S.barrier()
                    return
                for i in range(2):
                    self.copy('dve', cks[:, i, :], cckv[:, i, :], ['a_cckv'], [('a_cks', i)])
                    kq = cnt['pst']
                    cnt['pst'] += 1
                    ptq, ptqk = pstr[0], ('a_pst', 0)
                    self.tr(ptq[:, 0:128], cks[:, i, :], self.identb[:], [('a_cks', i), 'identb'], [ptqk])
                    self.copy(self.evac_eng(), ckvT[:, NT + 128 * i:NT + 128 * i + 128], ptq[:, 0:128], [ptqk], ['a_ckvT'])
                    self.copy('dve', KS[:, 16 + i, 64:96], ckpe[:, i, :], ['a_ckpe'], ['a_KS'])
                if mla <= 3:
                    S.barrier()
                    return
                for h in range(6):
                    hs = h % 2
                    self.ld(wq[hs][:], dr['mla_w_q_up'][l].rearrange("(k p) f -> p k f", p=128)[:, :, 96 * h:96 * h + 96], [('a_wq', hs)], eng='pool')
                    self.ld(wkv[hs][:], dr['mla_w_kv_up'][l][:, 128 * h:128 * h + 128], [('a_wkv', hs)], eng='pool')
                    mlah = int(os.environ.get('MLAH', '99'))
                    if mlah <= 1:
                        S.barrier()
                        return
                    for tt in range(16):
                        pos = 128 * tt
                        k = cnt['psp']
                        cnt['psp'] += 1
                        pp, ppk = psp[k % 2], ('a_psp', k % 2)
                        for f in range(2):
                            self.mm(pp[:, 0:96], cqnT[:, f, pos:pos + 128], wq[hs][:, f, :], f == 0, f == 1, [('a_wq', hs), 'a_cqnT'], [ppk])
                        kv = cnt['kvo']
                        cnt['kvo'] += 1
                        tkb, tkk = tk_[kv % 2], ('a_tk', kv % 2)
                        self.copy('act', tkb[:, 0:96], pp[:, 0:96], [ppk], [tkk])
                        self.copy('act', QS[:, tt, 0:64], tkb[:, 0:64], [tkk], ['a_QS'])
                        if mlah <= 2:
                            continue
                        src5 = tkb[:, 64:96].rearrange("p (n a h f) -> p n a h f", n=1, a=2, h=2)
                        qd = QS[:, tt, 64:96].rearrange("p (n a h f) -> p n a h f", n=1, a=2, h=2)
                        self.rope(src5, [(slice(0, 1), qd)], tt, 1, [r[:] for r in rt], [tkk], ['a_QS'])
                    if mlah <= 3:
                        S.barrier()
                        return
                    for kt in range(18):
                        k = cnt['psp']
                        cnt['psp'] += 1
                        pp, ppk = psp[k % 2], ('a_psp', k % 2)
                        self.mm(pp[:, 0:128], ckvT[:, 128 * kt:128 * kt + 128], wkv[hs][:], True, True, [('a_wkv', hs), 'a_ckvT'], [ppk])
                        self.copy('act', KS[:, kt, 0:64], pp[:, 0:64], [ppk], ['a_KS'])
                        self.copy('dve', Vh[hs][:, kt, 0:64], pp[:, 64:128], [ppk], [('a_V', hs)])
                    if mla <= 4:
                        S.barrier()
                        return
                    transposes(QS, 16, QT[hs], 'a_QS', ('a_QT', hs))
                    transposes(KS, 18, KT[hs], 'a_KS', ('a_KT', hs))
                    if mla <= 5:
                        S.barrier()
                        return

                    def postm(qb, h=h):
                        normalize(0, o0[:])
                        self.copy('act', ost[:], o0[:], ['a_o'], ['a_ost'])
                        out_transposes(qb, 3 + h // 2, 64 * (h % 2))
                    core(hs, [(0, 128)], 96 ** -0.5, postm)
                    if mla <= 6:
                        S.barrier()
                        return
                    if h % 2 == 1:
                        pair_wout(3 + h // 2)


_PROG = {}


def get_prog(stage=99):
    if stage not in _PROG:
        b = Builder(stage)
        _PROG[stage] = b.build()
    return _PROG[stage]


def rope_tables(n, grid_w=64, theta=10000.0):
    t = np.arange(n)
    row = (t // grid_w).astype(np.float32)
    col = (t % grid_w).astype(np.float32)
    inv = (theta ** (-np.arange(8, dtype=np.float32) / 8)).astype(np.float32)
    ang = np.concatenate([row[:, None] * inv[None], col[:, None] * inv[None]], axis=1).astype(np.float32)
    return np.cos(ang).astype(np.float32), np.sin(ang).astype(np.float32)


def _gsplit(a, axis):
    a = np.asarray(a, dtype=np.float32)
    sh = a.shape
    a = a.reshape(sh[:axis] + (8, 2) + sh[axis + 1:])
    a = np.moveaxis(a, axis + 1, axis)
    return np.ascontiguousarray(a)


def make_in_maps(inp):
    f = lambda a: np.ascontiguousarray(np.asarray(a, dtype=np.float32))
    shared = dict(
        w_ada=f(inp['w_ada']), b_ada=f(inp['b_ada']).reshape(DEPTH * 72, 128),
        norm_w=f(inp['norm_w']).reshape(DEPTH * 3 * 8, 128), final_norm_w=f(inp['final_norm_w']).reshape(8, 128),
        ffn_w_in=f(inp['ffn_w_in']), ffn_w_out=f(inp['ffn_w_out']), w_in=f(inp['w_in']), w_out=f(inp['w_out']),
        diff_lambda=f(inp['diff_lambda']).reshape(DEPTH, 128), diff_subln_w=f(inp['diff_subln_w']),
        mla_q_norm_w=f(inp['mla_q_norm_w']), mla_w_q_up=f(inp['mla_w_q_up']),
        mla_kv_norm_w=f(inp['mla_kv_norm_w']), mla_w_kv_up=f(inp['mla_w_kv_up']),
        s5_a_re=_gsplit(inp['s5_a_re'], 2), s5_a_im=_gsplit(inp['s5_a_im'], 2), s5_log_step=_gsplit(inp['s5_log_step'], 2),
        s5_b_re=_gsplit(inp['s5_b_re'], 2), s5_b_im=_gsplit(inp['s5_b_im'], 2),
        s5_c_re=_gsplit(inp['s5_c_re'], 2).reshape(DEPTH, 2, 2, 128, 64), s5_c_im=_gsplit(inp['s5_c_im'], 2).reshape(DEPTH, 2, 2, 128, 64),
        s5_d=f(inp['s5_d']), s5_w_glu=f(inp['s5_w_glu']), s5_b_glu=f(inp['s5_b_glu']).reshape(DEPTH, 2, 128),
    )
    tq = np.arange(128) // 16
    cm_f = (tq[:, None] <= tq[None, :]).astype(np.float32)
    cm_b = (tq[:, None] >= tq[None, :]).astype(np.float32)
    shared['cmask_f'] = cm_f
    shared['cmask_b'] = cm_b
    rc, rsn = rope_tables(NT)
    maps = []
    for core in range(8):
        m = dict(shared)
        if core < 4:
            b = core
            m['xin'] = f(inp['x_sample'][b])
            m['cvec'] = f(inp['c'][b]).reshape(8, 128)
            m['ctx_dk'] = f(inp['cache_diff_k'][b]).reshape(DEPTH, 256, 384)
            m['ctx_dv'] = f(inp['cache_diff_v'][b]).reshape(DEPTH, 256, 384)
            m['ctx_ckv'] = f(inp['cache_mla_ckv'][b])
            m['ctx_kpe'] = f(inp['cache_mla_kpe'][b])
            m['h0re'] = _gsplit(inp['state_s5_re'][b], 2)
            m['h0im'] = _gsplit(inp['state_s5_im'][b], 2)
            m['ropec'], m['ropes'] = rc, rsn
            augk = np.zeros((NT + 256, 9), np.float32)
            augk[:, 0] = 32.0
            augk[:, 8] = 1.0
            augq = np.zeros((NT, 9), np.float32)
            augq[:, 0] = 32.0
            augq[:, 8] = -1024.0
            m['flag'] = np.ones((128, 1), np.float32)
        else:
            i = core - 4
            m['xin'] = f(inp['x_prompt'][8 * i:8 * i + 8]).reshape(NT, D)
            m['cvec'] = f(inp['c_ctx']).reshape(8, 128)
            m['ctx_dk'] = np.zeros((DEPTH, 256, 384), np.float32)
            m['ctx_dv'] = np.zeros((DEPTH, 256, 384), np.float32)
            m['ctx_ckv'] = np.zeros((DEPTH, 256, 128), np.float32)
            m['ctx_kpe'] = np.zeros((DEPTH, 256, 32), np.float32)
            m['h0re'] = np.zeros((DEPTH, 2, 2, 8, 64), np.float32)
            m['h0im'] = np.zeros((DEPTH, 2, 2, 8, 64), np.float32)
            m['ropec'] = np.ones((NT, 16), np.float32)
            m['ropes'] = np.zeros((NT, 16), np.float32)
            seg = np.arange(NT) // 256
            augk = np.zeros((NT + 256, 9), np.float32)
            augk[np.arange(NT), seg] = 32.0
            augk[:, 8] = 1.0
            augq = np.zeros((NT, 9), np.float32)
            augq[np.arange(NT), seg] = 32.0
            augq[:, 8] = -1024.0
            m['flag'] = np.zeros((128, 1), np.float32)
        m['augk'] = augk
        m['augq'] = augq
        maps.append(m)
    return maps


def kernel(**inputs):
    nc = get_prog(STAGE)
    maps = make_in_maps(inputs)
    res = run_bass_kernel_spmd(nc, maps, core_ids=list(range(8)))
    r = res.results
    B, SEQ = 32, 256
    y_sample = np.stack([r[c]['y'] for c in range(4)], 0).astype(np.float32)
    y_prompt = np.concatenate([r[c]['y'].reshape(8, SEQ, D) for c in range(4, 8)], 0).astype(np.float32)

    def cat(name, tail):
        outs = []
        for c in range(4, 8):
            a = r[c][name].reshape(DEPTH, 8, SEQ, -1).transpose(1, 0, 2, 3)
            outs.append(a)
        a = np.concatenate(outs, 0)
        return np.ascontiguousarray(a.reshape((B, DEPTH, SEQ) + tail)).astype(np.float32)

    def cat5(name):
        outs = []
        for c in range(4, 8):
            a = r[c][name].reshape(DEPTH, 2, 8, 16, 64).transpose(2, 0, 1, 3, 4)
            outs.append(a)
        return np.ascontiguousarray(np.concatenate(outs, 0)).astype(np.float32)
    return (y_prompt, y_sample, cat('ndk', (6, 64)), cat('ndv', (6, 64)), cat('nckv', (128,)), cat('nkpe', (32,)),
            cat5('ns5re'), cat5('ns5im'))
```

```python
import contextlib
import math
import numpy as np
import concourse.bass as bass
import concourse.mybir as mybir
from concourse.bass_utils import run_bass_kernel_spmd

F32 = mybir.dt.float32
BF16 = mybir.dt.bfloat16
AF = mybir.ActivationFunctionType
ALU = mybir.AluOpType

D = 1024
NT = 2048
DEPTH = 2
DFF = 2816
NHT = 22
EPS = 1e-6
INC = 1824
STAGE = 99


class Sched:
    def __init__(self, nc, ndma=8):
        self.nc = nc
        self.engs = ['pe', 'act', 'dve', 'pool', 'sp']
        self.streams = {e: [] for e in self.engs}
        self.cnt = {e: 0 for e in self.engs}
        self.seen = {e: {} for e in self.engs}
        self.res = {}
        self.ndma = ndma
        self.dma_issued = {'sp': 0, 'pool': 0, 'act': 0}
        self.dma_last = {}
        self.final_tokens = []

    def _deps(self, eng, reads, writes):
        toks = {}

        def add(t):
            if t is None:
                return
            k, v = t
            if toks.get(k, 0) < v:
                toks[k] = v
        for r in reads:
            st = self.res.get(r)
            if st:
                add(st['w'])
        for w in writes:
            st = self.res.get(w)
            if st:
                add(st['w'])
                for t in st['r']:
                    add(t)
        out = []
        for k, v in toks.items():
            if eng == 'pe' and k == ('c', 'pe'):
                continue
            if self.seen[eng].get(k, 0) >= v:
                continue
            self.seen[eng][k] = v
            out.append((k, v))
        return out

    def _mark(self, tok, reads, writes):
        for r in reads:
            st = self.res.setdefault(r, {'w': None, 'r': []})
            st['r'].append(tok)
            if len(st['r']) > 64:
                mx = {}
                for k, v in st['r']:
                    if mx.get(k, 0) < v:
                        mx[k] = v
                st['r'] = list(mx.items())
        for w in writes:
            self.res[w] = {'w': tok, 'r': []}

    PSUM_NAMES = {'lxps', 'ad_pst', 'ad_psm', 'nm_pss', 'f_pag', 'f_pso', 'fin_ps', 'sa_ps', 'sb_psu', 'sb_pst', 'sc_ps',
                  'sc_psF', 'sd_ps', 'se_pst', 'se_psg', 'a_psp', 'a_pst', 'a_pss', 'a_pso'}

    def _excl(self, reads, writes):
        rd, wr = [], list(writes)
        for r in reads:
            nm = r if isinstance(r, str) else r[0]
            if nm in self.PSUM_NAMES:
                if r not in wr:
                    wr.append(r)
            else:
                rd.append(r)
        return rd, wr

    def op(self, eng, fn, reads=(), writes=()):
        reads, writes = self._excl(reads, writes)
        waits = self._deps(eng, reads, writes)
        self.cnt[eng] += 1
        tok = (('c', eng), self.cnt[eng])
        self.streams[eng].append((waits, fn, tok))
        self._mark(tok, reads, writes)
        return tok

    def dma(self, eng, fn, reads=(), writes=(), final=False):
        k = self.dma_issued[eng]
        self.dma_issued[eng] += 1
        slot = k % self.ndma
        val = 16 * (k // self.ndma + 1)
        key = ('d', eng, slot)
        waits = self._deps(eng, reads, writes)
        if val > 16 and self.seen[eng].get(key, 0) < val - 16:
            self.seen[eng][key] = val - 16
            waits.append((key, val - 16))
        tok = (key, val)
        self.dma_last[key] = val
        self.streams[eng].append((waits, fn, tok))
        self._mark(tok, reads, writes)
        if final:
            self.final_tokens.append(tok)
        return tok

    def barrier(self):
        allt = [(('c', e), self.cnt[e]) for e in ['pe', 'act', 'dve', 'pool'] if self.cnt[e]]
        allt += list(self.dma_last.items())
        for e in self.engs:
            waits = []
            for k, v in allt:
                if k == ('c', e):
                    continue
                if self.seen[e].get(k, 0) >= v:
                    continue
                self.seen[e][k] = v
                waits.append((k, v))
            if waits:
                self.streams[e].append((waits, None, None))

    def emit(self):
        nc = self.nc
        with contextlib.ExitStack() as es:
            sems = {}
            for e in ['pe', 'act', 'dve', 'pool']:
                sems[('c', e)] = es.enter_context(nc.semaphore('c_' + e))
            for e in ['sp', 'pool', 'act']:
                if self.dma_issued[e]:
                    for s in range(self.ndma):
                        sems[('d', e, s)] = es.enter_context(nc.semaphore('d_%s_%d' % (e, s)))
            block = es.enter_context(nc.Block())

            def run(engname, engobj):
                for waits, fn, tok in self.streams[engname]:
                    for k, v in waits:
                        engobj.wait_ge(sems[k], v)
                    if fn is None:
                        continue
                    ins = fn(engobj)
                    k, v = tok
                    ins.then_inc(sems[k], 16 if k[0] == 'd' else 1)
                if engname == 'sp':
                    for k, v in self.final_tokens:
                        engobj.wait_ge(sems[k], v)

            @block.sync
            def _(e):
                run('sp', e)

            @block.tensor
            def _(e):
                run('pe', e)

            @block.scalar
            def _(e):
                run('act', e)

            @block.vector
            def _(e):
                run('dve', e)

            @block.gpsimd
            def _(e):
                run('pool', e)


class Builder:
    def __init__(self, stage=99):
        self.stage = stage
        self.nc = bass.Bass("TRN2", target_bir_lowering=False)
        self.S = Sched(self.nc)
        self.es = contextlib.ExitStack()
        self.uid = 0
        self.rr = 0
        self._dr_cache = {}

    def din(self, name, shape):
        ap = self.nc.dram_tensor(name, list(shape), F32, kind="ExternalInput").ap()
        self._dr_cache[name] = ap
        return ap

    def _ap(self, name):
        return self._dr_cache[name]

    def dout(self, name, shape):
        return self.nc.dram_tensor(name, list(shape), F32, kind="ExternalOutput").ap()

    def sb(self, name, shape, dt, es=None):
        self.uid += 1
        return (es or self.es).enter_context(self.nc.sbuf_tensor("%s_u%d" % (name, self.uid), list(shape), dt))

    def psum(self, name, shape, dt, es):
        self.uid += 1
        return es.enter_context(self.nc.psum_tensor("%s_u%d" % (name, self.uid), list(shape), dt))

    def evac_eng(self):
        self.rr += 1
        return 'act' if self.rr % 2 else 'dve'

    def copy(self, eng, out, in_, reads, writes):
        if eng == 'act':
            self.S.op('act', lambda e: e.copy(out=out, in_=in_), reads, writes)
        else:
            self.S.op(eng, lambda e: e.tensor_copy(out=out, in_=in_), reads, writes)

    def tt(self, eng, out, a, b, op, reads, writes):
        self.S.op(eng, lambda e: e.tensor_tensor(out=out, in0=a, in1=b, op=op), reads, writes)

    def ts(self, eng, out, a, s1, s2, op0, op1, reads, writes):
        if op1 is None:
            self.S.op(eng, lambda e: e.tensor_scalar(out=out, in0=a, scalar1=s1, scalar2=None, op0=op0), reads, writes)
        else:
            self.S.op(eng, lambda e: e.tensor_scalar(out=out, in0=a, scalar1=s1, scalar2=s2, op0=op0, op1=op1), reads, writes)

    def stt(self, eng, out, a, s, b, op0, op1, reads, writes):
        self.S.op(eng, lambda e: e.scalar_tensor_tensor(out=out, in0=a, scalar=s, in1=b, op0=op0, op1=op1), reads, writes)

    def act(self, out, in_, func, reads, writes, bias=None, scale=None, accum=None):
        kw = {}
        if bias is not None:
            kw['bias'] = bias
        if scale is not None:
            kw['scale'] = scale
        if accum is not None:
            kw['accum_out'] = accum
        self.S.op('act', lambda e: e.activation(out=out, in_=in_, func=func, **kw), reads, writes)

    def mm(self, out, lhsT, rhs, start, stop, reads, writes):
        self.S.op('pe', lambda e: e.matmul(out, lhsT=lhsT, rhs=rhs, start=start, stop=stop), reads, writes)

    def tr(self, out, in_, ident, reads, writes):
        self.S.op('pe', lambda e: e.transpose(out=out, in_=in_, identity=ident), reads, writes)

    def ld(self, out, in_, writes, reads=(), eng='sp', **kw):
        self.S.dma(eng, lambda e: e.dma_start(out=out, in_=in_, **kw), reads, writes)

    def st(self, out, in_, reads, eng='sp', **kw):
        self.S.dma(eng, lambda e: e.dma_start(out=out, in_=in_, **kw), reads, (), final=True)

    def build(self):
        nc, S = self.nc, self.S
        din, dout, sb = self.din, self.dout, self.sb
        xin = din("xin", [NT, D])
        cvec = din("cvec", [8, 128])
        w_ada = din("w_ada", [DEPTH, D, 9 * D])
        b_ada = din("b_ada", [DEPTH * 72, 128])
        norm_w = din("norm_w", [DEPTH * 3 * 8, 128])
        fnw = din("final_norm_w", [8, 128])
        ffn_w_in = din("ffn_w_in", [DEPTH, 2, D, 2 * DFF])
        ffn_w_out = din("ffn_w_out", [DEPTH, 2, DFF, D])
        self.dr = dict(
            w_in=din("w_in", [DEPTH, D, INC]), w_out=din("w_out", [DEPTH, D, D]),
            ctx_dk=din("ctx_dk", [DEPTH, 256, 384]), ctx_dv=din("ctx_dv", [DEPTH, 256, 384]),
            ctx_ckv=din("ctx_ckv", [DEPTH, 256, 128]), ctx_kpe=din("ctx_kpe", [DEPTH, 256, 32]),
            h0re=din("h0re", [DEPTH, 2, 2, 8, 64]), h0im=din("h0im", [DEPTH, 2, 2, 8, 64]),
            ropec=din("ropec", [NT, 16]), ropes=din("ropes", [NT, 16]),
            augk=din("augk", [NT + 256, 9]), augq=din("augq", [NT, 9]),
            flag=din("flag", [128, 1]),
            diff_lambda=din("diff_lambda", [DEPTH, 128]), diff_subln_w=din("diff_subln_w", [DEPTH, 64]),
            mla_q_norm_w=din("mla_q_norm_w", [DEPTH, 256]), mla_w_q_up=din("mla_w_q_up", [DEPTH, 256, 576]),
            mla_kv_norm_w=din("mla_kv_norm_w", [DEPTH, 128]), mla_w_kv_up=din("mla_w_kv_up", [DEPTH, 128, 768]),
            s5_a_re=din("s5_a_re", [DEPTH, 2, 2, 8, 64]), s5_a_im=din("s5_a_im", [DEPTH, 2, 2, 8, 64]),
            s5_log_step=din("s5_log_step", [DEPTH, 2, 2, 8]),
            s5_b_re=din("s5_b_re", [DEPTH, 2, 2, 8, 64, 16]), s5_b_im=din("s5_b_im", [DEPTH, 2, 2, 8, 64, 16]),
            s5_c_re=din("s5_c_re", [DEPTH, 2, 2, 128, 64]), s5_c_im=din("s5_c_im", [DEPTH, 2, 2, 128, 64]),
            s5_d=din("s5_d", [DEPTH, 16, 16]), s5_w_glu=din("s5_w_glu", [DEPTH, 256, 256]),
            s5_b_glu=din("s5_b_glu", [DEPTH, 2, 128]),
            cmask_f=din("cmask_f", [128, 128]), cmask_b=din("cmask_b", [128, 128]),
        )
        self.y = dout("y", [NT, D])
        self.o = dict(
            ndk=dout("ndk", [DEPTH, NT, 384]), ndv=dout("ndv", [DEPTH, NT, 384]),
            nckv=dout("nckv", [DEPTH, NT, 128]), nkpe=dout("nkpe", [DEPTH, NT, 32]),
            ns5re=dout("ns5re", [DEPTH, 2, 8, 1024]), ns5im=dout("ns5im", [DEPTH, 2, 8, 1024]),
        )
        self.xT = sb("xT", [128, 8, NT], F32)
        self.identb = sb("identb", [128, 128], BF16)
        self.identf = sb("identf", [128, 128], F32)
        self.onesb = sb("onesb", [128, 128], BF16)
        self.epsc = sb("epsc", [128, 1], F32)
        self.mod = sb("mod", [128, DEPTH, 72], F32)
        self.cA = sb("cA", [128, DEPTH, 3, 8], F32)
        self.cB = sb("cB", [128, DEPTH, 3, 8], F32)
        self.cG = sb("cG", [128, DEPTH, 3, 8], F32)
        self.cF = sb("cF", [128, 8], F32)
        self.rs = sb("rs", [128, 2, 512], F32)
        self.wi_n = 0
        self.wo_n = 0

        self.setup_consts()
        self.load_x()
        self.adaln()
        S.barrier()
        for l in range(DEPTH):
            if self.stage >= 1:
                self.ffn(l, 0)
                S.barrier()
            if self.stage >= 3:
                self.mixer(l)
                S.barrier()
            if self.stage >= 2:
                self.ffn(l, 1)
                S.barrier()
            if self.stage < 4:
                break
        self.final()
        S.emit()
        return nc

    def setup_consts(self):
        S = self.S
        ib, iff, ob = self.identb, self.identf, self.onesb
        S.op('pool', lambda e: e.memset(ib[:], 0.0), (), ['identb'])
        S.op('pool', lambda e: e.affine_select(out=ib[:], in_=ib[:], compare_op=ALU.not_equal, fill=1.0, base=0,
                                               pattern=[[-1, 128]], channel_multiplier=1), ['identb'], ['identb'])
        S.op('pool', lambda e: e.memset(iff[:], 0.0), (), ['identf'])
        S.op('pool', lambda e: e.affine_select(out=iff[:], in_=iff[:], compare_op=ALU.not_equal, fill=1.0, base=0,
                                               pattern=[[-1, 128]], channel_multiplier=1), ['identf'], ['identf'])
        S.op('pool', lambda e: e.memset(ob[:], 1.0), (), ['onesb'])
        ep = self.epsc
        S.op('pool', lambda e: e.memset(ep[:], EPS), (), ['epsc'])

    def load_x(self):
        with contextlib.ExitStack() as es:
            xt = [self.sb("xtok%d" % i, [128, D], F32, es) for i in range(2)]
            ps = [self.psum("lxps%d" % i, [128, 512], F32, es) for i in range(4)]
            src = self._ap("xin").rearrange("(b p t) d -> b t p d", b=2, t=8)
            n = 0
            for b in range(2):
                for t in range(8):
                    buf = xt[n % 2]
                    bk = ('xtok', n % 2)
                    self.ld(buf[:], src[b, t], [bk])
                    pos = 1024 * b + 128 * t
                    for h in range(2):
                        pb = ps[(2 * n + h) % 4]
                        pk = ('lxps', (2 * n + h) % 4)
                        for jj in range(4):
                            j = 4 * h + jj
                            self.tr(pb[:, 128 * jj:128 * jj + 128], buf[:, 128 * j:128 * j + 128], self.identf[:],
                                    [bk, 'identf'], [pk])
                        dst = self.xT[:, 4 * h:4 * h + 4, pos:pos + 128]
                        srcp = pb[:].rearrange("p (j c) -> p j c", j=4)
                        self.copy(self.evac_eng(), dst, srcp, [pk],
                                  [('xT', j, pos // 512) for j in range(4 * h, 4 * h + 4)])
                    n += 1
        self.S.barrier()

    def adaln(self):
        S = self.S
        with contextlib.ExitStack() as es:
            rows = self.sb("ad_rows", [128, 3, 128], F32, es)
            rT = self.sb("ad_rT", [128, 3, 128], F32, es)
            scv = self.sb("ad_scv", [128, 8], F32, es)
            wa = [self.sb("ad_w%d" % i, [128, 8, 512], F32, es) for i in range(2)]
            pst = self.psum("ad_pst", [128, 512], F32, es)
            psm = self.psum("ad_psm", [128, 512], F32, es)
            S.op('dve', lambda e: e.memset(rows[:], 0.0), (), ['ad_rows'])
            self.ld(rows[0:8, 0, :], self._dr_cache['cvec'], ['ad_rows'], ['ad_rows'])
            self.ld(rows[8:16, 0, :], self._dr_cache['final_norm_w'], ['ad_rows'], ['ad_rows'])
            self.ld(rows[16:64, 0, :], self._dr_cache['norm_w'], ['ad_rows'], ['ad_rows'])
            self.ld(rows[:, 1, :], self._dr_cache['b_ada'][0:128, :], ['ad_rows'], ['ad_rows'])
            self.ld(rows[0:16, 2, :], self._dr_cache['b_ada'][128:144, :], ['ad_rows'], ['ad_rows'])
            for i in range(3):
                self.tr(pst[:, 128 * i:128 * i + 128], rows[:, i, :], self.identf[:], ['ad_rows', 'identf'], ['ad_pst'])
            self.copy('dve', rT[:].rearrange("p a b -> p (a b)"), pst[:, 0:384], ['ad_pst'], ['ad_rT'])
            self.act(scv[:], rT[:, 0, 0:8], AF.Silu, ['ad_rT'], ['ad_scv'])
            badaT = rT[:].rearrange("p a b -> p (a b)")[:, 128:128 + 144]
            n = 0
            for l in range(DEPTH):
                wv = self._dr_cache['w_ada'][l].rearrange("(kc p) f -> p kc f", p=128)
                for pc in range(18):
                    buf = wa[n % 2]
                    bk = ('ad_w', n % 2)
                    self.ld(buf[:], wv[:, :, 512 * pc:512 * pc + 512], [bk])
                    for ii in range(4):
                        i = 4 * pc + ii
                        for k in range(8):
                            self.mm(psm[:, l * 72 + i:l * 72 + i + 1], buf[:, k, 128 * ii:128 * ii + 128], scv[:, k:k + 1],
                                    k == 0, k == 7, [bk, 'ad_scv'], ['ad_psm'])
                    n += 1
            self.tt('dve', self.mod[:].rearrange("p l i -> p (l i)"), psm[:, 0:144], badaT, ALU.add,
                    ['ad_psm', 'ad_rT'], ['mod'])
            for l in range(DEPTH):
                for n3 in range(3):
                    nw = rT[:, 0, 16 + (l * 3 + n3) * 8:16 + (l * 3 + n3) * 8 + 8]
                    sh = self.mod[:, l, (3 * n3) * 8:(3 * n3) * 8 + 8]
                    sc = self.mod[:, l, (3 * n3 + 1) * 8:(3 * n3 + 1) * 8 + 8]
                    g = self.mod[:, l, (3 * n3 + 2) * 8:(3 * n3 + 2) * 8 + 8]
                    self.stt('dve', self.cA[:, l, n3, :], sc, 1.0, nw, ALU.add, ALU.mult, ['mod', 'ad_rT'], ['cA'])
                    self.copy('dve', self.cB[:, l, n3, :], sh, ['mod'], ['cB'])
                    self.ts('dve', self.cG[:, l, n3, :], g, (1.0 if n3 == 1 else 0.5), None, ALU.mult, None, ['mod'], ['cG'])
            self.copy('dve', self.cF[:], rT[:, 0, 8:16], ['ad_rT'], ['cF'])
            S.barrier()

    def norm_mod(self, blk, A, Bv, hm, hmcol, es_t, pss, hmkey):
        c0 = 512 * blk
        sq, tmp = es_t['sq'], es_t['tmp']
        xk = [('xT', j, blk) for j in range(8)]
        self.act(sq[:], self.xT[:, :, c0:c0 + 512], AF.Square, xk, ['nm_sq'])
        for j in range(8):
            self.mm(pss[:], self.onesb[:], sq[:, j, :], j == 0, j == 7, ['nm_sq', 'onesb'], ['nm_pss'])
        rb = blk % 2
        self.act(self.rs[:, rb, :], pss[:], AF.Sqrt, ['nm_pss'], [('rs', rb)], bias=self.epsc[:, 0:1], scale=1.0 / D)
        self.S.op('dve', lambda e: e.reciprocal(out=self.rs[:, rb, :], in_=self.rs[:, rb, :]), [('rs', rb)], [('rs', rb)])
        for j in range(8):
            tb = tmp[j % 2]
            self.tt('dve', tb[:], self.xT[:, j, c0:c0 + 512], self.rs[:, rb, :], ALU.mult,
                    [('xT', j, blk), ('rs', rb)], [('nm_tmp', j % 2)])
            if Bv is None:
                self.act(hm[:, j, hmcol:hmcol + 512], tb[:], AF.Identity, [('nm_tmp', j % 2)], [hmkey(j)], scale=A[:, j:j + 1])
            else:
                self.act(hm[:, j, hmcol:hmcol + 512], tb[:], AF.Identity, [('nm_tmp', j % 2)], [hmkey(j)],
                         scale=A[:, j:j + 1], bias=Bv[:, j:j + 1])

    def ffn(self, l, n):
        n3 = 0 if n == 0 else 2
        A, Bv, G = self.cA[:, l, n3, :], self.cB[:, l, n3, :], self.cG[:, l, n3, :]
        w_in = self._dr_cache['ffn_w_in'][l, n].rearrange("(kc p) f -> p kc f", p=128)
        w_out = self._dr_cache['ffn_w_out'][l, n].rearrange("(i p) d -> p i d", p=128)
        with contextlib.ExitStack() as es:
            hm = self.sb("f_hm", [128, 8, 1024], BF16, es)
            self.wi = [self.sb("wi%d" % i, [128, 8, 2, 256], BF16, es) for i in range(3)]
            self.wo = [self.sb("wo%d" % i, [128, NHT, 128], BF16, es) for i in range(3)]
            actb = self.sb("f_act", [128, NHT, 1024], BF16, es)
            sq = self.sb("f_sq", [128, 8, 512], BF16, es)
            tmp = [self.sb("f_tmp%d" % i, [128, 512], F32, es) for i in range(2)]
            sg = [self.sb("f_sg%d" % i, [128, 512], F32, es) for i in range(2)]
            pss = self.psum("f_pss", [128, 512], F32, es)
            pag = [self.psum("f_pag%d" % i, [128, 512], F32, es) for i in range(4)]
            pso = [self.psum("f_pso%d" % i, [128, 512], F32, es) for i in range(2)]
            est = {'sq': sq, 'tmp': tmp}
            npag = 0
            npso = 0
            for half in range(2):
                for bb in range(2):
                    blk = 2 * half + bb
                    self.norm_mod(blk, A, Bv, hm, 512 * bb, est, pss, lambda j, bb=bb: ('f_hm', j, bb))
                for pc in range(11):
                    slot = self.wi_n % 3
                    self.wi_n += 1
                    wb = self.wi[slot]
                    wk = ('wi', slot)
                    for ag in range(2):
                        cb = ag * DFF + 256 * pc
                        self.ld(wb[:, :, ag, :], w_in[:, :, cb:cb + 256], [wk], eng='pool')
                    for ii in range(2):
                        i = 2 * pc + ii
                        for bb in range(2):
                            pa = pag[npag % 4]
                            ka = ('f_pag', npag % 4)
                            pg = pag[(npag + 1) % 4]
                            kg = ('f_pag', (npag + 1) % 4)
                            npag += 2
                            for k in range(8):
                                self.mm(pa[:], wb[:, k, 0, 128 * ii:128 * ii + 128], hm[:, k, 512 * bb:512 * bb + 512],
                                        k == 0, k == 7, [wk, ('f_hm', k, bb)], [ka])
                            for k in range(8):
                                self.mm(pg[:], wb[:, k, 1, 128 * ii:128 * ii + 128], hm[:, k, 512 * bb:512 * bb + 512],
                                        k == 0, k == 7, [wk, ('f_hm', k, bb)], [kg])
                            sgi = (npag // 2) % 2
                            self.act(sg[sgi][:], pg[:], AF.Silu, [kg], [('f_sg', sgi)])
                            self.tt('dve', actb[:, i, 512 * bb:512 * bb + 512], sg[sgi][:], pa[:], ALU.mult,
                                    [('f_sg', sgi), ka], [('f_act', i, bb)])
                for jo in range(8):
                    slot = self.wo_n % 3
                    self.wo_n += 1
                    wb = self.wo[slot]
                    wk = ('wo', slot)
                    self.ld(wb[:], w_out[:, :, 128 * jo:128 * jo + 128], [wk], eng='pool')
                    for bb in range(2):
                        blk = 2 * half + bb
                        po = pso[npso % 2]
                        ko = ('f_pso', npso % 2)
                        npso += 1
                        for i in range(NHT):
                            self.mm(po[:], wb[:, i, :], actb[:, i, 512 * bb:512 * bb + 512], i == 0, i == NHT - 1,
                                    [wk, ('f_act', i, bb)], [ko])
                        xs = self.xT[:, jo, 512 * blk:512 * blk + 512]
                        self.stt('dve', xs, po[:], G[:, jo:jo + 1], xs, ALU.mult, ALU.add,
                                 [ko, ('xT', jo, blk)], [('xT', jo, blk)])

    def final(self):
        with contextlib.ExitStack() as es:
            yb = [self.sb("fin_y%d" % i, [128, 8, 512], F32, es) for i in range(1)]
            sq = self.sb("fin_sq", [128, 8, 512], BF16, es)
            tmp = [self.sb("fin_tmp%d" % i, [128, 512], F32, es) for i in range(2)]
            yt = [self.sb("fin_yt%d" % i, [128, D], F32, es) for i in range(2)]
            pss = self.psum("fin_pss", [128, 512], F32, es)
            ps = [self.psum("fin_ps%d" % i, [128, 512], F32, es) for i in range(4)]
            est = {'sq': sq, 'tmp': tmp}
            dst = self.y.rearrange("(b p t) d -> b t p d", b=2, t=8)
            n = 0
            for blk in range(4):
                self.norm_mod(blk, self.cF, None, yb[0], 0, est, pss, lambda j: ('fin_y', j))
                for tt4 in range(4):
                    tile = 4 * blk + tt4
                    b, t = tile // 8, tile % 8
                    ytb = yt[n % 2]
                    yk = ('fin_yt', n % 2)
                    for h in range(2):
                        pb = ps[(2 * n + h) % 4]
                        pk = ('fin_ps', (2 * n + h) % 4)
                        for jj in range(4):
                            j = 4 * h + jj
                            self.tr(pb[:, 128 * jj:128 * jj + 128], yb[0][:, j, 128 * tt4:128 * tt4 + 128], self.identf[:],
                                    [('fin_y', j), 'identf'], [pk])
                        self.copy(self.evac_eng(), ytb[:, 512 * h:512 * h + 512], pb[:], [pk], [yk])
                    self.st(dst[b, t], ytb[:], [yk])
                    n += 1

    def mixer(self, l):
        S = self.S
        A, Bv, G = self.cA[:, l, 1, :], self.cB[:, l, 1, :], self.cG[:, l, 1, :]
        with contextlib.ExitStack() as es:
            hm = self.sb("m_hm", [128, 8, NT], BF16, es)
            mo = None
            self.G2 = G
            with contextlib.ExitStack() as es2:
                sq = self.sb("m_sq", [128, 8, 512], BF16, es2)
                tmp = [self.sb("m_tmp%d" % i, [128, 512], F32, es2) for i in range(2)]
                pss = self.psum("m_pss", [128, 512], F32, es2)
                for blk in range(4):
                    self.norm_mod(blk, A, Bv, hm, 512 * blk, {'sq': sq, 'tmp': tmp}, pss,
                                  lambda j, blk=blk: ('m_hm', j, blk))
            import os
            mix = int(os.environ.get('MIX', '3'))
            S.barrier()
            if mix & 1:
                self.s5(l, hm, mo)
            S.barrier()
            if mix & 2:
                self.attn(l, hm, mo)

    def wout_part(self, l, ktiles, srcs, wbufs, psl, pskeys):
        wv = self._ap('w_out')[l].rearrange("(k p) d -> p k d", p=128)
        G = self.G2
        for i, kt in enumerate(ktiles):
            wb, wk = wbufs[i]
            self.ld(wb, wv[:, kt, :], [wk], eng='pool')
        n = 0
        for jo in range(8):
            for blk in range(4):
                po, pk = psl[n % len(psl)], pskeys[n % len(psl)]
                n += 1
                for i, kt in enumerate(ktiles):
                    wb, wk = wbufs[i]
                    src, kf = srcs[i]
                    self.mm(po[:], wb[:, 128 * jo:128 * jo + 128], src[:, 512 * blk:512 * blk + 512], i == 0, i == len(ktiles) - 1,
                            [wk, kf(blk)], [pk])
                xs = self.xT[:, jo, 512 * blk:512 * blk + 512]
                self.stt('dve', xs, po[:], G[:, jo:jo + 1], xs, ALU.mult, ALU.add, [pk, ('xT', jo, blk)], [('xT', jo, blk)])

    def cmul(self, outr, outi, ar, ai, br, bi, t1, t2, key_r, key_w, neg_im=False, eng='dve'):
        rd = list(key_r)
        self.tt(eng, t1, ar, br, ALU.mult, rd, ['cm_t1'])
        self.tt(eng, t2, ai, bi, ALU.mult, rd, ['cm_t2'])
        self.tt(eng, outr, t1, t2, ALU.subtract, ['cm_t1', 'cm_t2'] + rd, list(key_w))
        self.tt(eng, t1, ar, bi, ALU.mult, rd + list(key_w), ['cm_t1'])
        self.tt(eng, t2, ai, br, ALU.mult, rd + list(key_w), ['cm_t2'])
        if neg_im:
            self.stt(eng, outi, t1, -1.0, t2, ALU.mult, ALU.subtract, ['cm_t1', 'cm_t2'], list(key_w))
        else:
            self.tt(eng, outi, t1, t2, ALU.add, ['cm_t1', 'cm_t2'], list(key_w))

    def s5(self, l, hm, mo):
        S = self.S
        dr = self._dr_cache
        PI = math.pi
        with contextlib.ExitStack() as es:
            ToepT = [self.sb("s_toep%d" % d, [128, 16, 128], BF16, es) for d in range(2)]
            BSm = [self.sb("s_bsm%d" % d, [128, 16, 2, 64], BF16, es) for d in range(2)]
            CCm = [self.sb("s_ccm%d" % d, [128, 8, 2, 128], BF16, es) for d in range(2)]
            PWr = [self.sb("s_pwr%d" % d, [128, 8, 33], F32, es) for d in range(2)]
            PWi = [self.sb("s_pwi%d" % d, [128, 8, 33], F32, es) for d in range(2)]
            PWin = [self.sb("s_pwin%d" % d, [128, 8, 33], F32, es) for d in range(2)]
            h0r = [self.sb("s_h0r%d" % d, [128, 8], F32, es) for d in range(2)]
            h0i = [self.sb("s_h0i%d" % d, [128, 8], F32, es) for d in range(2)]
            U = self.sb("s_U", [128, 16, 256], BF16, es)
            flag = self.sb("s_flag", [128, 1], F32, es)
            self.ld(flag[:], dr['flag'], ['s_flag'])
            with contextlib.ExitStack() as ea:
                def t4(name):
                    return self.sb(name, [128, 8, 8, 16], F32, ea)
                cm1, cm2 = t4("sa_cm1"), t4("sa_cm2")
                BLr, BLi, CLr, CLi = t4("sa_blr"), t4("sa_bli"), t4("sa_clr"), t4("sa_cli")
                BSr, BSi, CCr, CCi = t4("sa_bsr"), t4("sa_bsi"), t4("sa_ccr"), t4("sa_cci")
                sm = self.sb("sa_sm", [128, 40, 8], F32, ea)
                bre = self.sb("sa_bre", [128, 8, 16], F32, ea)
                bim = self.sb("sa_bim", [128, 8, 16], F32, ea)
                bbr = self.sb("sa_bbr", [128, 8, 16], F32, ea)
                bbi = self.sb("sa_bbi", [128, 8, 16], F32, ea)
                cre = self.sb("sa_cre", [128, 8, 16], F32, ea)
                cim = self.sb("sa_cim", [128, 8, 16], F32, ea)
                crow = self.sb("sa_crow", [128, 2, 2, 64], F32, ea)
                Pr = self.sb("sa_Pr", [128, 8, 9], F32, ea)
                Pi_ = self.sb("sa_Pi", [128, 8, 9], F32, ea)
                Nr = self.sb("sa_Nr", [128, 8, 8], F32, ea)
                Ni = self.sb("sa_Ni", [128, 8, 8], F32, ea)
                Dcol = self.sb("sa_Dcol", [128, 16], F32, ea)
                cmk = [self.sb("sa_cmk%d" % d, [128, 128], F32, ea) for d in range(2)]
                tT = self.sb("sa_tT", [128, 128], F32, ea)
                cst = self.sb("sa_cst", [128, 2], F32, ea)
                psA = [self.psum("sa_ps%d" % i, [128, 512], F32, ea) for i in range(4)]
                S.op('dve', lambda e: e.memset(cst[:, 0:1], -PI), (), ['sa_cst'])
                self.ld(cmk[0][:], dr['cmask_f'], ['sa_cmk'])
                self.ld(cmk[1][:], dr['cmask_b'], ['sa_cmk'])
                for t in range(8):
                    self.ld(Dcol[16 * t:16 * t + 16, :], dr['s5_d'][l].rearrange("g c -> c g"), ['sa_Dcol'],
                            allow_slow_non_contiguous=True)
                npsA = 0
                for d in range(2):
                    K = ['sa']
                    are, aim, lst = sm[:, 0, :], sm[:, 1, :], sm[:, 2, :]
                    for gl in range(2):
                        ps_ = slice(64 * gl, 64 * gl + 64)
                        self.ld(sm[ps_, 0, :], dr['s5_a_re'][l, d, gl].rearrange("g p -> p g"), K, K, allow_slow_non_contiguous=True)
                        self.ld(sm[ps_, 1, :], dr['s5_a_im'][l, d, gl].rearrange("g p -> p g"), K, K, allow_slow_non_contiguous=True)
                        self.ld(sm[ps_, 2, :], dr['s5_log_step'][l, d, gl].partition_broadcast(64), K, K)
                        self.ld(h0r[d][ps_, :], dr['h0re'][l, d, gl].rearrange("g p -> p g"), K, K, allow_slow_non_contiguous=True)
                        self.ld(h0i[d][ps_, :], dr['h0im'][l, d, gl].rearrange("g p -> p g"), K, K, allow_slow_non_contiguous=True)
                        self.ld(bre[ps_, :, :], dr['s5_b_re'][l, d, gl].rearrange("g p c -> p g c"), K, K)
                        self.ld(bim[ps_, :, :], dr['s5_b_im'][l, d, gl].rearrange("g p c -> p g c"), K, K)
                        self.ld(crow[:, gl, 0, :], dr['s5_c_re'][l, d, gl], K, K)
                        self.ld(crow[:, gl, 1, :], dr['s5_c_im'][l, d, gl], K, K)
                    pc, pck = psA[npsA % 4], ('sa_ps', npsA % 4)
                    npsA += 1
                    for gl in range(2):
                        for ri in range(2):
                            self.mm(pc[64 * gl:64 * gl + 64, 128 * ri:128 * ri + 128], crow[:, gl, ri, :], self.identf[:], True, True, K + ['identf'], [pck])
                    self.copy('dve', cre[:].rearrange("p g c -> p (g c)"), pc[:, 0:128], [pck], K)
                    self.copy('dve', cim[:].rearrange("p g c -> p (g c)"), pc[:, 128:256], [pck], K)

                    import os
                    s5a = int(os.environ.get('S5A', '9'))
                    if s5a <= 1:
                        S.barrier()
                        return

                    def sop(fn):
                        S.op('dve', fn, K, K)
                    sl = lambda i: sm[:, i, :]
                    self.act(sl(3), lst, AF.Exp, K, K)
                    self.tt('dve', sl(4), are, sl(3), ALU.mult, K, K)
                    self.tt('dve', sl(5), aim, sl(3), ALU.mult, K, K)
                    self.act(sl(6), sl(4), AF.Exp, K, K)
                    self.act(sl(7), sl(4), AF.Exp, K, K, scale=-2.0)
                    MAGIC = 12582912.0
                    for (dst_i, off) in ((9, 0.0), (10, 0.25)):
                        self.ts('dve', sl(8), sl(5), 1.0 / (2 * PI), None, ALU.mult, None, K, K)
                        if off:
                            self.ts('dve', sl(8), sl(8), off, None, ALU.add, None, K, K)
                        self.ts('dve', sl(22), sl(8), MAGIC, None, ALU.add, None, K, K)
                        self.ts('dve', sl(22), sl(22), -MAGIC, None, ALU.add, None, K, K)
                        self.tt('dve', sl(8), sl(8), sl(22), ALU.subtract, K, K)
                        self.act(sl(dst_i), sl(8), AF.Sin, K, K, scale=2 * PI)
                    lr, li = sl(11), sl(12)
                    self.tt('dve', lr, sl(6), sl(10), ALU.mult, K, K)
                    self.tt('dve', li, sl(6), sl(9), ALU.mult, K, K)
                    ilr, ili = sl(13), sl(14)
                    self.tt('dve', ilr, lr, sl(7), ALU.mult, K, K)
                    self.stt('dve', ili, li, -1.0, sl(7), ALU.mult, ALU.mult, K, K)
                    self.tt('dve', sl(15), are, are, ALU.mult, K, K)
                    self.tt('dve', sl(16), aim, aim, ALU.mult, K, K)
                    self.tt('dve', sl(15), sl(15), sl(16), ALU.add, K, K)
                    sop(lambda e: e.reciprocal(out=sl(15), in_=sl(15)))
                    self.ts('dve', sl(16), lr, -1.0, None, ALU.add, None, K, K)
                    self.tt('dve', sl(17), sl(16), are, ALU.mult, K, K)
                    self.tt('dve', sl(18), li, aim, ALU.mult, K, K)
                    self.tt('dve', sl(17), sl(17), sl(18), ALU.add, K, K)
                    self.tt('dve', sl(17), sl(17), sl(15), ALU.mult, K, K)
                    self.tt('dve', sl(18), li, are, ALU.mult, K, K)
                    self.tt('dve', sl(19), sl(16), aim, ALU.mult, K, K)
                    self.tt('dve', sl(18), sl(18), sl(19), ALU.subtract, K, K)
                    self.tt('dve', sl(18), sl(18), sl(15), ALU.mult, K, K)
                    kb = lambda i: sm[:, i, :].unsqueeze(2).to_broadcast([128, 8, 16])
                    self.cmul(bbr[:], bbi[:], kb(17), kb(18), bre[:], bim[:], cm1[:, :, 0, :], cm2[:, :, 0, :], K, K)
                    sop(lambda e: e.memset(Pr[:, :, 0:1], 1.0))
                    sop(lambda e: e.memset(Pi_[:, :, 0:1], 0.0))
                    sop(lambda e: e.memset(Nr[:, :, 0:1], 1.0))
                    sop(lambda e: e.memset(Ni[:, :, 0:1], 0.0))
                    for k in range(1, 9):
                        self.cmul(Pr[:, :, k], Pi_[:, :, k], Pr[:, :, k - 1], Pi_[:, :, k - 1], lr, li, sl(20), sl(21), K, K)
                    for k in range(1, 8):
                        self.cmul(Nr[:, :, k], Ni[:, :, k], Nr[:, :, k - 1], Ni[:, :, k - 1], ilr, ili, sl(20), sl(21), K, K)
                    pr, pi = PWr[d], PWi[d]
                    self.copy('dve', pr[:, :, 1], Pr[:, :, 8], K, K)
                    self.copy('dve', pi[:, :, 1], Pi_[:, :, 8], K, K)
                    n = 1
                    while n < 32:
                        bshape = [128, 8, n]
                        self.cmul(pr[:, :, n + 1:2 * n + 1], pi[:, :, n + 1:2 * n + 1], pr[:, :, 1:n + 1], pi[:, :, 1:n + 1],
                                  pr[:, :, n:n + 1].to_broadcast(bshape), pi[:, :, n:n + 1].to_broadcast(bshape),
                                  cm1[:].rearrange("p a b c -> p a (b c)")[:, :, 0:n], cm2[:].rearrange("p a b c -> p a (b c)")[:, :, 0:n], K, K)
                        n *= 2
                    self.ts('dve', PWin[d][:], pi[:], -1.0, None, ALU.mult, None, K, K)
                    bsh = [128, 8, 8, 16]
                    bbR = bbr[:].unsqueeze(2).to_broadcast(bsh)
                    bbI = bbi[:].unsqueeze(2).to_broadcast(bsh)
                    cR = cre[:].unsqueeze(2).to_broadcast(bsh)
                    cI = cim[:].unsqueeze(2).to_broadcast(bsh)
                    pw = lambda T, sl_: T[:, :, sl_].unsqueeze(3).to_broadcast(bsh)
                    if d == 0:
                        self.cmul(BLr[:], BLi[:], bbR, bbI, pw(Nr, slice(0, 8)), pw(Ni, slice(0, 8)), cm1[:], cm2[:], K, K)
                        self.cmul(CLr[:], CLi[:], cR, cI, pw(Pr, slice(0, 8)), pw(Pi_, slice(0, 8)), cm1[:], cm2[:], K, K, neg_im=True)
                        self.cmul(BSr[:], BSi[:], bbR, bbI, pw(Pr, slice(7, None, -1)), pw(Pi_, slice(7, None, -1)), cm1[:], cm2[:], K, K)
                        self.cmul(CCr[:], CCi[:], cR, cI, pw(Pr, slice(1, 9)), pw(Pi_, slice(1, 9)), cm1[:], cm2[:], K, K, neg_im=True)
                    else:
                        self.cmul(BLr[:], BLi[:], bbR, bbI, pw(Pr, slice(0, 8)), pw(Pi_, slice(0, 8)), cm1[:], cm2[:], K, K)
                        self.cmul(CLr[:], CLi[:], cR, cI, pw(Nr, slice(0, 8)), pw(Ni, slice(0, 8)), cm1[:], cm2[:], K, K, neg_im=True)
                        self.copy('dve', BSr[:], BLr[:], K, K)
                        self.copy('dve', BSi[:], BLi[:], K, K)
                        self.cmul(CCr[:], CCi[:], cR, cI, pw(Pr, slice(8, 0, -1)), pw(Pi_, slice(8, 0, -1)), cm1[:], cm2[:], K, K, neg_im=True)
                    f2 = lambda T: T[:].rearrange("p a b c -> p a (b c)")
                    if s5a <= 2:
                        S.barrier()
                        return
                    for g in range(16):
                        gl, gh = g % 2, g // 2
                        ps_ = slice(64 * gl, 64 * gl + 64)
                        pt, ptk = psA[npsA % 4], ('sa_ps', npsA % 4)
                        npsA += 1
                        self.mm(pt[:, 0:128], f2(BLr)[ps_, gh, :], f2(CLr)[ps_, gh, :], True, False, K, [ptk])
                        self.mm(pt[:, 0:128], f2(BLi)[ps_, gh, :], f2(CLi)[ps_, gh, :], False, True, K, [ptk])
                        if d == 0:
                            self.tt('dve', tT[:], pt[:, 0:128], cmk[0][:], ALU.mult, [ptk, 'sa_cmk'], ['sa_tT'])
                            self.stt('dve', ToepT[0][:, g, :], self.identf[:], Dcol[:, g:g + 1], tT[:], ALU.mult, ALU.add,
                                     ['sa_tT', 'sa_Dcol', 'identf'], [('s_toep', 0)])
                        else:
                            self.tt('dve', ToepT[1][:, g, :], pt[:, 0:128], cmk[1][:], ALU.mult, [ptk, 'sa_cmk'], [('s_toep', 1)])
                    if s5a <= 3:
                        S.barrier()
                        return
                    for ri, T in enumerate((BSr, BSi)):
                        for g8 in range(2):
                            pt, ptk = psA[npsA % 4], ('sa_ps', npsA % 4)
                            npsA += 1
                            for hh in range(4):
                                gh = 4 * g8 + hh
                                self.tr(pt[:, 128 * hh:128 * hh + 128], f2(T)[:, gh, :], self.identf[:], K + ['identf'], [ptk])
                            self.copy('act', BSm[d][:, 8 * g8:8 * g8 + 8, ri, :], pt[:].rearrange("p (g q) -> p g q", g=8), [ptk], [('s_bsm', d)])
                    if s5a <= 4:
                        S.barrier()
                        return
                    self.copy('dve', CCm[d][:, :, 0, :], f2(CCr), K, [('s_ccm', d)])
                    self.copy('dve', CCm[d][:, :, 1, :], f2(CCi), K, [('s_ccm', d)])
                    if s5a <= 5:
                        S.barrier()
                        return
            S.barrier()
            import os
            s5stop = os.environ.get('S5STOP', 'Z')
            if s5stop == 'A':
                return
            with contextlib.ExitStack() as eb:
                wu = self.sb("sb_wu", [128, 8, 256], BF16, eb)
                ub = self.sb("sb_ub", [128, 2, 16, 8, 16], BF16, eb)
                psu = [self.psum("sb_psu%d" % i, [128, 512], F32, eb) for i in range(2)]
                pst = [self.psum("sb_pst%d" % i, [128, 1024], BF16, eb) for i in range(2)]
                self.ld(wu[:], self._ap('w_in')[l].rearrange("(k p) f -> p k f", p=128)[:, :, 1568:1824], ['sb_wu'], eng='pool')
                n = 0
                for b in range(2):
                    for t in range(8):
                        pos = 1024 * b + 128 * t
                        pu, puk = psu[n % 2], ('sb_psu', n % 2)
                        n += 1
                        for k in range(8):
                            self.mm(pu[:, 0:256], hm[:, k, pos:pos + 128], wu[:, k, :], k == 0, k == 7,
                                    ['sb_wu', ('m_hm', k, pos // 512)], [puk])
                        self.copy(self.evac_eng(), ub[:, b, :, t, :], pu[:, 0:256].rearrange("p (g c) -> p g c", g=16), [puk], [('sb_ub', b)])
                n = 0
                for b in range(2):
                    for q in range(4):
                        pt, ptk = pst[n % 2], ('sb_pst', n % 2)
                        n += 1
                        for gg in range(4):
                            g = 4 * q + gg
                            self.tr(pt[:, 128 * gg:128 * gg + 128], ub[:, b, g, :, :].rearrange("p t c -> p (t c)"), self.identb[:],
                                    [('sb_ub', b), 'identb'], [ptk])
                        self.copy(self.evac_eng(), U[:, 4 * q:4 * q + 4, 128 * b:128 * b + 128],
                                  pt[:, 0:512].rearrange("p (g c) -> p g c", g=4), [ptk], ['s_U'])
            S.barrier()
            if s5stop == 'B':
                return
            with contextlib.ExitStack() as ec:
                Hp = [[self.sb("sc_hp%d%d" % (d, ri), [128, 8, 256], BF16, ec) for ri in range(2)] for d in range(2)]
                with contextlib.ExitStack() as ec2:
                    La = [self.sb("sc_la%d" % ri, [128, 8, 256], F32, ec2) for ri in range(2)]
                    Lb = [self.sb("sc_lb%d" % ri, [128, 8, 256], F32, ec2) for ri in range(2)]
                    Cy = [self.sb("sc_cy%d" % ri, [128, 8, 8], F32, ec2) for ri in range(2)]
                    sm2 = self.sb("sc_sm", [128, 4, 8], F32, ec2)
                    Fsb = self.sb("sc_F", [128, 2, 64], F32, ec2)
                    FT = self.sb("sc_FT", [64, 2, 128], F32, ec2)
                    psS = [[self.psum("sc_ps%d%d" % (ri, q), [128, 512], F32, ec2) for q in range(3)] for ri in range(2)]
                    psF = self.psum("sc_psF", [128, 512], F32, ec2)
                    for d in range(2):
                        KL = ['sc_L']
                        for ri in range(2):
                            for q4 in range(4):
                                pb, pbk = psS[ri][q4 % 3], ('sc_ps', ri, q4 % 3)
                                for hh in range(2):
                                    gh = 2 * q4 + hh
                                    for gl in range(2):
                                        g = 2 * gh + gl
                                        self.mm(pb[64 * gl:64 * gl + 64, 256 * hh:256 * hh + 256], BSm[d][:, g, ri, :], U[:, g, :], True, True,
                                                [('s_bsm', d), 's_U'], [pbk])
                                self.copy(self.evac_eng(), La[ri][:, 2 * q4:2 * q4 + 2, :], pb[:].rearrange("p (a c) -> p a c", a=2), [pbk], KL)
                        cur, nxt = La, Lb
                        v5 = lambda T: T[:].rearrange("p a (s k) -> p a s k", k=32)
                        for dd in (1, 2, 4, 8, 16):
                            for ri in range(2):
                                if d == 0:
                                    self.copy('act', v5(nxt[ri])[:, :, :, 0:dd], v5(cur[ri])[:, :, :, 0:dd], KL, KL)
                                else:
                                    self.copy('act', v5(nxt[ri])[:, :, :, 32 - dd:32], v5(cur[ri])[:, :, :, 32 - dd:32], KL, KL)
                            for gh in range(8):
                                vv = lambda T: T[:, gh, :].rearrange("p (s k) -> p s k", k=32)
                                if d == 0:
                                    dst = slice(dd, 32)
                                    src = slice(0, 32 - dd)
                                else:
                                    dst = slice(0, 32 - dd)
                                    src = slice(dd, 32)
                                lr_ = PWr[d][:, gh, dd:dd + 1]
                                li_ = PWi[d][:, gh, dd:dd + 1]
                                lin_ = PWin[d][:, gh, dd:dd + 1]
                                self.stt('dve', vv(nxt[0])[:, :, dst], vv(cur[0])[:, :, src], lr_, vv(cur[0])[:, :, dst], ALU.mult, ALU.add, KL, KL)
                                self.stt('dve', vv(nxt[0])[:, :, dst], vv(cur[1])[:, :, src], lin_, vv(nxt[0])[:, :, dst], ALU.mult, ALU.add, KL, KL)
                                self.stt('dve', vv(nxt[1])[:, :, dst], vv(cur[0])[:, :, src], li_, vv(cur[1])[:, :, dst], ALU.mult, ALU.add, KL, KL)
                                self.stt('dve', vv(nxt[1])[:, :, dst], vv(cur[1])[:, :, src], lr_, vv(nxt[1])[:, :, dst], ALU.mult, ALU.add, KL, KL)
                            cur, nxt = nxt, cur
                        L = cur
                        E = [v5(L[ri])[:, :, :, 31 if d == 0 else 0] for ri in range(2)]
                        l32r, l32i = PWr[d][:, :, 32], PWi[d][:, :, 32]
                        order = list(range(8)) if d == 0 else list(range(7, -1, -1))
                        s0 = order[0]
                        self.copy('dve', Cy[0][:, :, s0], h0r[d][:], KL, KL)
                        self.copy('dve', Cy[1][:, :, s0], h0i[d][:], KL, KL)
                        for idx in range(1, 8):
                            s, sp_ = order[idx], order[idx - 1]
                            self.cmul(sm2[:, 0, :], sm2[:, 1, :], l32r, l32i, Cy[0][:, :, sp_], Cy[1][:, :, sp_], sm2[:, 2, :], sm2[:, 3, :], KL, KL)
                            for ri in range(2):
                                self.tt('dve', sm2[:, ri, :], sm2[:, ri, :], E[ri][:, :, sp_], ALU.add, KL, KL)
                                self.ts('dve', Cy[ri][:, :, s], sm2[:, ri, :], flag[:, 0:1], None, ALU.mult, None, KL + ['s_flag'], KL)
                        sh4 = [128, 8, 8, 32]
                        if d == 0:
                            pwv = lambda T: T[:, :, 1:33].unsqueeze(2).to_broadcast(sh4)
                        else:
                            pwv = lambda T: T[:, :, 32:0:-1].unsqueeze(2).to_broadcast(sh4)
                        cyv = lambda ri: Cy[ri][:].unsqueeze(3).to_broadcast(sh4)
                        t1v = v5(nxt[0])
                        for (ri, a, b_, op) in ((0, PWr[d], 0, ALU.add), (0, PWi[d], 1, ALU.subtract), (1, PWr[d], 1, ALU.add), (1, PWi[d], 0, ALU.add)):
                            self.tt('dve', t1v, pwv(a), cyv(b_), ALU.mult, KL, ['sc_t1'])
                            self.tt('dve', v5(L[ri]), v5(L[ri]), t1v, op, KL + ['sc_t1'], KL)
                        for ri in range(2):
                            hv = v5(Hp[d][ri])
                            if d == 0:
                                self.copy('act', hv[:, :, :, 1:32], v5(L[ri])[:, :, :, 0:31], KL, [('sc_hp', d)])
                                self.copy('dve', hv[:, :, :, 0], Cy[ri][:], KL, [('sc_hp', d)])
                            else:
                                self.copy('act', hv[:, :, :, 0:31], v5(L[ri])[:, :, :, 1:32], KL, [('sc_hp', d)])
                                self.copy('dve', hv[:, :, :, 31], Cy[ri][:], KL, [('sc_hp', d)])
                        for ri in range(2):
                            self.copy('dve', Fsb[:, ri, :].rearrange("p (a s) -> p a s", a=8), E[ri], KL, ['sc_F'])
                            self.tr(psF[0:64, 128 * ri:128 * ri + 128], Fsb[:, ri, :], self.identf[:], ['sc_F', 'identf'], ['sc_psF'])
                        self.copy('dve', FT[:].rearrange("p a b -> p (a b)"), psF[0:64, 0:256], ['sc_psF'], ['sc_FT'])
                        for ri, nm in enumerate(('ns5re', 'ns5im')):
                            for gh in range(8):
                                self.st(self.o[nm][l, d][:, 128 * gh:128 * gh + 128], FT[8 * gh:8 * gh + 8, ri, :], ['sc_FT'])
                S.barrier()
                if s5stop == 'C':
                    return
                with contextlib.ExitStack() as ed:
                    Ysb = self.sb("sd_Y", [128, 16, 256], BF16, ed)
                    g1 = [self.sb("sd_g%d" % i, [128, 512], F32, ed) for i in range(2)]
                    psY = [self.psum("sd_ps%d" % i, [128, 512], F32, ed) for i in range(3)]
                    for q in range(8):
                        py, pyk = psY[q % 3], ('sd_ps', q % 3)
                        for hh in range(2):
                            g = 2 * q + hh
                            gl, gh = g % 2, g // 2
                            ps_ = slice(64 * gl, 64 * gl + 64)
                            o = py[:, 256 * hh:256 * hh + 256]
                            for d in range(2):
                                self.mm(o, ToepT[d][:, g, :], U[:, g, :], d == 0, False, [('s_toep', d), 's_U'], [pyk])
                                self.mm(o, CCm[d][ps_, gh, 0, :], Hp[d][0][ps_, gh, :], False, False, [('s_ccm', d), ('sc_hp', d)], [pyk])
                                self.mm(o, CCm[d][ps_, gh, 1, :], Hp[d][1][ps_, gh, :], False, d == 1, [('s_ccm', d), ('sc_hp', d)], [pyk])
                        gb, gk = g1[q % 2], ('sd_g', q % 2)
                        self.act(gb[:], py[:], AF.Square, [pyk], [gk])
                        self.ts('dve', gb[:], gb[:], 0.044715, None, ALU.mult, None, [gk], [gk])
                        self.ts('dve', gb[:], gb[:], 1.0, None, ALU.add, None, [gk], [gk])
                        self.tt('dve', gb[:], gb[:], py[:], ALU.mult, [gk, pyk], [gk])
                        self.act(gb[:], gb[:], AF.Sigmoid, [gk], [gk], scale=1.5957691216)
                        self.tt('dve', Ysb[:, 2 * q:2 * q + 2, :], gb[:].rearrange("p (a c) -> p a c", a=2),
                                py[:].rearrange("p (a c) -> p a c", a=2), ALU.mult, [gk, pyk], ['sd_Y'])
                    S.barrier()
                    ytok = self.sb("se_ytok", [128, 2, 8, 256], BF16, ed)
                    ygT = self.sb("se_ygT", [128, 2, NT], BF16, ed)
                    mos = self.sb("se_mo", [128, 2, NT], BF16, ed)
                    wob = [self.sb("se_wo%d" % i, [128, D], BF16, ed) for i in range(2)]
                    wg = self.sb("se_wg", [128, 2, 256], BF16, ed)
                    bg = self.sb("se_bg", [128, 2], F32, ed)
                    sg = [self.sb("se_sg%d" % i, [128, 512], F32, ed) for i in range(2)]
                    pst = [self.psum("se_pst%d" % i, [128, 1024], BF16, ed) for i in range(2)]
                    psg = [self.psum("se_psg%d" % i, [128, 512], F32, ed) for i in range(2)]
                    self.ld(wg[:], dr['s5_w_glu'][l].rearrange("(k p) f -> p k f", p=128), ['se_wg'], eng='pool')
                    self.ld(bg[:], dr['s5_b_glu'][l].rearrange("a p -> p a"), ['se_bg'], allow_slow_non_contiguous=True)
                    n = 0
                    for b in range(2):
                        for q in range(4):
                            pt, ptk = pst[n % 2], ('se_pst', n % 2)
                            n += 1
                            for gg in range(4):
                                g = 4 * q + gg
                                self.tr(pt[:, 128 * gg:128 * gg + 128], Ysb[:, g, 128 * b:128 * b + 128], self.identb[:], ['sd_Y', 'identb'], [ptk])
                            dst = ytok[:, b].rearrange("p t (g c) -> p g t c", c=16)[:, 4 * q:4 * q + 4]
                            self.copy(self.evac_eng(), dst, pt[:, 0:512].rearrange("p (g t c) -> p g t c", g=4, t=8), [ptk], [('se_ytok', b)])
                    for b in range(2):
                        for f in range(2):
                            for h4 in range(2):
                                pt, ptk = pst[n % 2], ('se_pst', n % 2)
                                n += 1
                                for tt_ in range(4):
                                    t = 4 * h4 + tt_
                                    self.tr(pt[:, 128 * tt_:128 * tt_ + 128], ytok[:, b, t, 128 * f:128 * f + 128], self.identb[:],
                                            [('se_ytok', b), 'identb'], [ptk])
                                blk = 2 * b + h4
                                self.copy(self.evac_eng(), ygT[:, f, 512 * blk:512 * blk + 512], pt[:, 0:512], [ptk], [('se_ygT', f, blk)])
                    n = 0
                    for fo in range(2):
                        for blk in range(4):
                            pg, pgk = psg[n % 2], ('se_psg', n % 2)
                            sgb, sgk = sg[n % 2], ('se_sg', n % 2)
                            n += 1
                            for k in range(2):
                                self.mm(pg[:], wg[:, k, 128 * fo:128 * fo + 128], ygT[:, k, 512 * blk:512 * blk + 512], k == 0, k == 1,
                                        ['se_wg', ('se_ygT', k, blk)], [pgk])
                            self.act(sgb[:], pg[:], AF.Sigmoid, [pgk, 'se_bg'], [sgk], bias=bg[:, fo:fo + 1])
                            self.tt('dve', mos[:, fo, 512 * blk:512 * blk + 512], sgb[:], ygT[:, fo, 512 * blk:512 * blk + 512], ALU.mult,
                                    [sgk, ('se_ygT', fo, blk)], [('se_mo', fo, blk)])
                    self.wout_part(l, [6, 7], [(mos[:, fo, :], (lambda blk, fo=fo: ('se_mo', fo, blk))) for fo in range(2)],
                                   [(wob[i][:], ('se_wo', i)) for i in range(2)], psg, [('se_psg', 0), ('se_psg', 1)])

    def rope(self, src5, dsts, tt, nb, tmps, rkey, wkeys):
        b, t = tt // 8, tt % 8
        tk = ['rp_t']
        if nb == 1:
            cosb = self.rc[:, b, t, :].rearrange("p (a f) -> p a f", a=2)
            sinb = self.rsn[:, b, t, :].rearrange("p (a f) -> p a f", a=2)
            x1, x2 = src5[:, 0, :, 0, :], src5[:, 0, :, 1, :]
            t1, t2, t3, t4 = [T[:, 0:16].rearrange("p (a f) -> p a f", a=2) for T in tmps]
        else:
            sh = [128, nb, 2, 8]
            cosb = self.rc[:, b, t, :].rearrange("p (a f) -> p a f", a=2).unsqueeze(1).to_broadcast(sh)
            sinb = self.rsn[:, b, t, :].rearrange("p (a f) -> p a f", a=2).unsqueeze(1).to_broadcast(sh)
            x1, x2 = src5[:, :, :, 0, :], src5[:, :, :, 1, :]
            t1, t2, t3, t4 = [T[:, 0:nb * 16].rearrange("p (n a f) -> p n a f", a=2, f=8) for T in tmps]
        self.tt('dve', t1, x1, cosb, ALU.mult, rkey + ['rope_tab'], tk)
        self.tt('dve', t2, x2, sinb, ALU.mult, rkey + ['rope_tab'], tk)
        self.tt('dve', t3, x1, sinb, ALU.mult, rkey + ['rope_tab'], tk)
        self.tt('dve', t4, x2, cosb, ALU.mult, rkey + ['rope_tab'], tk)
        for (bs, dst5), wk in zip(dsts, wkeys):
            if nb == 1:
                self.tt('dve', dst5[:, 0, :, 0, :], t1, t2, ALU.subtract, tk, [wk])
                self.tt('dve', dst5[:, 0, :, 1, :], t3, t4, ALU.add, tk, [wk])
            else:
                self.tt('dve', dst5[:, :, :, 0, :], t1[:, bs], t2[:, bs], ALU.subtract, tk, [wk])
                self.tt('dve', dst5[:, :, :, 1, :], t3[:, bs], t4[:, bs], ALU.add, tk, [wk])

    def attn(self, l, hm, mo):
        S = self.S
        dr = self._dr_cache
        lam_init = 0.8 - 0.6 * math.exp(-0.3 * l)
        with contextlib.ExitStack() as es:
            sb = lambda n, s, d: self.sb(n, s, d, es)
            self.rc = sb("a_rc", [128, 2, 8, 16], F32)
            self.rsn = sb("a_rs", [128, 2, 8, 16], F32)
            aq = sb("a_aq", [128, 16, 9], F32)
            ak = sb("a_ak", [128, 18, 9], F32)
            QS = sb("a_QS", [128, 16, 128], BF16)
            KS = sb("a_KS", [128, 18, 128], BF16)
            QT = [sb("a_QT%d" % i, [128, NT], BF16) for i in range(2)]
            KT = [sb("a_KT%d" % i, [128, NT + 256], BF16) for i in range(2)]
            Vh = [sb("a_V%d" % i, [128, 18, 72], BF16) for i in range(2)]
            PT = [sb("a_PT%d" % i, [128, 512], BF16) for i in range(4)]
            moh = [sb("a_moh%d" % i, [128, NT], BF16) for i in range(2)]
            wob = [sb("a_wo%d" % i, [128, D], BF16) for i in range(2)]
            rt = [sb("a_rt%d" % i, [128, 64], F32) for i in range(4)]
            o0 = sb("a_o0", [128, 4, 64], F32)
            o1 = sb("a_o1", [128, 4, 64], F32)
            osq = sb("a_osq", [128, 4, 64], F32)
            ost = sb("a_ost", [128, 4, 64], BF16)
            sml = sb("a_sml", [128, 16], F32)
            dl = sb("a_dl", [128, 128], F32)
            subw = sb("a_subw", [128, 64], F32)
            lamt = sb("a_lam", [128, 4], F32)
            oTs = [sb("a_oTs%d" % i, [128, 512], F32) for i in range(2)]
            edf = contextlib.ExitStack()
            wh = [self.sb("a_wh%d" % i, [128, 8, 192], BF16, edf) for i in range(2)]
            cdk = self.sb("a_cdk", [128, 2, 384], F32, edf)
            cdv = self.sb("a_cdv", [128, 2, 384], F32, edf)
            kvo = [self.sb("a_kvo%d" % i, [128, 128], F32, edf) for i in range(2)]
            psp = [self.psum("a_psp%d" % i, [128, 512], F32, es) for i in range(2)]
            pstr = [self.psum("a_pst%d" % i, [128, 1024], BF16, es) for i in range(1)]
            NPSS = 3
            pss = [self.psum("a_pss%d" % i, [128, 512], F32, es) for i in range(NPSS)]
            pso = [self.psum("a_pso%d" % i, [128, 512], F32, es) for i in range(2)]
            cnt = {'psp': 0, 'pst': 0, 'pss': 0, 'PT': 0, 'kvo': 0}
            pss_l = [(pss[i][:], ('a_pss', i)) for i in range(NPSS)] + [(pstr[0][:].bitcast(F32), ('a_pst', 0))]
            assert list(pss_l[3][0].shape) == [128, 512], pss_l[3][0].shape

            self.ld(self.rc[:], dr['ropec'].rearrange("(b p t) f -> p b t f", b=2, t=8), ['rope_tab'])
            self.ld(self.rsn[:], dr['ropes'].rearrange("(b p t) f -> p b t f", b=2, t=8), ['rope_tab'])
            self.ld(aq[:].rearrange("p (b t) f -> p b t f", b=2), dr['augq'].rearrange("(b p t) f -> p b t f", b=2, t=8), ['a_aq'])
            self.ld(ak[:, 0:16, :].rearrange("p (b t) f -> p b t f", b=2), dr['augk'][0:NT].rearrange("(b p t) f -> p b t f", b=2, t=8), ['a_ak'])
            self.ld(ak[:, 16:18, :], dr['augk'][NT:NT + 256].rearrange("(i p) f -> p i f", p=128), ['a_ak'])
            self.ld(cdk[:], dr['ctx_dk'][l].rearrange("(i p) f -> p i f", p=128), ['a_cdk'])
            self.ld(cdv[:], dr['ctx_dv'][l].rearrange("(i p) f -> p i f", p=128), ['a_cdv'])
            self.ld(dl[:], dr['diff_lambda'][l].partition_broadcast(128), ['a_dl'])
            self.ld(subw[:], dr['diff_subln_w'][l].partition_broadcast(128), ['a_subw'])
            LK = ['a_lam']
            self.tt('dve', o0[:, 0, :].rearrange("p (a f) -> p a f", a=2), dl[:].rearrange("p (a b f) -> p a b f", a=2, b=2)[:, :, 0, :],
                    dl[:].rearrange("p (a b f) -> p a b f", a=2, b=2)[:, :, 1, :], ALU.mult, ['a_dl'], LK)
            S.op('dve', lambda e: e.tensor_reduce(out=lamt[:, 1:3], in_=o0[:, 0, :].rearrange("p (a f) -> p a f", a=2),
                                                  axis=mybir.AxisListType.X, op=ALU.add), LK, LK)
            self.act(lamt[:, 1:3], lamt[:, 1:3], AF.Exp, LK, LK)
            self.tt('dve', lamt[:, 3:4], lamt[:, 2:3], lamt[:, 1:2], ALU.subtract, LK, LK)
            self.ts('dve', lamt[:, 0:1], lamt[:, 3:4], -lam_init, None, ALU.add, None, LK, LK)
            self.ts('dve', subw[:], subw[:], 1.0 - lam_init, None, ALU.mult, None, ['a_subw'], ['a_subw'])
            S.op('dve', lambda e: e.memset(self.epsc[:, 0:1], EPS), (), ['epsc'])

            def init_staging(qcols, kcols):
                S.op('dve', lambda e: e.memset(QS[:], 0.0), ['a_QS'], ['a_QS'])
                S.op('dve', lambda e: e.memset(KS[:], 0.0), ['a_KS'], ['a_KS'])
                for c0 in qcols:
                    self.copy('dve', QS[:, :, c0:c0 + 9], aq[:], ['a_aq'], ['a_QS'])
                for c0 in kcols:
                    self.copy('dve', KS[:, :, c0:c0 + 9], ak[:], ['a_ak'], ['a_KS'])
            for i in range(2):
                S.op('dve', lambda e, i=i: e.memset(Vh[i][:, :, 64:65], 1.0), [('a_V', i)], [('a_V', i)])

            def transposes(src, ntile, dstT, skey, dkey):
                for q in range((ntile + 3) // 4):
                    k = cnt['pst']
                    cnt['pst'] += 1
                    pt, ptk = pstr[0], ('a_pst', 0)
                    m = min(4, ntile - 4 * q)
                    for i in range(m):
                        self.tr(pt[:, 128 * i:128 * i + 128], src[:, 4 * q + i, :], self.identb[:], [skey, 'identb'], [ptk])
                    self.copy(self.evac_eng(), dstT[:, 512 * q:512 * q + 128 * m], pt[:, 0:128 * m], [ptk], [dkey])

            def core(hs, comps, scale, post):
                qt_, kt_, vh_ = QT[hs], KT[hs], Vh[hs]
                ncmp = len(comps)
                steps = [(qb, ci, kt) for qb in range(4) for kt in range(18) for ci in range(ncmp)]
                n = len(steps)
                slots = {}

                def score(i):
                    qb, ci, kt = steps[i]
                    r0, nr = comps[ci]
                    k = cnt['pss']
                    cnt['pss'] += 1
                    ps_, psk = pss_l[k % 4]
                    slots[i] = (ps_, psk)
                    self.mm(ps_, kt_[r0:r0 + nr, 128 * kt:128 * kt + 128], qt_[r0:r0 + nr, 512 * qb:512 * qb + 512], True, True,
                            [('a_KT', hs), ('a_QT', hs)], [psk])
                for j in range(2):
                    score(j)
                for i in range(n):
                    qb, ci, kt = steps[i]
                    if ncmp == 2:
                        if i % 2 == 0:
                            for j in (i + 2, i + 3):
                                if j < n:
                                    score(j)
                    elif i + 2 < n:
                        score(i + 2)
                    ps_, psk = slots.pop(i)
                    po, pok = pso[ci], ('a_pso', ci)
                    k2 = cnt['PT']
                    cnt['PT'] += 1
                    pt, ptk = PT[k2 % 4], ('a_PT', k2 % 4)
                    self.act(pt[:], ps_, AF.Exp, [psk], [ptk], scale=scale)
                    self.mm(po[0:65, :], vh_[:, kt, 0:65], pt[:], kt == 0, kt == 17, [ptk, ('a_V', hs)], [pok])
                    if kt == 17:
                        self.copy('dve', oTs[ci][0:65, :], po[0:65, :], [pok], [('a_oTs', ci)])
                        for qt in range(4):
                            self.tr(psp[ci][:, 128 * qt:128 * qt + 65], oTs[ci][0:65, 128 * qt:128 * qt + 128], self.identf[0:65, 0:65],
                                    [('a_oTs', ci), 'identf'], [('a_psp', ci)])
                        if ci == ncmp - 1:
                            post(qb)

            def normalize(ci, dst):
                po = psp[ci][:].rearrange("p (q f) -> p q f", f=128)
                S.op('dve', lambda e: e.reciprocal(out=sml[:, 4 * ci:4 * ci + 4], in_=po[:, :, 64]), [('a_psp', ci)], ['a_sml'])
                self.tt('dve', dst, po[:, :, 0:64], sml[:, 4 * ci:4 * ci + 4].unsqueeze(2).to_broadcast([128, 4, 64]), ALU.mult,
                        [('a_psp', ci), 'a_sml'], ['a_o'])

            def out_transposes(qb, jt, roff):
                k = cnt['psp']
                cnt['psp'] += 1
                pt, ptk = psp[k % 2], ('a_psp', k % 2)
                for qt in range(4):
                    self.mm(pt[roff:roff + 64, 128 * qt:128 * qt + 128], ost[:, qt, :], self.identb[:], True, True, ['a_ost', 'identb'], [ptk])
                self.copy(self.evac_eng(), moh[jt % 2][roff:roff + 64, 512 * qb:512 * qb + 512], pt[roff:roff + 64, :], [ptk], [('a_moh', jt % 2, qb)])

            def pair_wout(jt):
                self.wout_part(l, [jt], [(moh[jt % 2][:], (lambda blk, jt=jt: ('a_moh', jt % 2, blk)))],
                               [(wob[jt % 2][:], ('a_wo', jt % 2))], psp, [('a_psp', 0), ('a_psp', 1)])

            w_in_v = self._ap('w_in')[l].rearrange("(k p) f -> p k f", p=128)
            import os
            att = int(os.environ.get('ATT', '99'))
            init_staging((32, 96), (32, 96))
            if att <= 1:
                S.barrier()
                edf.close()
                return
            for h in range(6):
                hs = h % 2
                whb, whk = wh[hs], ('a_wh', hs)
                for i3 in range(3):
                    self.ld(whb[:, :, 64 * i3:64 * i3 + 64], w_in_v[:, :, 384 * i3 + 64 * h:384 * i3 + 64 * h + 64], [whk], eng='pool')
                for tt in range(16):
                    pos = 128 * tt
                    b, t = tt // 8, tt % 8
                    k = cnt['psp']
                    cnt['psp'] += 1
                    pp, ppk = psp[k % 2], ('a_psp', k % 2)
                    for kk in range(8):
                        self.mm(pp[:, 0:192], hm[:, kk, pos:pos + 128], whb[:, kk, :], kk == 0, kk == 7, [whk, ('m_hm', kk, pos // 512)], [ppk])
                    src5 = pp[:, 0:128].rearrange("p (n a h f) -> p n a h f", n=4, a=2, h=2)
                    qd = QS[:, tt, :].rearrange("p (c x) -> p c x", c=2)[:, :, 0:32].rearrange("p c (a h f) -> p c a h f", a=2, h=2)
                    kd = KS[:, tt, :].rearrange("p (c x) -> p c x", c=2)[:, :, 0:32].rearrange("p c (a h f) -> p c a h f", a=2, h=2)
                    self.rope(src5, [(slice(0, 2), qd), (slice(2, 4), kd)], tt, 4, [r[:] for r in rt], [ppk], ['a_QS', 'a_KS'])
                    kv = cnt['kvo']
                    cnt['kvo'] += 1
                    kvb, kvk = kvo[kv % 2], ('a_kvo', kv % 2)
                    self.copy('act', kvb[:], pp[:, 64:192], [ppk], [kvk])
                    rows_k = self.o['ndk'][l].rearrange("(b p t) f -> b t p f", b=2, t=8)[b, t]
                    rows_v = self.o['ndv'][l].rearrange("(b p t) f -> b t p f", b=2, t=8)[b, t]
                    self.st(rows_k[:, 64 * h:64 * h + 64], kvb[:, 0:64], [kvk])
                    self.st(rows_v[:, 64 * h:64 * h + 64], kvb[:, 64:128], [kvk])
                    self.copy('act', Vh[hs][:, tt, 0:64], pp[:, 128:192], [ppk], [('a_V', hs)])
                for i in range(2):
                    kd = KS[:, 16 + i, :].rearrange("p (c x) -> p c x", c=2)[:, :, 0:32]
                    self.copy('dve', kd, cdk[:, i, 64 * h:64 * h + 64].rearrange("p (c x) -> p c x", c=2), ['a_cdk'], ['a_KS'])
                    self.copy('dve', Vh[hs][:, 16 + i, 0:64], cdv[:, i, 64 * h:64 * h + 64], ['a_cdv'], [('a_V', hs)])
                if att <= 2:
                    S.barrier()
                    edf.close()
                    return
                transposes(QS, 16, QT[hs], 'a_QS', ('a_QT', hs))
                transposes(KS, 18, KT[hs], 'a_KS', ('a_KT', hs))
                if att <= 3:
                    S.barrier()
                    edf.close()
                    return

                def post(qb, h=h):
                    normalize(0, o0[:])
                    normalize(1, o1[:])
                    self.stt('dve', o0[:], o1[:], lamt[:, 0:1], o0[:], ALU.mult, ALU.add, ['a_o', 'a_lam'], ['a_o'])
                    self.tt('dve', osq[:], o0[:], o0[:], ALU.mult, ['a_o'], ['a_osq'])
                    S.op('dve', lambda e: e.tensor_reduce(out=sml[:, 8:12], in_=osq[:], axis=mybir.AxisListType.X, op=ALU.add),
                         ['a_osq'], ['a_sml'])
                    self.act(sml[:, 8:12], sml[:, 8:12], AF.Sqrt, ['a_sml', 'epsc'], ['a_sml'], bias=self.epsc[:, 0:1], scale=1.0 / 64)
                    S.op('dve', lambda e: e.reciprocal(out=sml[:, 8:12], in_=sml[:, 8:12]), ['a_sml'], ['a_sml'])
                    self.tt('dve', o0[:], o0[:], sml[:, 8:12].unsqueeze(2).to_broadcast([128, 4, 64]), ALU.mult, ['a_o', 'a_sml'], ['a_o'])
                    self.tt('dve', ost[:], o0[:], subw[:].unsqueeze(1).to_broadcast([128, 4, 64]), ALU.mult, ['a_o', 'a_subw'], ['a_ost'])
                    out_transposes(qb, h // 2, 64 * (h % 2))
                core(hs, [(0, 64), (64, 64)], 32 ** -0.5, post)
                if att <= 4:
                    S.barrier()
                    edf.close()
                    return
                if h % 2 == 1:
                    pair_wout(h // 2)
            S.barrier()
            edf.close()
            if att <= 5:
                return
            with contextlib.ExitStack() as em:
                sbm = lambda n, s, d: self.sb(n, s, d, em)
                wm = sbm("a_wm", [128, 8, 416], BF16)
                cqnT = sbm("a_cqnT", [128, 2, NT], BF16)
                ckvT = sbm("a_ckvT", [128, NT + 256], BF16)
                cqs = sbm("a_cqs", [128, 2, 256], BF16)
                cks = sbm("a_cks", [128, 2, 128], BF16)
                qnw = sbm("a_qnw", [128, 256], F32)
                kvnw = sbm("a_kvnw", [128, 128], F32)
                cckv = sbm("a_cckv", [128, 2, 128], F32)
                ckpe = sbm("a_ckpe", [128, 2, 32], F32)
                tq = sbm("a_tq", [128, 256], F32)
                tk_ = [sbm("a_tk%d" % i, [128, 160], F32) for i in range(2)]
                wq = [sbm("a_wq%d" % i, [128, 2, 96], BF16) for i in range(2)]
                wkv = [sbm("a_wkv%d" % i, [128, 128], BF16) for i in range(2)]
                init_staging((96,), (96,))
                self.ld(wm[:], w_in_v[:, :, 1152:1568], ['a_wm'], eng='pool')
                self.ld(qnw[:], dr['mla_q_norm_w'][l].partition_broadcast(128), ['a_qnw'])
                self.ld(kvnw[:], dr['mla_kv_norm_w'][l].partition_broadcast(128), ['a_kvnw'])
                self.ld(cckv[:], dr['ctx_ckv'][l].rearrange("(i p) f -> p i f", p=128), ['a_cckv'])
                self.ld(ckpe[:], dr['ctx_kpe'][l].rearrange("(i p) f -> p i f", p=128), ['a_ckpe'])
                mla = int(os.environ.get('MLA', '99'))
                if mla <= 1:
                    S.barrier()
                    return
                for tt in range(16):
                    pos = 128 * tt
                    b, t = tt // 8, tt % 8
                    k = cnt['psp']
                    cnt['psp'] += 1
                    pp, ppk = psp[k % 2], ('a_psp', k % 2)
                    for kk in range(8):
                        self.mm(pp[:, 0:416], hm[:, kk, pos:pos + 128], wm[:, kk, :], kk == 0, kk == 7, ['a_wm', ('m_hm', kk, pos // 512)], [ppk])
                    self.act(tq[:], pp[:, 0:256], AF.Square, [ppk], ['a_tq'], accum=sml[:, 12:13])
                    self.act(sml[:, 12:13], sml[:, 12:13], AF.Sqrt, ['a_tq'], ['a_sml2'], bias=self.epsc[:, 0:1], scale=1.0 / 256)
                    S.op('dve', lambda e: e.reciprocal(out=sml[:, 12:13], in_=sml[:, 12:13]), ['a_sml2'], ['a_sml2'])
                    cb = tt % 2
                    self.stt('dve', cqs[:, cb, :], pp[:, 0:256], sml[:, 12:13], qnw[:], ALU.mult, ALU.mult, [ppk, 'a_sml2', 'a_qnw'], [('a_cqs', cb)])
                    kq = cnt['pst']
                    cnt['pst'] += 1
                    ptq, ptqk = pstr[0], ('a_pst', 0)
                    for f in range(2):
                        self.tr(ptq[:, 128 * f:128 * f + 128], cqs[:, cb, 128 * f:128 * f + 128], self.identb[:], [('a_cqs', cb), 'identb'], [ptqk])
                    self.copy(self.evac_eng(), cqnT[:, :, pos:pos + 128], ptq[:, 0:256].rearrange("p (f c) -> p f c", f=2), [ptqk], ['a_cqnT'])
                    mlap = int(os.environ.get('MLAP', '99'))
                    if mlap <= 1:
                        continue
                    kv = cnt['kvo']
                    cnt['kvo'] += 1
                    tkb, tkk = tk_[kv % 2], ('a_tk', kv % 2)
                    self.act(tq[:, 0:128], pp[:, 256:384], AF.Square, [ppk, 'a_tq'], ['a_tq'], accum=sml[:, 13:14])
                    self.act(sml[:, 13:14], sml[:, 13:14], AF.Sqrt, ['a_tq'], ['a_sml3'], bias=self.epsc[:, 0:1], scale=1.0 / 128)
                    S.op('dve', lambda e: e.reciprocal(out=sml[:, 13:14], in_=sml[:, 13:14]), ['a_sml3'], ['a_sml3'])
                    self.stt('dve', tkb[:, 0:128], pp[:, 256:384], sml[:, 13:14], kvnw[:], ALU.mult, ALU.mult, [ppk, 'a_sml3', 'a_kvnw'], [tkk])
                    self.copy('act', tkb[:, 128:160], pp[:, 384:416], [ppk], [tkk])
                    self.copy('act', cks[:, cb, :], tkb[:, 0:128], [tkk], [('a_cks', cb)])
                    self.tr(ptq[:, 256:384], cks[:, cb, :], self.identb[:], [('a_cks', cb), 'identb'], [ptqk])
                    self.copy(self.evac_eng(), ckvT[:, pos:pos + 128], ptq[:, 256:384], [ptqk], ['a_ckvT'])
                    if mlap <= 2:
                        continue
                    rows_c = self.o['nckv'][l].rearrange("(b p t) f -> b t p f", b=2, t=8)[b, t]
                    rows_p = self.o['nkpe'][l].rearrange("(b p t) f -> b t p f", b=2, t=8)[b, t]
                    self.st(rows_c, tkb[:, 0:128], [tkk])
                    self.st(rows_p, tkb[:, 128:160], [tkk])
                    if mlap <= 3:
                        continue
                    src5 = pp[:, 384:416].rearrange("p (n a h f) -> p n a h f", n=1, a=2, h=2)
                    kd = KS[:, tt, 64:96].rearrange("p (n a h f) -> p n a h f", n=1, a=2, h=2)
                    self.rope(src5, [(slice(0, 1), kd)], tt, 1, [r[:] for r in rt], [ppk], ['a_KS'])
                if mla <= 2:
                    S.barrier()
                    return
                for i in range(2):
                    self.copy('dve', cks[:, i, :], cckv[:, i, :], ['a_cckv'], [('a_cks', i)])
                    kq = cnt['pst']
                    cnt['pst'] += 1
                    ptq, ptqk = pstr[0], ('a_pst', 0)
                    self.tr(ptq[:, 0:128], cks[:, i, :], self.identb[:], [('a_cks', i), 'identb'], [ptqk])
                    self.copy(self.evac_eng(), ckvT[:, NT + 128 * i:NT + 128 * i + 128], ptq[:, 0:128], [ptqk], ['a_ckvT'])
                    self.copy('dve', KS[:, 16 + i, 64:96], ckpe[:, i, :], ['a_ckpe'], ['a_KS'])
                if mla <= 3:
                    S.barrier()
                    return
                for h in range(6):
                    hs = h % 2
                    self.ld(wq[hs][:], dr['mla_w_q_up'][l].rearrange("(k p) f -> p k f", p=128)[:, :, 96 * h:96 * h + 96], [('a_wq', hs)], eng='pool')
                    self.ld(wkv[hs][:], dr['mla_w_kv_up'][l][:, 128 * h:128 * h + 128], [('a_wkv', hs)], eng='pool')
                    mlah = int(os.environ.get('MLAH', '99'))
                    if mlah <= 1:
                        S.barrier()
                        return
                    for tt in range(16):
                        pos = 128 * tt
                        k = cnt['psp']
                        cnt['psp'] += 1
                        pp, ppk = psp[k % 2], ('a_psp', k % 2)
                        for f in range(2):
                            self.mm(pp[:, 0:96], cqnT[:, f, pos:pos + 128], wq[hs][:, f, :], f == 0, f == 1, [('a_wq', hs), 'a_cqnT'], [ppk])
                        kv = cnt['kvo']
                        cnt['kvo'] += 1
                        tkb, tkk = tk_[kv % 2], ('a_tk', kv % 2)
                        self.copy('act', tkb[:, 0:96], pp[:, 0:96], [ppk], [tkk])
                        self.copy('act', QS[:, tt, 0:64], tkb[:, 0:64], [tkk], ['a_QS'])
                        if mlah <= 2:
                            continue
                        src5 = tkb[:, 64:96].rearrange("p (n a h f) -> p n a h f", n=1, a=2, h=2)
                        qd = QS[:, tt, 64:96].rearrange("p (n a h f) -> p n a h f", n=1, a=2, h=2)
                        self.rope(src5, [(slice(0, 1), qd)], tt, 1, [r[:] for r in rt], [tkk], ['a_QS'])
                    if mlah <= 3:
                        S.barrier()
                        return
                    for kt in range(18):
                        k = cnt['psp']
                        cnt['psp'] += 1
                        pp, ppk = psp[k % 2], ('a_psp', k % 2)
                        self.mm(pp[:, 0:128], ckvT[:, 128 * kt:128 * kt + 128], wkv[hs][:], True, True, [('a_wkv', hs), 'a_ckvT'], [ppk])
                        self.copy('act', KS[:, kt, 0:64], pp[:, 0:64], [ppk], ['a_KS'])
                        self.copy('dve', Vh[hs][:, kt, 0:64], pp[:, 64:128], [ppk], [('a_V', hs)])
                    if mla <= 4:
                        S.barrier()
                        return
                    transposes(QS, 16, QT[hs], 'a_QS', ('a_QT', hs))
                    transposes(KS, 18, KT[hs], 'a_KS', ('a_KT', hs))
                    if mla <= 5:
                        S.barrier()
                        return

                    def postm(qb, h=h):
                        normalize(0, o0[:])
                        self.copy('act', ost[:], o0[:], ['a_o'], ['a_ost'])
                        out_transposes(qb, 3 + h // 2, 64 * (h % 2))
                    core(hs, [(0, 128)], 96 ** -0.5, postm)
                    if mla <= 6:
                        S.barrier()
                        return
                    if h % 2 == 1:
                        pair_wout(3 + h // 2)


_PROG = {}


def get_prog(stage=99):
    if stage not in _PROG:
        b = Builder(stage)
        _PROG[stage] = b.build()
    return _PROG[stage]


def rope_tables(n, grid_w=64, theta=10000.0):
    t = np.arange(n)
    row = (t // grid_w).astype(np.float32)
    col = (t % grid_w).astype(np.float32)
    inv = (theta ** (-np.arange(8, dtype=np.float32) / 8)).astype(np.float32)
    ang = np.concatenate([row[:, None] * inv[None], col[:, None] * inv[None]], axis=1).astype(np.float32)
    return np.cos(ang).astype(np.float32), np.sin(ang).astype(np.float32)


def _gsplit(a, axis):
    a = np.asarray(a, dtype=np.float32)
    sh = a.shape
    a = a.reshape(sh[:axis] + (8, 2) + sh[axis + 1:])
    a = np.moveaxis(a, axis + 1, axis)
    return np.ascontiguousarray(a)


def make_in_maps(inp):
    f = lambda a: np.ascontiguousarray(np.asarray(a, dtype=np.float32))
    shared = dict(
        w_ada=f(inp['w_ada']), b_ada=f(inp['b_ada']).reshape(DEPTH * 72, 128),
        norm_w=f(inp['norm_w']).reshape(DEPTH * 3 * 8, 128), final_norm_w=f(inp['final_norm_w']).reshape(8, 128),
        ffn_w_in=f(inp['ffn_w_in']), ffn_w_out=f(inp['ffn_w_out']), w_in=f(inp['w_in']), w_out=f(inp['w_out']),
        diff_lambda=f(inp['diff_lambda']).reshape(DEPTH, 128), diff_subln_w=f(inp['diff_subln_w']),
        mla_q_norm_w=f(inp['mla_q_norm_w']), mla_w_q_up=f(inp['mla_w_q_up']),
        mla_kv_norm_w=f(inp['mla_kv_norm_w']), mla_w_kv_up=f(inp['mla_w_kv_up']),
        s5_a_re=_gsplit(inp['s5_a_re'], 2), s5_a_im=_gsplit(inp['s5_a_im'], 2), s5_log_step=_gsplit(inp['s5_log_step'], 2),
        s5_b_re=_gsplit(inp['s5_b_re'], 2), s5_b_im=_gsplit(inp['s5_b_im'], 2),
        s5_c_re=_gsplit(inp['s5_c_re'], 2).reshape(DEPTH, 2, 2, 128, 64), s5_c_im=_gsplit(inp['s5_c_im'], 2).reshape(DEPTH, 2, 2, 128, 64),
        s5_d=f(inp['s5_d']), s5_w_glu=f(inp['s5_w_glu']), s5_b_glu=f(inp['s5_b_glu']).reshape(DEPTH, 2, 128),
    )
    tq = np.arange(128) // 16
    cm_f = (tq[:, None] <= tq[None, :]).astype(np.float32)
    cm_b = (tq[:, None] >= tq[None, :]).astype(np.float32)
    shared['cmask_f'] = cm_f
    shared['cmask_b'] = cm_b
    rc, rsn = rope_tables(NT)
    maps = []
    for core in range(8):
        m = dict(shared)
        if core < 4:
            b = core
            m['xin'] = f(inp['x_sample'][b])
            m['cvec'] = f(inp['c'][b]).reshape(8, 128)
            m['ctx_dk'] = f(inp['cache_diff_k'][b]).reshape(DEPTH, 256, 384)
            m['ctx_dv'] = f(inp['cache_diff_v'][b]).reshape(DEPTH, 256, 384)
            m['ctx_ckv'] = f(inp['cache_mla_ckv'][b])
            m['ctx_kpe'] = f(inp['cache_mla_kpe'][b])
            m['h0re'] = _gsplit(inp['state_s5_re'][b], 2)
            m['h0im'] = _gsplit(inp['state_s5_im'][b], 2)
            m['ropec'], m['ropes'] = rc, rsn
            augk = np.zeros((NT + 256, 9), np.float32)
            augk[:, 0] = 32.0
            augk[:, 8] = 1.0
            augq = np.zeros((NT, 9), np.float32)
            augq[:, 0] = 32.0
            augq[:, 8] = -1024.0
            m['flag'] = np.ones((128, 1), np.float32)
        else:
            i = core - 4
            m['xin'] = f(inp['x_prompt'][8 * i:8 * i + 8]).reshape(NT, D)
            m['cvec'] = f(inp['c_ctx']).reshape(8, 128)
            m['ctx_dk'] = np.zeros((DEPTH, 256, 384), np.float32)
            m['ctx_dv'] = np.zeros((DEPTH, 256, 384), np.float32)
            m['ctx_ckv'] = np.zeros((DEPTH, 256, 128), np.float32)
            m['ctx_kpe'] = np.zeros((DEPTH, 256, 32), np.float32)
            m['h0re'] = np.zeros((DEPTH, 2, 2, 8, 64), np.float32)
            m['h0im'] = np.zeros((DEPTH, 2, 2, 8, 64), np.float32)
            m['ropec'] = np.ones((NT, 16), np.float32)
            m['ropes'] = np.zeros((NT, 16), np.float32)
            seg = np.arange(NT) // 256
            augk = np.zeros((NT + 256, 9), np.float32)
            augk[np.arange(NT), seg] = 32.0
            augk[:, 8] = 1.0
            augq = np.zeros((NT, 9), np.float32)
            augq[np.arange(NT), seg] = 32.0
            augq[:, 8] = -1024.0
            m['flag'] = np.zeros((128, 1), np.float32)
        m['augk'] = augk
        m['augq'] = augq
        maps.append(m)
    return maps


def kernel(**inputs):
    nc = get_prog(STAGE)
    maps = make_in_maps(inputs)
    res = run_bass_kernel_spmd(nc, maps, core_ids=list(range(8)))
    r = res.results
    B, SEQ = 32, 256
    y_sample = np.stack([r[c]['y'] for c in range(4)], 0).astype(np.float32)
    y_prompt = np.concatenate([r[c]['y'].reshape(8, SEQ, D) for c in range(4, 8)], 0).astype(np.float32)

    def cat(name, tail):
        outs = []
        for c in range(4, 8):
            a = r[c][name].reshape(DEPTH, 8, SEQ, -1).transpose(1, 0, 2, 3)
            outs.append(a)
        a = np.concatenate(outs, 0)
        return np.ascontiguousarray(a.reshape((B, DEPTH, SEQ) + tail)).astype(np.float32)

    def cat5(name):
        outs = []
        for c in range(4, 8):
            a = r[c][name].reshape(DEPTH, 2, 8, 16, 64).transpose(2, 0, 1, 3, 4)
            outs.append(a)
        return np.ascontiguousarray(np.concatenate(outs, 0)).astype(np.float32)
    return (y_prompt, y_sample, cat('ndk', (6, 64)), cat('ndv', (6, 64)), cat('nckv', (128,)), cat('nkpe', (32,)),
            cat5('ns5re'), cat5('ns5im'))
```

```python
import contextlib
import math
import numpy as np
import concourse.bass as bass
import concourse.mybir as mybir
from concourse.bass_utils import run_bass_kernel_spmd

F32 = mybir.dt.float32
BF16 = mybir.dt.bfloat16
AF = mybir.ActivationFunctionType
ALU = mybir.AluOpType

D = 1024
NT = 2048
DEPTH = 2
DFF = 2816
NHT = 22
EPS = 1e-6
INC = 1824
STAGE = 99


class Sched:
    def __init__(self, nc, ndma=8):
        self.nc = nc
        self.engs = ['pe', 'act', 'dve', 'pool', 'sp']
        self.streams = {e: [] for e in self.engs}
        self.cnt = {e: 0 for e in self.engs}
        self.seen = {e: {} for e in self.engs}
        self.res = {}
        self.ndma = ndma
        self.dma_issued = {'sp': 0, 'pool': 0, 'act': 0}
        self.dma_last = {}
        self.final_tokens = []

    def _deps(self, eng, reads, writes):
        toks = {}

        def add(t):
            if t is None:
                return
            k, v = t
            if toks.get(k, 0) < v:
                toks[k] = v
        for r in reads:
            st = self.res.get(r)
            if st:
                add(st['w'])
        for w in writes:
            st = self.res.get(w)
            if st:
                add(st['w'])
                for t in st['r']:
                    add(t)
        out = []
        for k, v in toks.items():
            if eng == 'pe' and k == ('c', 'pe'):
                continue
            if self.seen[eng].get(k, 0) >= v:
                continue
            self.seen[eng][k] = v
            out.append((k, v))
        return out

    def _mark(self, tok, reads, writes):
        for r in reads:
            st = self.res.setdefault(r, {'w': None, 'r': []})
            st['r'].append(tok)
            if len(st['r']) > 64:
                mx = {}
                for k, v in st['r']:
                    if mx.get(k, 0) < v:
                        mx[k] = v
                st['r'] = list(mx.items())
        for w in writes:
            self.res[w] = {'w': tok, 'r': []}

    PSUM_NAMES = {'lxps', 'ad_pst', 'ad_psm', 'nm_pss', 'f_pag', 'f_pso', 'fin_ps', 'sa_ps', 'sb_psu', 'sb_pst', 'sc_ps',
                  'sc_psF', 'sd_ps', 'se_pst', 'se_psg', 'a_psp', 'a_pst', 'a_pss', 'a_pso'}

    def _excl(self, reads, writes):
        rd, wr = [], list(writes)
        for r in reads:
            nm = r if isinstance(r, str) else r[0]
            if nm in self.PSUM_NAMES:
                if r not in wr:
                    wr.append(r)
            else:
                rd.append(r)
        return rd, wr

    def op(self, eng, fn, reads=(), writes=()):
        reads, writes = self._excl(reads, writes)
        waits = self._deps(eng, reads, writes)
        self.cnt[eng] += 1
        tok = (('c', eng), self.cnt[eng])
        self.streams[eng].append((waits, fn, tok))
        self._mark(tok, reads, writes)
        return tok

    def dma(self, eng, fn, reads=(), writes=(), final=False):
        k = self.dma_issued[eng]
        self.dma_issued[eng] += 1
        slot = k % self.ndma
        val = 16 * (k // self.ndma + 1)
        key = ('d', eng, slot)
        waits = self._deps(eng, reads, writes)
        if val > 16 and self.seen[eng].get(key, 0) < val - 16:
            self.seen[eng][key] = val - 16
            waits.append((key, val - 16))
        tok = (key, val)
        self.dma_last[key] = val
        self.streams[eng].append((waits, fn, tok))
        self._mark(tok, reads, writes)
        if final:
            self.final_tokens.append(tok)
        return tok

    def barrier(self):
        allt = [(('c', e), self.cnt[e]) for e in ['pe', 'act', 'dve', 'pool'] if self.cnt[e]]
        allt += list(self.dma_last.items())
        for e in self.engs:
            waits = []
            for k, v in allt:
                if k == ('c', e):
                    continue
                if self.seen[e].get(k, 0) >= v:
                    continue
                self.seen[e][k] = v
                waits.append((k, v))
            if waits:
                self.streams[e].append((waits, None, None))

    def emit(self):
        nc = self.nc
        with contextlib.ExitStack() as es:
            sems = {}
            for e in ['pe', 'act', 'dve', 'pool']:
                sems[('c', e)] = es.enter_context(nc.semaphore('c_' + e))
            for e in ['sp', 'pool', 'act']:
                if self.dma_issued[e]:
                    for s in range(self.ndma):
                        sems[('d', e, s)] = es.enter_context(nc.semaphore('d_%s_%d' % (e, s)))
            block = es.enter_context(nc.Block())

            def run(engname, engobj):
                for waits, fn, tok in self.streams[engname]:
                    for k, v in waits:
                        engobj.wait_ge(sems[k], v)
                    if fn is None:
                        continue
                    ins = fn(engobj)
                    k, v = tok
                    ins.then_inc(sems[k], 16 if k[0] == 'd' else 1)
                if engname == 'sp':
                    for k, v in self.final_tokens:
                        engobj.wait_ge(sems[k], v)

            @block.sync
            def _(e):
                run('sp', e)

            @block.tensor
            def _(e):
                run('pe', e)

            @block.scalar
            def _(e):
                run('act', e)

            @block.vector
            def _(e):
                run('dve', e)

            @block.gpsimd
            def _(e):
                run('pool', e)


class Builder:
    def __init__(self, stage=99):
        self.stage = stage
        self.nc = bass.Bass("TRN2", target_bir_lowering=False)
        self.S = Sched(self.nc)
        self.es = contextlib.ExitStack()
        self.uid = 0
        self.rr = 0
        self._dr_cache = {}

    def din(self, name, shape):
        ap = self.nc.dram_tensor(name, list(shape), F32, kind="ExternalInput").ap()
        self._dr_cache[name] = ap
        return ap

    def _ap(self, name):
        return self._dr_cache[name]

    def dout(self, name, shape):
        return self.nc.dram_tensor(name, list(shape), F32, kind="ExternalOutput").ap()

    def sb(self, name, shape, dt, es=None):
        self.uid += 1
        return (es or self.es).enter_context(self.nc.sbuf_tensor("%s_u%d" % (name, self.uid), list(shape), dt))

    def psum(self, name, shape, dt, es):
        self.uid += 1
        return es.enter_context(self.nc.psum_tensor("%s_u%d" % (name, self.uid), list(shape), dt))

    def evac_eng(self):
        self.rr += 1
        return 'act' if self.rr % 2 else 'dve'

    def copy(self, eng, out, in_, reads, writes):
        if eng == 'act':
            self.S.op('act', lambda e: e.copy(out=out, in_=in_), reads, writes)
        else:
            self.S.op(eng, lambda e: e.tensor_copy(out=out, in_=in_), reads, writes)

    def tt(self, eng, out, a, b, op, reads, writes):
        self.S.op(eng, lambda e: e.tensor_tensor(out=out, in0=a, in1=b, op=op), reads, writes)

    def ts(self, eng, out, a, s1, s2, op0, op1, reads, writes):
        if op1 is None:
            self.S.op(eng, lambda e: e.tensor_scalar(out=out, in0=a, scalar1=s1, scalar2=None, op0=op0), reads, writes)
        else:
            self.S.op(eng, lambda e: e.tensor_scalar(out=out, in0=a, scalar1=s1, scalar2=s2, op0=op0, op1=op1), reads, writes)

    def stt(self, eng, out, a, s, b, op0, op1, reads, writes):
        self.S.op(eng, lambda e: e.scalar_tensor_tensor(out=out, in0=a, scalar=s, in1=b, op0=op0, op1=op1), reads, writes)

    def act(self, out, in_, func, reads, writes, bias=None, scale=None, accum=None):
        kw = {}
        if bias is not None:
            kw['bias'] = bias
        if scale is not None:
            kw['scale'] = scale
        if accum is not None:
            kw['accum_out'] = accum
        self.S.op('act', lambda e: e.activation(out=out, in_=in_, func=func, **kw), reads, writes)

    def mm(self, out, lhsT, rhs, start, stop, reads, writes):
        self.S.op('pe', lambda e: e.matmul(out, lhsT=lhsT, rhs=rhs, start=start, stop=stop), reads, writes)

    def tr(self, out, in_, ident, reads, writes):
        self.S.op('pe', lambda e: e.transpose(out=out, in_=in_, identity=ident), reads, writes)

    def ld(self, out, in_, writes, reads=(), eng='sp', **kw):
        self.S.dma(eng, lambda e: e.dma_start(out=out, in_=in_, **kw), reads, writes)

    def st(self, out, in_, reads, eng='sp', **kw):
        self.S.dma(eng, lambda e: e.dma_start(out=out, in_=in_, **kw), reads, (), final=True)

    def build(self):
        nc, S = self.nc, self.S
        din, dout, sb = self.din, self.dout, self.sb
        xin = din("xin", [NT, D])
        cvec = din("cvec", [8, 128])
        w_ada = din("w_ada", [DEPTH, D, 9 * D])
        b_ada = din("b_ada", [DEPTH * 72, 128])
        norm_w = din("norm_w", [DEPTH * 3 * 8, 128])
        fnw = din("final_norm_w", [8, 128])
        ffn_w_in = din("ffn_w_in", [DEPTH, 2, D, 2 * DFF])
        ffn_w_out = din("ffn_w_out", [DEPTH, 2, DFF, D])
        self.dr = dict(
            w_in=din("w_in", [DEPTH, D, INC]), w_out=din("w_out", [DEPTH, D, D]),
            ctx_dk=din("ctx_dk", [DEPTH, 256, 384]), ctx_dv=din("ctx_dv", [DEPTH, 256, 384]),
            ctx_ckv=din("ctx_ckv", [DEPTH, 256, 128]), ctx_kpe=din("ctx_kpe", [DEPTH, 256, 32]),
            h0re=din("h0re", [DEPTH, 2, 2, 8, 64]), h0im=din("h0im", [DEPTH, 2, 2, 8, 64]),
            ropec=din("ropec", [NT, 16]), ropes=din("ropes", [NT, 16]),
            augk=din("augk", [NT + 256, 9]), augq=din("augq", [NT, 9]),
            flag=din("flag", [128, 1]),
            diff_lambda=din("diff_lambda", [DEPTH, 128]), diff_subln_w=din("diff_subln_w", [DEPTH, 64]),
            mla_q_norm_w=din("mla_q_norm_w", [DEPTH, 256]), mla_w_q_up=din("mla_w_q_up", [DEPTH, 256, 576]),
            mla_kv_norm_w=din("mla_kv_norm_w", [DEPTH, 128]), mla_w_kv_up=din("mla_w_kv_up", [DEPTH, 128, 768]),
            s5_a_re=din("s5_a_re", [DEPTH, 2, 2, 8, 64]), s5_a_im=din("s5_a_im", [DEPTH, 2, 2, 8, 64]),
            s5_log_step=din("s5_log_step", [DEPTH, 2, 2, 8]),
            s5_b_re=din("s5_b_re", [DEPTH, 2, 2, 8, 64, 16]), s5_b_im=din("s5_b_im", [DEPTH, 2, 2, 8, 64, 16]),
            s5_c_re=din("s5_c_re", [DEPTH, 2, 2, 128, 64]), s5_c_im=din("s5_c_im", [DEPTH, 2, 2, 128, 64]),
            s5_d=din("s5_d", [DEPTH, 16, 16]), s5_w_glu=din("s5_w_glu", [DEPTH, 256, 256]),
            s5_b_glu=din("s5_b_glu", [DEPTH, 2, 128]),
            cmask_f=din("cmask_f", [128, 128]), cmask_b=din("cmask_b", [128, 128]),
        )
        self.y = dout("y", [NT, D])
        self.o = dict(
            ndk=dout("ndk", [DEPTH, NT, 384]), ndv=dout("ndv", [DEPTH, NT, 384]),
            nckv=dout("nckv", [DEPTH, NT, 128]), nkpe=dout("nkpe", [DEPTH, NT, 32]),
            ns5re=dout("ns5re", [DEPTH, 2, 8, 1024]), ns5im=dout("ns5im", [DEPTH, 2, 8, 1024]),
        )
        self.xT = sb("xT", [128, 8, NT], F32)
        self.identb = sb("identb", [128, 128], BF16)
        self.identf = sb("identf", [128, 128], F32)
        self.onesb = sb("onesb", [128, 128], BF16)
        self.epsc = sb("epsc", [128, 1], F32)
        self.mod = sb("mod", [128, DEPTH, 72], F32)
        self.cA = sb("cA", [128, DEPTH, 3, 8], F32)
        self.cB = sb("cB", [128, DEPTH, 3, 8], F32)
        self.cG = sb("cG", [128, DEPTH, 3, 8], F32)
        self.cF = sb("cF", [128, 8], F32)
        self.rs = sb("rs", [128, 2, 512], F32)
        self.wi_n = 0
        self.wo_n = 0

        self.setup_consts()
        self.load_x()
        self.adaln()
        S.barrier()
        for l in range(DEPTH):
            if self.stage >= 1:
                self.ffn(l, 0)
                S.barrier()
            if self.stage >= 3:
                self.mixer(l)
                S.barrier()
            if self.stage >= 2:
                self.ffn(l, 1)
                S.barrier()
            if self.stage < 4:
                break
        self.final()
        S.emit()
        return nc

    def setup_consts(self):
        S = self.S
        ib, iff, ob = self.identb, self.identf, self.onesb
        S.op('pool', lambda e: e.memset(ib[:], 0.0), (), ['identb'])
        S.op('pool', lambda e: e.affine_select(out=ib[:], in_=ib[:], compare_op=ALU.not_equal, fill=1.0, base=0,
                                               pattern=[[-1, 128]], channel_multiplier=1), ['identb'], ['identb'])
        S.op('pool', lambda e: e.memset(iff[:], 0.0), (), ['identf'])
        S.op('pool', lambda e: e.affine_select(out=iff[:], in_=iff[:], compare_op=ALU.not_equal, fill=1.0, base=0,
                                               pattern=[[-1, 128]], channel_multiplier=1), ['identf'], ['identf'])
        S.op('pool', lambda e: e.memset(ob[:], 1.0), (), ['onesb'])
        ep = self.epsc
        S.op('pool', lambda e: e.memset(ep[:], EPS), (), ['epsc'])

    def load_x(self):
        with contextlib.ExitStack() as es:
            xt = [self.sb("xtok%d" % i, [128, D], F32, es) for i in range(2)]
            ps = [self.psum("lxps%d" % i, [128, 512], F32, es) for i in range(4)]
            src = self._ap("xin").rearrange("(b p t) d -> b t p d", b=2, t=8)
            n = 0
            for b in range(2):
                for t in range(8):
                    buf = xt[n % 2]
                    bk = ('xtok', n % 2)
                    self.ld(buf[:], src[b, t], [bk])
                    pos = 1024 * b + 128 * t
                    for h in range(2):
                        pb = ps[(2 * n + h) % 4]
                        pk = ('lxps', (2 * n + h) % 4)
                        for jj in range(4):
                            j = 4 * h + jj
                            self.tr(pb[:, 128 * jj:128 * jj + 128], buf[:, 128 * j:128 * j + 128], self.identf[:],
                                    [bk, 'identf'], [pk])
                        dst = self.xT[:, 4 * h:4 * h + 4, pos:pos + 128]
                        srcp = pb[:].rearrange("p (j c) -> p j c", j=4)
                        self.copy(self.evac_eng(), dst, srcp, [pk],
                                  [('xT', j, pos // 512) for j in range(4 * h, 4 * h + 4)])
                    n += 1
        self.S.barrier()

    def adaln(self):
        S = self.S
        with contextlib.ExitStack() as es:
            rows = self.sb("ad_rows", [128, 3, 128], F32, es)
            rT = self.sb("ad_rT", [128, 3, 128], F32, es)
            scv = self.sb("ad_scv", [128, 8], F32, es)
            wa = [self.sb("ad_w%d" % i, [128, 8, 512], F32, es) for i in range(2)]
            pst = self.psum("ad_pst", [128, 512], F32, es)
            psm = self.psum("ad_psm", [128, 512], F32, es)
            S.op('dve', lambda e: e.memset(rows[:], 0.0), (), ['ad_rows'])
            self.ld(rows[0:8, 0, :], self._dr_cache['cvec'], ['ad_rows'], ['ad_rows'])
            self.ld(rows[8:16, 0, :], self._dr_cache['final_norm_w'], ['ad_rows'], ['ad_rows'])
            self.ld(rows[16:64, 0, :], self._dr_cache['norm_w'], ['ad_rows'], ['ad_rows'])
            self.ld(rows[:, 1, :], self._dr_cache['b_ada'][0:128, :], ['ad_rows'], ['ad_rows'])
            self.ld(rows[0:16, 2, :], self._dr_cache['b_ada'][128:144, :], ['ad_rows'], ['ad_rows'])
            for i in range(3):
                self.tr(pst[:, 128 * i:128 * i + 128], rows[:, i, :], self.identf[:], ['ad_rows', 'identf'], ['ad_pst'])
            self.copy('dve', rT[:].rearrange("p a b -> p (a b)"), pst[:, 0:384], ['ad_pst'], ['ad_rT'])
            self.act(scv[:], rT[:, 0, 0:8], AF.Silu, ['ad_rT'], ['ad_scv'])
            badaT = rT[:].rearrange("p a b -> p (a b)")[:, 128:128 + 144]
            n = 0
            for l in range(DEPTH):
                wv = self._dr_cache['w_ada'][l].rearrange("(kc p) f -> p kc f", p=128)
                for pc in range(18):
                    buf = wa[n % 2]
                    bk = ('ad_w', n % 2)
                    self.ld(buf[:], wv[:, :, 512 * pc:512 * pc + 512], [bk])
                    for ii in range(4):
                        i = 4 * pc + ii
                        for k in range(8):
                            self.mm(psm[:, l * 72 + i:l * 72 + i + 1], buf[:, k, 128 * ii:128 * ii + 128], scv[:, k:k + 1],
                                    k == 0, k == 7, [bk, 'ad_scv'], ['ad_psm'])
                    n += 1
            self.tt('dve', self.mod[:].rearrange("p l i -> p (l i)"), psm[:, 0:144], badaT, ALU.add,
                    ['ad_psm', 'ad_rT'], ['mod'])
            for l in range(DEPTH):
                for n3 in range(3):
                    nw = rT[:, 0, 16 + (l * 3 + n3) * 8:16 + (l * 3 + n3) * 8 + 8]
                    sh = self.mod[:, l, (3 * n3) * 8:(3 * n3) * 8 + 8]
                    sc = self.mod[:, l, (3 * n3 + 1) * 8:(3 * n3 + 1) * 8 + 8]
                    g = self.mod[:, l, (3 * n3 + 2) * 8:(3 * n3 + 2) * 8 + 8]
                    self.stt('dve', self.cA[:, l, n3, :], sc, 1.0, nw, ALU.add, ALU.mult, ['mod', 'ad_rT'], ['cA'])
                    self.copy('dve', self.cB[:, l, n3, :], sh, ['mod'], ['cB'])
                    self.ts('dve', self.cG[:, l, n3, :], g, (1.0 if n3 == 1 else 0.5), None, ALU.mult, None, ['mod'], ['cG'])
            self.copy('dve', self.cF[:], rT[:, 0, 8:16], ['ad_rT'], ['cF'])
            S.barrier()

    def norm_mod(self, blk, A, Bv, hm, hmcol, es_t, pss, hmkey):
        c0 = 512 * blk
        sq, tmp = es_t['sq'], es_t['tmp']
        xk = [('xT', j, blk) for j in range(8)]
        self.act(sq[:], self.xT[:, :, c0:c0 + 512], AF.Square, xk, ['nm_sq'])
        for j in range(8):
            self.mm(pss[:], self.onesb[:], sq[:, j, :], j == 0, j == 7, ['nm_sq', 'onesb'], ['nm_pss'])
        rb = blk % 2
        self.act(self.rs[:, rb, :], pss[:], AF.Sqrt, ['nm_pss'], [('rs', rb)], bias=self.epsc[:, 0:1], scale=1.0 / D)
        self.S.op('dve', lambda e: e.reciprocal(out=self.rs[:, rb, :], in_=self.rs[:, rb, :]), [('rs', rb)], [('rs', rb)])
        for j in range(8):
            tb = tmp[j % 2]
            self.tt('dve', tb[:], self.xT[:, j, c0:c0 + 512], self.rs[:, rb, :], ALU.mult,
                    [('xT', j, blk), ('rs', rb)], [('nm_tmp', j % 2)])
            if Bv is None:
                self.act(hm[:, j, hmcol:hmcol + 512], tb[:], AF.Identity, [('nm_tmp', j % 2)], [hmkey(j)], scale=A[:, j:j + 1])
            else:
                self.act(hm[:, j, hmcol:hmcol + 512], tb[:], AF.Identity, [('nm_tmp', j % 2)], [hmkey(j)],
                         scale=A[:, j:j + 1], bias=Bv[:, j:j + 1])

    def ffn(self, l, n):
        n3 = 0 if n == 0 else 2
        A, Bv, G = self.cA[:, l, n3, :], self.cB[:, l, n3, :], self.cG[:, l, n3, :]
        w_in = self._dr_cache['ffn_w_in'][l, n].rearrange("(kc p) f -> p kc f", p=128)
        w_out = self._dr_cache['ffn_w_out'][l, n].rearrange("(i p) d -> p i d", p=128)
        with contextlib.ExitStack() as es:
            hm = self.sb("f_hm", [128, 8, 1024], BF16, es)
            self.wi = [self.sb("wi%d" % i, [128, 8, 2, 256], BF16, es) for i in range(3)]
            self.wo = [self.sb("wo%d" % i, [128, NHT, 128], BF16, es) for i in range(3)]
            actb = self.sb("f_act", [128, NHT, 1024], BF16, es)
            sq = self.sb("f_sq", [128, 8, 512], BF16, es)
            tmp = [self.sb("f_tmp%d" % i, [128, 512], F32, es) for i in range(2)]
            sg = [self.sb("f_sg%d" % i, [128, 512], F32, es) for i in range(2)]
            pss = self.psum("f_pss", [128, 512], F32, es)
            pag = [self.psum("f_pag%d" % i, [128, 512], F32, es) for i in range(4)]
            pso = [self.psum("f_pso%d" % i, [128, 512], F32, es) for i in range(2)]
            est = {'sq': sq, 'tmp': tmp}
            npag = 0
            npso = 0
            for half in range(2):
                for bb in range(2):
                    blk = 2 * half + bb
                    self.norm_mod(blk, A, Bv, hm, 512 * bb, est, pss, lambda j, bb=bb: ('f_hm', j, bb))
                for pc in range(11):
                    slot = self.wi_n % 3
                    self.wi_n += 1
                    wb = self.wi[slot]
                    wk = ('wi', slot)
                    for ag in range(2):
                        cb = ag * DFF + 256 * pc
                        self.ld(wb[:, :, ag, :], w_in[:, :, cb:cb + 256], [wk], eng='pool')
                    for ii in range(2):
                        i = 2 * pc + ii
                        for bb in range(2):
                            pa = pag[npag % 4]
                            ka = ('f_pag', npag % 4)
                            pg = pag[(npag + 1) % 4]
                            kg = ('f_pag', (npag + 1) % 4)
                            npag += 2
                            for k in range(8):
                                self.mm(pa[:], wb[:, k, 0, 128 * ii:128 * ii + 128], hm[:, k, 512 * bb:512 * bb + 512],
                                        k == 0, k == 7, [wk, ('f_hm', k, bb)], [ka])
                            for k in range(8):
                                self.mm(pg[:], wb[:, k, 1, 128 * ii:128 * ii + 128], hm[:, k, 512 * bb:512 * bb + 512],
                                        k == 0, k == 7, [wk, ('f_hm', k, bb)], [kg])
                            sgi = (npag // 2) % 2
                            self.act(sg[sgi][:], pg[:], AF.Silu, [kg], [('f_sg', sgi)])
                            self.tt('dve', actb[:, i, 512 * bb:512 * bb + 512], sg[sgi][:], pa[:], ALU.mult,
                                    [('f_sg', sgi), ka], [('f_act', i, bb)])
                for jo in range(8):
                    slot = self.wo_n % 3
                    self.wo_n += 1
                    wb = self.wo[slot]
                    wk = ('wo', slot)
                    self.ld(wb[:], w_out[:, :, 128 * jo:128 * jo + 128], [wk], eng='pool')
                    for bb in range(2):
                        blk = 2 * half + bb
                        po = pso[npso % 2]
                        ko = ('f_pso', npso % 2)
                        npso += 1
                        for i in range(NHT):
                            self.mm(po[:], wb[:, i, :], actb[:, i, 512 * bb:512 * bb + 512], i == 0, i == NHT - 1,
                                    [wk, ('f_act', i, bb)], [ko])
                        xs = self.xT[:, jo, 512 * blk:512 * blk + 512]
                        self.stt('dve', xs, po[:], G[:, jo:jo + 1], xs, ALU.mult, ALU.add,
                                 [ko, ('xT', jo, blk)], [('xT', jo, blk)])

    def final(self):
        with contextlib.ExitStack() as es:
            yb = [self.sb("fin_y%d" % i, [128, 8, 512], F32, es) for i in range(1)]
            sq = self.sb("fin_sq", [128, 8, 512], BF16, es)
            tmp = [self.sb("fin_tmp%d" % i, [128, 512], F32, es) for i in range(2)]
            yt = [self.sb("fin_yt%d" % i, [128, D], F32, es) for i in range(2)]
            pss = self.psum("fin_pss", [128, 512], F32, es)
            ps = [self.psum("fin_ps%d" % i, [128, 512], F32, es) for i in range(4)]
            est = {'sq': sq, 'tmp': tmp}
            dst = self.y.rearrange("(b p t) d -> b t p d", b=2, t=8)
            n = 0
            for blk in range(4):
                self.norm_mod(blk, self.cF, None, yb[0], 0, est, pss, lambda j: ('fin_y', j))
                for tt4 in range(4):
                    tile = 4 * blk + tt4
                    b, t = tile // 8, tile % 8
                    ytb = yt[n % 2]
                    yk = ('fin_yt', n % 2)
                    for h in range(2):
                        pb = ps[(2 * n + h) % 4]
                        pk = ('fin_ps', (2 * n + h) % 4)
                        for jj in range(4):
                            j = 4 * h + jj
                            self.tr(pb[:, 128 * jj:128 * jj + 128], yb[0][:, j, 128 * tt4:128 * tt4 + 128], self.identf[:],
                                    [('fin_y', j), 'identf'], [pk])
                        self.copy(self.evac_eng(), ytb[:, 512 * h:512 * h + 512], pb[:], [pk], [yk])
                    self.st(dst[b, t], ytb[:], [yk])
                    n += 1

    def mixer(self, l):
        S = self.S
        A, Bv, G = self.cA[:, l, 1, :], self.cB[:, l, 1, :], self.cG[:, l, 1, :]
        with contextlib.ExitStack() as es:
            hm = self.sb("m_hm", [128, 8, NT], BF16, es)
            mo = None
            self.G2 = G
            with contextlib.ExitStack() as es2:
                sq = self.sb("m_sq", [128, 8, 512], BF16, es2)
                tmp = [self.sb("m_tmp%d" % i, [128, 512], F32, es2) for i in range(2)]
                pss = self.psum("m_pss", [128, 512], F32, es2)
                for blk in range(4):
                    self.norm_mod(blk, A, Bv, hm, 512 * blk, {'sq': sq, 'tmp': tmp}, pss,
                                  lambda j, blk=blk: ('m_hm', j, blk))
            import os
            mix = int(os.environ.get('MIX', '3'))
            S.barrier()
            if mix & 1:
                self.s5(l, hm, mo)
            S.barrier()
            if mix & 2:
                self.attn(l, hm, mo)

    def wout_part(self, l, ktiles, srcs, wbufs, psl, pskeys):
        wv = self._ap('w_out')[l].rearrange("(k p) d -> p k d", p=128)
        G = self.G2
        for i, kt in enumerate(ktiles):
            wb, wk = wbufs[i]
            self.ld(wb, wv[:, kt, :], [wk], eng='pool')
        n = 0
        for jo in range(8):
            for blk in range(4):
                po, pk = psl[n % len(psl)], pskeys[n % len(psl)]
                n += 1
                for i, kt in enumerate(ktiles):
                    wb, wk = wbufs[i]
                    src, kf = srcs[i]
                    self.mm(po[:], wb[:, 128 * jo:128 * jo + 128], src[:, 512 * blk:512 * blk + 512], i == 0, i == len(ktiles) - 1,
                            [wk, kf(blk)], [pk])
                xs = self.xT[:, jo, 512 * blk:512 * blk + 512]
                self.stt('dve', xs, po[:], G[:, jo:jo + 1], xs, ALU.mult, ALU.add, [pk, ('xT', jo, blk)], [('xT', jo, blk)])

    def cmul(self, outr, outi, ar, ai, br, bi, t1, t2, key_r, key_w, neg_im=False, eng='dve'):
        rd = list(key_r)
        self.tt(eng, t1, ar, br, ALU.mult, rd, ['cm_t1'])
        self.tt(eng, t2, ai, bi, ALU.mult, rd, ['cm_t2'])
        self.tt(eng, outr, t1, t2, ALU.subtract, ['cm_t1', 'cm_t2'] + rd, list(key_w))
        self.tt(eng, t1, ar, bi, ALU.mult, rd + list(key_w), ['cm_t1'])
        self.tt(eng, t2, ai, br, ALU.mult, rd + list(key_w), ['cm_t2'])
        if neg_im:
            self.stt(eng, outi, t1, -1.0, t2, ALU.mult, ALU.subtract, ['cm_t1', 'cm_t2'], list(key_w))
        else:
            self.tt(eng, outi, t1, t2, ALU.add, ['cm_t1', 'cm_t2'], list(key_w))

    def s5(self, l, hm, mo):
        S = self.S
        dr = self._dr_cache
        PI = math.pi
        with contextlib.ExitStack() as es:
            ToepT = [self.sb("s_toep%d" % d, [128, 16, 128], BF16, es) for d in range(2)]
            BSm = [self.sb("s_bsm%d" % d, [128, 16, 2, 64], BF16, es) for d in range(2)]
            CCm = [self.sb("s_ccm%d" % d, [128, 8, 2, 128], BF16, es) for d in range(2)]
            PWr = [self.sb("s_pwr%d" % d, [128, 8, 33], F32, es) for d in range(2)]
            PWi = [self.sb("s_pwi%d" % d, [128, 8, 33], F32, es) for d in range(2)]
            PWin = [self.sb("s_pwin%d" % d, [128, 8, 33], F32, es) for d in range(2)]
            h0r = [self.sb("s_h0r%d" % d, [128, 8], F32, es) for d in range(2)]
            h0i = [self.sb("s_h0i%d" % d, [128, 8], F32, es) for d in range(2)]
            U = self.sb("s_U", [128, 16, 256], BF16, es)
            flag = self.sb("s_flag", [128, 1], F32, es)
            self.ld(flag[:], dr['flag'], ['s_flag'])
            with contextlib.ExitStack() as ea:
                def t4(name):
                    return self.sb(name, [128, 8, 8, 16], F32, ea)
                cm1, cm2 = t4("sa_cm1"), t4("sa_cm2")
                BLr, BLi, CLr, CLi = t4("sa_blr"), t4("sa_bli"), t4("sa_clr"), t4("sa_cli")
                BSr, BSi, CCr, CCi = t4("sa_bsr"), t4("sa_bsi"), t4("sa_ccr"), t4("sa_cci")
                sm = self.sb("sa_sm", [128, 40, 8], F32, ea)
                bre = self.sb("sa_bre", [128, 8, 16], F32, ea)
                bim = self.sb("sa_bim", [128, 8, 16], F32, ea)
                bbr = self.sb("sa_bbr", [128, 8, 16], F32, ea)
                bbi = self.sb("sa_bbi", [128, 8, 16], F32, ea)
                cre = self.sb("sa_cre", [128, 8, 16], F32, ea)
                cim = self.sb("sa_cim", [128, 8, 16], F32, ea)
                crow = self.sb("sa_crow", [128, 2, 2, 64], F32, ea)
                Pr = self.sb("sa_Pr", [128, 8, 9], F32, ea)
                Pi_ = self.sb("sa_Pi", [128, 8, 9], F32, ea)
                Nr = self.sb("sa_Nr", [128, 8, 8], F32, ea)
                Ni = self.sb("sa_Ni", [128, 8, 8], F32, ea)
                Dcol = self.sb("sa_Dcol", [128, 16], F32, ea)
                cmk = [self.sb("sa_cmk%d" % d, [128, 128], F32, ea) for d in range(2)]
                tT = self.sb("sa_tT", [128, 128], F32, ea)
                cst = self.sb("sa_cst", [128, 2], F32, ea)
                psA = [self.psum("sa_ps%d" % i, [128, 512], F32, ea) for i in range(4)]
                S.op('dve', lambda e: e.memset(cst[:, 0:1], -PI), (), ['sa_cst'])
                self.ld(cmk[0][:], dr['cmask_f'], ['sa_cmk'])
                self.ld(cmk[1][:], dr['cmask_b'], ['sa_cmk'])
                for t in range(8):
                    self.ld(Dcol[16 * t:16 * t + 16, :], dr['s5_d'][l].rearrange("g c -> c g"), ['sa_Dcol'],
                            allow_slow_non_contiguous=True)
                npsA = 0
                for d in range(2):
                    K = ['sa']
                    are, aim, lst = sm[:, 0, :], sm[:, 1, :], sm[:, 2, :]
                    for gl in range(2):
                        ps_ = slice(64 * gl, 64 * gl + 64)
                        self.ld(sm[ps_, 0, :], dr['s5_a_re'][l, d, gl].rearrange("g p -> p g"), K, K, allow_slow_non_contiguous=True)
                        self.ld(sm[ps_, 1, :], dr['s5_a_im'][l, d, gl].rearrange("g p -> p g"), K, K, allow_slow_non_contiguous=True)
                        self.ld(sm[ps_, 2, :], dr['s5_log_step'][l, d, gl].partition_broadcast(64), K, K)
                        self.ld(h0r[d][ps_, :], dr['h0re'][l, d, gl].rearrange("g p -> p g"), K, K, allow_slow_non_contiguous=True)
                        self.ld(h0i[d][ps_, :], dr['h0im'][l, d, gl].rearrange("g p -> p g"), K, K, allow_slow_non_contiguous=True)
                        self.ld(bre[ps_, :, :], dr['s5_b_re'][l, d, gl].rearrange("g p c -> p g c"), K, K)
                        self.ld(bim[ps_, :, :], dr['s5_b_im'][l, d, gl].rearrange("g p c -> p g c"), K, K)
                        self.ld(crow[:, gl, 0, :], dr['s5_c_re'][l, d, gl], K, K)
                        self.ld(crow[:, gl, 1, :], dr['s5_c_im'][l, d, gl], K, K)
                    pc, pck = psA[npsA % 4], ('sa_ps', npsA % 4)
                    npsA += 1
                    for gl in range(2):
                        for ri in range(2):
                            self.mm(pc[64 * gl:64 * gl + 64, 128 * ri:128 * ri + 128], crow[:, gl, ri, :], self.identf[:], True, True, K + ['identf'], [pck])
                    self.copy('dve', cre[:].rearrange("p g c -> p (g c)"), pc[:, 0:128], [pck], K)
                    self.copy('dve', cim[:].rearrange("p g c -> p (g c)"), pc[:, 128:256], [pck], K)

                    import os
                    s5a = int(os.environ.get('S5A', '9'))
                    if s5a <= 1:
                        S.barrier()
                        return

                    def sop(fn):
                        S.op('dve', fn, K, K)
                    sl = lambda i: sm[:, i, :]
                    self.act(sl(3), lst, AF.Exp, K, K)
                    self.tt('dve', sl(4), are, sl(3), ALU.mult, K, K)
                    self.tt('dve', sl(5), aim, sl(3), ALU.mult, K, K)
                    self.act(sl(6), sl(4), AF.Exp, K, K)
                    self.act(sl(7), sl(4), AF.Exp, K, K, scale=-2.0)
                    MAGIC = 12582912.0
                    for (dst_i, off) in ((9, 0.0), (10, 0.25)):
                        self.ts('dve', sl(8), sl(5), 1.0 / (2 * PI), None, ALU.mult, None, K, K)
                        if off:
                            self.ts('dve', sl(8), sl(8), off, None, ALU.add, None, K, K)
                        self.ts('dve', sl(22), sl(8), MAGIC, None, ALU.add, None, K, K)
                        self.ts('dve', sl(22), sl(22), -MAGIC, None, ALU.add, None, K, K)
                        self.tt('dve', sl(8), sl(8), sl(22), ALU.subtract, K, K)
                        self.act(sl(dst_i), sl(8), AF.Sin, K, K, scale=2 * PI)
                    lr, li = sl(11), sl(12)
                    self.tt('dve', lr, sl(6), sl(10), ALU.mult, K, K)
                    self.tt('dve', li, sl(6), sl(9), ALU.mult, K, K)
                    ilr, ili = sl(13), sl(14)
                    self.tt('dve', ilr, lr, sl(7), ALU.mult, K, K)
                    self.stt('dve', ili, li, -1.0, sl(7), ALU.mult, ALU.mult, K, K)
                    self.tt('dve', sl(15), are, are, ALU.mult, K, K)
                    self.tt('dve', sl(16), aim, aim, ALU.mult, K, K)
                    self.tt('dve', sl(15), sl(15), sl(16), ALU.add, K, K)
                    sop(lambda e: e.reciprocal(out=sl(15), in_=sl(15)))
                    self.ts('dve', sl(16), lr, -1.0, None, ALU.add, None, K, K)
                    self.tt('dve', sl(17), sl(16), are, ALU.mult, K, K)
                    self.tt('dve', sl(18), li, aim, ALU.mult, K, K)
                    self.tt('dve', sl(17), sl(17), sl(18), ALU.add, K, K)
                    self.tt('dve', sl(17), sl(17), sl(15), ALU.mult, K, K)
                    self.tt('dve', sl(18), li, are, ALU.mult, K, K)
                    self.tt('dve', sl(19), sl(16), aim, ALU.mult, K, K)
                    self.tt('dve', sl(18), sl(18), sl(19), ALU.subtract, K, K)
                    self.tt('dve', sl(18), sl(18), sl(15), ALU.mult, K, K)
                    kb = lambda i: sm[:, i, :].unsqueeze(2).to_broadcast([128, 8, 16])
                    self.cmul(bbr[:], bbi[:], kb(17), kb(18), bre[:], bim[:], cm1[:, :, 0, :], cm2[:, :, 0, :], K, K)
                    sop(lambda e: e.memset(Pr[:, :, 0:1], 1.0))
                    sop(lambda e: e.memset(Pi_[:, :, 0:1], 0.0))
                    sop(lambda e: e.memset(Nr[:, :, 0:1], 1.0))
                    sop(lambda e: e.memset(Ni[:, :, 0:1], 0.0))
                    for k in range(1, 9):
                        self.cmul(Pr[:, :, k], Pi_[:, :, k], Pr[:, :, k - 1], Pi_[:, :, k - 1], lr, li, sl(20), sl(21), K, K)
                    for k in range(1, 8):
                        self.cmul(Nr[:, :, k], Ni[:, :, k], Nr[:, :, k - 1], Ni[:, :, k - 1], ilr, ili, sl(20), sl(21), K, K)
                    pr, pi = PWr[d], PWi[d]
                    self.copy('dve', pr[:, :, 1], Pr[:, :, 8], K, K)
                    self.copy('dve', pi[:, :, 1], Pi_[:, :, 8], K, K)
                    n = 1
                    while n < 32:
                        bshape = [128, 8, n]
                        self.cmul(pr[:, :, n + 1:2 * n + 1], pi[:, :, n + 1:2 * n + 1], pr[:, :, 1:n + 1], pi[:, :, 1:n + 1],
                                  pr[:, :, n:n + 1].to_broadcast(bshape), pi[:, :, n:n + 1].to_broadcast(bshape),
                                  cm1[:].rearrange("p a b c -> p a (b c)")[:, :, 0:n], cm2[:].rearrange("p a b c -> p a (b c)")[:, :, 0:n], K, K)
                        n *= 2
                    self.ts('dve', PWin[d][:, :, 1:33], pi[:, :, 1:33], -1.0, None, ALU.mult, None, K, K)
                    bsh = [128, 8, 8, 16]
                    bbR = bbr[:].unsqueeze(2).to_broadcast(bsh)
                    bbI = bbi[:].unsqueeze(2).to_broadcast(bsh)
                    cR = cre[:].unsqueeze(2).to_broadcast(bsh)
                    cI = cim[:].unsqueeze(2).to_broadcast(bsh)
                    pw = lambda T, sl_: T[:, :, sl_].unsqueeze(3).to_broadcast(bsh)
                    if d == 0:
                        self.cmul(BLr[:], BLi[:], bbR, bbI, pw(Nr, slice(0, 8)), pw(Ni, slice(0, 8)), cm1[:], cm2[:], K, K)
                        self.cmul(CLr[:], CLi[:], cR, cI, pw(Pr, slice(0, 8)), pw(Pi_, slice(0, 8)), cm1[:], cm2[:], K, K, neg_im=True)
                        self.cmul(BSr[:], BSi[:], bbR, bbI, pw(Pr, slice(7, None, -1)), pw(Pi_, slice(7, None, -1)), cm1[:], cm2[:], K, K)
                        self.cmul(CCr[:], CCi[:], cR, cI, pw(Pr, slice(1, 9)), pw(Pi_, slice(1, 9)), cm1[:], cm2[:], K, K, neg_im=True)
                    else:
                        self.cmul(BLr[:], BLi[:], bbR, bbI, pw(Pr, slice(0, 8)), pw(Pi_, slice(0, 8)), cm1[:], cm2[:], K, K)
                        self.cmul(CLr[:], CLi[:], cR, cI, pw(Nr, slice(0, 8)), pw(Ni, slice(0, 8)), cm1[:], cm2[:], K, K, neg_im=True)
                        self.copy('dve', BSr[:], BLr[:], K, K)
                        self.copy('dve', BSi[:], BLi[:], K, K)
                        self.cmul(CCr[:], CCi[:], cR, cI, pw(Pr, slice(8, 0, -1)), pw(Pi_, slice(8, 0, -1)), cm1[:], cm2[:], K, K, neg_im=True)
                    f2 = lambda T: T[:].rearrange("p a b c -> p a (b c)")
                    if s5a <= 2:
                        S.barrier()
                        return
                    for g in range(16):
                        gl, gh = g % 2, g // 2
                        ps_ = slice(64 * gl, 64 * gl + 64)
                        pt, ptk = psA[npsA % 4], ('sa_ps', npsA % 4)
                        npsA += 1
                        self.mm(pt[:, 0:128], f2(BLr)[ps_, gh, :], f2(CLr)[ps_, gh, :], True, False, K, [ptk])
                        self.mm(pt[:, 0:128], f2(BLi)[ps_, gh, :], f2(CLi)[ps_, gh, :], False, True, K, [ptk])
                        if d == 0:
                            self.tt('dve', tT[:], pt[:, 0:128], cmk[0][:], ALU.mult, [ptk, 'sa_cmk'], ['sa_tT'])
                            self.stt('dve', ToepT[0][:, g, :], self.identf[:], Dcol[:, g:g + 1], tT[:], ALU.mult, ALU.add,
                                     ['sa_tT', 'sa_Dcol', 'identf'], [('s_toep', 0)])
                        else:
                            self.tt('dve', ToepT[1][:, g, :], pt[:, 0:128], cmk[1][:], ALU.mult, [ptk, 'sa_cmk'], [('s_toep', 1)])
                    if s5a <= 3:
                        S.barrier()
                        return
                    for ri, T in enumerate((BSr, BSi)):
                        for g8 in range(2):
                            pt, ptk = psA[npsA % 4], ('sa_ps', npsA % 4)
                            npsA += 1
                            for hh in range(4):
                                gh = 4 * g8 + hh
                                self.tr(pt[:, 128 * hh:128 * hh + 128], f2(T)[:, gh, :], self.identf[:], K + ['identf'], [ptk])
                            self.copy('act', BSm[d][:, 8 * g8:8 * g8 + 8, ri, :], pt[:].rearrange("p (g q) -> p g q", g=8), [ptk], [('s_bsm', d)])
                    if s5a <= 4:
                        S.barrier()
                        return
                    self.copy('dve', CCm[d][:, :, 0, :], f2(CCr), K, [('s_ccm', d)])
                    self.copy('dve', CCm[d][:, :, 1, :], f2(CCi), K, [('s_ccm', d)])
                    if s5a <= 5:
                        S.barrier()
                        return
            S.barrier()
            import os
            s5stop = os.environ.get('S5STOP', 'Z')
            if s5stop == 'A':
                return
            with contextlib.ExitStack() as eb:
                wu = self.sb("sb_wu", [128, 8, 256], BF16, eb)
                ub = self.sb("sb_ub", [128, 2, 16, 8, 16], BF16, eb)
                psu = [self.psum("sb_psu%d" % i, [128, 512], F32, eb) for i in range(2)]
                pst = [self.psum("sb_pst%d" % i, [128, 1024], BF16, eb) for i in range(2)]
                self.ld(wu[:], self._ap('w_in')[l].rearrange("(k p) f -> p k f", p=128)[:, :, 1568:1824], ['sb_wu'], eng='pool')
                n = 0
                for b in range(2):
                    for t in range(8):
                        pos = 1024 * b + 128 * t
                        pu, puk = psu[n % 2], ('sb_psu', n % 2)
                        n += 1
                        for k in range(8):
                            self.mm(pu[:, 0:256], hm[:, k, pos:pos + 128], wu[:, k, :], k == 0, k == 7,
                                    ['sb_wu', ('m_hm', k, pos // 512)], [puk])
                        self.copy(self.evac_eng(), ub[:, b, :, t, :], pu[:, 0:256].rearrange("p (g c) -> p g c", g=16), [puk], [('sb_ub', b)])
                n = 0
                for b in range(2):
                    for q in range(4):
                        pt, ptk = pst[n % 2], ('sb_pst', n % 2)
                        n += 1
                        for gg in range(4):
                            g = 4 * q + gg
                            self.tr(pt[:, 128 * gg:128 * gg + 128], ub[:, b, g, :, :].rearrange("p t c -> p (t c)"), self.identb[:],
                                    [('sb_ub', b), 'identb'], [ptk])
                        self.copy(self.evac_eng(), U[:, 4 * q:4 * q + 4, 128 * b:128 * b + 128],
                                  pt[:, 0:512].rearrange("p (g c) -> p g c", g=4), [ptk], ['s_U'])
            S.barrier()
            if s5stop == 'B':
                return
            with contextlib.ExitStack() as ec:
                Hp = [[self.sb("sc_hp%d%d" % (d, ri), [128, 8, 256], BF16, ec) for ri in range(2)] for d in range(2)]
                with contextlib.ExitStack() as ec2:
                    La = [self.sb("sc_la%d" % ri, [128, 8, 256], F32, ec2) for ri in range(2)]
                    Lb = [self.sb("sc_lb%d" % ri, [128, 8, 256], F32, ec2) for ri in range(2)]
                    Cy = [self.sb("sc_cy%d" % ri, [128, 8, 8], F32, ec2) for ri in range(2)]
                    sm2 = self.sb("sc_sm", [128, 4, 8], F32, ec2)
                    Fsb = self.sb("sc_F", [128, 2, 64], F32, ec2)
                    FT = self.sb("sc_FT", [64, 2, 128], F32, ec2)
                    psS = [[self.psum("sc_ps%d%d" % (ri, q), [128, 512], F32, ec2) for q in range(3)] for ri in range(2)]
                    psF = self.psum("sc_psF", [128, 512], F32, ec2)
                    for d in range(2):
                        KL = ['sc_L']
                        for ri in range(2):
                            for q4 in range(4):
                                pb, pbk = psS[ri][q4 % 3], ('sc_ps', ri, q4 % 3)
                                for hh in range(2):
                                    gh = 2 * q4 + hh
                                    for gl in range(2):
                                        g = 2 * gh + gl
                                        self.mm(pb[64 * gl:64 * gl + 64, 256 * hh:256 * hh + 256], BSm[d][:, g, ri, :], U[:, g, :], True, True,
                                                [('s_bsm', d), 's_U'], [pbk])
                                self.copy(self.evac_eng(), La[ri][:, 2 * q4:2 * q4 + 2, :], pb[:].rearrange("p (a c) -> p a c", a=2), [pbk], KL)
                        cur, nxt = La, Lb
                        v5 = lambda T: T[:].rearrange("p a (s k) -> p a s k", k=32)
                        for dd in (1, 2, 4, 8, 16):
                            for ri in range(2):
                                if d == 0:
                                    self.copy('act', v5(nxt[ri])[:, :, :, 0:dd], v5(cur[ri])[:, :, :, 0:dd], KL, KL)
                                else:
                                    self.copy('act', v5(nxt[ri])[:, :, :, 32 - dd:32], v5(cur[ri])[:, :, :, 32 - dd:32], KL, KL)
                            for gh in range(8):
                                vv = lambda T: T[:, gh, :].rearrange("p (s k) -> p s k", k=32)
                                if d == 0:
                                    dst = slice(dd, 32)
                                    src = slice(0, 32 - dd)
                                else:
                                    dst = slice(0, 32 - dd)
                                    src = slice(dd, 32)
                                lr_ = PWr[d][:, gh, dd:dd + 1]
                                li_ = PWi[d][:, gh, dd:dd + 1]
                                lin_ = PWin[d][:, gh, dd:dd + 1]
                                self.stt('dve', vv(nxt[0])[:, :, dst], vv(cur[0])[:, :, src], lr_, vv(cur[0])[:, :, dst], ALU.mult, ALU.add, KL, KL)
                                self.stt('dve', vv(nxt[0])[:, :, dst], vv(cur[1])[:, :, src], lin_, vv(nxt[0])[:, :, dst], ALU.mult, ALU.add, KL, KL)
                                self.stt('dve', vv(nxt[1])[:, :, dst], vv(cur[0])[:, :, src], li_, vv(cur[1])[:, :, dst], ALU.mult, ALU.add, KL, KL)
                                self.stt('dve', vv(nxt[1])[:, :, dst], vv(cur[1])[:, :, src], lr_, vv(nxt[1])[:, :, dst], ALU.mult, ALU.add, KL, KL)
                            cur, nxt = nxt, cur
                        L = cur
                        E = [v5(L[ri])[:, :, :, 31 if d == 0 else 0] for ri in range(2)]
                        l32r, l32i = PWr[d][:, :, 32], PWi[d][:, :, 32]
                        order = list(range(8)) if d == 0 else list(range(7, -1, -1))
                        s0 = order[0]
                        self.copy('dve', Cy[0][:, :, s0], h0r[d][:], KL, KL)
                        self.copy('dve', Cy[1][:, :, s0], h0i[d][:], KL, KL)
                        for idx in range(1, 8):
                            s, sp_ = order[idx], order[idx - 1]
                            self.cmul(sm2[:, 0, :], sm2[:, 1, :], l32r, l32i, Cy[0][:, :, sp_], Cy[1][:, :, sp_], sm2[:, 2, :], sm2[:, 3, :], KL, KL)
                            for ri in range(2):
                                self.tt('dve', sm2[:, ri, :], sm2[:, ri, :], E[ri][:, :, sp_], ALU.add, KL, KL)
                                self.ts('dve', Cy[ri][:, :, s], sm2[:, ri, :], flag[:, 0:1], None, ALU.mult, None, KL + ['s_flag'], KL)
                        sh4 = [128, 8, 8, 32]
                        if d == 0:
                            pwv = lambda T: T[:, :, 1:33].unsqueeze(2).to_broadcast(sh4)
                        else:
                            pwv = lambda T: T[:, :, 32:0:-1].unsqueeze(2).to_broadcast(sh4)
                        cyv = lambda ri: Cy[ri][:].unsqueeze(3).to_broadcast(sh4)
                        t1v = v5(nxt[0])
                        for (ri, a, b_, op) in ((0, PWr[d], 0, ALU.add), (0, PWi[d], 1, ALU.subtract), (1, PWr[d], 1, ALU.add), (1, PWi[d], 0, ALU.add)):
                            self.tt('dve', t1v, pwv(a), cyv(b_), ALU.mult, KL, ['sc_t1'])
                            self.tt('dve', v5(L[ri]), v5(L[ri]), t1v, op, KL + ['sc_t1'], KL)
                        for ri in range(2):
                            hv = v5(Hp[d][ri])
                            if d == 0:
                                self.copy('act', hv[:, :, :, 1:32], v5(L[ri])[:, :, :, 0:31], KL, [('sc_hp', d)])
                                self.copy('dve', hv[:, :, :, 0], Cy[ri][:], KL, [('sc_hp', d)])
                            else:
                                self.copy('act', hv[:, :, :, 0:31], v5(L[ri])[:, :, :, 1:32], KL, [('sc_hp', d)])
                                self.copy('dve', hv[:, :, :, 31], Cy[ri][:], KL, [('sc_hp', d)])
                        for ri in range(2):
                            self.copy('dve', Fsb[:, ri, :].rearrange("p (a s) -> p a s", a=8), E[ri], KL, ['sc_F'])
                            self.tr(psF[0:64, 128 * ri:128 * ri + 128], Fsb[:, ri, :], self.identf[:], ['sc_F', 'identf'], ['sc_psF'])
                        self.copy('dve', FT[:].rearrange("p a b -> p (a b)"), psF[0:64, 0:256], ['sc_psF'], ['sc_FT'])
                        for ri, nm in enumerate(('ns5re', 'ns5im')):
                            for gh in range(8):
                                self.st(self.o[nm][l, d][:, 128 * gh:128 * gh + 128], FT[8 * gh:8 * gh + 8, ri, :], ['sc_FT'])
                S.barrier()
                if s5stop == 'C':
                    return
                with contextlib.ExitStack() as ed:
                    Ysb = self.sb("sd_Y", [128, 16, 256], BF16, ed)
                    g1 = [self.sb("sd_g%d" % i, [128, 512], F32, ed) for i in range(2)]
                    psY = [self.psum("sd_ps%d" % i, [128, 512], F32, ed) for i in range(3)]
                    for q in range(8):
                        py, pyk = psY[q % 3], ('sd_ps', q % 3)
                        for hh in range(2):
                            g = 2 * q + hh
                            gl, gh = g % 2, g // 2
                            ps_ = slice(64 * gl, 64 * gl + 64)
                            o = py[:, 256 * hh:256 * hh + 256]
                            for d in range(2):
                                self.mm(o, ToepT[d][:, g, :], U[:, g, :], d == 0, False, [('s_toep', d), 's_U'], [pyk])
                                self.mm(o, CCm[d][ps_, gh, 0, :], Hp[d][0][ps_, gh, :], False, False, [('s_ccm', d), ('sc_hp', d)], [pyk])
                                self.mm(o, CCm[d][ps_, gh, 1, :], Hp[d][1][ps_, gh, :], False, d == 1, [('s_ccm', d), ('sc_hp', d)], [pyk])
                        gb, gk = g1[q % 2], ('sd_g', q % 2)
                        self.act(gb[:], py[:], AF.Square, [pyk], [gk])
                        self.ts('dve', gb[:], gb[:], 0.044715, None, ALU.mult, None, [gk], [gk])
                        self.ts('dve', gb[:], gb[:], 1.0, None, ALU.add, None, [gk], [gk])
                        self.tt('dve', gb[:], gb[:], py[:], ALU.mult, [gk, pyk], [gk])
                        self.act(gb[:], gb[:], AF.Sigmoid, [gk], [gk], scale=1.5957691216)
                        self.tt('dve', Ysb[:, 2 * q:2 * q + 2, :], gb[:].rearrange("p (a c) -> p a c", a=2),
                                py[:].rearrange("p (a c) -> p a c", a=2), ALU.mult, [gk, pyk], ['sd_Y'])
                    S.barrier()
                    ytok = self.sb("se_ytok", [128, 2, 8, 256], BF16, ed)
                    ygT = self.sb("se_ygT", [128, 2, NT], BF16, ed)
                    mos = self.sb("se_mo", [128, 2, NT], BF16, ed)
                    wob = [self.sb("se_wo%d" % i, [128, D], BF16, ed) for i in range(2)]
                    wg = self.sb("se_wg", [128, 2, 256], BF16, ed)
                    bg = self.sb("se_bg", [128, 2], F32, ed)
                    sg = [self.sb("se_sg%d" % i, [128, 512], F32, ed) for i in range(2)]
                    pst = [self.psum("se_pst%d" % i, [128, 1024], BF16, ed) for i in range(2)]
                    psg = [self.psum("se_psg%d" % i, [128, 512], F32, ed) for i in range(2)]
                    self.ld(wg[:], dr['s5_w_glu'][l].rearrange("(k p) f -> p k f", p=128), ['se_wg'], eng='pool')
                    self.ld(bg[:], dr['s5_b_glu'][l].rearrange("a p -> p a"), ['se_bg'], allow_slow_non_contiguous=True)
                    n = 0
                    for b in range(2):
                        for q in range(4):
                            pt, ptk = pst[n % 2], ('se_pst', n % 2)
                            n += 1
                            for gg in range(4):
                                g = 4 * q + gg
                                self.tr(pt[:, 128 * gg:128 * gg + 128], Ysb[:, g, 128 * b:128 * b + 128], self.identb[:], ['sd_Y', 'identb'], [ptk])
                            dst = ytok[:, b].rearrange("p t (g c) -> p g t c", c=16)[:, 4 * q:4 * q + 4]
                            self.copy(self.evac_eng(), dst, pt[:, 0:512].rearrange("p (g t c) -> p g t c", g=4, t=8), [ptk], [('se_ytok', b)])
                    for b in range(2):
                        for f in range(2):
                            for h4 in range(2):
                                pt, ptk = pst[n % 2], ('se_pst', n % 2)
                                n += 1
                                for tt_ in range(4):
                                    t = 4 * h4 + tt_
                                    self.tr(pt[:, 128 * tt_:128 * tt_ + 128], ytok[:, b, t, 128 * f:128 * f + 128], self.identb[:],
                                            [('se_ytok', b), 'identb'], [ptk])
                                blk = 2 * b + h4
                                self.copy(self.evac_eng(), ygT[:, f, 512 * blk:512 * blk + 512], pt[:, 0:512], [ptk], [('se_ygT', f, blk)])
                    n = 0
                    for fo in range(2):
                        for blk in range(4):
                            pg, pgk = psg[n % 2], ('se_psg', n % 2)
                            sgb, sgk = sg[n % 2], ('se_sg', n % 2)
                            n += 1
                            for k in range(2):
                                self.mm(pg[:], wg[:, k, 128 * fo:128 * fo + 128], ygT[:, k, 512 * blk:512 * blk + 512], k == 0, k == 1,
                                        ['se_wg', ('se_ygT', k, blk)], [pgk])
                            self.act(sgb[:], pg[:], AF.Sigmoid, [pgk, 'se_bg'], [sgk], bias=bg[:, fo:fo + 1])
                            self.tt('dve', mos[:, fo, 512 * blk:512 * blk + 512], sgb[:], ygT[:, fo, 512 * blk:512 * blk + 512], ALU.mult,
                                    [sgk, ('se_ygT', fo, blk)], [('se_mo', fo, blk)])
                    self.wout_part(l, [6, 7], [(mos[:, fo, :], (lambda blk, fo=fo: ('se_mo', fo, blk))) for fo in range(2)],
                                   [(wob[i][:], ('se_wo', i)) for i in range(2)], psg, [('se_psg', 0), ('se_psg', 1)])

    def rope(self, src5, dsts, tt, nb, tmps, rkey, wkeys):
        b, t = tt // 8, tt % 8
        tk = ['rp_t']
        if nb == 1:
            cosb = self.rc[:, b, t, :].rearrange("p (a f) -> p a f", a=2)
            sinb = self.rsn[:, b, t, :].rearrange("p (a f) -> p a f", a=2)
            x1, x2 = src5[:, 0, :, 0, :], src5[:, 0, :, 1, :]
            t1, t2, t3, t4 = [T[:, 0:16].rearrange("p (a f) -> p a f", a=2) for T in tmps]
        else:
            sh = [128, nb, 2, 8]
            cosb = self.rc[:, b, t, :].rearrange("p (a f) -> p a f", a=2).unsqueeze(1).to_broadcast(sh)
            sinb = self.rsn[:, b, t, :].rearrange("p (a f) -> p a f", a=2).unsqueeze(1).to_broadcast(sh)
            x1, x2 = src5[:, :, :, 0, :], src5[:, :, :, 1, :]
            t1, t2, t3, t4 = [T[:, 0:nb * 16].rearrange("p (n a f) -> p n a f", a=2, f=8) for T in tmps]
        self.tt('dve', t1, x1, cosb, ALU.mult, rkey + ['rope_tab'], tk)
        self.tt('dve', t2, x2, sinb, ALU.mult, rkey + ['rope_tab'], tk)
        self.tt('dve', t3, x1, sinb, ALU.mult, rkey + ['rope_tab'], tk)
        self.tt('dve', t4, x2, cosb, ALU.mult, rkey + ['rope_tab'], tk)
        for (bs, dst5), wk in zip(dsts, wkeys):
            if nb == 1:
                self.tt('dve', dst5[:, 0, :, 0, :], t1, t2, ALU.subtract, tk, [wk])
                self.tt('dve', dst5[:, 0, :, 1, :], t3, t4, ALU.add, tk, [wk])
            else:
                self.tt('dve', dst5[:, :, :, 0, :], t1[:, bs], t2[:, bs], ALU.subtract, tk, [wk])
                self.tt('dve', dst5[:, :, :, 1, :], t3[:, bs], t4[:, bs], ALU.add, tk, [wk])

    def attn(self, l, hm, mo):
        S = self.S
        dr = self._dr_cache
        lam_init = 0.8 - 0.6 * math.exp(-0.3 * l)
        with contextlib.ExitStack() as es:
            sb = lambda n, s, d: self.sb(n, s, d, es)
            self.rc = sb("a_rc", [128, 2, 8, 16], F32)
            self.rsn = sb("a_rs", [128, 2, 8, 16], F32)
            aq = sb("a_aq", [128, 16, 9], F32)
            ak = sb("a_ak", [128, 18, 9], F32)
            QS = sb("a_QS", [128, 16, 128], BF16)
            KS = sb("a_KS", [128, 18, 128], BF16)
            QT = [sb("a_QT%d" % i, [128, NT], BF16) for i in range(2)]
            KT = [sb("a_KT%d" % i, [128, NT + 256], BF16) for i in range(2)]
            Vh = [sb("a_V%d" % i, [128, 18, 72], BF16) for i in range(2)]
            PT = [sb("a_PT%d" % i, [128, 512], BF16) for i in range(4)]
            moh = [sb("a_moh%d" % i, [128, NT], BF16) for i in range(2)]
            wob = [sb("a_wo%d" % i, [128, D], BF16) for i in range(2)]
            rt = [sb("a_rt%d" % i, [128, 64], F32) for i in range(4)]
            o0 = sb("a_o0", [128, 4, 64], F32)
            o1 = sb("a_o1", [128, 4, 64], F32)
            osq = sb("a_osq", [128, 4, 64], F32)
            ost = sb("a_ost", [128, 4, 64], BF16)
            sml = sb("a_sml", [128, 16], F32)
            dl = sb("a_dl", [128, 128], F32)
            subw = sb("a_subw", [128, 64], F32)
            lamt = sb("a_lam", [128, 4], F32)
            oTs = [sb("a_oTs%d" % i, [128, 512], F32) for i in range(2)]
            edf = contextlib.ExitStack()
            wh = [self.sb("a_wh%d" % i, [128, 8, 192], BF16, edf) for i in range(2)]
            cdk = self.sb("a_cdk", [128, 2, 384], F32, edf)
            cdv = self.sb("a_cdv", [128, 2, 384], F32, edf)
            kvo = [self.sb("a_kvo%d" % i, [128, 128], F32, edf) for i in range(2)]
            psp = [self.psum("a_psp%d" % i, [128, 512], F32, es) for i in range(2)]
            pstr = [self.psum("a_pst%d" % i, [128, 1024], BF16, es) for i in range(1)]
            NPSS = 3
            pss = [self.psum("a_pss%d" % i, [128, 512], F32, es) for i in range(NPSS)]
            pso = [self.psum("a_pso%d" % i, [128, 512], F32, es) for i in range(2)]
            cnt = {'psp': 0, 'pst': 0, 'pss': 0, 'PT': 0, 'kvo': 0}
            pss_l = [(pss[i][:], ('a_pss', i)) for i in range(NPSS)] + [(pstr[0][:].bitcast(F32), ('a_pst', 0))]
            assert list(pss_l[3][0].shape) == [128, 512], pss_l[3][0].shape

            self.ld(self.rc[:], dr['ropec'].rearrange("(b p t) f -> p b t f", b=2, t=8), ['rope_tab'])
            self.ld(self.rsn[:], dr['ropes'].rearrange("(b p t) f -> p b t f", b=2, t=8), ['rope_tab'])
            self.ld(aq[:].rearrange("p (b t) f -> p b t f", b=2), dr['augq'].rearrange("(b p t) f -> p b t f", b=2, t=8), ['a_aq'])
            self.ld(ak[:, 0:16, :].rearrange("p (b t) f -> p b t f", b=2), dr['augk'][0:NT].rearrange("(b p t) f -> p b t f", b=2, t=8), ['a_ak'])
            self.ld(ak[:, 16:18, :], dr['augk'][NT:NT + 256].rearrange("(i p) f -> p i f", p=128), ['a_ak'])
            self.ld(cdk[:], dr['ctx_dk'][l].rearrange("(i p) f -> p i f", p=128), ['a_cdk'])
            self.ld(cdv[:], dr['ctx_dv'][l].rearrange("(i p) f -> p i f", p=128), ['a_cdv'])
            self.ld(dl[:], dr['diff_lambda'][l].partition_broadcast(128), ['a_dl'])
            self.ld(subw[:], dr['diff_subln_w'][l].partition_broadcast(128), ['a_subw'])
            LK = ['a_lam']
            self.tt('dve', o0[:, 0, :].rearrange("p (a f) -> p a f", a=2), dl[:].rearrange("p (a b f) -> p a b f", a=2, b=2)[:, :, 0, :],
                    dl[:].rearrange("p (a b f) -> p a b f", a=2, b=2)[:, :, 1, :], ALU.mult, ['a_dl'], LK)
            S.op('dve', lambda e: e.tensor_reduce(out=lamt[:, 1:3], in_=o0[:, 0, :].rearrange("p (a f) -> p a f", a=2),
                                                  axis=mybir.AxisListType.X, op=ALU.add), LK, LK)
            self.act(lamt[:, 1:3], lamt[:, 1:3], AF.Exp, LK, LK)
            self.tt('dve', lamt[:, 3:4], lamt[:, 2:3], lamt[:, 1:2], ALU.subtract, LK, LK)
            self.ts('dve', lamt[:, 0:1], lamt[:, 3:4], -lam_init, None, ALU.add, None, LK, LK)
            self.ts('dve', subw[:], subw[:], 1.0 - lam_init, None, ALU.mult, None, ['a_subw'], ['a_subw'])
            S.op('dve', lambda e: e.memset(self.epsc[:, 0:1], EPS), (), ['epsc'])

            def init_staging(qcols, kcols):
                S.op('dve', lambda e: e.memset(QS[:], 0.0), ['a_QS'], ['a_QS'])
                S.op('dve', lambda e: e.memset(KS[:], 0.0), ['a_KS'], ['a_KS'])
                for c0 in qcols:
                    self.copy('dve', QS[:, :, c0:c0 + 9], aq[:], ['a_aq'], ['a_QS'])
                for c0 in kcols:
                    self.copy('dve', KS[:, :, c0:c0 + 9], ak[:], ['a_ak'], ['a_KS'])
            for i in range(2):
                S.op('dve', lambda e, i=i: e.memset(Vh[i][:, :, 64:65], 1.0), [('a_V', i)], [('a_V', i)])

            def transposes(src, ntile, dstT, skey, dkey):
                for q in range((ntile + 3) // 4):
                    k = cnt['pst']
                    cnt['pst'] += 1
                    pt, ptk = pstr[0], ('a_pst', 0)
                    m = min(4, ntile - 4 * q)
                    for i in range(m):
                        self.tr(pt[:, 128 * i:128 * i + 128], src[:, 4 * q + i, :], self.identb[:], [skey, 'identb'], [ptk])
                    self.copy(self.evac_eng(), dstT[:, 512 * q:512 * q + 128 * m], pt[:, 0:128 * m], [ptk], [dkey])

            def core(hs, comps, scale, post):
                qt_, kt_, vh_ = QT[hs], KT[hs], Vh[hs]
                ncmp = len(comps)
                steps = [(qb, ci, kt) for qb in range(4) for kt in range(18) for ci in range(ncmp)]
                n = len(steps)
                slots = {}

                def score(i):
                    qb, ci, kt = steps[i]
                    r0, nr = comps[ci]
                    k = cnt['pss']
                    cnt['pss'] += 1
                    ps_, psk = pss_l[k % 4]
                    slots[i] = (ps_, psk)
                    self.mm(ps_, kt_[r0:r0 + nr, 128 * kt:128 * kt + 128], qt_[r0:r0 + nr, 512 * qb:512 * qb + 512], True, True,
                            [('a_KT', hs), ('a_QT', hs)], [psk])
                for j in range(2):
                    score(j)
                for i in range(n):
                    qb, ci, kt = steps[i]
                    if ncmp == 2:
                        if i % 2 == 0:
                            for j in (i + 2, i + 3):
                                if j < n:
                                    score(j)
                    elif i + 2 < n:
                        score(i + 2)
                    ps_, psk = slots.pop(i)
                    po, pok = pso[ci], ('a_pso', ci)
                    k2 = cnt['PT']
                    cnt['PT'] += 1
                    pt, ptk = PT[k2 % 4], ('a_PT', k2 % 4)
                    self.act(pt[:], ps_, AF.Exp, [psk], [ptk], scale=scale)
                    self.mm(po[0:65, :], vh_[:, kt, 0:65], pt[:], kt == 0, kt == 17, [ptk, ('a_V', hs)], [pok])
                    if kt == 17:
                        self.copy('dve', oTs[ci][0:65, :], po[0:65, :], [pok], [('a_oTs', ci)])
                        for qt in range(4):
                            self.tr(psp[ci][:, 128 * qt:128 * qt + 65], oTs[ci][0:65, 128 * qt:128 * qt + 128], self.identf[0:65, 0:65],
                                    [('a_oTs', ci), 'identf'], [('a_psp', ci)])
                        if ci == ncmp - 1:
                            post(qb)

            def normalize(ci, dst):
                po = psp[ci][:].rearrange("p (q f) -> p q f", f=128)
                S.op('dve', lambda e: e.reciprocal(out=sml[:, 4 * ci:4 * ci + 4], in_=po[:, :, 64]), [('a_psp', ci)], ['a_sml'])
                self.tt('dve', dst, po[:, :, 0:64], sml[:, 4 * ci:4 * ci + 4].unsqueeze(2).to_broadcast([128, 4, 64]), ALU.mult,
                        [('a_psp', ci), 'a_sml'], ['a_o'])

            def out_transposes(qb, jt, roff):
                k = cnt['psp']
                cnt['psp'] += 1
                pt, ptk = psp[k % 2], ('a_psp', k % 2)
                for qt in range(4):
                    self.mm(pt[roff:roff + 64, 128 * qt:128 * qt + 128], ost[:, qt, :], self.identb[:], True, True, ['a_ost', 'identb'], [ptk])
                self.copy(self.evac_eng(), moh[jt % 2][roff:roff + 64, 512 * qb:512 * qb + 512], pt[roff:roff + 64, :], [ptk], [('a_moh', jt % 2, qb)])

            def pair_wout(jt):
                self.wout_part(l, [jt], [(moh[jt % 2][:], (lambda blk, jt=jt: ('a_moh', jt % 2, blk)))],
                               [(wob[jt % 2][:], ('a_wo', jt % 2))], psp, [('a_psp', 0), ('a_psp', 1)])

            w_in_v = self._ap('w_in')[l].rearrange("(k p) f -> p k f", p=128)
            import os
            att = int(os.environ.get('ATT', '99'))
            init_staging((32, 96), (32, 96))
            if att <= 1:
                S.barrier()
                edf.close()
                return
            for h in range(6):
                hs = h % 2
                whb, whk = wh[hs], ('a_wh', hs)
                for i3 in range(3):
                    self.ld(whb[:, :, 64 * i3:64 * i3 + 64], w_in_v[:, :, 384 * i3 + 64 * h:384 * i3 + 64 * h + 64], [whk], eng='pool')
                for tt in range(16):
                    pos = 128 * tt
                    b, t = tt // 8, tt % 8
                    k = cnt['psp']
                    cnt['psp'] += 1
                    pp, ppk = psp[k % 2], ('a_psp', k % 2)
                    for kk in range(8):
                        self.mm(pp[:, 0:192], hm[:, kk, pos:pos + 128], whb[:, kk, :], kk == 0, kk == 7, [whk, ('m_hm', kk, pos // 512)], [ppk])
                    src5 = pp[:, 0:128].rearrange("p (n a h f) -> p n a h f", n=4, a=2, h=2)
                    qd = QS[:, tt, :].rearrange("p (c x) -> p c x", c=2)[:, :, 0:32].rearrange("p c (a h f) -> p c a h f", a=2, h=2)
                    kd = KS[:, tt, :].rearrange("p (c x) -> p c x", c=2)[:, :, 0:32].rearrange("p c (a h f) -> p c a h f", a=2, h=2)
                    self.rope(src5, [(slice(0, 2), qd), (slice(2, 4), kd)], tt, 4, [r[:] for r in rt], [ppk], ['a_QS', 'a_KS'])
                    kv = cnt['kvo']
                    cnt['kvo'] += 1
                    kvb, kvk = kvo[kv % 2], ('a_kvo', kv % 2)
                    self.copy('act', kvb[:], pp[:, 64:192], [ppk], [kvk])
                    rows_k = self.o['ndk'][l].rearrange("(b p t) f -> b t p f", b=2, t=8)[b, t]
                    rows_v = self.o['ndv'][l].rearrange("(b p t) f -> b t p f", b=2, t=8)[b, t]
                    self.st(rows_k[:, 64 * h:64 * h + 64], kvb[:, 0:64], [kvk])
                    self.st(rows_v[:, 64 * h:64 * h + 64], kvb[:, 64:128], [kvk])
                    self.copy('act', Vh[hs][:, tt, 0:64], pp[:, 128:192], [ppk], [('a_V', hs)])
                for i in range(2):
                    kd = KS[:, 16 + i, :].rearrange("p (c x) -> p c x", c=2)[:, :, 0:32]
                    self.copy('dve', kd, cdk[:, i, 64 * h:64 * h + 64].rearrange("p (c x) -> p c x", c=2), ['a_cdk'], ['a_KS'])
                    self.copy('dve', Vh[hs][:, 16 + i, 0:64], cdv[:, i, 64 * h:64 * h + 64], ['a_cdv'], [('a_V', hs)])
                if att <= 2:
                    S.barrier()
                    edf.close()
                    return
                transposes(QS, 16, QT[hs], 'a_QS', ('a_QT', hs))
                transposes(KS, 18, KT[hs], 'a_KS', ('a_KT', hs))
                if att <= 3:
                    S.barrier()
                    edf.close()
                    return

                def post(qb, h=h):
                    normalize(0, o0[:])
                    normalize(1, o1[:])
                    self.stt('dve', o0[:], o1[:], lamt[:, 0:1], o0[:], ALU.mult, ALU.add, ['a_o', 'a_lam'], ['a_o'])
                    self.tt('dve', osq[:], o0[:], o0[:], ALU.mult, ['a_o'], ['a_osq'])
                    S.op('dve', lambda e: e.tensor_reduce(out=sml[:, 8:12], in_=osq[:], axis=mybir.AxisListType.X, op=ALU.add),
                         ['a_osq'], ['a_sml'])
                    self.act(sml[:, 8:12], sml[:, 8:12], AF.Sqrt, ['a_sml', 'epsc'], ['a_sml'], bias=self.epsc[:, 0:1], scale=1.0 / 64)
                    S.op('dve', lambda e: e.reciprocal(out=sml[:, 8:12], in_=sml[:, 8:12]), ['a_sml'], ['a_sml'])
                    self.tt('dve', o0[:], o0[:], sml[:, 8:12].unsqueeze(2).to_broadcast([128, 4, 64]), ALU.mult, ['a_o', 'a_sml'], ['a_o'])
                    self.tt('dve', ost[:], o0[:], subw[:].unsqueeze(1).to_broadcast([128, 4, 64]), ALU.mult, ['a_o', 'a_subw'], ['a_ost'])
                    out_transposes(qb, h // 2, 64 * (h % 2))
                core(hs, [(0, 64), (64, 64)], 32 ** -0.5, post)
                if att <= 4:
                    S.barrier()
                    edf.close()
                    return
                if h % 2 == 1:
                    pair_wout(h // 2)
            S.barrier()
            edf.close()
            if att <= 5:
                return
            with contextlib.ExitStack() as em:
                sbm = lambda n, s, d: self.sb(n, s, d, em)
                wm = sbm("a_wm", [128, 8, 416], BF16)
                cqnT = sbm("a_cqnT", [128, 2, NT], BF16)
                ckvT = sbm("a_ckvT", [128, NT + 256], BF16)
                cqs = sbm("a_cqs", [128, 2, 256], BF16)
                cks = sbm("a_cks", [128, 2, 128], BF16)
                qnw = sbm("a_qnw", [128, 256], F32)
                kvnw = sbm("a_kvnw", [128, 128], F32)
                cckv = sbm("a_cckv", [128, 2, 128], F32)
                ckpe = sbm("a_ckpe", [128, 2, 32], F32)
                tq = sbm("a_tq", [128, 256], F32)
                tk_ = [sbm("a_tk%d" % i, [128, 160], F32) for i in range(2)]
                wq = [sbm("a_wq%d" % i, [128, 2, 96], BF16) for i in range(2)]
                wkv = [sbm("a_wkv%d" % i, [128, 128], BF16) for i in range(2)]
                init_staging((96,), (96,))
                self.ld(wm[:], w_in_v[:, :, 1152:1568], ['a_wm'], eng='pool')
                self.ld(qnw[:], dr['mla_q_norm_w'][l].partition_broadcast(128), ['a_qnw'])
                self.ld(kvnw[:], dr['mla_kv_norm_w'][l].partition_broadcast(128), ['a_kvnw'])
                self.ld(cckv[:], dr['ctx_ckv'][l].rearrange("(i p) f -> p i f", p=128), ['a_cckv'])
                self.ld(ckpe[:], dr['ctx_kpe'][l].rearrange("(i p) f -> p i f", p=128), ['a_ckpe'])
                mla = int(os.environ.get('MLA', '99'))
                if mla <= 1:
                    S.barrier()
                    return
                for tt in range(16):
                    pos = 128 * tt
                    b, t = tt // 8, tt % 8
                    k = cnt['psp']
                    cnt['psp'] += 1
                    pp, ppk = psp[k % 2], ('a_psp', k % 2)
                    for kk in range(8):
                        self.mm(pp[:, 0:416], hm[:, kk, pos:pos + 128], wm[:, kk, :], kk == 0, kk == 7, ['a_wm', ('m_hm', kk, pos // 512)], [ppk])
                    self.act(tq[:], pp[:, 0:256], AF.Square, [ppk], ['a_tq'], accum=sml[:, 12:13])
                    self.act(sml[:, 12:13], sml[:, 12:13], AF.Sqrt, ['a_tq'], ['a_sml2'], bias=self.epsc[:, 0:1], scale=1.0 / 256)
                    S.op('dve', lambda e: e.reciprocal(out=sml[:, 12:13], in_=sml[:, 12:13]), ['a_sml2'], ['a_sml2'])
                    cb = tt % 2
                    self.stt('dve', cqs[:, cb, :], pp[:, 0:256], sml[:, 12:13], qnw[:], ALU.mult, ALU.mult, [ppk, 'a_sml2', 'a_qnw'], [('a_cqs', cb)])
                    kq = cnt['pst']
                    cnt['pst'] += 1
                    ptq, ptqk = pstr[0], ('a_pst', 0)
                    for f in range(2):
                        self.tr(ptq[:, 128 * f:128 * f + 128], cqs[:, cb, 128 * f:128 * f + 128], self.identb[:], [('a_cqs', cb), 'identb'], [ptqk])
                    self.copy(self.evac_eng(), cqnT[:, :, pos:pos + 128], ptq[:, 0:256].rearrange("p (f c) -> p f c", f=2), [ptqk], ['a_cqnT'])
                    mlap = int(os.environ.get('MLAP', '99'))
                    if mlap <= 1:
                        continue
                    kv = cnt['kvo']
                    cnt['kvo'] += 1
                    tkb, tkk = tk_[kv % 2], ('a_tk', kv % 2)
                    self.act(tq[:, 0:128], pp[:, 256:384], AF.Square, [ppk, 'a_tq'], ['a_tq'], accum=sml[:, 13:14])
                    self.act(sml[:, 13:14], sml[:, 13:14], AF.Sqrt, ['a_tq'], ['a_sml3'], bias=self.epsc[:, 0:1], scale=1.0 / 128)
                    S.op('dve', lambda e: e.reciprocal(out=sml[:, 13:14], in_=sml[:, 13:14]), ['a_sml3'], ['a_sml3'])
                    self.stt('dve', tkb[:, 0:128], pp[:, 256:384], sml[:, 13:14], kvnw[:], ALU.mult, ALU.mult, [ppk, 'a_sml3', 'a_kvnw'], [tkk])
                    self.copy('act', tkb[:, 128:160], pp[:, 384:416], [ppk], [tkk])
                    self.copy('act', cks[:, cb, :], tkb[:, 0:128], [tkk], [('a_cks', cb)])
                    self.tr(ptq[:, 256:384], cks[:, cb, :], self.identb[:], [('a_cks', cb), 'identb'], [ptqk])
                    self.copy(self.evac_eng(), ckvT[:, pos:pos + 128], ptq[:, 256:384], [ptqk], ['a_ckvT'])
                    if mlap <= 2:
                        continue
                    rows_c = self.o['nckv'][l].rearrange("(b p t) f -> b t p f", b=2, t=8)[b, t]
                    rows_p = self.o['nkpe'][l].rearrange("(b p t) f -> b t p f", b=2, t=8)[b, t]
                    self.st(rows_c, tkb[:, 0:128], [tkk])
                    self.st(rows_p, tkb[:, 128:160], [tkk])
                    if mlap <= 3:
                        continue
                    src5 = pp[:, 384:416].rearrange("p (n a h f) -> p n a h f", n=1, a=2, h=2)
                    kd = KS[:, tt, 64:96].rearrange("p (n a h f) -> p n a h f", n=1, a=2, h=2)
                    self.rope(src5, [(slice(0, 1), kd)], tt, 1, [r[:] for r in rt], [ppk], ['a_KS'])
                if mla <= 2:
                    S.barrier()
                    return
                for i in range(2):
                    self.copy('dve', cks[:, i, :], cckv[:, i, :], ['a_cckv'], [('a_cks', i)])
                    kq = cnt['pst']
                    cnt['pst'] += 1
                    ptq, ptqk = pstr[0], ('a_pst', 0)
                    self.tr(ptq[:, 0:128], cks[:, i, :], self.identb[:], [('a_cks', i), 'identb'], [ptqk])
                    self.copy(self.evac_eng(), ckvT[:, NT + 128 * i:NT + 128 * i + 128], ptq[:, 0:128], [ptqk], ['a_ckvT'])
                    self.copy('dve', KS[:, 16 + i, 64:96], ckpe[:, i, :], ['a_ckpe'], ['a_KS'])
                if mla <= 3:
                    S.barrier()
                    return
                for h in range(6):
                    hs = h % 2
                    self.ld(wq[hs][:], dr['mla_w_q_up'][l].rearrange("(k p) f -> p k f", p=128)[:, :, 96 * h:96 * h + 96], [('a_wq', hs)], eng='pool')
                    self.ld(wkv[hs][:], dr['mla_w_kv_up'][l][:, 128 * h:128 * h + 128], [('a_wkv', hs)], eng='pool')
                    mlah = int(os.environ.get('MLAH', '99'))
                    if mlah <= 1:
                        S.barrier()
                        return
                    for tt in range(16):
                        pos = 128 * tt
                        k = cnt['psp']
                        cnt['psp'] += 1
                        pp, ppk = psp[k % 2], ('a_psp', k % 2)
                        for f in range(2):
                            self.mm(pp[:, 0:96], cqnT[:, f, pos:pos + 128], wq[hs][:, f, :], f == 0, f == 1, [('a_wq', hs), 'a_cqnT'], [ppk])
                        kv = cnt['kvo']
                        cnt['kvo'] += 1
                        tkb, tkk = tk_[kv % 2], ('a_tk', kv % 2)
                        self.copy('act', tkb[:, 0:96], pp[:, 0:96], [ppk], [tkk])
                        self.copy('act', QS[:, tt, 0:64], tkb[:, 0:64], [tkk], ['a_QS'])
                        if mlah <= 2:
                            continue
                        src5 = tkb[:, 64:96].rearrange("p (n a h f) -> p n a h f", n=1, a=2, h=2)
                        qd = QS[:, tt, 64:96].rearrange("p (n a h f) -> p n a h f", n=1, a=2, h=2)
                        self.rope(src5, [(slice(0, 1), qd)], tt, 1, [r[:] for r in rt], [tkk], ['a_QS'])
                    if mlah <= 3:
                        S.barrier()
                        return
                    for kt in range(18):
                        k = cnt['psp']
                        cnt['psp'] += 1
                        pp, ppk = psp[k % 2], ('a_psp', k % 2)
                        self.mm(pp[:, 0:128], ckvT[:, 128 * kt:128 * kt + 128], wkv[hs][:], True, True, [('a_wkv', hs), 'a_ckvT'], [ppk])
                        self.copy('act', KS[:, kt, 0:64], pp[:, 0:64], [ppk], ['a_KS'])
                        self.copy('dve', Vh[hs][:, kt, 0:64], pp[:, 64:128], [ppk], [('a_V', hs)])
                    if mla <= 4:
                        S.barrier()
                        return
                    transposes(QS, 16, QT[hs], 'a_QS', ('a_QT', hs))
                    transposes(KS, 18, KT[hs], 'a_KS', ('a_KT', hs))
                    if mla <= 5:
                        S.barrier()
                        return

                    def postm(qb, h=h):
                        normalize(0, o0[:])
                        self.copy('act', ost[:], o0[:], ['a_o'], ['a_ost'])
                        out_transposes(qb, 3 + h // 2, 64 * (h % 2))
                    core(hs, [(0, 128)], 96 ** -0.5, postm)
                    if mla <= 6:
                        S.barrier()
                        return
                    if h % 2 == 1:
                        pair_wout(3 + h // 2)


_PROG = {}


def get_prog(stage=99):
    if stage not in _PROG:
        b = Builder(stage)
        _PROG[stage] = b.build()
    return _PROG[stage]


def rope_tables(n, grid_w=64, theta=10000.0):
    t = np.arange(n)
    row = (t // grid_w).astype(np.float32)
    col = (t % grid_w).astype(np.float32)
    inv = (theta ** (-np.arange(8, dtype=np.float32) / 8)).astype(np.float32)
    ang = np.concatenate([row[:, None] * inv[None], col[:, None] * inv[None]], axis=1).astype(np.float32)
    return np.cos(ang).astype(np.float32), np.sin(ang).astype(np.float32)


def _gsplit(a, axis):
    a = np.asarray(a, dtype=np.float32)
    sh = a.shape
    a = a.reshape(sh[:axis] + (8, 2) + sh[axis + 1:])
    a = np.moveaxis(a, axis + 1, axis)
    return np.ascontiguousarray(a)


def make_in_maps(inp):
    f = lambda a: np.ascontiguousarray(np.asarray(a, dtype=np.float32))
    shared = dict(
        w_ada=f(inp['w_ada']), b_ada=f(inp['b_ada']).reshape(DEPTH * 72, 128),
        norm_w=f(inp['norm_w']).reshape(DEPTH * 3 * 8, 128), final_norm_w=f(inp['final_norm_w']).reshape(8, 128),
        ffn_w_in=f(inp['ffn_w_in']), ffn_w_out=f(inp['ffn_w_out']), w_in=f(inp['w_in']), w_out=f(inp['w_out']),
        diff_lambda=f(inp['diff_lambda']).reshape(DEPTH, 128), diff_subln_w=f(inp['diff_subln_w']),
        mla_q_norm_w=f(inp['mla_q_norm_w']), mla_w_q_up=f(inp['mla_w_q_up']),
        mla_kv_norm_w=f(inp['mla_kv_norm_w']), mla_w_kv_up=f(inp['mla_w_kv_up']),
        s5_a_re=_gsplit(inp['s5_a_re'], 2), s5_a_im=_gsplit(inp['s5_a_im'], 2), s5_log_step=_gsplit(inp['s5_log_step'], 2),
        s5_b_re=_gsplit(inp['s5_b_re'], 2), s5_b_im=_gsplit(inp['s5_b_im'], 2),
        s5_c_re=_gsplit(inp['s5_c_re'], 2).reshape(DEPTH, 2, 2, 128, 64), s5_c_im=_gsplit(inp['s5_c_im'], 2).reshape(DEPTH, 2, 2, 128, 64),
        s5_d=f(inp['s5_d']), s5_w_glu=f(inp['s5_w_glu']), s5_b_glu=f(inp['s5_b_glu']).reshape(DEPTH, 2, 128),
    )
    tq = np.arange(128) // 16
    cm_f = (tq[:, None] <= tq[None, :]).astype(np.float32)
    cm_b = (tq[:, None] >= tq[None, :]).astype(np.float32)
    shared['cmask_f'] = cm_f
    shared['cmask_b'] = cm_b
    rc, rsn = rope_tables(NT)
    maps = []
    for core in range(8):
        m = dict(shared)
        if core < 4:
            b = core
            m['xin'] = f(inp['x_sample'][b])
            m['cvec'] = f(inp['c'][b]).reshape(8, 128)
            m['ctx_dk'] = f(inp['cache_diff_k'][b]).reshape(DEPTH, 256, 384)
            m['ctx_dv'] = f(inp['cache_diff_v'][b]).reshape(DEPTH, 256, 384)
            m['ctx_ckv'] = f(inp['cache_mla_ckv'][b])
            m['ctx_kpe'] = f(inp['cache_mla_kpe'][b])
            m['h0re'] = _gsplit(inp['state_s5_re'][b], 2)
            m['h0im'] = _gsplit(inp['state_s5_im'][b], 2)
            m['ropec'], m['ropes'] = rc, rsn
            augk = np.zeros((NT + 256, 9), np.float32)
            augk[:, 0] = 32.0
            augk[:, 8] = 1.0
            augq = np.zeros((NT, 9), np.float32)
            augq[:, 0] = 32.0
            augq[:, 8] = -1024.0
            m['flag'] = np.ones((128, 1), np.float32)
        else:
            i = core - 4
            m['xin'] = f(inp['x_prompt'][8 * i:8 * i + 8]).reshape(NT, D)
            m['cvec'] = f(inp['c_ctx']).reshape(8, 128)
            m['ctx_dk'] = np.zeros((DEPTH, 256, 384), np.float32)
            m['ctx_dv'] = np.zeros((DEPTH, 256, 384), np.float32)
            m['ctx_ckv'] = np.zeros((DEPTH, 256, 128), np.float32)
            m['ctx_kpe'] = np.zeros((DEPTH, 256, 32), np.float32)
            m['h0re'] = np.zeros((DEPTH, 2, 2, 8, 64), np.float32)
            m['h0im'] = np.zeros((DEPTH, 2, 2, 8, 64), np.float32)
            m['ropec'] = np.ones((NT, 16), np.float32)
            m['ropes'] = np.zeros((NT, 16), np.float32)
            seg = np.arange(NT) // 256
            augk = np.zeros((NT + 256, 9), np.float32)
            augk[np.arange(NT), seg] = 32.0
            augk[:, 8] = 1.0
            augq = np.zeros((NT, 9), np.float32)
            augq[np.arange(NT), seg] = 32.0
            augq[:, 8] = -1024.0
            m['flag'] = np.zeros((128, 1), np.float32)
        m['augk'] = augk
        m['augq'] = augq
        maps.append(m)
    return maps


def kernel(**inputs):
    nc = get_prog(STAGE)
    maps = make_in_maps(inputs)
    res = run_bass_kernel_spmd(nc, maps, core_ids=list(range(8)))
    r = res.results
    B, SEQ = 32, 256
    y_sample = np.stack([r[c]['y'] for c in range(4)], 0).astype(np.float32)
    y_prompt = np.concatenate([r[c]['y'].reshape(8, SEQ, D) for c in range(4, 8)], 0).astype(np.float32)

    def cat(name, tail):
        outs = []
        for c in range(4, 8):
            a = r[c][name].reshape(DEPTH, 8, SEQ, -1).transpose(1, 0, 2, 3)
            outs.append(a)
        a = np.concatenate(outs, 0)
        return np.ascontiguousarray(a.reshape((B, DEPTH, SEQ) + tail)).astype(np.float32)

    def cat5(name):
        outs = []
        for c in range(4, 8):
            a = r[c][name].reshape(DEPTH, 2, 8, 16, 64).transpose(2, 0, 1, 3, 4)
            outs.append(a)
        return np.ascontiguousarray(np.concatenate(outs, 0)).astype(np.float32)
    return (y_prompt, y_sample, cat('ndk', (6, 64)), cat('ndv', (6, 64)), cat('nckv', (128,)), cat('nkpe', (32,)),
            cat5('ns5re'), cat5('ns5im'))
```

```python
import contextlib
import math
import numpy as np
import concourse.bass as bass
import concourse.mybir as mybir
from concourse.bass_utils import run_bass_kernel_spmd

F32 = mybir.dt.float32
BF16 = mybir.dt.bfloat16
AF = mybir.ActivationFunctionType
ALU = mybir.AluOpType

D = 1024
NT = 2048
DEPTH = 2
DFF = 2816
NHT = 22
EPS = 1e-6
INC = 1824
STAGE = 99


class Sched:
    def __init__(self, nc, ndma=8):
        self.nc = nc
        self.engs = ['pe', 'act', 'dve', 'pool', 'sp']
        self.streams = {e: [] for e in self.engs}
        self.cnt = {e: 0 for e in self.engs}
        self.seen = {e: {} for e in self.engs}
        self.res = {}
        self.ndma = ndma
        self.dma_issued = {'sp': 0, 'pool': 0, 'act': 0}
        self.dma_last = {}
        self.final_tokens = []

    def _deps(self, eng, reads, writes):
        toks = {}

        def add(t):
            if t is None:
                return
            k, v = t
            if toks.get(k, 0) < v:
                toks[k] = v
        for r in reads:
            st = self.res.get(r)
            if st:
                add(st['w'])
        for w in writes:
            st = self.res.get(w)
            if st:
                add(st['w'])
                for t in st['r']:
                    add(t)
        out = []
        for k, v in toks.items():
            if eng == 'pe' and k == ('c', 'pe'):
                continue
            if self.seen[eng].get(k, 0) >= v:
                continue
            self.seen[eng][k] = v
            out.append((k, v))
        return out

    def _mark(self, tok, reads, writes):
        for r in reads:
            st = self.res.setdefault(r, {'w': None, 'r': []})
            st['r'].append(tok)
            if len(st['r']) > 64:
                mx = {}
                for k, v in st['r']:
                    if mx.get(k, 0) < v:
                        mx[k] = v
                st['r'] = list(mx.items())
        for w in writes:
            self.res[w] = {'w': tok, 'r': []}

    PSUM_NAMES = {'lxps', 'ad_pst', 'ad_psm', 'nm_pss', 'f_pag', 'f_pso', 'fin_ps', 'sa_ps', 'sb_psu', 'sb_pst', 'sc_ps',
                  'sc_psF', 'sd_ps', 'se_pst', 'se_psg', 'a_psp', 'a_pst', 'a_pss', 'a_pso'}

    def _excl(self, reads, writes):
        rd, wr = [], list(writes)
        for r in reads:
            nm = r if isinstance(r, str) else r[0]
            if nm in self.PSUM_NAMES:
                if r not in wr:
                    wr.append(r)
            else:
                rd.append(r)
        return rd, wr

    def op(self, eng, fn, reads=(), writes=()):
        reads, writes = self._excl(reads, writes)
        waits = self._deps(eng, reads, writes)
        self.cnt[eng] += 1
        tok = (('c', eng), self.cnt[eng])
        self.streams[eng].append((waits, fn, tok))
        self._mark(tok, reads, writes)
        return tok

    def dma(self, eng, fn, reads=(), writes=(), final=False):
        k = self.dma_issued[eng]
        self.dma_issued[eng] += 1
        slot = k % self.ndma
        val = 16 * (k // self.ndma + 1)
        key = ('d', eng, slot)
        waits = self._deps(eng, reads, writes)
        if val > 16 and self.seen[eng].get(key, 0) < val - 16:
            self.seen[eng][key] = val - 16
            waits.append((key, val - 16))
        tok = (key, val)
        self.dma_last[key] = val
        self.streams[eng].append((waits, fn, tok))
        self._mark(tok, reads, writes)
        if final:
            self.final_tokens.append(tok)
        return tok

    def barrier(self):
        allt = [(('c', e), self.cnt[e]) for e in ['pe', 'act', 'dve', 'pool'] if self.cnt[e]]
        allt += list(self.dma_last.items())
        for e in self.engs:
            waits = []
            for k, v in allt:
                if k == ('c', e):
                    continue
                if self.seen[e].get(k, 0) >= v:
                    continue
                self.seen[e][k] = v
                waits.append((k, v))
            if waits:
                self.streams[e].append((waits, None, None))

    def emit(self):
        nc = self.nc
        with contextlib.ExitStack() as es:
            sems = {}
            for e in ['pe', 'act', 'dve', 'pool']:
                sems[('c', e)] = es.enter_context(nc.semaphore('c_' + e))
            for e in ['sp', 'pool', 'act']:
                if self.dma_issued[e]:
                    for s in range(self.ndma):
                        sems[('d', e, s)] = es.enter_context(nc.semaphore('d_%s_%d' % (e, s)))
            block = es.enter_context(nc.Block())

            def run(engname, engobj):
                for waits, fn, tok in self.streams[engname]:
                    for k, v in waits:
                        engobj.wait_ge(sems[k], v)
                    if fn is None:
                        continue
                    ins = fn(engobj)
                    k, v = tok
                    ins.then_inc(sems[k], 16 if k[0] == 'd' else 1)
                if engname == 'sp':
                    for k, v in self.final_tokens:
                        engobj.wait_ge(sems[k], v)

            @block.sync
            def _(e):
                run('sp', e)

            @block.tensor
            def _(e):
                run('pe', e)

            @block.scalar
            def _(e):
                run('act', e)

            @block.vector
            def _(e):
                run('dve', e)

            @block.gpsimd
            def _(e):
                run('pool', e)


class Builder:
    def __init__(self, stage=99):
        self.stage = stage
        self.nc = bass.Bass("TRN2", target_bir_lowering=False)
        self.S = Sched(self.nc)
        self.es = contextlib.ExitStack()
        self.uid = 0
        self.rr = 0
        self._dr_cache = {}

    def din(self, name, shape):
        ap = self.nc.dram_tensor(name, list(shape), F32, kind="ExternalInput").ap()
        self._dr_cache[name] = ap
        return ap

    def _ap(self, name):
        return self._dr_cache[name]

    def dout(self, name, shape):
        return self.nc.dram_tensor(name, list(shape), F32, kind="ExternalOutput").ap()

    def sb(self, name, shape, dt, es=None):
        self.uid += 1
        return (es or self.es).enter_context(self.nc.sbuf_tensor("%s_u%d" % (name, self.uid), list(shape), dt))

    def psum(self, name, shape, dt, es):
        self.uid += 1
        return es.enter_context(self.nc.psum_tensor("%s_u%d" % (name, self.uid), list(shape), dt))

    def evac_eng(self):
        self.rr += 1
        return 'act' if self.rr % 2 else 'dve'

    def copy(self, eng, out, in_, reads, writes):
        if eng == 'act':
            self.S.op('act', lambda e: e.copy(out=out, in_=in_), reads, writes)
        else:
            self.S.op(eng, lambda e: e.tensor_copy(out=out, in_=in_), reads, writes)

    def tt(self, eng, out, a, b, op, reads, writes):
        self.S.op(eng, lambda e: e.tensor_tensor(out=out, in0=a, in1=b, op=op), reads, writes)

    def ts(self, eng, out, a, s1, s2, op0, op1, reads, writes):
        if op1 is None:
            self.S.op(eng, lambda e: e.tensor_scalar(out=out, in0=a, scalar1=s1, scalar2=None, op0=op0), reads, writes)
        else:
            self.S.op(eng, lambda e: e.tensor_scalar(out=out, in0=a, scalar1=s1, scalar2=s2, op0=op0, op1=op1), reads, writes)

    def stt(self, eng, out, a, s, b, op0, op1, reads, writes):
        self.S.op(eng, lambda e: e.scalar_tensor_tensor(out=out, in0=a, scalar=s, in1=b, op0=op0, op1=op1), reads, writes)

    def act(self, out, in_, func, reads, writes, bias=None, scale=None, accum=None):
        kw = {}
        if bias is not None:
            kw['bias'] = bias
        if scale is not None:
            kw['scale'] = scale
        if accum is not None:
            kw['accum_out'] = accum
        self.S.op('act', lambda e: e.activation(out=out, in_=in_, func=func, **kw), reads, writes)

    def mm(self, out, lhsT, rhs, start, stop, reads, writes):
        self.S.op('pe', lambda e: e.matmul(out, lhsT=lhsT, rhs=rhs, start=start, stop=stop), reads, writes)

    def tr(self, out, in_, ident, reads, writes):
        self.S.op('pe', lambda e: e.transpose(out=out, in_=in_, identity=ident), reads, writes)

    def ld(self, out, in_, writes, reads=(), eng='sp', **kw):
        self.S.dma(eng, lambda e: e.dma_start(out=out, in_=in_, **kw), reads, writes)

    def st(self, out, in_, reads, eng='sp', **kw):
        self.S.dma(eng, lambda e: e.dma_start(out=out, in_=in_, **kw), reads, (), final=True)

    def build(self):
        nc, S = self.nc, self.S
        din, dout, sb = self.din, self.dout, self.sb
        xin = din("xin", [NT, D])
        cvec = din("cvec", [8, 128])
        w_ada = din("w_ada", [DEPTH, D, 9 * D])
        b_ada = din("b_ada", [DEPTH * 72, 128])
        norm_w = din("norm_w", [DEPTH * 3 * 8, 128])
        fnw = din("final_norm_w", [8, 128])
        ffn_w_in = din("ffn_w_in", [DEPTH, 2, D, 2 * DFF])
        ffn_w_out = din("ffn_w_out", [DEPTH, 2, DFF, D])
        self.dr = dict(
            w_in=din("w_in", [DEPTH, D, INC]), w_out=din("w_out", [DEPTH, D, D]),
            ctx_dk=din("ctx_dk", [DEPTH, 256, 384]), ctx_dv=din("ctx_dv", [DEPTH, 256, 384]),
            ctx_ckv=din("ctx_ckv", [DEPTH, 256, 128]), ctx_kpe=din("ctx_kpe", [DEPTH, 256, 32]),
            h0re=din("h0re", [DEPTH, 2, 2, 8, 64]), h0im=din("h0im", [DEPTH, 2, 2, 8, 64]),
            ropec=din("ropec", [NT, 16]), ropes=din("ropes", [NT, 16]),
            augk=din("augk", [NT + 256, 9]), augq=din("augq", [NT, 9]),
            flag=din("flag", [128, 1]),
            diff_lambda=din("diff_lambda", [DEPTH, 128]), diff_subln_w=din("diff_subln_w", [DEPTH, 64]),
            mla_q_norm_w=din("mla_q_norm_w", [DEPTH, 256]), mla_w_q_up=din("mla_w_q_up", [DEPTH, 256, 576]),
            mla_kv_norm_w=din("mla_kv_norm_w", [DEPTH, 128]), mla_w_kv_up=din("mla_w_kv_up", [DEPTH, 128, 768]),
            s5_a_re=din("s5_a_re", [DEPTH, 2, 2, 8, 64]), s5_a_im=din("s5_a_im", [DEPTH, 2, 2, 8, 64]),
            s5_log_step=din("s5_log_step", [DEPTH, 2, 2, 8]),
            s5_b_re=din("s5_b_re", [DEPTH, 2, 2, 8, 64, 16]), s5_b_im=din("s5_b_im", [DEPTH, 2, 2, 8, 64, 16]),
            s5_c_re=din("s5_c_re", [DEPTH, 2, 2, 128, 64]), s5_c_im=din("s5_c_im", [DEPTH, 2, 2, 128, 64]),
            s5_d=din("s5_d", [DEPTH, 16, 16]), s5_w_glu=din("s5_w_glu", [DEPTH, 256, 256]),
            s5_b_glu=din("s5_b_glu", [DEPTH, 2, 128]),
            cmask_f=din("cmask_f", [128, 128]), cmask_b=din("cmask_b", [128, 128]),
        )
        self.y = dout("y", [NT, D])
        self.o = dict(
            ndk=dout("ndk", [DEPTH, NT, 384]), ndv=dout("ndv", [DEPTH, NT, 384]),
            nckv=dout("nckv", [DEPTH, NT, 128]), nkpe=dout("nkpe", [DEPTH, NT, 32]),
            ns5re=dout("ns5re", [DEPTH, 2, 8, 1024]), ns5im=dout("ns5im", [DEPTH, 2, 8, 1024]),
        )
        self.xT = sb("xT", [128, 8, NT], F32)
        self.identb = sb("identb", [128, 128], BF16)
        self.identf = sb("identf", [128, 128], F32)
        self.onesb = sb("onesb", [128, 128], BF16)
        self.epsc = sb("epsc", [128, 1], F32)
        self.mod = sb("mod", [128, DEPTH, 72], F32)
        self.cA = sb("cA", [128, DEPTH, 3, 8], F32)
        self.cB = sb("cB", [128, DEPTH, 3, 8], F32)
        self.cG = sb("cG", [128, DEPTH, 3, 8], F32)
        self.cF = sb("cF", [128, 8], F32)
        self.rs = sb("rs", [128, 2, 512], F32)
        self.wi_n = 0
        self.wo_n = 0

        self.setup_consts()
        self.load_x()
        self.adaln()
        S.barrier()
        for l in range(DEPTH):
            if self.stage >= 1:
                self.ffn(l, 0)
                S.barrier()
            if self.stage >= 3:
                self.mixer(l)
                S.barrier()
            if self.stage >= 2:
                self.ffn(l, 1)
                S.barrier()
            if self.stage < 4:
                break
        self.final()
        S.emit()
        return nc

    def setup_consts(self):
        S = self.S
        ib, iff, ob = self.identb, self.identf, self.onesb
        S.op('pool', lambda e: e.memset(ib[:], 0.0), (), ['identb'])
        S.op('pool', lambda e: e.affine_select(out=ib[:], in_=ib[:], compare_op=ALU.not_equal, fill=1.0, base=0,
                                               pattern=[[-1, 128]], channel_multiplier=1), ['identb'], ['identb'])
        S.op('pool', lambda e: e.memset(iff[:], 0.0), (), ['identf'])
        S.op('pool', lambda e: e.affine_select(out=iff[:], in_=iff[:], compare_op=ALU.not_equal, fill=1.0, base=0,
                                               pattern=[[-1, 128]], channel_multiplier=1), ['identf'], ['identf'])
        S.op('pool', lambda e: e.memset(ob[:], 1.0), (), ['onesb'])
        ep = self.epsc
        S.op('pool', lambda e: e.memset(ep[:], EPS), (), ['epsc'])

    def load_x(self):
        with contextlib.ExitStack() as es:
            xt = [self.sb("xtok%d" % i, [128, D], F32, es) for i in range(2)]
            ps = [self.psum("lxps%d" % i, [128, 512], F32, es) for i in range(4)]
            src = self._ap("xin").rearrange("(b p t) d -> b t p d", b=2, t=8)
            n = 0
            for b in range(2):
                for t in range(8):
                    buf = xt[n % 2]
                    bk = ('xtok', n % 2)
                    self.ld(buf[:], src[b, t], [bk])
                    pos = 1024 * b + 128 * t
                    for h in range(2):
                        pb = ps[(2 * n + h) % 4]
                        pk = ('lxps', (2 * n + h) % 4)
                        for jj in range(4):
                            j = 4 * h + jj
                            self.tr(pb[:, 128 * jj:128 * jj + 128], buf[:, 128 * j:128 * j + 128], self.identf[:],
                                    [bk, 'identf'], [pk])
                        dst = self.xT[:, 4 * h:4 * h + 4, pos:pos + 128]
                        srcp = pb[:].rearrange("p (j c) -> p j c", j=4)
                        self.copy(self.evac_eng(), dst, srcp, [pk],
                                  [('xT', j, pos // 512) for j in range(4 * h, 4 * h + 4)])
                    n += 1
        self.S.barrier()

    def adaln(self):
        S = self.S
        with contextlib.ExitStack() as es:
            rows = self.sb("ad_rows", [128, 3, 128], F32, es)
            rT = self.sb("ad_rT", [128, 3, 128], F32, es)
            scv = self.sb("ad_scv", [128, 8], F32, es)
            wa = [self.sb("ad_w%d" % i, [128, 8, 512], F32, es) for i in range(2)]
            pst = self.psum("ad_pst", [128, 512], F32, es)
            psm = self.psum("ad_psm", [128, 512], F32, es)
            S.op('dve', lambda e: e.memset(rows[:], 0.0), (), ['ad_rows'])
            self.ld(rows[0:8, 0, :], self._dr_cache['cvec'], ['ad_rows'], ['ad_rows'])
            self.ld(rows[8:16, 0, :], self._dr_cache['final_norm_w'], ['ad_rows'], ['ad_rows'])
            self.ld(rows[16:64, 0, :], self._dr_cache['norm_w'], ['ad_rows'], ['ad_rows'])
            self.ld(rows[:, 1, :], self._dr_cache['b_ada'][0:128, :], ['ad_rows'], ['ad_rows'])
            self.ld(rows[0:16, 2, :], self._dr_cache['b_ada'][128:144, :], ['ad_rows'], ['ad_rows'])
            for i in range(3):
                self.tr(pst[:, 128 * i:128 * i + 128], rows[:, i, :], self.identf[:], ['ad_rows', 'identf'], ['ad_pst'])
            self.copy('dve', rT[:].rearrange("p a b -> p (a b)"), pst[:, 0:384], ['ad_pst'], ['ad_rT'])
            self.act(scv[:], rT[:, 0, 0:8], AF.Silu, ['ad_rT'], ['ad_scv'])
            badaT = rT[:].rearrange("p a b -> p (a b)")[:, 128:128 + 144]
            n = 0
            for l in range(DEPTH):
                wv = self._dr_cache['w_ada'][l].rearrange("(kc p) f -> p kc f", p=128)
                for pc in range(18):
                    buf = wa[n % 2]
                    bk = ('ad_w', n % 2)
                    self.ld(buf[:], wv[:, :, 512 * pc:512 * pc + 512], [bk])
                    for ii in range(4):
                        i = 4 * pc + ii
                        for k in range(8):
                            self.mm(psm[:, l * 72 + i:l * 72 + i + 1], buf[:, k, 128 * ii:128 * ii + 128], scv[:, k:k + 1],
                                    k == 0, k == 7, [bk, 'ad_scv'], ['ad_psm'])
                    n += 1
            self.tt('dve', self.mod[:].rearrange("p l i -> p (l i)"), psm[:, 0:144], badaT, ALU.add,
                    ['ad_psm', 'ad_rT'], ['mod'])
            for l in range(DEPTH):
                for n3 in range(3):
                    nw = rT[:, 0, 16 + (l * 3 + n3) * 8:16 + (l * 3 + n3) * 8 + 8]
                    sh = self.mod[:, l, (3 * n3) * 8:(3 * n3) * 8 + 8]
                    sc = self.mod[:, l, (3 * n3 + 1) * 8:(3 * n3 + 1) * 8 + 8]
                    g = self.mod[:, l, (3 * n3 + 2) * 8:(3 * n3 + 2) * 8 + 8]
                    self.stt('dve', self.cA[:, l, n3, :], sc, 1.0, nw, ALU.add, ALU.mult, ['mod', 'ad_rT'], ['cA'])
                    self.copy('dve', self.cB[:, l, n3, :], sh, ['mod'], ['cB'])
                    self.ts('dve', self.cG[:, l, n3, :], g, (1.0 if n3 == 1 else 0.5), None, ALU.mult, None, ['mod'], ['cG'])
            self.copy('dve', self.cF[:], rT[:, 0, 8:16], ['ad_rT'], ['cF'])
            S.barrier()

    def norm_mod(self, blk, A, Bv, hm, hmcol, es_t, pss, hmkey):
        c0 = 512 * blk
        sq, tmp = es_t['sq'], es_t['tmp']
        xk = [('xT', j, blk) for j in range(8)]
        self.act(sq[:], self.xT[:, :, c0:c0 + 512], AF.Square, xk, ['nm_sq'])
        for j in range(8):
            self.mm(pss[:], self.onesb[:], sq[:, j, :], j == 0, j == 7, ['nm_sq', 'onesb'], ['nm_pss'])
        rb = blk % 2
        self.act(self.rs[:, rb, :], pss[:], AF.Sqrt, ['nm_pss'], [('rs', rb)], bias=self.epsc[:, 0:1], scale=1.0 / D)
        self.S.op('dve', lambda e: e.reciprocal(out=self.rs[:, rb, :], in_=self.rs[:, rb, :]), [('rs', rb)], [('rs', rb)])
        for j in range(8):
            tb = tmp[j % 2]
            self.tt('dve', tb[:], self.xT[:, j, c0:c0 + 512], self.rs[:, rb, :], ALU.mult,
                    [('xT', j, blk), ('rs', rb)], [('nm_tmp', j % 2)])
            if Bv is None:
                self.act(hm[:, j, hmcol:hmcol + 512], tb[:], AF.Identity, [('nm_tmp', j % 2)], [hmkey(j)], scale=A[:, j:j + 1])
            else:
                self.act(hm[:, j, hmcol:hmcol + 512], tb[:], AF.Identity, [('nm_tmp', j % 2)], [hmkey(j)],
                         scale=A[:, j:j + 1], bias=Bv[:, j:j + 1])

    def ffn(self, l, n):
        n3 = 0 if n == 0 else 2
        A, Bv, G = self.cA[:, l, n3, :], self.cB[:, l, n3, :], self.cG[:, l, n3, :]
        w_in = self._dr_cache['ffn_w_in'][l, n].rearrange("(kc p) f -> p kc f", p=128)
        w_out = self._dr_cache['ffn_w_out'][l, n].rearrange("(i p) d -> p i d", p=128)
        with contextlib.ExitStack() as es:
            hm = self.sb("f_hm", [128, 8, 1024], BF16, es)
            self.wi = [self.sb("wi%d" % i, [128, 8, 2, 256], BF16, es) for i in range(3)]
            self.wo = [self.sb("wo%d" % i, [128, NHT, 128], BF16, es) for i in range(3)]
            actb = self.sb("f_act", [128, NHT, 1024], BF16, es)
            sq = self.sb("f_sq", [128, 8, 512], BF16, es)
            tmp = [self.sb("f_tmp%d" % i, [128, 512], F32, es) for i in range(2)]
            sg = [self.sb("f_sg%d" % i, [128, 512], F32, es) for i in range(2)]
            pss = self.psum("f_pss", [128, 512], F32, es)
            pag = [self.psum("f_pag%d" % i, [128, 512], F32, es) for i in range(4)]
            pso = [self.psum("f_pso%d" % i, [128, 512], F32, es) for i in range(2)]
            est = {'sq': sq, 'tmp': tmp}
            npag = 0
            npso = 0
            for half in range(2):
                for bb in range(2):
                    blk = 2 * half + bb
                    self.norm_mod(blk, A, Bv, hm, 512 * bb, est, pss, lambda j, bb=bb: ('f_hm', j, bb))
                for pc in range(11):
                    slot = self.wi_n % 3
                    self.wi_n += 1
                    wb = self.wi[slot]
                    wk = ('wi', slot)
                    for ag in range(2):
                        cb = ag * DFF + 256 * pc
                        self.ld(wb[:, :, ag, :], w_in[:, :, cb:cb + 256], [wk], eng='pool')
                    for ii in range(2):
                        i = 2 * pc + ii
                        for bb in range(2):
                            pa = pag[npag % 4]
                            ka = ('f_pag', npag % 4)
                            pg = pag[(npag + 1) % 4]
                            kg = ('f_pag', (npag + 1) % 4)
                            npag += 2
                            for k in range(8):
                                self.mm(pa[:], wb[:, k, 0, 128 * ii:128 * ii + 128], hm[:, k, 512 * bb:512 * bb + 512],
                                        k == 0, k == 7, [wk, ('f_hm', k, bb)], [ka])
                            for k in range(8):
                                self.mm(pg[:], wb[:, k, 1, 128 * ii:128 * ii + 128], hm[:, k, 512 * bb:512 * bb + 512],
                                        k == 0, k == 7, [wk, ('f_hm', k, bb)], [kg])
                            sgi = (npag // 2) % 2
                            self.act(sg[sgi][:], pg[:], AF.Silu, [kg], [('f_sg', sgi)])
                            self.tt('dve', actb[:, i, 512 * bb:512 * bb + 512], sg[sgi][:], pa[:], ALU.mult,
                                    [('f_sg', sgi), ka], [('f_act', i, bb)])
                for jo in range(8):
                    slot = self.wo_n % 3
                    self.wo_n += 1
                    wb = self.wo[slot]
                    wk = ('wo', slot)
                    self.ld(wb[:], w_out[:, :, 128 * jo:128 * jo + 128], [wk], eng='pool')
                    for bb in range(2):
                        blk = 2 * half + bb
                        po = pso[npso % 2]
                        ko = ('f_pso', npso % 2)
                        npso += 1
                        for i in range(NHT):
                            self.mm(po[:], wb[:, i, :], actb[:, i, 512 * bb:512 * bb + 512], i == 0, i == NHT - 1,
                                    [wk, ('f_act', i, bb)], [ko])
                        xs = self.xT[:, jo, 512 * blk:512 * blk + 512]
                        self.stt('dve', xs, po[:], G[:, jo:jo + 1], xs, ALU.mult, ALU.add,
                                 [ko, ('xT', jo, blk)], [('xT', jo, blk)])

    def final(self):
        with contextlib.ExitStack() as es:
            yb = [self.sb("fin_y%d" % i, [128, 8, 512], F32, es) for i in range(1)]
            sq = self.sb("fin_sq", [128, 8, 512], BF16, es)
            tmp = [self.sb("fin_tmp%d" % i, [128, 512], F32, es) for i in range(2)]
            yt = [self.sb("fin_yt%d" % i, [128, D], F32, es) for i in range(2)]
            pss = self.psum("fin_pss", [128, 512], F32, es)
            ps = [self.psum("fin_ps%d" % i, [128, 512], F32, es) for i in range(4)]
            est = {'sq': sq, 'tmp': tmp}
            dst = self.y.rearrange("(b p t) d -> b t p d", b=2, t=8)
            n = 0
            for blk in range(4):
                self.norm_mod(blk, self.cF, None, yb[0], 0, est, pss, lambda j: ('fin_y', j))
                for tt4 in range(4):
                    tile = 4 * blk + tt4
                    b, t = tile // 8, tile % 8
                    ytb = yt[n % 2]
                    yk = ('fin_yt', n % 2)
                    for h in range(2):
                        pb = ps[(2 * n + h) % 4]
                        pk = ('fin_ps', (2 * n + h) % 4)
                        for jj in range(4):
                            j = 4 * h + jj
                            self.tr(pb[:, 128 * jj:128 * jj + 128], yb[0][:, j, 128 * tt4:128 * tt4 + 128], self.identf[:],
                                    [('fin_y', j), 'identf'], [pk])
                        self.copy(self.evac_eng(), ytb[:, 512 * h:512 * h + 512], pb[:], [pk], [yk])
                    self.st(dst[b, t], ytb[:], [yk])
                    n += 1

    def mixer(self, l):
        S = self.S
        A, Bv, G = self.cA[:, l, 1, :], self.cB[:, l, 1, :], self.cG[:, l, 1, :]
        with contextlib.ExitStack() as es:
            hm = self.sb("m_hm", [128, 8, NT], BF16, es)
            mo = None
            self.G2 = G
            with contextlib.ExitStack() as es2:
                sq = self.sb("m_sq", [128, 8, 512], BF16, es2)
                tmp = [self.sb("m_tmp%d" % i, [128, 512], F32, es2) for i in range(2)]
                pss = self.psum("m_pss", [128, 512], F32, es2)
                for blk in range(4):
                    self.norm_mod(blk, A, Bv, hm, 512 * blk, {'sq': sq, 'tmp': tmp}, pss,
                                  lambda j, blk=blk: ('m_hm', j, blk))
            import os
            mix = int(os.environ.get('MIX', '3'))
            S.barrier()
            if mix & 1:
                self.s5(l, hm, mo)
            S.barrier()
            if mix & 2:
                self.attn(l, hm, mo)

    def wout_part(self, l, ktiles, srcs, wbufs, psl, pskeys):
        wv = self._ap('w_out')[l].rearrange("(k p) d -> p k d", p=128)
        G = self.G2
        for i, kt in enumerate(ktiles):
            wb, wk = wbufs[i]
            self.ld(wb, wv[:, kt, :], [wk], eng='pool')
        n = 0
        for jo in range(8):
            for blk in range(4):
                po, pk = psl[n % len(psl)], pskeys[n % len(psl)]
                n += 1
                for i, kt in enumerate(ktiles):
                    wb, wk = wbufs[i]
                    src, kf = srcs[i]
                    self.mm(po[:], wb[:, 128 * jo:128 * jo + 128], src[:, 512 * blk:512 * blk + 512], i == 0, i == len(ktiles) - 1,
                            [wk, kf(blk)], [pk])
                xs = self.xT[:, jo, 512 * blk:512 * blk + 512]
                self.stt('dve', xs, po[:], G[:, jo:jo + 1], xs, ALU.mult, ALU.add, [pk, ('xT', jo, blk)], [('xT', jo, blk)])

    def cmul(self, outr, outi, ar, ai, br, bi, t1, t2, key_r, key_w, neg_im=False, eng='dve'):
        rd = list(key_r)
        self.tt(eng, t1, ar, br, ALU.mult, rd, ['cm_t1'])
        self.tt(eng, t2, ai, bi, ALU.mult, rd, ['cm_t2'])
        self.tt(eng, outr, t1, t2, ALU.subtract, ['cm_t1', 'cm_t2'] + rd, list(key_w))
        self.tt(eng, t1, ar, bi, ALU.mult, rd + list(key_w), ['cm_t1'])
        self.tt(eng, t2, ai, br, ALU.mult, rd + list(key_w), ['cm_t2'])
        if neg_im:
            self.stt(eng, outi, t1, -1.0, t2, ALU.mult, ALU.subtract, ['cm_t1', 'cm_t2'], list(key_w))
        else:
            self.tt(eng, outi, t1, t2, ALU.add, ['cm_t1', 'cm_t2'], list(key_w))

    def s5(self, l, hm, mo):
        S = self.S
        dr = self._dr_cache
        PI = math.pi
        with contextlib.ExitStack() as es:
            ToepT = [self.sb("s_toep%d" % d, [128, 16, 128], BF16, es) for d in range(2)]
            BSm = [self.sb("s_bsm%d" % d, [128, 16, 2, 64], BF16, es) for d in range(2)]
            CCm = [self.sb("s_ccm%d" % d, [128, 8, 2, 128], BF16, es) for d in range(2)]
            PWr = [self.sb("s_pwr%d" % d, [128, 8, 33], F32, es) for d in range(2)]
            PWi = [self.sb("s_pwi%d" % d, [128, 8, 33], F32, es) for d in range(2)]
            PWin = [self.sb("s_pwin%d" % d, [128, 8, 33], F32, es) for d in range(2)]
            h0r = [self.sb("s_h0r%d" % d, [128, 8], F32, es) for d in range(2)]
            h0i = [self.sb("s_h0i%d" % d, [128, 8], F32, es) for d in range(2)]
            U = self.sb("s_U", [128, 16, 256], BF16, es)
            flag = self.sb("s_flag", [128, 1], F32, es)
            self.ld(flag[:], dr['flag'], ['s_flag'])
            with contextlib.ExitStack() as ea:
                def t4(name):
                    return self.sb(name, [128, 8, 8, 16], F32, ea)
                cm1, cm2 = t4("sa_cm1"), t4("sa_cm2")
                BLr, BLi, CLr, CLi = t4("sa_blr"), t4("sa_bli"), t4("sa_clr"), t4("sa_cli")
                BSr, BSi, CCr, CCi = t4("sa_bsr"), t4("sa_bsi"), t4("sa_ccr"), t4("sa_cci")
                sm = self.sb("sa_sm", [128, 40, 8], F32, ea)
                bre = self.sb("sa_bre", [128, 8, 16], F32, ea)
                bim = self.sb("sa_bim", [128, 8, 16], F32, ea)
                bbr = self.sb("sa_bbr", [128, 8, 16], F32, ea)
                bbi = self.sb("sa_bbi", [128, 8, 16], F32, ea)
                cre = self.sb("sa_cre", [128, 8, 16], F32, ea)
                cim = self.sb("sa_cim", [128, 8, 16], F32, ea)
                crow = self.sb("sa_crow", [128, 2, 2, 64], F32, ea)
                Pr = self.sb("sa_Pr", [128, 8, 9], F32, ea)
                Pi_ = self.sb("sa_Pi", [128, 8, 9], F32, ea)
                Nr = self.sb("sa_Nr", [128, 8, 8], F32, ea)
                Ni = self.sb("sa_Ni", [128, 8, 8], F32, ea)
                Dcol = self.sb("sa_Dcol", [128, 16], F32, ea)
                cmk = [self.sb("sa_cmk%d" % d, [128, 128], F32, ea) for d in range(2)]
                tT = self.sb("sa_tT", [128, 128], F32, ea)
                cst = self.sb("sa_cst", [128, 2], F32, ea)
                psA = [self.psum("sa_ps%d" % i, [128, 512], F32, ea) for i in range(4)]
                S.op('dve', lambda e: e.memset(cst[:, 0:1], -PI), (), ['sa_cst'])
                self.ld(cmk[0][:], dr['cmask_f'], ['sa_cmk'])
                self.ld(cmk[1][:], dr['cmask_b'], ['sa_cmk'])
                for t in range(8):
                    self.ld(Dcol[16 * t:16 * t + 16, :], dr['s5_d'][l].rearrange("g c -> c g"), ['sa_Dcol'],
                            allow_slow_non_contiguous=True)
                npsA = 0
                for d in range(2):
                    K = ['sa']
                    are, aim, lst = sm[:, 0, :], sm[:, 1, :], sm[:, 2, :]
                    for gl in range(2):
                        ps_ = slice(64 * gl, 64 * gl + 64)
                        self.ld(sm[ps_, 0, :], dr['s5_a_re'][l, d, gl].rearrange("g p -> p g"), K, K, allow_slow_non_contiguous=True)
                        self.ld(sm[ps_, 1, :], dr['s5_a_im'][l, d, gl].rearrange("g p -> p g"), K, K, allow_slow_non_contiguous=True)
                        self.ld(sm[ps_, 2, :], dr['s5_log_step'][l, d, gl].partition_broadcast(64), K, K)
                        self.ld(h0r[d][ps_, :], dr['h0re'][l, d, gl].rearrange("g p -> p g"), K, K, allow_slow_non_contiguous=True)
                        self.ld(h0i[d][ps_, :], dr['h0im'][l, d, gl].rearrange("g p -> p g"), K, K, allow_slow_non_contiguous=True)
                        self.ld(bre[ps_, :, :], dr['s5_b_re'][l, d, gl].rearrange("g p c -> p g c"), K, K)
                        self.ld(bim[ps_, :, :], dr['s5_b_im'][l, d, gl].rearrange("g p c -> p g c"), K, K)
                        self.ld(crow[:, gl, 0, :], dr['s5_c_re'][l, d, gl], K, K)
                        self.ld(crow[:, gl, 1, :], dr['s5_c_im'][l, d, gl], K, K)
                    pc, pck = psA[npsA % 4], ('sa_ps', npsA % 4)
                    npsA += 1
                    for gl in range(2):
                        for ri in range(2):
                            self.mm(pc[64 * gl:64 * gl + 64, 128 * ri:128 * ri + 128], crow[:, gl, ri, :], self.identf[:], True, True, K + ['identf'], [pck])
                    self.copy('dve', cre[:].rearrange("p g c -> p (g c)"), pc[:, 0:128], [pck], K)
                    self.copy('dve', cim[:].rearrange("p g c -> p (g c)"), pc[:, 128:256], [pck], K)

                    import os
                    s5a = int(os.environ.get('S5A', '9'))
                    if s5a <= 1:
                        S.barrier()
                        return

                    def sop(fn):
                        S.op('dve', fn, K, K)
                    sl = lambda i: sm[:, i, :]
                    self.act(sl(3), lst, AF.Exp, K, K)
                    self.tt('dve', sl(4), are, sl(3), ALU.mult, K, K)
                    self.tt('dve', sl(5), aim, sl(3), ALU.mult, K, K)
                    self.act(sl(6), sl(4), AF.Exp, K, K)
                    self.act(sl(7), sl(4), AF.Exp, K, K, scale=-2.0)
                    MAGIC = 12582912.0
                    for (dst_i, off) in ((9, 0.0), (10, 0.25)):
                        self.ts('dve', sl(8), sl(5), 1.0 / (2 * PI), None, ALU.mult, None, K, K)
                        if off:
                            self.ts('dve', sl(8), sl(8), off, None, ALU.add, None, K, K)
                        self.ts('dve', sl(22), sl(8), MAGIC, None, ALU.add, None, K, K)
                        self.ts('dve', sl(22), sl(22), -MAGIC, None, ALU.add, None, K, K)
                        self.tt('dve', sl(8), sl(8), sl(22), ALU.subtract, K, K)
                        self.act(sl(dst_i), sl(8), AF.Sin, K, K, scale=2 * PI)
                    lr, li = sl(11), sl(12)
                    self.tt('dve', lr, sl(6), sl(10), ALU.mult, K, K)
                    self.tt('dve', li, sl(6), sl(9), ALU.mult, K, K)
                    ilr, ili = sl(13), sl(14)
                    self.tt('dve', ilr, lr, sl(7), ALU.mult, K, K)
                    self.stt('dve', ili, li, -1.0, sl(7), ALU.mult, ALU.mult, K, K)
                    self.tt('dve', sl(15), are, are, ALU.mult, K, K)
                    self.tt('dve', sl(16), aim, aim, ALU.mult, K, K)
                    self.tt('dve', sl(15), sl(15), sl(16), ALU.add, K, K)
                    sop(lambda e: e.reciprocal(out=sl(15), in_=sl(15)))
                    self.ts('dve', sl(16), lr, -1.0, None, ALU.add, None, K, K)
                    self.tt('dve', sl(17), sl(16), are, ALU.mult, K, K)
                    self.tt('dve', sl(18), li, aim, ALU.mult, K, K)
                    self.tt('dve', sl(17), sl(17), sl(18), ALU.add, K, K)
                    self.tt('dve', sl(17), sl(17), sl(15), ALU.mult, K, K)
                    self.tt('dve', sl(18), li, are, ALU.mult, K, K)
                    self.tt('dve', sl(19), sl(16), aim, ALU.mult, K, K)
                    self.tt('dve', sl(18), sl(18), sl(19), ALU.subtract, K, K)
                    self.tt('dve', sl(18), sl(18), sl(15), ALU.mult, K, K)
                    kb = lambda i: sm[:, i, :].unsqueeze(2).to_broadcast([128, 8, 16])
                    self.cmul(bbr[:], bbi[:], kb(17), kb(18), bre[:], bim[:], cm1[:, :, 0, :], cm2[:, :, 0, :], K, K)
                    sop(lambda e: e.memset(Pr[:, :, 0:1], 1.0))
                    sop(lambda e: e.memset(Pi_[:, :, 0:1], 0.0))
                    sop(lambda e: e.memset(Nr[:, :, 0:1], 1.0))
                    sop(lambda e: e.memset(Ni[:, :, 0:1], 0.0))
                    for k in range(1, 9):
                        self.cmul(Pr[:, :, k], Pi_[:, :, k], Pr[:, :, k - 1], Pi_[:, :, k - 1], lr, li, sl(20), sl(21), K, K)
                    for k in range(1, 8):
                        self.cmul(Nr[:, :, k], Ni[:, :, k], Nr[:, :, k - 1], Ni[:, :, k - 1], ilr, ili, sl(20), sl(21), K, K)
                    pr, pi = PWr[d], PWi[d]
                    self.copy('dve', pr[:, :, 1], Pr[:, :, 8], K, K)
                    self.copy('dve', pi[:, :, 1], Pi_[:, :, 8], K, K)
                    n = 1
                    while n < 32:
                        bshape = [128, 8, n]
                        self.cmul(pr[:, :, n + 1:2 * n + 1], pi[:, :, n + 1:2 * n + 1], pr[:, :, 1:n + 1], pi[:, :, 1:n + 1],
                                  pr[:, :, n:n + 1].to_broadcast(bshape), pi[:, :, n:n + 1].to_broadcast(bshape),
                                  cm1[:].rearrange("p a b c -> p a (b c)")[:, :, 0:n], cm2[:].rearrange("p a b c -> p a (b c)")[:, :, 0:n], K, K)
                        n *= 2
                    self.ts('dve', PWin[d][:, :, 1:33], pi[:, :, 1:33], -1.0, None, ALU.mult, None, K, K)
                    bsh = [128, 8, 8, 16]
                    bbR = bbr[:].unsqueeze(2).to_broadcast(bsh)
                    bbI = bbi[:].unsqueeze(2).to_broadcast(bsh)
                    cR = cre[:].unsqueeze(2).to_broadcast(bsh)
                    cI = cim[:].unsqueeze(2).to_broadcast(bsh)
                    pw = lambda T, sl_: T[:, :, sl_].unsqueeze(3).to_broadcast(bsh)
                    if d == 0:
                        self.cmul(BLr[:], BLi[:], bbR, bbI, pw(Nr, slice(0, 8)), pw(Ni, slice(0, 8)), cm1[:], cm2[:], K, K)
                        self.cmul(CLr[:], CLi[:], cR, cI, pw(Pr, slice(0, 8)), pw(Pi_, slice(0, 8)), cm1[:], cm2[:], K, K, neg_im=True)
                        self.cmul(BSr[:], BSi[:], bbR, bbI, pw(Pr, slice(7, None, -1)), pw(Pi_, slice(7, None, -1)), cm1[:], cm2[:], K, K)
                        self.cmul(CCr[:], CCi[:], cR, cI, pw(Pr, slice(1, 9)), pw(Pi_, slice(1, 9)), cm1[:], cm2[:], K, K, neg_im=True)
                    else:
                        self.cmul(BLr[:], BLi[:], bbR, bbI, pw(Pr, slice(0, 8)), pw(Pi_, slice(0, 8)), cm1[:], cm2[:], K, K)
                        self.cmul(CLr[:], CLi[:], cR, cI, pw(Nr, slice(0, 8)), pw(Ni, slice(0, 8)), cm1[:], cm2[:], K, K, neg_im=True)
                        self.copy('dve', BSr[:], BLr[:], K, K)
                        self.copy('dve', BSi[:], BLi[:], K, K)
                        self.cmul(CCr[:], CCi[:], cR, cI, pw(Pr, slice(8, 0, -1)), pw(Pi_, slice(8, 0, -1)), cm1[:], cm2[:], K, K, neg_im=True)
                    f2 = lambda T: T[:].rearrange("p a b c -> p a (b c)")
                    if s5a <= 2:
                        S.barrier()
                        return
                    for g in range(16):
                        gl, gh = g % 2, g // 2
                        ps_ = slice(64 * gl, 64 * gl + 64)
                        pt, ptk = psA[npsA % 4], ('sa_ps', npsA % 4)
                        npsA += 1
                        self.mm(pt[:, 0:128], f2(BLr)[ps_, gh, :], f2(CLr)[ps_, gh, :], True, False, K, [ptk])
                        self.mm(pt[:, 0:128], f2(BLi)[ps_, gh, :], f2(CLi)[ps_, gh, :], False, True, K, [ptk])
                        if d == 0:
                            self.tt('dve', tT[:], pt[:, 0:128], cmk[0][:], ALU.mult, [ptk, 'sa_cmk'], ['sa_tT'])
                            self.stt('dve', ToepT[0][:, g, :], self.identf[:], Dcol[:, g:g + 1], tT[:], ALU.mult, ALU.add,
                                     ['sa_tT', 'sa_Dcol', 'identf'], [('s_toep', 0)])
                        else:
                            self.tt('dve', ToepT[1][:, g, :], pt[:, 0:128], cmk[1][:], ALU.mult, [ptk, 'sa_cmk'], [('s_toep', 1)])
                    if s5a <= 3:
                        S.barrier()
                        return
                    for ri, T in enumerate((BSr, BSi)):
                        for g8 in range(2):
                            pt, ptk = psA[npsA % 4], ('sa_ps', npsA % 4)
                            npsA += 1
                            for hh in range(4):
                                gh = 4 * g8 + hh
                                self.tr(pt[:, 128 * hh:128 * hh + 128], f2(T)[:, gh, :], self.identf[:], K + ['identf'], [ptk])
                            self.copy('act', BSm[d][:, 8 * g8:8 * g8 + 8, ri, :], pt[:].rearrange("p (g q) -> p g q", g=8), [ptk], [('s_bsm', d)])
                    if s5a <= 4:
                        S.barrier()
                        return
                    self.copy('dve', CCm[d][:, :, 0, :], f2(CCr), K, [('s_ccm', d)])
                    self.copy('dve', CCm[d][:, :, 1, :], f2(CCi), K, [('s_ccm', d)])
                    if s5a <= 5:
                        S.barrier()
                        return
            S.barrier()
            import os
            s5stop = os.environ.get('S5STOP', 'Z')
            if s5stop == 'A':
                return
            with contextlib.ExitStack() as eb:
                wu = self.sb("sb_wu", [128, 8, 256], BF16, eb)
                ub = self.sb("sb_ub", [128, 2, 16, 8, 16], BF16, eb)
                psu = [self.psum("sb_psu%d" % i, [128, 512], F32, eb) for i in range(2)]
                pst = [self.psum("sb_pst%d" % i, [128, 1024], BF16, eb) for i in range(2)]
                self.ld(wu[:], self._ap('w_in')[l].rearrange("(k p) f -> p k f", p=128)[:, :, 1568:1824], ['sb_wu'], eng='pool')
                n = 0
                for b in range(2):
                    for t in range(8):
                        pos = 1024 * b + 128 * t
                        pu, puk = psu[n % 2], ('sb_psu', n % 2)
                        n += 1
                        for k in range(8):
                            self.mm(pu[:, 0:256], hm[:, k, pos:pos + 128], wu[:, k, :], k == 0, k == 7,
                                    ['sb_wu', ('m_hm', k, pos // 512)], [puk])
                        self.copy(self.evac_eng(), ub[:, b, :, t, :], pu[:, 0:256].rearrange("p (g c) -> p g c", g=16), [puk], [('sb_ub', b)])
                n = 0
                for b in range(2):
                    for q in range(4):
                        pt, ptk = pst[n % 2], ('sb_pst', n % 2)
                        n += 1
                        for gg in range(4):
                            g = 4 * q + gg
                            self.tr(pt[:, 128 * gg:128 * gg + 128], ub[:, b, g, :, :].rearrange("p t c -> p (t c)"), self.identb[:],
                                    [('sb_ub', b), 'identb'], [ptk])
                        self.copy(self.evac_eng(), U[:, 4 * q:4 * q + 4, 128 * b:128 * b + 128],
                                  pt[:, 0:512].rearrange("p (g c) -> p g c", g=4), [ptk], ['s_U'])
            S.barrier()
            if s5stop == 'B':
                return
            with contextlib.ExitStack() as ec:
                Hp = [[self.sb("sc_hp%d%d" % (d, ri), [128, 8, 256], BF16, ec) for ri in range(2)] for d in range(2)]
                with contextlib.ExitStack() as ec2:
                    La = [self.sb("sc_la%d" % ri, [128, 8, 256], F32, ec2) for ri in range(2)]
                    Lb = [self.sb("sc_lb%d" % ri, [128, 8, 256], F32, ec2) for ri in range(2)]
                    Cy = [self.sb("sc_cy%d" % ri, [128, 8, 8], F32, ec2) for ri in range(2)]
                    sm2 = self.sb("sc_sm", [128, 4, 8], F32, ec2)
                    Fsb = self.sb("sc_F", [128, 2, 64], F32, ec2)
                    FT = self.sb("sc_FT", [64, 2, 128], F32, ec2)
                    psS = [[self.psum("sc_ps%d%d" % (ri, q), [128, 512], F32, ec2) for q in range(3)] for ri in range(2)]
                    psF = self.psum("sc_psF", [128, 512], F32, ec2)
                    for d in range(2):
                        KL = ['sc_L']
                        for ri in range(2):
                            for q4 in range(4):
                                pb, pbk = psS[ri][q4 % 3], ('sc_ps', ri, q4 % 3)
                                for hh in range(2):
                                    gh = 2 * q4 + hh
                                    for gl in range(2):
                                        g = 2 * gh + gl
                                        self.mm(pb[64 * gl:64 * gl + 64, 256 * hh:256 * hh + 256], BSm[d][:, g, ri, :], U[:, g, :], True, True,
                                                [('s_bsm', d), 's_U'], [pbk])
                                self.copy(self.evac_eng(), La[ri][:, 2 * q4:2 * q4 + 2, :], pb[:].rearrange("p (a c) -> p a c", a=2), [pbk], KL)
                        cur, nxt = La, Lb
                        v5 = lambda T: T[:].rearrange("p a (s k) -> p a s k", k=32)
                        for dd in (1, 2, 4, 8, 16):
                            for ri in range(2):
                                if d == 0:
                                    self.copy('act', v5(nxt[ri])[:, :, :, 0:dd], v5(cur[ri])[:, :, :, 0:dd], KL, KL)
                                else:
                                    self.copy('act', v5(nxt[ri])[:, :, :, 32 - dd:32], v5(cur[ri])[:, :, :, 32 - dd:32], KL, KL)
                            for gh in range(8):
                                vv = lambda T: T[:, gh, :].rearrange("p (s k) -> p s k", k=32)
                                if d == 0:
                                    dst = slice(dd, 32)
                                    src = slice(0, 32 - dd)
                                else:
                                    dst = slice(0, 32 - dd)
                                    src = slice(dd, 32)
                                lr_ = PWr[d][:, gh, dd:dd + 1]
                                li_ = PWi[d][:, gh, dd:dd + 1]
                                lin_ = PWin[d][:, gh, dd:dd + 1]
                                self.stt('dve', vv(nxt[0])[:, :, dst], vv(cur[0])[:, :, src], lr_, vv(cur[0])[:, :, dst], ALU.mult, ALU.add, KL, KL)
                                self.stt('dve', vv(nxt[0])[:, :, dst], vv(cur[1])[:, :, src], lin_, vv(nxt[0])[:, :, dst], ALU.mult, ALU.add, KL, KL)
                                self.stt('dve', vv(nxt[1])[:, :, dst], vv(cur[0])[:, :, src], li_, vv(cur[1])[:, :, dst], ALU.mult, ALU.add, KL, KL)
                                self.stt('dve', vv(nxt[1])[:, :, dst], vv(cur[1])[:, :, src], lr_, vv(nxt[1])[:, :, dst], ALU.mult, ALU.add, KL, KL)
                            cur, nxt = nxt, cur
                        L = cur
                        E = [v5(L[ri])[:, :, :, 31 if d == 0 else 0] for ri in range(2)]
                        l32r, l32i = PWr[d][:, :, 32], PWi[d][:, :, 32]
                        order = list(range(8)) if d == 0 else list(range(7, -1, -1))
                        s0 = order[0]
                        self.copy('dve', Cy[0][:, :, s0], h0r[d][:], KL, KL)
                        self.copy('dve', Cy[1][:, :, s0], h0i[d][:], KL, KL)
                        for idx in range(1, 8):
                            s, sp_ = order[idx], order[idx - 1]
                            self.cmul(sm2[:, 0, :], sm2[:, 1, :], l32r, l32i, Cy[0][:, :, sp_], Cy[1][:, :, sp_], sm2[:, 2, :], sm2[:, 3, :], KL, KL)
                            for ri in range(2):
                                self.tt('dve', sm2[:, ri, :], sm2[:, ri, :], E[ri][:, :, sp_], ALU.add, KL, KL)
                                self.ts('dve', Cy[ri][:, :, s], sm2[:, ri, :], flag[:, 0:1], None, ALU.mult, None, KL + ['s_flag'], KL)
                        sh4 = [128, 8, 8, 32]
                        if d == 0:
                            pwv = lambda T: T[:, :, 1:33].unsqueeze(2).to_broadcast(sh4)
                        else:
                            pwv = lambda T: T[:, :, 32:0:-1].unsqueeze(2).to_broadcast(sh4)
                        cyv = lambda ri: Cy[ri][:].unsqueeze(3).to_broadcast(sh4)
                        t1v = v5(nxt[0])
                        for (ri, a, b_, op) in ((0, PWr[d], 0, ALU.add), (0, PWi[d], 1, ALU.subtract), (1, PWr[d], 1, ALU.add), (1, PWi[d], 0, ALU.add)):
                            self.tt('dve', t1v, pwv(a), cyv(b_), ALU.mult, KL, ['sc_t1'])
                            self.tt('dve', v5(L[ri]), v5(L[ri]), t1v, op, KL + ['sc_t1'], KL)
                        for ri in range(2):
                            hv = v5(Hp[d][ri])
                            if d == 0:
                                self.copy('act', hv[:, :, :, 1:32], v5(L[ri])[:, :, :, 0:31], KL, [('sc_hp', d)])
                                self.copy('dve', hv[:, :, :, 0], Cy[ri][:], KL, [('sc_hp', d)])
                            else:
                                self.copy('act', hv[:, :, :, 0:31], v5(L[ri])[:, :, :, 1:32], KL, [('sc_hp', d)])
                                self.copy('dve', hv[:, :, :, 31], Cy[ri][:], KL, [('sc_hp', d)])
                        for ri in range(2):
                            self.copy('dve', Fsb[:, ri, :].rearrange("p (a s) -> p a s", a=8), E[ri], KL, ['sc_F'])
                            self.tr(psF[0:64, 128 * ri:128 * ri + 128], Fsb[:, ri, :], self.identf[:], ['sc_F', 'identf'], ['sc_psF'])
                        self.copy('dve', FT[:].rearrange("p a b -> p (a b)"), psF[0:64, 0:256], ['sc_psF'], ['sc_FT'])
                        for ri, nm in enumerate(('ns5re', 'ns5im')):
                            for gh in range(8):
                                self.st(self.o[nm][l, d][:, 128 * gh:128 * gh + 128], FT[8 * gh:8 * gh + 8, ri, :], ['sc_FT'])
                S.barrier()
                if s5stop == 'C':
                    return
                with contextlib.ExitStack() as ed:
                    Ysb = self.sb("sd_Y", [128, 16, 256], BF16, ed)
                    g1 = [self.sb("sd_g%d" % i, [128, 512], F32, ed) for i in range(2)]
                    psY = [self.psum("sd_ps%d" % i, [128, 512], F32, ed) for i in range(3)]
                    for q in range(8):
                        py, pyk = psY[q % 3], ('sd_ps', q % 3)
                        for hh in range(2):
                            g = 2 * q + hh
                            gl, gh = g % 2, g // 2
                            ps_ = slice(64 * gl, 64 * gl + 64)
                            o = py[:, 256 * hh:256 * hh + 256]
                            for d in range(2):
                                self.mm(o, ToepT[d][:, g, :], U[:, g, :], d == 0, False, [('s_toep', d), 's_U'], [pyk])
                                self.mm(o, CCm[d][ps_, gh, 0, :], Hp[d][0][ps_, gh, :], False, False, [('s_ccm', d), ('sc_hp', d)], [pyk])
                                self.mm(o, CCm[d][ps_, gh, 1, :], Hp[d][1][ps_, gh, :], False, d == 1, [('s_ccm', d), ('sc_hp', d)], [pyk])
                        gb, gk = g1[q % 2], ('sd_g', q % 2)
                        self.act(gb[:], py[:], AF.Square, [pyk], [gk])
                        self.ts('dve', gb[:], gb[:], 0.044715, None, ALU.mult, None, [gk], [gk])
                        self.ts('dve', gb[:], gb[:], 1.0, None, ALU.add, None, [gk], [gk])
                        self.tt('dve', gb[:], gb[:], py[:], ALU.mult, [gk, pyk], [gk])
                        self.act(gb[:], gb[:], AF.Sigmoid, [gk], [gk], scale=1.5957691216)
                        self.tt('dve', Ysb[:, 2 * q:2 * q + 2, :], gb[:].rearrange("p (a c) -> p a c", a=2),
                                py[:].rearrange("p (a c) -> p a c", a=2), ALU.mult, [gk, pyk], ['sd_Y'])
                    S.barrier()
                    ytok = self.sb("se_ytok", [128, 2, 8, 256], BF16, ed)
                    ygT = self.sb("se_ygT", [128, 2, NT], BF16, ed)
                    mos = self.sb("se_mo", [128, 2, NT], BF16, ed)
                    wob = [self.sb("se_wo%d" % i, [128, D], BF16, ed) for i in range(2)]
                    wg = self.sb("se_wg", [128, 2, 256], BF16, ed)
                    bg = self.sb("se_bg", [128, 2], F32, ed)
                    sg = [self.sb("se_sg%d" % i, [128, 512], F32, ed) for i in range(2)]
                    pst = [self.psum("se_pst%d" % i, [128, 1024], BF16, ed) for i in range(2)]
                    psg = [self.psum("se_psg%d" % i, [128, 512], F32, ed) for i in range(2)]
                    self.ld(wg[:], dr['s5_w_glu'][l].rearrange("(k p) f -> p k f", p=128), ['se_wg'], eng='pool')
                    self.ld(bg[:], dr['s5_b_glu'][l].rearrange("a p -> p a"), ['se_bg'], allow_slow_non_contiguous=True)
                    n = 0
                    for b in range(2):
                        for q in range(4):
                            pt, ptk = pst[n % 2], ('se_pst', n % 2)
                            n += 1
                            for gg in range(4):
                                g = 4 * q + gg
                                self.tr(pt[:, 128 * gg:128 * gg + 128], Ysb[:, g, 128 * b:128 * b + 128], self.identb[:], ['sd_Y', 'identb'], [ptk])
                            dst = ytok[:, b].rearrange("p t (g c) -> p g t c", c=16)[:, 4 * q:4 * q + 4]
                            self.copy(self.evac_eng(), dst, pt[:, 0:512].rearrange("p (g t c) -> p g t c", g=4, t=8), [ptk], [('se_ytok', b)])
                    for b in range(2):
                        for f in range(2):
                            for h4 in range(2):
                                pt, ptk = pst[n % 2], ('se_pst', n % 2)
                                n += 1
                                for tt_ in range(4):
                                    t = 4 * h4 + tt_
                                    self.tr(pt[:, 128 * tt_:128 * tt_ + 128], ytok[:, b, t, 128 * f:128 * f + 128], self.identb[:],
                                            [('se_ytok', b), 'identb'], [ptk])
                                blk = 2 * b + h4
                                self.copy(self.evac_eng(), ygT[:, f, 512 * blk:512 * blk + 512], pt[:, 0:512], [ptk], [('se_ygT', f, blk)])
                    n = 0
                    for fo in range(2):
                        for blk in range(4):
                            pg, pgk = psg[n % 2], ('se_psg', n % 2)
                            sgb, sgk = sg[n % 2], ('se_sg', n % 2)
                            n += 1
                            for k in range(2):
                                self.mm(pg[:], wg[:, k, 128 * fo:128 * fo + 128], ygT[:, k, 512 * blk:512 * blk + 512], k == 0, k == 1,
                                        ['se_wg', ('se_ygT', k, blk)], [pgk])
                            self.act(sgb[:], pg[:], AF.Sigmoid, [pgk, 'se_bg'], [sgk], bias=bg[:, fo:fo + 1])
                            self.tt('dve', mos[:, fo, 512 * blk:512 * blk + 512], sgb[:], ygT[:, fo, 512 * blk:512 * blk + 512], ALU.mult,
                                    [sgk, ('se_ygT', fo, blk)], [('se_mo', fo, blk)])
                    self.wout_part(l, [6, 7], [(mos[:, fo, :], (lambda blk, fo=fo: ('se_mo', fo, blk))) for fo in range(2)],
                                   [(wob[i][:], ('se_wo', i)) for i in range(2)], psg, [('se_psg', 0), ('se_psg', 1)])

    def rope(self, src5, dsts, tt, nb, tmps, rkey, wkeys):
        b, t = tt // 8, tt % 8
        tk = ['rp_t']
        if nb == 1:
            cosb = self.rc[:, b, t, :].rearrange("p (a f) -> p a f", a=2)
            sinb = self.rsn[:, b, t, :].rearrange("p (a f) -> p a f", a=2)
            x1, x2 = src5[:, 0, :, 0, :], src5[:, 0, :, 1, :]
            t1, t2, t3, t4 = [T[:, 0:16].rearrange("p (a f) -> p a f", a=2) for T in tmps]
        else:
            sh = [128, nb, 2, 8]
            cosb = self.rc[:, b, t, :].rearrange("p (a f) -> p a f", a=2).unsqueeze(1).to_broadcast(sh)
            sinb = self.rsn[:, b, t, :].rearrange("p (a f) -> p a f", a=2).unsqueeze(1).to_broadcast(sh)
            x1, x2 = src5[:, :, :, 0, :], src5[:, :, :, 1, :]
            t1, t2, t3, t4 = [T[:, 0:nb * 16].rearrange("p (n a f) -> p n a f", a=2, f=8) for T in tmps]
        self.tt('dve', t1, x1, cosb, ALU.mult, rkey + ['rope_tab'], tk)
        self.tt('dve', t2, x2, sinb, ALU.mult, rkey + ['rope_tab'], tk)
        self.tt('dve', t3, x1, sinb, ALU.mult, rkey + ['rope_tab'], tk)
        self.tt('dve', t4, x2, cosb, ALU.mult, rkey + ['rope_tab'], tk)
        for (bs, dst5), wk in zip(dsts, wkeys):
            if nb == 1:
                self.tt('dve', dst5[:, 0, :, 0, :], t1, t2, ALU.subtract, tk, [wk])
                self.tt('dve', dst5[:, 0, :, 1, :], t3, t4, ALU.add, tk, [wk])
            else:
                self.tt('dve', dst5[:, :, :, 0, :], t1[:, bs], t2[:, bs], ALU.subtract, tk, [wk])
                self.tt('dve', dst5[:, :, :, 1, :], t3[:, bs], t4[:, bs], ALU.add, tk, [wk])

    def attn(self, l, hm, mo):
        S = self.S
        dr = self._dr_cache
        lam_init = 0.8 - 0.6 * math.exp(-0.3 * l)
        with contextlib.ExitStack() as es:
            sb = lambda n, s, d: self.sb(n, s, d, es)
            self.rc = sb("a_rc", [128, 2, 8, 16], F32)
            self.rsn = sb("a_rs", [128, 2, 8, 16], F32)
            aq = sb("a_aq", [128, 16, 9], F32)
            ak = sb("a_ak", [128, 18, 9], F32)
            QS = sb("a_QS", [128, 16, 128], BF16)
            KS = sb("a_KS", [128, 18, 128], BF16)
            QT = [sb("a_QT%d" % i, [128, NT], BF16) for i in range(2)]
            KT = [sb("a_KT%d" % i, [128, NT + 256], BF16) for i in range(2)]
            Vh = [sb("a_V%d" % i, [128, 18, 72], BF16) for i in range(2)]
            PT = [sb("a_PT%d" % i, [128, 512], BF16) for i in range(4)]
            moh = [sb("a_moh%d" % i, [128, NT], BF16) for i in range(2)]
            wob = [sb("a_wo%d" % i, [128, D], BF16) for i in range(2)]
            rt = [sb("a_rt%d" % i, [128, 64], F32) for i in range(4)]
            o0 = sb("a_o0", [128, 4, 64], F32)
            o1 = sb("a_o1", [128, 4, 64], F32)
            osq = sb("a_osq", [128, 4, 64], F32)
            ost = sb("a_ost", [128, 4, 64], BF16)
            sml = sb("a_sml", [128, 16], F32)
            dl = sb("a_dl", [128, 128], F32)
            subw = sb("a_subw", [128, 64], F32)
            lamt = sb("a_lam", [128, 4], F32)
            oTs = [sb("a_oTs%d" % i, [128, 512], F32) for i in range(2)]
            edf = contextlib.ExitStack()
            wh = [self.sb("a_wh%d" % i, [128, 8, 192], BF16, edf) for i in range(2)]
            cdk = self.sb("a_cdk", [128, 2, 384], F32, edf)
            cdv = self.sb("a_cdv", [128, 2, 384], F32, edf)
            kvo = [self.sb("a_kvo%d" % i, [128, 128], F32, edf) for i in range(2)]
            psp = [self.psum("a_psp%d" % i, [128, 512], F32, es) for i in range(2)]
            pstr = [self.psum("a_pst%d" % i, [128, 1024], BF16, es) for i in range(1)]
            NPSS = 3
            pss = [self.psum("a_pss%d" % i, [128, 512], F32, es) for i in range(NPSS)]
            pso = [self.psum("a_pso%d" % i, [128, 512], F32, es) for i in range(2)]
            cnt = {'psp': 0, 'pst': 0, 'pss': 0, 'PT': 0, 'kvo': 0}
            pss_l = [(pss[i][:], ('a_pss', i)) for i in range(NPSS)] + [(pstr[0][:].bitcast(F32), ('a_pst', 0))]
            assert list(pss_l[3][0].shape) == [128, 512], pss_l[3][0].shape

            self.ld(self.rc[:], dr['ropec'].rearrange("(b p t) f -> p b t f", b=2, t=8), ['rope_tab'])
            self.ld(self.rsn[:], dr['ropes'].rearrange("(b p t) f -> p b t f", b=2, t=8), ['rope_tab'])
            self.ld(aq[:].rearrange("p (b t) f -> p b t f", b=2), dr['augq'].rearrange("(b p t) f -> p b t f", b=2, t=8), ['a_aq'])
            self.ld(ak[:, 0:16, :].rearrange("p (b t) f -> p b t f", b=2), dr['augk'][0:NT].rearrange("(b p t) f -> p b t f", b=2, t=8), ['a_ak'])
            self.ld(ak[:, 16:18, :], dr['augk'][NT:NT + 256].rearrange("(i p) f -> p i f", p=128), ['a_ak'])
            self.ld(cdk[:], dr['ctx_dk'][l].rearrange("(i p) f -> p i f", p=128), ['a_cdk'])
            self.ld(cdv[:], dr['ctx_dv'][l].rearrange("(i p) f -> p i f", p=128), ['a_cdv'])
            self.ld(dl[:], dr['diff_lambda'][l].partition_broadcast(128), ['a_dl'])
            self.ld(subw[:], dr['diff_subln_w'][l].partition_broadcast(128), ['a_subw'])
            LK = ['a_lam']
            self.tt('dve', o0[:, 0, :].rearrange("p (a f) -> p a f", a=2), dl[:].rearrange("p (a b f) -> p a b f", a=2, b=2)[:, :, 0, :],
                    dl[:].rearrange("p (a b f) -> p a b f", a=2, b=2)[:, :, 1, :], ALU.mult, ['a_dl'], LK)
            S.op('dve', lambda e: e.tensor_reduce(out=lamt[:, 1:3], in_=o0[:, 0, :].rearrange("p (a f) -> p a f", a=2),
                                                  axis=mybir.AxisListType.X, op=ALU.add), LK, LK)
            self.act(lamt[:, 1:3], lamt[:, 1:3], AF.Exp, LK, LK)
            self.tt('dve', lamt[:, 3:4], lamt[:, 2:3], lamt[:, 1:2], ALU.subtract, LK, LK)
            self.ts('dve', lamt[:, 0:1], lamt[:, 3:4], -lam_init, None, ALU.add, None, LK, LK)
            self.ts('dve', subw[:], subw[:], 1.0 - lam_init, None, ALU.mult, None, ['a_subw'], ['a_subw'])
            S.op('dve', lambda e: e.memset(self.epsc[:, 0:1], EPS), (), ['epsc'])

            def init_staging(qcols, kcols):
                S.op('dve', lambda e: e.memset(QS[:], 0.0), ['a_QS'], ['a_QS'])
                S.op('dve', lambda e: e.memset(KS[:], 0.0), ['a_KS'], ['a_KS'])
                for c0 in qcols:
                    self.copy('dve', QS[:, :, c0:c0 + 9], aq[:], ['a_aq'], ['a_QS'])
                for c0 in kcols:
                    self.copy('dve', KS[:, :, c0:c0 + 9], ak[:], ['a_ak'], ['a_KS'])
            for i in range(2):
                S.op('dve', lambda e, i=i: e.memset(Vh[i][:, :, 64:65], 1.0), [('a_V', i)], [('a_V', i)])

            def transposes(src, ntile, dstT, skey, dkey):
                for q in range((ntile + 3) // 4):
                    k = cnt['pst']
                    cnt['pst'] += 1
                    pt, ptk = pstr[0], ('a_pst', 0)
                    m = min(4, ntile - 4 * q)
                    for i in range(m):
                        self.tr(pt[:, 128 * i:128 * i + 128], src[:, 4 * q + i, :], self.identb[:], [skey, 'identb'], [ptk])
                    self.copy(self.evac_eng(), dstT[:, 512 * q:512 * q + 128 * m], pt[:, 0:128 * m], [ptk], [dkey])

            def core(hs, comps, scale, post, filler=None):
                qt_, kt_, vh_ = QT[hs], KT[hs], Vh[hs]
                ncmp = len(comps)
                steps = [(qb, ci, kt) for qb in range(4) for kt in range(18) for ci in range(ncmp)]
                n = len(steps)
                slots = {}

                def score(i):
                    qb, ci, kt = steps[i]
                    r0, nr = comps[ci]
                    k = cnt['pss']
                    cnt['pss'] += 1
                    ps_, psk = pss_l[k % 4]
                    slots[i] = (ps_, psk)
                    self.mm(ps_, kt_[r0:r0 + nr, 128 * kt:128 * kt + 128], qt_[r0:r0 + nr, 512 * qb:512 * qb + 512], True, True,
                            [('a_KT', hs), ('a_QT', hs)], [psk])
                filler = list(filler or [])
                stride = max(1, n // (len(filler) + 1)) if filler else 0
                for j in range(2):
                    score(j)
                for i in range(n):
                    qb, ci, kt = steps[i]
                    if ncmp == 2:
                        if i % 2 == 0:
                            for j in (i + 2, i + 3):
                                if j < n:
                                    score(j)
                    elif i + 2 < n:
                        score(i + 2)
                    ps_, psk = slots.pop(i)
                    po, pok = pso[ci], ('a_pso', ci)
                    k2 = cnt['PT']
                    cnt['PT'] += 1
                    pt, ptk = PT[k2 % 4], ('a_PT', k2 % 4)
                    self.act(pt[:], ps_, AF.Exp, [psk], [ptk], scale=scale)
                    self.mm(po[0:65, :], vh_[:, kt, 0:65], pt[:], kt == 0, kt == 17, [ptk, ('a_V', hs)], [pok])
                    if kt == 17:
                        self.copy('dve', oTs[ci][0:65, :], po[0:65, :], [pok], [('a_oTs', ci)])
                        for qt in range(4):
                            self.tr(po[:, 128 * qt:128 * qt + 65], oTs[ci][0:65, 128 * qt:128 * qt + 128], self.identf[0:65, 0:65],
                                    [('a_oTs', ci), 'identf'], [pok])
                        if ci == ncmp - 1:
                            post(qb)
                    if filler and (i + 1) % stride == 0:
                        filler.pop(0)()
                while filler:
                    filler.pop(0)()

            def normalize(ci, dst):
                po = pso[ci][:].rearrange("p (q f) -> p q f", f=128)
                S.op('dve', lambda e: e.reciprocal(out=sml[:, 4 * ci:4 * ci + 4], in_=po[:, :, 64]), [('a_pso', ci)], ['a_sml'])
                self.tt('dve', dst, po[:, :, 0:64], sml[:, 4 * ci:4 * ci + 4].unsqueeze(2).to_broadcast([128, 4, 64]), ALU.mult,
                        [('a_pso', ci), 'a_sml'], ['a_o'])

            def out_transposes(qb, jt, roff):
                pt, ptk = pso[0], ('a_pso', 0)
                for qt in range(4):
                    self.mm(pt[roff:roff + 64, 128 * qt:128 * qt + 128], ost[:, qt, :], self.identb[:], True, True, ['a_ost', 'identb'], [ptk])
                self.copy(self.evac_eng(), moh[jt % 2][roff:roff + 64, 512 * qb:512 * qb + 512], pt[roff:roff + 64, :], [ptk], [('a_moh', jt % 2, qb)])

            def pair_wout(jt):
                self.wout_part(l, [jt], [(moh[jt % 2][:], (lambda blk, jt=jt: ('a_moh', jt % 2, blk)))],
                               [(wob[jt % 2][:], ('a_wo', jt % 2))], psp, [('a_psp', 0), ('a_psp', 1)])

            w_in_v = self._ap('w_in')[l].rearrange("(k p) f -> p k f", p=128)
            import os
            att = int(os.environ.get('ATT', '99'))
            init_staging((32, 96), (32, 96))
            if att <= 1:
                S.barrier()
                edf.close()
                return
            pk = {'n': 0}

            def pbank():
                k = pk['n']
                pk['n'] += 1
                return psp[k % 2], ('a_psp', k % 2)

            def linearize(chunks):
                seq, prevB = [], None
                for A, B in chunks:
                    if A is not None:
                        seq.append(A)
                    if prevB is not None:
                        seq.append(prevB)
                    prevB = B
                if prevB is not None:
                    seq.append(prevB)
                return seq

            def tr_chunks(src, ntile, dstT, skey, dkey):
                out = []
                for q in range((ntile + 3) // 4):
                    m = min(4, ntile - 4 * q)
                    st = {}

                    def A(q=q, m=m, st=st):
                        pb, pbk = pbank()
                        st['b'] = (pb, pbk)
                        ptv = pb[:].bitcast(BF16)
                        for i in range(m):
                            self.tr(ptv[:, 128 * i:128 * i + 128], src[:, 4 * q + i, :], self.identb[:], [skey, 'identb'], [pbk])

                    def B(q=q, m=m, st=st):
                        pb, pbk = st['b']
                        ptv = pb[:].bitcast(BF16)
                        self.copy(self.evac_eng(), dstT[:, 512 * q:512 * q + 128 * m], ptv[:, 0:128 * m], [pbk], [dkey])
                    out.append((A, B))
                return out

            def diff_prologue(h):
                hs = h % 2
                whb, whk = wh[hs], ('a_wh', hs)
                chunks = []

                def LD():
                    for i3 in range(3):
                        self.ld(whb[:, :, 64 * i3:64 * i3 + 64], w_in_v[:, :, 384 * i3 + 64 * h:384 * i3 + 64 * h + 64], [whk], eng='pool')
                chunks.append((LD, None))
                for tt in range(16):
                    st = {}

                    def A(tt=tt, st=st):
                        pos = 128 * tt
                        pp, ppk = pbank()
                        st['b'] = (pp, ppk)
                        for kk in range(8):
                            self.mm(pp[:, 0:192], hm[:, kk, pos:pos + 128], whb[:, kk, :], kk == 0, kk == 7,
                                    [whk, ('m_hm', kk, pos // 512)], [ppk])

                    def B(tt=tt, st=st):
                        pp, ppk = st['b']
                        b, t = tt // 8, tt % 8
                        src5 = pp[:, 0:128].rearrange("p (n a h f) -> p n a h f", n=4, a=2, h=2)
                        qd = QS[:, tt, :].rearrange("p (c x) -> p c x", c=2)[:, :, 0:32].rearrange("p c (a h f) -> p c a h f", a=2, h=2)
                        kd = KS[:, tt, :].rearrange("p (c x) -> p c x", c=2)[:, :, 0:32].rearrange("p c (a h f) -> p c a h f", a=2, h=2)
                        self.rope(src5, [(slice(0, 2), qd), (slice(2, 4), kd)], tt, 4, [r[:] for r in rt], [ppk], ['a_QS', 'a_KS'])
                        kv = cnt['kvo']
                        cnt['kvo'] += 1
                        kvb, kvk = kvo[kv % 2], ('a_kvo', kv % 2)
                        self.copy('act', kvb[:], pp[:, 64:192], [ppk], [kvk])
                        rows_k = self.o['ndk'][l].rearrange("(b p t) f -> b t p f", b=2, t=8)[b, t]
                        rows_v = self.o['ndv'][l].rearrange("(b p t) f -> b t p f", b=2, t=8)[b, t]
                        self.st(rows_k[:, 64 * h:64 * h + 64], kvb[:, 0:64], [kvk])
                        self.st(rows_v[:, 64 * h:64 * h + 64], kvb[:, 64:128], [kvk])
                        self.copy('act', Vh[hs][:, tt, 0:64], pp[:, 128:192], [ppk], [('a_V', hs)])
                    chunks.append((A, B))

                def CTX():
                    for i in range(2):
                        kd = KS[:, 16 + i, :].rearrange("p (c x) -> p c x", c=2)[:, :, 0:32]
                        self.copy('dve', kd, cdk[:, i, 64 * h:64 * h + 64].rearrange("p (c x) -> p c x", c=2), ['a_cdk'], ['a_KS'])
                        self.copy('dve', Vh[hs][:, 16 + i, 0:64], cdv[:, i, 64 * h:64 * h + 64], ['a_cdv'], [('a_V', hs)])
                chunks.append((None, CTX))
                chunks += tr_chunks(QS, 16, QT[hs], 'a_QS', ('a_QT', hs))
                chunks += tr_chunks(KS, 18, KT[hs], 'a_KS', ('a_KT', hs))
                return linearize(chunks)

            for f_ in diff_prologue(0):
                f_()
            for h in range(6):
                hs = h % 2

                def post(qb, h=h):
                    normalize(0, o0[:])
                    normalize(1, o1[:])
                    self.stt('dve', o0[:], o1[:], lamt[:, 0:1], o0[:], ALU.mult, ALU.add, ['a_o', 'a_lam'], ['a_o'])
                    self.tt('dve', osq[:], o0[:], o0[:], ALU.mult, ['a_o'], ['a_osq'])
                    S.op('dve', lambda e: e.tensor_reduce(out=sml[:, 8:12], in_=osq[:], axis=mybir.AxisListType.X, op=ALU.add),
                         ['a_osq'], ['a_sml'])
                    self.act(sml[:, 8:12], sml[:, 8:12], AF.Sqrt, ['a_sml', 'epsc'], ['a_sml'], bias=self.epsc[:, 0:1], scale=1.0 / 64)
                    S.op('dve', lambda e: e.reciprocal(out=sml[:, 8:12], in_=sml[:, 8:12]), ['a_sml'], ['a_sml'])
                    self.tt('dve', o0[:], o0[:], sml[:, 8:12].unsqueeze(2).to_broadcast([128, 4, 64]), ALU.mult, ['a_o', 'a_sml'], ['a_o'])
                    self.tt('dve', ost[:], o0[:], subw[:].unsqueeze(1).to_broadcast([128, 4, 64]), ALU.mult, ['a_o', 'a_subw'], ['a_ost'])
                    out_transposes(qb, h // 2, 64 * (h % 2))
                nxt = diff_prologue(h + 1) if h + 1 < 6 else []
                core(hs, [(0, 64), (64, 64)], 32 ** -0.5, post, filler=nxt)
                if h % 2 == 1:
                    pair_wout(h // 2)
            S.barrier()
            edf.close()
            if att <= 5:
                return
            with contextlib.ExitStack() as em:
                sbm = lambda n, s, d: self.sb(n, s, d, em)
                wm = sbm("a_wm", [128, 8, 416], BF16)
                cqnT = sbm("a_cqnT", [128, 2, NT], BF16)
                ckvT = sbm("a_ckvT", [128, NT + 256], BF16)
                cqs = sbm("a_cqs", [128, 2, 256], BF16)
                cks = sbm("a_cks", [128, 2, 128], BF16)
                qnw = sbm("a_qnw", [128, 256], F32)
                kvnw = sbm("a_kvnw", [128, 128], F32)
                cckv = sbm("a_cckv", [128, 2, 128], F32)
                ckpe = sbm("a_ckpe", [128, 2, 32], F32)
                tq = sbm("a_tq", [128, 256], F32)
                tk_ = [sbm("a_tk%d" % i, [128, 160], F32) for i in range(2)]
                wq = [sbm("a_wq%d" % i, [128, 2, 96], BF16) for i in range(2)]
                wkv = [sbm("a_wkv%d" % i, [128, 128], BF16) for i in range(2)]
                init_staging((96,), (96,))
                self.ld(wm[:], w_in_v[:, :, 1152:1568], ['a_wm'], eng='pool')
                self.ld(qnw[:], dr['mla_q_norm_w'][l].partition_broadcast(128), ['a_qnw'])
                self.ld(kvnw[:], dr['mla_kv_norm_w'][l].partition_broadcast(128), ['a_kvnw'])
                self.ld(cckv[:], dr['ctx_ckv'][l].rearrange("(i p) f -> p i f", p=128), ['a_cckv'])
                self.ld(ckpe[:], dr['ctx_kpe'][l].rearrange("(i p) f -> p i f", p=128), ['a_ckpe'])
                mla = int(os.environ.get('MLA', '99'))
                if mla <= 1:
                    S.barrier()
                    return
                for tt in range(16):
                    pos = 128 * tt
                    b, t = tt // 8, tt % 8
                    k = cnt['psp']
                    cnt['psp'] += 1
                    pp, ppk = psp[k % 2], ('a_psp', k % 2)
                    for kk in range(8):
                        self.mm(pp[:, 0:416], hm[:, kk, pos:pos + 128], wm[:, kk, :], kk == 0, kk == 7, ['a_wm', ('m_hm', kk, pos // 512)], [ppk])
                    self.act(tq[:], pp[:, 0:256], AF.Square, [ppk], ['a_tq'], accum=sml[:, 12:13])
                    self.act(sml[:, 12:13], sml[:, 12:13], AF.Sqrt, ['a_tq'], ['a_sml2'], bias=self.epsc[:, 0:1], scale=1.0 / 256)
                    S.op('dve', lambda e: e.reciprocal(out=sml[:, 12:13], in_=sml[:, 12:13]), ['a_sml2'], ['a_sml2'])
                    cb = tt % 2
                    self.stt('dve', cqs[:, cb, :], pp[:, 0:256], sml[:, 12:13], qnw[:], ALU.mult, ALU.mult, [ppk, 'a_sml2', 'a_qnw'], [('a_cqs', cb)])
                    kq = cnt['pst']
                    cnt['pst'] += 1
                    ptq, ptqk = pstr[0], ('a_pst', 0)
                    for f in range(2):
                        self.tr(ptq[:, 128 * f:128 * f + 128], cqs[:, cb, 128 * f:128 * f + 128], self.identb[:], [('a_cqs', cb), 'identb'], [ptqk])
                    self.copy(self.evac_eng(), cqnT[:, :, pos:pos + 128], ptq[:, 0:256].rearrange("p (f c) -> p f c", f=2), [ptqk], ['a_cqnT'])
                    mlap = int(os.environ.get('MLAP', '99'))
                    if mlap <= 1:
                        continue
                    kv = cnt['kvo']
                    cnt['kvo'] += 1
                    tkb, tkk = tk_[kv % 2], ('a_tk', kv % 2)
                    self.act(tq[:, 0:128], pp[:, 256:384], AF.Square, [ppk, 'a_tq'], ['a_tq'], accum=sml[:, 13:14])
                    self.act(sml[:, 13:14], sml[:, 13:14], AF.Sqrt, ['a_tq'], ['a_sml3'], bias=self.epsc[:, 0:1], scale=1.0 / 128)
                    S.op('dve', lambda e: e.reciprocal(out=sml[:, 13:14], in_=sml[:, 13:14]), ['a_sml3'], ['a_sml3'])
                    self.stt('dve', tkb[:, 0:128], pp[:, 256:384], sml[:, 13:14], kvnw[:], ALU.mult, ALU.mult, [ppk, 'a_sml3', 'a_kvnw'], [tkk])
                    self.copy('act', tkb[:, 128:160], pp[:, 384:416], [ppk], [tkk])
                    self.copy('act', cks[:, cb, :], tkb[:, 0:128], [tkk], [('a_cks', cb)])
                    self.tr(ptq[:, 256:384], cks[:, cb, :], self.identb[:], [('a_cks', cb), 'identb'], [ptqk])
                    self.copy(self.evac_eng(), ckvT[:, pos:pos + 128], ptq[:, 256:384], [ptqk], ['a_ckvT'])
                    if mlap <= 2:
                        continue
                    rows_c = self.o['nckv'][l].rearrange("(b p t) f -> b t p f", b=2, t=8)[b, t]
                    rows_p = self.o['nkpe'][l].rearrange("(b p t) f -> b t p f", b=2, t=8)[b, t]
                    self.st(rows_c, tkb[:, 0:128], [tkk])
                    self.st(rows_p, tkb[:, 128:160], [tkk])
                    if mlap <= 3:
                        continue
                    src5 = pp[:, 384:416].rearrange("p (n a h f) -> p n a h f", n=1, a=2, h=2)
                    kd = KS[:, tt, 64:96].rearrange("p (n a h f) -> p n a h f", n=1, a=2, h=2)
                    self.rope(src5, [(slice(0, 1), kd)], tt, 1, [r[:] for r in rt], [ppk], ['a_KS'])
                if mla <= 2:
                    S.barrier()
                    return
                for i in range(2):
                    self.copy('dve', cks[:, i, :], cckv[:, i, :], ['a_cckv'], [('a_cks', i)])
                    kq = cnt['pst']
                    cnt['pst'] += 1
                    ptq, ptqk = pstr[0], ('a_pst', 0)
                    self.tr(ptq[:, 0:128], cks[:, i, :], self.identb[:], [('a_cks', i), 'identb'], [ptqk])
                    self.copy(self.evac_eng(), ckvT[:, NT + 128 * i:NT + 128 * i + 128], ptq[:, 0:128], [ptqk], ['a_ckvT'])
                    self.copy('dve', KS[:, 16 + i, 64:96], ckpe[:, i, :], ['a_ckpe'], ['a_KS'])
                if mla <= 3:
                    S.barrier()
                    return
                def mla_prologue(h):
                    hs = h % 2
                    chunks = []

                    def LD():
                        self.ld(wq[hs][:], dr['mla_w_q_up'][l].rearrange("(k p) f -> p k f", p=128)[:, :, 96 * h:96 * h + 96], [('a_wq', hs)], eng='pool')
                        self.ld(wkv[hs][:], dr['mla_w_kv_up'][l][:, 128 * h:128 * h + 128], [('a_wkv', hs)], eng='pool')
                    chunks.append((LD, None))
                    for tt in range(16):
                        st = {}

                        def A(tt=tt, st=st):
                            pos = 128 * tt
                            pp, ppk = pbank()
                            st['b'] = (pp, ppk)
                            for f in range(2):
                                self.mm(pp[:, 0:96], cqnT[:, f, pos:pos + 128], wq[hs][:, f, :], f == 0, f == 1, [('a_wq', hs), 'a_cqnT'], [ppk])

                        def B(tt=tt, st=st):
                            pp, ppk = st['b']
                            kv = cnt['kvo']
                            cnt['kvo'] += 1
                            tkb, tkk = tk_[kv % 2], ('a_tk', kv % 2)
                            self.copy('act', tkb[:, 0:96], pp[:, 0:96], [ppk], [tkk])
                            self.copy('act', QS[:, tt, 0:64], tkb[:, 0:64], [tkk], ['a_QS'])
                            src5 = tkb[:, 64:96].rearrange("p (n a h f) -> p n a h f", n=1, a=2, h=2)
                            qd = QS[:, tt, 64:96].rearrange("p (n a h f) -> p n a h f", n=1, a=2, h=2)
                            self.rope(src5, [(slice(0, 1), qd)], tt, 1, [r[:] for r in rt], [tkk], ['a_QS'])
                        chunks.append((A, B))
                    for kt in range(18):
                        st = {}

                        def A(kt=kt, st=st):
                            pp, ppk = pbank()
                            st['b'] = (pp, ppk)
                            self.mm(pp[:, 0:128], ckvT[:, 128 * kt:128 * kt + 128], wkv[hs][:], True, True, [('a_wkv', hs), 'a_ckvT'], [ppk])

                        def B(kt=kt, st=st):
                            pp, ppk = st['b']
                            self.copy('act', KS[:, kt, 0:64], pp[:, 0:64], [ppk], ['a_KS'])
                            self.copy('dve', Vh[hs][:, kt, 0:64], pp[:, 64:128], [ppk], [('a_V', hs)])
                        chunks.append((A, B))
                    chunks += tr_chunks(QS, 16, QT[hs], 'a_QS', ('a_QT', hs))
                    chunks += tr_chunks(KS, 18, KT[hs], 'a_KS', ('a_KT', hs))
                    return linearize(chunks)

                for f_ in mla_prologue(0):
                    f_()
                for h in range(6):
                    hs = h % 2

                    def postm(qb, h=h):
                        normalize(0, o0[:])
                        self.copy('act', ost[:], o0[:], ['a_o'], ['a_ost'])
                        out_transposes(qb, 3 + h // 2, 64 * (h % 2))
                    nxt = mla_prologue(h + 1) if h + 1 < 6 else []
                    core(hs, [(0, 128)], 96 ** -0.5, postm, filler=nxt)
                    if h % 2 == 1:
                        pair_wout(3 + h // 2)


_PROG = {}


def get_prog(stage=99):
    if stage not in _PROG:
        b = Builder(stage)
        _PROG[stage] = b.build()
    return _PROG[stage]


def rope_tables(n, grid_w=64, theta=10000.0):
    t = np.arange(n)
    row = (t // grid_w).astype(np.float32)
    col = (t % grid_w).astype(np.float32)
    inv = (theta ** (-np.arange(8, dtype=np.float32) / 8)).astype(np.float32)
    ang = np.concatenate([row[:, None] * inv[None], col[:, None] * inv[None]], axis=1).astype(np.float32)
    return np.cos(ang).astype(np.float32), np.sin(ang).astype(np.float32)


def _gsplit(a, axis):
    a = np.asarray(a, dtype=np.float32)
    sh = a.shape
    a = a.reshape(sh[:axis] + (8, 2) + sh[axis + 1:])
    a = np.moveaxis(a, axis + 1, axis)
    return np.ascontiguousarray(a)


def make_in_maps(inp):
    f = lambda a: np.ascontiguousarray(np.asarray(a, dtype=np.float32))
    shared = dict(
        w_ada=f(inp['w_ada']), b_ada=f(inp['b_ada']).reshape(DEPTH * 72, 128),
        norm_w=f(inp['norm_w']).reshape(DEPTH * 3 * 8, 128), final_norm_w=f(inp['final_norm_w']).reshape(8, 128),
        ffn_w_in=f(inp['ffn_w_in']), ffn_w_out=f(inp['ffn_w_out']), w_in=f(inp['w_in']), w_out=f(inp['w_out']),
        diff_lambda=f(inp['diff_lambda']).reshape(DEPTH, 128), diff_subln_w=f(inp['diff_subln_w']),
        mla_q_norm_w=f(inp['mla_q_norm_w']), mla_w_q_up=f(inp['mla_w_q_up']),
        mla_kv_norm_w=f(inp['mla_kv_norm_w']), mla_w_kv_up=f(inp['mla_w_kv_up']),
        s5_a_re=_gsplit(inp['s5_a_re'], 2), s5_a_im=_gsplit(inp['s5_a_im'], 2), s5_log_step=_gsplit(inp['s5_log_step'], 2),
        s5_b_re=_gsplit(inp['s5_b_re'], 2), s5_b_im=_gsplit(inp['s5_b_im'], 2),
        s5_c_re=_gsplit(inp['s5_c_re'], 2).reshape(DEPTH, 2, 2, 128, 64), s5_c_im=_gsplit(inp['s5_c_im'], 2).reshape(DEPTH, 2, 2, 128, 64),
        s5_d=f(inp['s5_d']), s5_w_glu=f(inp['s5_w_glu']), s5_b_glu=f(inp['s5_b_glu']).reshape(DEPTH, 2, 128),
    )
    tq = np.arange(128) // 16
    cm_f = (tq[:, None] <= tq[None, :]).astype(np.float32)
    cm_b = (tq[:, None] >= tq[None, :]).astype(np.float32)
    shared['cmask_f'] = cm_f
    shared['cmask_b'] = cm_b
    rc, rsn = rope_tables(NT)
    maps = []
    for core in range(8):
        m = dict(shared)
        if core < 4:
            b = core
            m['xin'] = f(inp['x_sample'][b])
            m['cvec'] = f(inp['c'][b]).reshape(8, 128)
            m['ctx_dk'] = f(inp['cache_diff_k'][b]).reshape(DEPTH, 256, 384)
            m['ctx_dv'] = f(inp['cache_diff_v'][b]).reshape(DEPTH, 256, 384)
            m['ctx_ckv'] = f(inp['cache_mla_ckv'][b])
            m['ctx_kpe'] = f(inp['cache_mla_kpe'][b])
            m['h0re'] = _gsplit(inp['state_s5_re'][b], 2)
            m['h0im'] = _gsplit(inp['state_s5_im'][b], 2)
            m['ropec'], m['ropes'] = rc, rsn
            augk = np.zeros((NT + 256, 9), np.float32)
            augk[:, 0] = 32.0
            augk[:, 8] = 1.0
            augq = np.zeros((NT, 9), np.float32)
            augq[:, 0] = 32.0
            augq[:, 8] = -1024.0
            m['flag'] = np.ones((128, 1), np.float32)
        else:
            i = core - 4
            m['xin'] = f(inp['x_prompt'][8 * i:8 * i + 8]).reshape(NT, D)
            m['cvec'] = f(inp['c_ctx']).reshape(8, 128)
            m['ctx_dk'] = np.zeros((DEPTH, 256, 384), np.float32)
            m['ctx_dv'] = np.zeros((DEPTH, 256, 384), np.float32)
            m['ctx_ckv'] = np.zeros((DEPTH, 256, 128), np.float32)
            m['ctx_kpe'] = np.zeros((DEPTH, 256, 32), np.float32)
            m['h0re'] = np.zeros((DEPTH, 2, 2, 8, 64), np.float32)
            m['h0im'] = np.zeros((DEPTH, 2, 2, 8, 64), np.float32)
            m['ropec'] = np.ones((NT, 16), np.float32)
            m['ropes'] = np.zeros((NT, 16), np.float32)
            seg = np.arange(NT) // 256
            augk = np.zeros((NT + 256, 9), np.float32)
            augk[np.arange(NT), seg] = 32.0
            augk[:, 8] = 1.0
            augq = np.zeros((NT, 9), np.float32)
            augq[np.arange(NT), seg] = 32.0
            augq[:, 8] = -1024.0
            m['flag'] = np.zeros((128, 1), np.float32)
        m['augk'] = augk
        m['augq'] = augq
        maps.append(m)
    return maps


def kernel(**inputs):
    nc = get_prog(STAGE)
    maps = make_in_maps(inputs)
    res = run_bass_kernel_spmd(nc, maps, core_ids=list(range(8)))
    r = res.results
    B, SEQ = 32, 256
    y_sample = np.stack([r[c]['y'] for c in range(4)], 0).astype(np.float32)
    y_prompt = np.concatenate([r[c]['y'].reshape(8, SEQ, D) for c in range(4, 8)], 0).astype(np.float32)

    def cat(name, tail):
        outs = []
        for c in range(4, 8):
            a = r[c][name].reshape(DEPTH, 8, SEQ, -1).transpose(1, 0, 2, 3)
            outs.append(a)
        a = np.concatenate(outs, 0)
        return np.ascontiguousarray(a.reshape((B, DEPTH, SEQ) + tail)).astype(np.float32)

    def cat5(name):
        outs = []
        for c in range(4, 8):
            a = r[c][name].reshape(DEPTH, 2, 8, 16, 64).transpose(2, 0, 1, 3, 4)
            outs.append(a)
        return np.ascontiguousarray(np.concatenate(outs, 0)).astype(np.float32)
    return (y_prompt, y_sample, cat('ndk', (6, 64)), cat('ndv', (6, 64)), cat('nckv', (128,)), cat('nkpe', (32,)),
            cat5('ns5re'), cat5('ns5im'))
```

```python
import contextlib
import math
import numpy as np
import concourse.bass as bass
import concourse.mybir as mybir
from concourse.bass_utils import run_bass_kernel_spmd

F32 = mybir.dt.float32
BF16 = mybir.dt.bfloat16
AF = mybir.ActivationFunctionType
ALU = mybir.AluOpType

D = 1024
NT = 2048
DEPTH = 2
DFF = 2816
NHT = 22
EPS = 1e-6
INC = 1824
STAGE = 99


class Sched:
    def __init__(self, nc, ndma=8):
        self.nc = nc
        self.engs = ['pe', 'act', 'dve', 'pool', 'sp']
        self.streams = {e: [] for e in self.engs}
        self.cnt = {e: 0 for e in self.engs}
        self.seen = {e: {} for e in self.engs}
        self.res = {}
        self.ndma = ndma
        self.dma_issued = {'sp': 0, 'pool': 0, 'act': 0}
        self.dma_last = {}
        self.final_tokens = []

    def _deps(self, eng, reads, writes):
        toks = {}

        def add(t):
            if t is None:
                return
            k, v = t
            if toks.get(k, 0) < v:
                toks[k] = v
        for r in reads:
            st = self.res.get(r)
            if st:
                add(st['w'])
        for w in writes:
            st = self.res.get(w)
            if st:
                add(st['w'])
                for t in st['r']:
                    add(t)
        out = []
        for k, v in toks.items():
            if eng == 'pe' and k == ('c', 'pe'):
                continue
            if self.seen[eng].get(k, 0) >= v:
                continue
            self.seen[eng][k] = v
            out.append((k, v))
        return out

    def _mark(self, tok, reads, writes):
        for r in reads:
            st = self.res.setdefault(r, {'w': None, 'r': []})
            st['r'].append(tok)
            if len(st['r']) > 64:
                mx = {}
                for k, v in st['r']:
                    if mx.get(k, 0) < v:
                        mx[k] = v
                st['r'] = list(mx.items())
        for w in writes:
            self.res[w] = {'w': tok, 'r': []}

    PSUM_NAMES = {'lxps', 'ad_pst', 'ad_psm', 'nm_pss', 'f_pag', 'f_pso', 'fin_ps', 'sa_ps', 'sb_psu', 'sb_pst', 'sc_ps',
                  'sc_psF', 'sd_ps', 'se_pst', 'se_psg', 'a_psp', 'a_pst', 'a_pss', 'a_pso'}

    def _excl(self, reads, writes):
        rd, wr = [], list(writes)
        for r in reads:
            nm = r if isinstance(r, str) else r[0]
            if nm in self.PSUM_NAMES:
                if r not in wr:
                    wr.append(r)
            else:
                rd.append(r)
        return rd, wr

    def op(self, eng, fn, reads=(), writes=()):
        reads, writes = self._excl(reads, writes)
        waits = self._deps(eng, reads, writes)
        self.cnt[eng] += 1
        tok = (('c', eng), self.cnt[eng])
        self.streams[eng].append((waits, fn, tok))
        self._mark(tok, reads, writes)
        return tok

    def dma(self, eng, fn, reads=(), writes=(), final=False):
        k = self.dma_issued[eng]
        self.dma_issued[eng] += 1
        slot = k % self.ndma
        val = 16 * (k // self.ndma + 1)
        key = ('d', eng, slot)
        waits = self._deps(eng, reads, writes)
        if val > 16 and self.seen[eng].get(key, 0) < val - 16:
            self.seen[eng][key] = val - 16
            waits.append((key, val - 16))
        tok = (key, val)
        self.dma_last[key] = val
        self.streams[eng].append((waits, fn, tok))
        self._mark(tok, reads, writes)
        if final:
            self.final_tokens.append(tok)
        return tok

    def barrier(self):
        allt = [(('c', e), self.cnt[e]) for e in ['pe', 'act', 'dve', 'pool'] if self.cnt[e]]
        allt += list(self.dma_last.items())
        for e in self.engs:
            waits = []
            for k, v in allt:
                if k == ('c', e):
                    continue
                if self.seen[e].get(k, 0) >= v:
                    continue
                self.seen[e][k] = v
                waits.append((k, v))
            if waits:
                self.streams[e].append((waits, None, None))

    def emit(self):
        nc = self.nc
        with contextlib.ExitStack() as es:
            sems = {}
            for e in ['pe', 'act', 'dve', 'pool']:
                sems[('c', e)] = es.enter_context(nc.semaphore('c_' + e))
            for e in ['sp', 'pool', 'act']:
                if self.dma_issued[e]:
                    for s in range(self.ndma):
                        sems[('d', e, s)] = es.enter_context(nc.semaphore('d_%s_%d' % (e, s)))
            block = es.enter_context(nc.Block())

            def run(engname, engobj):
                for waits, fn, tok in self.streams[engname]:
                    for k, v in waits:
                        engobj.wait_ge(sems[k], v)
                    if fn is None:
                        continue
                    ins = fn(engobj)
                    k, v = tok
                    ins.then_inc(sems[k], 16 if k[0] == 'd' else 1)
                if engname == 'sp':
                    for k, v in self.final_tokens:
                        engobj.wait_ge(sems[k], v)

            @block.sync
            def _(e):
                run('sp', e)

            @block.tensor
            def _(e):
                run('pe', e)

            @block.scalar
            def _(e):
                run('act', e)

            @block.vector
            def _(e):
                run('dve', e)

            @block.gpsimd
            def _(e):
                run('pool', e)


class Builder:
    def __init__(self, stage=99):
        self.stage = stage
        self.nc = bass.Bass("TRN2", target_bir_lowering=False)
        self.S = Sched(self.nc)
        self.es = contextlib.ExitStack()
        self.uid = 0
        self.rr = 0
        self._dr_cache = {}

    def din(self, name, shape):
        ap = self.nc.dram_tensor(name, list(shape), F32, kind="ExternalInput").ap()
        self._dr_cache[name] = ap
        return ap

    def _ap(self, name):
        return self._dr_cache[name]

    def dout(self, name, shape):
        return self.nc.dram_tensor(name, list(shape), F32, kind="ExternalOutput").ap()

    def sb(self, name, shape, dt, es=None):
        self.uid += 1
        return (es or self.es).enter_context(self.nc.sbuf_tensor("%s_u%d" % (name, self.uid), list(shape), dt))

    def psum(self, name, shape, dt, es):
        self.uid += 1
        return es.enter_context(self.nc.psum_tensor("%s_u%d" % (name, self.uid), list(shape), dt))

    def evac_eng(self):
        self.rr += 1
        return 'act' if self.rr % 2 else 'dve'

    def copy(self, eng, out, in_, reads, writes):
        if eng == 'act':
            self.S.op('act', lambda e: e.copy(out=out, in_=in_), reads, writes)
        else:
            self.S.op(eng, lambda e: e.tensor_copy(out=out, in_=in_), reads, writes)

    def tt(self, eng, out, a, b, op, reads, writes):
        self.S.op(eng, lambda e: e.tensor_tensor(out=out, in0=a, in1=b, op=op), reads, writes)

    def ts(self, eng, out, a, s1, s2, op0, op1, reads, writes):
        if op1 is None:
            self.S.op(eng, lambda e: e.tensor_scalar(out=out, in0=a, scalar1=s1, scalar2=None, op0=op0), reads, writes)
        else:
            self.S.op(eng, lambda e: e.tensor_scalar(out=out, in0=a, scalar1=s1, scalar2=s2, op0=op0, op1=op1), reads, writes)

    def stt(self, eng, out, a, s, b, op0, op1, reads, writes):
        self.S.op(eng, lambda e: e.scalar_tensor_tensor(out=out, in0=a, scalar=s, in1=b, op0=op0, op1=op1), reads, writes)

    def act(self, out, in_, func, reads, writes, bias=None, scale=None, accum=None):
        kw = {}
        if bias is not None:
            kw['bias'] = bias
        if scale is not None:
            kw['scale'] = scale
        if accum is not None:
            kw['accum_out'] = accum
        self.S.op('act', lambda e: e.activation(out=out, in_=in_, func=func, **kw), reads, writes)

    def mm(self, out, lhsT, rhs, start, stop, reads, writes):
        self.S.op('pe', lambda e: e.matmul(out, lhsT=lhsT, rhs=rhs, start=start, stop=stop), reads, writes)

    def tr(self, out, in_, ident, reads, writes):
        self.S.op('pe', lambda e: e.transpose(out=out, in_=in_, identity=ident), reads, writes)

    def ld(self, out, in_, writes, reads=(), eng='sp', **kw):
        self.S.dma(eng, lambda e: e.dma_start(out=out, in_=in_, **kw), reads, writes)

    def st(self, out, in_, reads, eng='sp', **kw):
        self.S.dma(eng, lambda e: e.dma_start(out=out, in_=in_, **kw), reads, (), final=True)

    def build(self):
        nc, S = self.nc, self.S
        din, dout, sb = self.din, self.dout, self.sb
        xin = din("xin", [NT, D])
        cvec = din("cvec", [8, 128])
        w_ada = din("w_ada", [DEPTH, D, 9 * D])
        b_ada = din("b_ada", [DEPTH * 72, 128])
        norm_w = din("norm_w", [DEPTH * 3 * 8, 128])
        fnw = din("final_norm_w", [8, 128])
        ffn_w_in = din("ffn_w_in", [DEPTH, 2, D, 2 * DFF])
        ffn_w_out = din("ffn_w_out", [DEPTH, 2, DFF, D])
        self.dr = dict(
            w_in=din("w_in", [DEPTH, D, INC]), w_out=din("w_out", [DEPTH, D, D]),
            ctx_dk=din("ctx_dk", [DEPTH, 256, 384]), ctx_dv=din("ctx_dv", [DEPTH, 256, 384]),
            ctx_ckv=din("ctx_ckv", [DEPTH, 256, 128]), ctx_kpe=din("ctx_kpe", [DEPTH, 256, 32]),
            h0re=din("h0re", [DEPTH, 2, 2, 8, 64]), h0im=din("h0im", [DEPTH, 2, 2, 8, 64]),
            ropec=din("ropec", [NT, 16]), ropes=din("ropes", [NT, 16]),
            augk=din("augk", [NT + 256, 9]), augq=din("augq", [NT, 9]),
            flag=din("flag", [128, 1]),
            diff_lambda=din("diff_lambda", [DEPTH, 128]), diff_subln_w=din("diff_subln_w", [DEPTH, 64]),
            mla_q_norm_w=din("mla_q_norm_w", [DEPTH, 256]), mla_w_q_up=din("mla_w_q_up", [DEPTH, 256, 576]),
            mla_kv_norm_w=din("mla_kv_norm_w", [DEPTH, 128]), mla_w_kv_up=din("mla_w_kv_up", [DEPTH, 128, 768]),
            s5_a_re=din("s5_a_re", [DEPTH, 2, 2, 8, 64]), s5_a_im=din("s5_a_im", [DEPTH, 2, 2, 8, 64]),
            s5_log_step=din("s5_log_step", [DEPTH, 2, 2, 8]),
            s5_b_re=din("s5_b_re", [DEPTH, 2, 2, 8, 64, 16]), s5_b_im=din("s5_b_im", [DEPTH, 2, 2, 8, 64, 16]),
            s5_c_re=din("s5_c_re", [DEPTH, 2, 2, 128, 64]), s5_c_im=din("s5_c_im", [DEPTH, 2, 2, 128, 64]),
            s5_d=din("s5_d", [DEPTH, 16, 16]), s5_w_glu=din("s5_w_glu", [DEPTH, 256, 256]),
            s5_b_glu=din("s5_b_glu", [DEPTH, 2, 128]),
            cmask_f=din("cmask_f", [128, 128]), cmask_b=din("cmask_b", [128, 128]),
        )
        self.y = dout("y", [NT, D])
        self.o = dict(
            ndk=dout("ndk", [DEPTH, NT, 384]), ndv=dout("ndv", [DEPTH, NT, 384]),
            nckv=dout("nckv", [DEPTH, NT, 128]), nkpe=dout("nkpe", [DEPTH, NT, 32]),
            ns5re=dout("ns5re", [DEPTH, 2, 8, 1024]), ns5im=dout("ns5im", [DEPTH, 2, 8, 1024]),
        )
        self.xT = sb("xT", [128, 8, NT], F32)
        self.identb = sb("identb", [128, 128], BF16)
        self.identf = sb("identf", [128, 128], F32)
        self.onesb = sb("onesb", [128, 128], BF16)
        self.epsc = sb("epsc", [128, 1], F32)
        self.mod = sb("mod", [128, DEPTH, 72], F32)
        self.cA = sb("cA", [128, DEPTH, 3, 8], F32)
        self.cB = sb("cB", [128, DEPTH, 3, 8], F32)
        self.cG = sb("cG", [128, DEPTH, 3, 8], F32)
        self.cF = sb("cF", [128, 8], F32)
        self.rs = sb("rs", [128, 2, 512], F32)
        self.wi_n = 0
        self.wo_n = 0

        self.setup_consts()
        self.load_x()
        self.adaln()
        S.barrier()
        for l in range(DEPTH):
            if self.stage >= 1:
                self.ffn(l, 0)
                S.barrier()
            if self.stage >= 3:
                self.mixer(l)
                S.barrier()
            if self.stage >= 2:
                self.ffn(l, 1)
                S.barrier()
            if self.stage < 4:
                break
        self.final()
        S.emit()
        return nc

    def setup_consts(self):
        S = self.S
        ib, iff, ob = self.identb, self.identf, self.onesb
        S.op('pool', lambda e: e.memset(ib[:], 0.0), (), ['identb'])
        S.op('pool', lambda e: e.affine_select(out=ib[:], in_=ib[:], compare_op=ALU.not_equal, fill=1.0, base=0,
                                               pattern=[[-1, 128]], channel_multiplier=1), ['identb'], ['identb'])
        S.op('pool', lambda e: e.memset(iff[:], 0.0), (), ['identf'])
        S.op('pool', lambda e: e.affine_select(out=iff[:], in_=iff[:], compare_op=ALU.not_equal, fill=1.0, base=0,
                                               pattern=[[-1, 128]], channel_multiplier=1), ['identf'], ['identf'])
        S.op('pool', lambda e: e.memset(ob[:], 1.0), (), ['onesb'])
        ep = self.epsc
        S.op('pool', lambda e: e.memset(ep[:], EPS), (), ['epsc'])

    def load_x(self):
        with contextlib.ExitStack() as es:
            xt = [self.sb("xtok%d" % i, [128, D], F32, es) for i in range(2)]
            ps = [self.psum("lxps%d" % i, [128, 512], F32, es) for i in range(4)]
            src = self._ap("xin").rearrange("(b p t) d -> b t p d", b=2, t=8)
            n = 0
            for b in range(2):
                for t in range(8):
                    buf = xt[n % 2]
                    bk = ('xtok', n % 2)
                    self.ld(buf[:], src[b, t], [bk])
                    pos = 1024 * b + 128 * t
                    for h in range(2):
                        pb = ps[(2 * n + h) % 4]
                        pk = ('lxps', (2 * n + h) % 4)
                        for jj in range(4):
                            j = 4 * h + jj
                            self.tr(pb[:, 128 * jj:128 * jj + 128], buf[:, 128 * j:128 * j + 128], self.identf[:],
                                    [bk, 'identf'], [pk])
                        dst = self.xT[:, 4 * h:4 * h + 4, pos:pos + 128]
                        srcp = pb[:].rearrange("p (j c) -> p j c", j=4)
                        self.copy(self.evac_eng(), dst, srcp, [pk],
                                  [('xT', j, pos // 512) for j in range(4 * h, 4 * h + 4)])
                    n += 1
        self.S.barrier()

    def adaln(self):
        S = self.S
        with contextlib.ExitStack() as es:
            rows = self.sb("ad_rows", [128, 3, 128], F32, es)
            rT = self.sb("ad_rT", [128, 3, 128], F32, es)
            scv = self.sb("ad_scv", [128, 8], F32, es)
            wa = [self.sb("ad_w%d" % i, [128, 8, 512], F32, es) for i in range(2)]
            pst = self.psum("ad_pst", [128, 512], F32, es)
            psm = self.psum("ad_psm", [128, 512], F32, es)
            S.op('dve', lambda e: e.memset(rows[:], 0.0), (), ['ad_rows'])
            self.ld(rows[0:8, 0, :], self._dr_cache['cvec'], ['ad_rows'], ['ad_rows'])
            self.ld(rows[8:16, 0, :], self._dr_cache['final_norm_w'], ['ad_rows'], ['ad_rows'])
            self.ld(rows[16:64, 0, :], self._dr_cache['norm_w'], ['ad_rows'], ['ad_rows'])
            self.ld(rows[:, 1, :], self._dr_cache['b_ada'][0:128, :], ['ad_rows'], ['ad_rows'])
            self.ld(rows[0:16, 2, :], self._dr_cache['b_ada'][128:144, :], ['ad_rows'], ['ad_rows'])
            for i in range(3):
                self.tr(pst[:, 128 * i:128 * i + 128], rows[:, i, :], self.identf[:], ['ad_rows', 'identf'], ['ad_pst'])
            self.copy('dve', rT[:].rearrange("p a b -> p (a b)"), pst[:, 0:384], ['ad_pst'], ['ad_rT'])
            self.act(scv[:], rT[:, 0, 0:8], AF.Silu, ['ad_rT'], ['ad_scv'])
            badaT = rT[:].rearrange("p a b -> p (a b)")[:, 128:128 + 144]
            n = 0
            for l in range(DEPTH):
                wv = self._dr_cache['w_ada'][l].rearrange("(kc p) f -> p kc f", p=128)
                for pc in range(18):
                    buf = wa[n % 2]
                    bk = ('ad_w', n % 2)
                    self.ld(buf[:], wv[:, :, 512 * pc:512 * pc + 512], [bk])
                    for ii in range(4):
                        i = 4 * pc + ii
                        for k in range(8):
                            self.mm(psm[:, l * 72 + i:l * 72 + i + 1], buf[:, k, 128 * ii:128 * ii + 128], scv[:, k:k + 1],
                                    k == 0, k == 7, [bk, 'ad_scv'], ['ad_psm'])
                    n += 1
            self.tt('dve', self.mod[:].rearrange("p l i -> p (l i)"), psm[:, 0:144], badaT, ALU.add,
                    ['ad_psm', 'ad_rT'], ['mod'])
            for l in range(DEPTH):
                for n3 in range(3):
                    nw = rT[:, 0, 16 + (l * 3 + n3) * 8:16 + (l * 3 + n3) * 8 + 8]
                    sh = self.mod[:, l, (3 * n3) * 8:(3 * n3) * 8 + 8]
                    sc = self.mod[:, l, (3 * n3 + 1) * 8:(3 * n3 + 1) * 8 + 8]
                    g = self.mod[:, l, (3 * n3 + 2) * 8:(3 * n3 + 2) * 8 + 8]
                    self.stt('dve', self.cA[:, l, n3, :], sc, 1.0, nw, ALU.add, ALU.mult, ['mod', 'ad_rT'], ['cA'])
                    self.copy('dve', self.cB[:, l, n3, :], sh, ['mod'], ['cB'])
                    self.ts('dve', self.cG[:, l, n3, :], g, (1.0 if n3 == 1 else 0.5), None, ALU.mult, None, ['mod'], ['cG'])
            self.copy('dve', self.cF[:], rT[:, 0, 8:16], ['ad_rT'], ['cF'])
            S.barrier()

    def norm_mod(self, blk, A, Bv, hm, hmcol, es_t, pss, hmkey):
        c0 = 512 * blk
        sq, tmp = es_t['sq'], es_t['tmp']
        xk = [('xT', j, blk) for j in range(8)]
        self.act(sq[:], self.xT[:, :, c0:c0 + 512], AF.Square, xk, ['nm_sq'])
        for j in range(8):
            self.mm(pss[:], self.onesb[:], sq[:, j, :], j == 0, j == 7, ['nm_sq', 'onesb'], ['nm_pss'])
        rb = blk % 2
        self.act(self.rs[:, rb, :], pss[:], AF.Sqrt, ['nm_pss'], [('rs', rb)], bias=self.epsc[:, 0:1], scale=1.0 / D)
        self.S.op('dve', lambda e: e.reciprocal(out=self.rs[:, rb, :], in_=self.rs[:, rb, :]), [('rs', rb)], [('rs', rb)])
        for j in range(8):
            tb = tmp[j % 2]
            self.tt('dve', tb[:], self.xT[:, j, c0:c0 + 512], self.rs[:, rb, :], ALU.mult,
                    [('xT', j, blk), ('rs', rb)], [('nm_tmp', j % 2)])
            if Bv is None:
                self.act(hm[:, j, hmcol:hmcol + 512], tb[:], AF.Identity, [('nm_tmp', j % 2)], [hmkey(j)], scale=A[:, j:j + 1])
            else:
                self.act(hm[:, j, hmcol:hmcol + 512], tb[:], AF.Identity, [('nm_tmp', j % 2)], [hmkey(j)],
                         scale=A[:, j:j + 1], bias=Bv[:, j:j + 1])

    def ffn(self, l, n):
        n3 = 0 if n == 0 else 2
        A, Bv, G = self.cA[:, l, n3, :], self.cB[:, l, n3, :], self.cG[:, l, n3, :]
        w_in = self._dr_cache['ffn_w_in'][l, n].rearrange("(kc p) f -> p kc f", p=128)
        w_out = self._dr_cache['ffn_w_out'][l, n].rearrange("(i p) d -> p i d", p=128)
        with contextlib.ExitStack() as es:
            hm = self.sb("f_hm", [128, 8, 1024], BF16, es)
            self.wi = [self.sb("wi%d" % i, [128, 8, 2, 256], BF16, es) for i in range(3)]
            self.wo = [self.sb("wo%d" % i, [128, NHT, 128], BF16, es) for i in range(3)]
            actb = self.sb("f_act", [128, NHT, 1024], BF16, es)
            sq = self.sb("f_sq", [128, 8, 512], BF16, es)
            tmp = [self.sb("f_tmp%d" % i, [128, 512], F32, es) for i in range(2)]
            sg = [self.sb("f_sg%d" % i, [128, 512], F32, es) for i in range(2)]
            pss = self.psum("f_pss", [128, 512], F32, es)
            pag = [self.psum("f_pag%d" % i, [128, 512], F32, es) for i in range(4)]
            pso = [self.psum("f_pso%d" % i, [128, 512], F32, es) for i in range(2)]
            est = {'sq': sq, 'tmp': tmp}
            npag = 0
            npso = 0
            for half in range(2):
                for bb in range(2):
                    blk = 2 * half + bb
                    self.norm_mod(blk, A, Bv, hm, 512 * bb, est, pss, lambda j, bb=bb: ('f_hm', j, bb))
                for pc in range(11):
                    slot = self.wi_n % 3
                    self.wi_n += 1
                    wb = self.wi[slot]
                    wk = ('wi', slot)
                    for ag in range(2):
                        cb = ag * DFF + 256 * pc
                        self.ld(wb[:, :, ag, :], w_in[:, :, cb:cb + 256], [wk], eng='pool')
                    for ii in range(2):
                        i = 2 * pc + ii
                        for bb in range(2):
                            pa = pag[npag % 4]
                            ka = ('f_pag', npag % 4)
                            pg = pag[(npag + 1) % 4]
                            kg = ('f_pag', (npag + 1) % 4)
                            npag += 2
                            for k in range(8):
                                self.mm(pa[:], wb[:, k, 0, 128 * ii:128 * ii + 128], hm[:, k, 512 * bb:512 * bb + 512],
                                        k == 0, k == 7, [wk, ('f_hm', k, bb)], [ka])
                            for k in range(8):
                                self.mm(pg[:], wb[:, k, 1, 128 * ii:128 * ii + 128], hm[:, k, 512 * bb:512 * bb + 512],
                                        k == 0, k == 7, [wk, ('f_hm', k, bb)], [kg])
                            sgi = (npag // 2) % 2
                            self.act(sg[sgi][:], pg[:], AF.Silu, [kg], [('f_sg', sgi)])
                            self.tt('dve', actb[:, i, 512 * bb:512 * bb + 512], sg[sgi][:], pa[:], ALU.mult,
                                    [('f_sg', sgi), ka], [('f_act', i, bb)])
                for jo in range(8):
                    slot = self.wo_n % 3
                    self.wo_n += 1
                    wb = self.wo[slot]
                    wk = ('wo', slot)
                    self.ld(wb[:], w_out[:, :, 128 * jo:128 * jo + 128], [wk], eng='pool')
                    for bb in range(2):
                        blk = 2 * half + bb
                        po = pso[npso % 2]
                        ko = ('f_pso', npso % 2)
                        npso += 1
                        for i in range(NHT):
                            self.mm(po[:], wb[:, i, :], actb[:, i, 512 * bb:512 * bb + 512], i == 0, i == NHT - 1,
                                    [wk, ('f_act', i, bb)], [ko])
                        xs = self.xT[:, jo, 512 * blk:512 * blk + 512]
                        self.stt('dve', xs, po[:], G[:, jo:jo + 1], xs, ALU.mult, ALU.add,
                                 [ko, ('xT', jo, blk)], [('xT', jo, blk)])

    def final(self):
        with contextlib.ExitStack() as es:
            yb = [self.sb("fin_y%d" % i, [128, 8, 512], F32, es) for i in range(1)]
            sq = self.sb("fin_sq", [128, 8, 512], BF16, es)
            tmp = [self.sb("fin_tmp%d" % i, [128, 512], F32, es) for i in range(2)]
            yt = [self.sb("fin_yt%d" % i, [128, D], F32, es) for i in range(2)]
            pss = self.psum("fin_pss", [128, 512], F32, es)
            ps = [self.psum("fin_ps%d" % i, [128, 512], F32, es) for i in range(4)]
            est = {'sq': sq, 'tmp': tmp}
            dst = self.y.rearrange("(b p t) d -> b t p d", b=2, t=8)
            n = 0
            for blk in range(4):
                self.norm_mod(blk, self.cF, None, yb[0], 0, est, pss, lambda j: ('fin_y', j))
                for tt4 in range(4):
                    tile = 4 * blk + tt4
                    b, t = tile // 8, tile % 8
                    ytb = yt[n % 2]
                    yk = ('fin_yt', n % 2)
                    for h in range(2):
                        pb = ps[(2 * n + h) % 4]
                        pk = ('fin_ps', (2 * n + h) % 4)
                        for jj in range(4):
                            j = 4 * h + jj
                            self.tr(pb[:, 128 * jj:128 * jj + 128], yb[0][:, j, 128 * tt4:128 * tt4 + 128], self.identf[:],
                                    [('fin_y', j), 'identf'], [pk])
                        self.copy(self.evac_eng(), ytb[:, 512 * h:512 * h + 512], pb[:], [pk], [yk])
                    self.st(dst[b, t], ytb[:], [yk])
                    n += 1

    def mixer(self, l):
        S = self.S
        A, Bv, G = self.cA[:, l, 1, :], self.cB[:, l, 1, :], self.cG[:, l, 1, :]
        with contextlib.ExitStack() as es:
            hm = self.sb("m_hm", [128, 8, NT], BF16, es)
            mo = None
            self.G2 = G
            with contextlib.ExitStack() as es2:
                sq = self.sb("m_sq", [128, 8, 512], BF16, es2)
                tmp = [self.sb("m_tmp%d" % i, [128, 512], F32, es2) for i in range(2)]
                pss = self.psum("m_pss", [128, 512], F32, es2)
                for blk in range(4):
                    self.norm_mod(blk, A, Bv, hm, 512 * blk, {'sq': sq, 'tmp': tmp}, pss,
                                  lambda j, blk=blk: ('m_hm', j, blk))
            import os
            mix = int(os.environ.get('MIX', '3'))
            S.barrier()
            if mix & 1:
                self.s5(l, hm, mo)
            S.barrier()
            if mix & 2:
                self.attn(l, hm, mo)

    def wout_part(self, l, ktiles, srcs, wbufs, psl, pskeys):
        wv = self._ap('w_out')[l].rearrange("(k p) d -> p k d", p=128)
        G = self.G2
        for i, kt in enumerate(ktiles):
            wb, wk = wbufs[i]
            self.ld(wb, wv[:, kt, :], [wk], eng='pool')
        n = 0
        for jo in range(8):
            for blk in range(4):
                po, pk = psl[n % len(psl)], pskeys[n % len(psl)]
                n += 1
                for i, kt in enumerate(ktiles):
                    wb, wk = wbufs[i]
                    src, kf = srcs[i]
                    self.mm(po[:], wb[:, 128 * jo:128 * jo + 128], src[:, 512 * blk:512 * blk + 512], i == 0, i == len(ktiles) - 1,
                            [wk, kf(blk)], [pk])
                xs = self.xT[:, jo, 512 * blk:512 * blk + 512]
                self.stt('dve', xs, po[:], G[:, jo:jo + 1], xs, ALU.mult, ALU.add, [pk, ('xT', jo, blk)], [('xT', jo, blk)])

    def cmul(self, outr, outi, ar, ai, br, bi, t1, t2, key_r, key_w, neg_im=False, eng='dve'):
        rd = list(key_r)
        self.tt(eng, t1, ar, br, ALU.mult, rd, ['cm_t1'])
        self.tt(eng, t2, ai, bi, ALU.mult, rd, ['cm_t2'])
        self.tt(eng, outr, t1, t2, ALU.subtract, ['cm_t1', 'cm_t2'] + rd, list(key_w))
        self.tt(eng, t1, ar, bi, ALU.mult, rd + list(key_w), ['cm_t1'])
        self.tt(eng, t2, ai, br, ALU.mult, rd + list(key_w), ['cm_t2'])
        if neg_im:
            self.stt(eng, outi, t1, -1.0, t2, ALU.mult, ALU.subtract, ['cm_t1', 'cm_t2'], list(key_w))
        else:
            self.tt(eng, outi, t1, t2, ALU.add, ['cm_t1', 'cm_t2'], list(key_w))

    def s5(self, l, hm, mo):
        S = self.S
        dr = self._dr_cache
        PI = math.pi
        with contextlib.ExitStack() as es:
            ToepT = [self.sb("s_toep%d" % d, [128, 16, 128], BF16, es) for d in range(2)]
            BSm = [self.sb("s_bsm%d" % d, [128, 16, 2, 64], BF16, es) for d in range(2)]
            CCm = [self.sb("s_ccm%d" % d, [128, 8, 2, 128], BF16, es) for d in range(2)]
            PWr = [self.sb("s_pwr%d" % d, [128, 8, 33], F32, es) for d in range(2)]
            PWi = [self.sb("s_pwi%d" % d, [128, 8, 33], F32, es) for d in range(2)]
            PWin = [self.sb("s_pwin%d" % d, [128, 8, 33], F32, es) for d in range(2)]
            h0r = [self.sb("s_h0r%d" % d, [128, 8], F32, es) for d in range(2)]
            h0i = [self.sb("s_h0i%d" % d, [128, 8], F32, es) for d in range(2)]
            U = self.sb("s_U", [128, 16, 256], BF16, es)
            flag = self.sb("s_flag", [128, 1], F32, es)
            self.ld(flag[:], dr['flag'], ['s_flag'])
            with contextlib.ExitStack() as ea:
                def t4(name):
                    return self.sb(name, [128, 8, 8, 16], F32, ea)
                cm1, cm2 = t4("sa_cm1"), t4("sa_cm2")
                BLr, BLi, CLr, CLi = t4("sa_blr"), t4("sa_bli"), t4("sa_clr"), t4("sa_cli")
                BSr, BSi, CCr, CCi = t4("sa_bsr"), t4("sa_bsi"), t4("sa_ccr"), t4("sa_cci")
                sm = self.sb("sa_sm", [128, 40, 8], F32, ea)
                bbr = self.sb("sa_bbr", [128, 8, 16], F32, ea)
                bbi = self.sb("sa_bbi", [128, 8, 16], F32, ea)
                cre = self.sb("sa_cre", [128, 8, 16], F32, ea)
                cim = self.sb("sa_cim", [128, 8, 16], F32, ea)
                Pr = self.sb("sa_Pr", [128, 8, 9], F32, ea)
                Pi_ = self.sb("sa_Pi", [128, 8, 9], F32, ea)
                Nr = self.sb("sa_Nr", [128, 8, 8], F32, ea)
                Ni = self.sb("sa_Ni", [128, 8, 8], F32, ea)
                Dcol = self.sb("sa_Dcol", [128, 16], F32, ea)
                cmk = [self.sb("sa_cmk%d" % d, [128, 128], F32, ea) for d in range(2)]
                tT = self.sb("sa_tT", [128, 128], F32, ea)
                cst = self.sb("sa_cst", [128, 2], F32, ea)
                psA = [self.psum("sa_ps%d" % i, [128, 512], F32, ea) for i in range(4)]
                S.op('dve', lambda e: e.memset(cst[:, 0:1], -PI), (), ['sa_cst'])
                self.ld(cmk[0][:], dr['cmask_f'], ['sa_cmk'])
                self.ld(cmk[1][:], dr['cmask_b'], ['sa_cmk'])
                for t in range(8):
                    self.ld(Dcol[16 * t:16 * t + 16, :], dr['s5_d'][l].rearrange("g c -> c g"), [('sa_Dcol', t)],
                            allow_slow_non_contiguous=True)
                npsA = 0
                ldsm = [self.sb("sa_ldsm%d" % d, [128, 3, 8], F32, ea) for d in range(2)]
                bre_d = [self.sb("sa_bre%d" % d, [128, 8, 16], F32, ea) for d in range(2)]
                bim_d = [self.sb("sa_bim%d" % d, [128, 8, 16], F32, ea) for d in range(2)]
                crow_d = [self.sb("sa_crow%d" % d, [128, 2, 2, 64], F32, ea) for d in range(2)]
                ldkeys = [[], []]

                def ldk(d):
                    k = ('sa_ld', d, len(ldkeys[d]))
                    ldkeys[d].append(k)
                    return [k]
                for d in range(2):
                    for gl in range(2):
                        ps_ = slice(64 * gl, 64 * gl + 64)
                        self.ld(ldsm[d][ps_, 0, :], dr['s5_a_re'][l, d, gl].rearrange("g p -> p g"), ldk(d), allow_slow_non_contiguous=True)
                        self.ld(ldsm[d][ps_, 1, :], dr['s5_a_im'][l, d, gl].rearrange("g p -> p g"), ldk(d), allow_slow_non_contiguous=True)
                        self.ld(ldsm[d][ps_, 2, :], dr['s5_log_step'][l, d, gl].partition_broadcast(64), ldk(d))
                        self.ld(h0r[d][ps_, :], dr['h0re'][l, d, gl].rearrange("g p -> p g"), ldk(d), allow_slow_non_contiguous=True)
                        self.ld(h0i[d][ps_, :], dr['h0im'][l, d, gl].rearrange("g p -> p g"), ldk(d), allow_slow_non_contiguous=True)
                        self.ld(bre_d[d][ps_, :, :], dr['s5_b_re'][l, d, gl].rearrange("g p c -> p g c"), ldk(d))
                        self.ld(bim_d[d][ps_, :, :], dr['s5_b_im'][l, d, gl].rearrange("g p c -> p g c"), ldk(d))
                        self.ld(crow_d[d][:, gl, 0, :], dr['s5_c_re'][l, d, gl], ldk(d))
                        self.ld(crow_d[d][:, gl, 1, :], dr['s5_c_im'][l, d, gl], ldk(d))
                for d in range(2):
                    K = ['sa']
                    are, aim, lst = ldsm[d][:, 0, :], ldsm[d][:, 1, :], ldsm[d][:, 2, :]
                    bre, bim, crow = bre_d[d], bim_d[d], crow_d[d]
                    S.op('dve', lambda e: e.memset(sm[:, 39, 0:1], 0.0), ldkeys[d] + K, K)
                    pc, pck = psA[npsA % 4], ('sa_ps', npsA % 4)
                    npsA += 1
                    for gl in range(2):
                        for ri in range(2):
                            self.mm(pc[64 * gl:64 * gl + 64, 128 * ri:128 * ri + 128], crow[:, gl, ri, :], self.identf[:], True, True, K + ['identf'], [pck])
                    self.copy('dve', cre[:].rearrange("p g c -> p (g c)"), pc[:, 0:128], [pck], K)
                    self.copy('dve', cim[:].rearrange("p g c -> p (g c)"), pc[:, 128:256], [pck], K)

                    import os
                    s5a = int(os.environ.get('S5A', '9'))
                    if s5a <= 1:
                        S.barrier()
                        return

                    def sop(fn):
                        S.op('dve', fn, K, K)
                    sl = lambda i: sm[:, i, :]
                    self.act(sl(3), lst, AF.Exp, K, K)
                    self.tt('dve', sl(4), are, sl(3), ALU.mult, K, K)
                    self.tt('dve', sl(5), aim, sl(3), ALU.mult, K, K)
                    self.act(sl(6), sl(4), AF.Exp, K, K)
                    self.act(sl(7), sl(4), AF.Exp, K, K, scale=-2.0)
                    MAGIC = 12582912.0
                    for (dst_i, off) in ((9, 0.0), (10, 0.25)):
                        self.ts('dve', sl(8), sl(5), 1.0 / (2 * PI), None, ALU.mult, None, K, K)
                        if off:
                            self.ts('dve', sl(8), sl(8), off, None, ALU.add, None, K, K)
                        self.ts('dve', sl(22), sl(8), MAGIC, None, ALU.add, None, K, K)
                        self.ts('dve', sl(22), sl(22), -MAGIC, None, ALU.add, None, K, K)
                        self.tt('dve', sl(8), sl(8), sl(22), ALU.subtract, K, K)
                        self.act(sl(dst_i), sl(8), AF.Sin, K, K, scale=2 * PI)
                    lr, li = sl(11), sl(12)
                    self.tt('dve', lr, sl(6), sl(10), ALU.mult, K, K)
                    self.tt('dve', li, sl(6), sl(9), ALU.mult, K, K)
                    ilr, ili = sl(13), sl(14)
                    self.tt('dve', ilr, lr, sl(7), ALU.mult, K, K)
                    self.stt('dve', ili, li, -1.0, sl(7), ALU.mult, ALU.mult, K, K)
                    self.tt('dve', sl(15), are, are, ALU.mult, K, K)
                    self.tt('dve', sl(16), aim, aim, ALU.mult, K, K)
                    self.tt('dve', sl(15), sl(15), sl(16), ALU.add, K, K)
                    sop(lambda e: e.reciprocal(out=sl(15), in_=sl(15)))
                    self.ts('dve', sl(16), lr, -1.0, None, ALU.add, None, K, K)
                    self.tt('dve', sl(17), sl(16), are, ALU.mult, K, K)
                    self.tt('dve', sl(18), li, aim, ALU.mult, K, K)
                    self.tt('dve', sl(17), sl(17), sl(18), ALU.add, K, K)
                    self.tt('dve', sl(17), sl(17), sl(15), ALU.mult, K, K)
                    self.tt('dve', sl(18), li, are, ALU.mult, K, K)
                    self.tt('dve', sl(19), sl(16), aim, ALU.mult, K, K)
                    self.tt('dve', sl(18), sl(18), sl(19), ALU.subtract, K, K)
                    self.tt('dve', sl(18), sl(18), sl(15), ALU.mult, K, K)
                    kb = lambda i: sm[:, i, :].unsqueeze(2).to_broadcast([128, 8, 16])
                    self.cmul(bbr[:], bbi[:], kb(17), kb(18), bre[:], bim[:], cm1[:, :, 0, :], cm2[:, :, 0, :], K, K)
                    sop(lambda e: e.memset(Pr[:, :, 0:1], 1.0))
                    sop(lambda e: e.memset(Pi_[:, :, 0:1], 0.0))
                    sop(lambda e: e.memset(Nr[:, :, 0:1], 1.0))
                    sop(lambda e: e.memset(Ni[:, :, 0:1], 0.0))
                    for k in range(1, 9):
                        self.cmul(Pr[:, :, k], Pi_[:, :, k], Pr[:, :, k - 1], Pi_[:, :, k - 1], lr, li, sl(20), sl(21), K, K)
                    for k in range(1, 8):
                        self.cmul(Nr[:, :, k], Ni[:, :, k], Nr[:, :, k - 1], Ni[:, :, k - 1], ilr, ili, sl(20), sl(21), K, K)
                    pr, pi = PWr[d], PWi[d]
                    self.copy('dve', pr[:, :, 1], Pr[:, :, 8], K, K)
                    self.copy('dve', pi[:, :, 1], Pi_[:, :, 8], K, K)
                    n = 1
                    while n < 32:
                        bshape = [128, 8, n]
                        self.cmul(pr[:, :, n + 1:2 * n + 1], pi[:, :, n + 1:2 * n + 1], pr[:, :, 1:n + 1], pi[:, :, 1:n + 1],
                                  pr[:, :, n:n + 1].to_broadcast(bshape), pi[:, :, n:n + 1].to_broadcast(bshape),
                                  cm1[:].rearrange("p a b c -> p a (b c)")[:, :, 0:n], cm2[:].rearrange("p a b c -> p a (b c)")[:, :, 0:n], K, K)
                        n *= 2
                    self.ts('dve', PWin[d][:, :, 1:33], pi[:, :, 1:33], -1.0, None, ALU.mult, None, K, K)
                    bsh = [128, 8, 8, 16]
                    bbR = bbr[:].unsqueeze(2).to_broadcast(bsh)
                    bbI = bbi[:].unsqueeze(2).to_broadcast(bsh)
                    cR = cre[:].unsqueeze(2).to_broadcast(bsh)
                    cI = cim[:].unsqueeze(2).to_broadcast(bsh)
                    pw = lambda T, sl_: T[:, :, sl_].unsqueeze(3).to_broadcast(bsh)
                    if d == 0:
                        self.cmul(BLr[:], BLi[:], bbR, bbI, pw(Nr, slice(0, 8)), pw(Ni, slice(0, 8)), cm1[:], cm2[:], K, K)
                        self.cmul(CLr[:], CLi[:], cR, cI, pw(Pr, slice(0, 8)), pw(Pi_, slice(0, 8)), cm1[:], cm2[:], K, K, neg_im=True)
                        self.cmul(BSr[:], BSi[:], bbR, bbI, pw(Pr, slice(7, None, -1)), pw(Pi_, slice(7, None, -1)), cm1[:], cm2[:], K, K)
                        self.cmul(CCr[:], CCi[:], cR, cI, pw(Pr, slice(1, 9)), pw(Pi_, slice(1, 9)), cm1[:], cm2[:], K, K, neg_im=True)
                    else:
                        self.cmul(BLr[:], BLi[:], bbR, bbI, pw(Pr, slice(0, 8)), pw(Pi_, slice(0, 8)), cm1[:], cm2[:], K, K)
                        self.cmul(CLr[:], CLi[:], cR, cI, pw(Nr, slice(0, 8)), pw(Ni, slice(0, 8)), cm1[:], cm2[:], K, K, neg_im=True)
                        self.copy('dve', BSr[:], BLr[:], K, K)
                        self.copy('dve', BSi[:], BLi[:], K, K)
                        self.cmul(CCr[:], CCi[:], cR, cI, pw(Pr, slice(8, 0, -1)), pw(Pi_, slice(8, 0, -1)), cm1[:], cm2[:], K, K, neg_im=True)
                    f2 = lambda T: T[:].rearrange("p a b c -> p a (b c)")
                    if s5a <= 2:
                        S.barrier()
                        return
                    for g in range(16):
                        gl, gh = g % 2, g // 2
                        ps_ = slice(64 * gl, 64 * gl + 64)
                        pt, ptk = psA[npsA % 4], ('sa_ps', npsA % 4)
                        npsA += 1
                        self.mm(pt[:, 0:128], f2(BLr)[ps_, gh, :], f2(CLr)[ps_, gh, :], True, False, K, [ptk])
                        self.mm(pt[:, 0:128], f2(BLi)[ps_, gh, :], f2(CLi)[ps_, gh, :], False, True, K, [ptk])
                        if d == 0:
                            self.tt('dve', tT[:], pt[:, 0:128], cmk[0][:], ALU.mult, [ptk, 'sa_cmk'], ['sa_tT'])
                            self.stt('dve', ToepT[0][:, g, :], self.identf[:], Dcol[:, g:g + 1], tT[:], ALU.mult, ALU.add,
                                     ['sa_tT', 'identf'] + [('sa_Dcol', t_) for t_ in range(8)], [('s_toep', 0)])
                        else:
                            self.tt('dve', ToepT[1][:, g, :], pt[:, 0:128], cmk[1][:], ALU.mult, [ptk, 'sa_cmk'], [('s_toep', 1)])
                    if s5a <= 3:
                        S.barrier()
                        return
                    for ri, T in enumerate((BSr, BSi)):
                        for g8 in range(2):
                            pt, ptk = psA[npsA % 4], ('sa_ps', npsA % 4)
                            npsA += 1
                            for hh in range(4):
                                gh = 4 * g8 + hh
                                self.tr(pt[:, 128 * hh:128 * hh + 128], f2(T)[:, gh, :], self.identf[:], K + ['identf'], [ptk])
                            self.copy('act', BSm[d][:, 8 * g8:8 * g8 + 8, ri, :], pt[:].rearrange("p (g q) -> p g q", g=8), [ptk], [('s_bsm', d)])
                    if s5a <= 4:
                        S.barrier()
                        return
                    self.copy('dve', CCm[d][:, :, 0, :], f2(CCr), K, [('s_ccm', d)])
                    self.copy('dve', CCm[d][:, :, 1, :], f2(CCi), K, [('s_ccm', d)])
                    if s5a <= 5:
                        S.barrier()
                        return
            S.barrier()
            import os
            s5stop = os.environ.get('S5STOP', 'Z')
            if s5stop == 'A':
                return
            with contextlib.ExitStack() as eb:
                wu = self.sb("sb_wu", [128, 8, 256], BF16, eb)
                ub = self.sb("sb_ub", [128, 2, 16, 8, 16], BF16, eb)
                psu = [self.psum("sb_psu%d" % i, [128, 512], F32, eb) for i in range(2)]
                pst = [self.psum("sb_pst%d" % i, [128, 1024], BF16, eb) for i in range(2)]
                self.ld(wu[:], self._ap('w_in')[l].rearrange("(k p) f -> p k f", p=128)[:, :, 1568:1824], ['sb_wu'], eng='pool')
                n = 0
                for b in range(2):
                    for t in range(8):
                        pos = 1024 * b + 128 * t
                        pu, puk = psu[n % 2], ('sb_psu', n % 2)
                        n += 1
                        for k in range(8):
                            self.mm(pu[:, 0:256], hm[:, k, pos:pos + 128], wu[:, k, :], k == 0, k == 7,
                                    ['sb_wu', ('m_hm', k, pos // 512)], [puk])
                        self.copy(self.evac_eng(), ub[:, b, :, t, :], pu[:, 0:256].rearrange("p (g c) -> p g c", g=16), [puk], [('sb_ub', b)])
                n = 0
                for b in range(2):
                    for q in range(4):
                        pt, ptk = pst[n % 2], ('sb_pst', n % 2)
                        n += 1
                        for gg in range(4):
                            g = 4 * q + gg
                            self.tr(pt[:, 128 * gg:128 * gg + 128], ub[:, b, g, :, :].rearrange("p t c -> p (t c)"), self.identb[:],
                                    [('sb_ub', b), 'identb'], [ptk])
                        self.copy(self.evac_eng(), U[:, 4 * q:4 * q + 4, 128 * b:128 * b + 128],
                                  pt[:, 0:512].rearrange("p (g c) -> p g c", g=4), [ptk], ['s_U'])
            S.barrier()
            if s5stop == 'B':
                return
            with contextlib.ExitStack() as ec:
                Hp = [[self.sb("sc_hp%d%d" % (d, ri), [128, 8, 256], BF16, ec) for ri in range(2)] for d in range(2)]
                with contextlib.ExitStack() as ec2:
                    La = [self.sb("sc_la%d" % ri, [128, 8, 256], F32, ec2) for ri in range(2)]
                    Lb = [self.sb("sc_lb%d" % ri, [128, 8, 256], F32, ec2) for ri in range(2)]
                    Cy = [self.sb("sc_cy%d" % ri, [128, 8, 8], F32, ec2) for ri in range(2)]
                    sm2 = self.sb("sc_sm", [128, 4, 8], F32, ec2)
                    Fsb = self.sb("sc_F", [128, 2, 64], F32, ec2)
                    FT = self.sb("sc_FT", [64, 2, 128], F32, ec2)
                    psS = [[self.psum("sc_ps%d%d" % (ri, q), [128, 512], F32, ec2) for q in range(3)] for ri in range(2)]
                    psF = self.psum("sc_psF", [128, 512], F32, ec2)
                    for d in range(2):
                        KL = ['sc_L']
                        for ri in range(2):
                            for q4 in range(4):
                                pb, pbk = psS[ri][q4 % 3], ('sc_ps', ri, q4 % 3)
                                for hh in range(2):
                                    gh = 2 * q4 + hh
                                    for gl in range(2):
                                        g = 2 * gh + gl
                                        self.mm(pb[64 * gl:64 * gl + 64, 256 * hh:256 * hh + 256], BSm[d][:, g, ri, :], U[:, g, :], True, True,
                                                [('s_bsm', d), 's_U'], [pbk])
                                self.copy(self.evac_eng(), La[ri][:, 2 * q4:2 * q4 + 2, :], pb[:].rearrange("p (a c) -> p a c", a=2), [pbk], KL)
                        cur, nxt = La, Lb
                        v5 = lambda T: T[:].rearrange("p a (s k) -> p a s k", k=32)
                        for dd in (1, 2, 4, 8, 16):
                            for ri in range(2):
                                if d == 0:
                                    self.copy('act', v5(nxt[ri])[:, :, :, 0:dd], v5(cur[ri])[:, :, :, 0:dd], KL, KL)
                                else:
                                    self.copy('act', v5(nxt[ri])[:, :, :, 32 - dd:32], v5(cur[ri])[:, :, :, 32 - dd:32], KL, KL)
                            for gh in range(8):
                                vv = lambda T: T[:, gh, :].rearrange("p (s k) -> p s k", k=32)
                                if d == 0:
                                    dst = slice(dd, 32)
                                    src = slice(0, 32 - dd)
                                else:
                                    dst = slice(0, 32 - dd)
                                    src = slice(dd, 32)
                                lr_ = PWr[d][:, gh, dd:dd + 1]
                                li_ = PWi[d][:, gh, dd:dd + 1]
                                lin_ = PWin[d][:, gh, dd:dd + 1]
                                self.stt('dve', vv(nxt[0])[:, :, dst], vv(cur[0])[:, :, src], lr_, vv(cur[0])[:, :, dst], ALU.mult, ALU.add, KL, KL)
                                self.stt('dve', vv(nxt[0])[:, :, dst], vv(cur[1])[:, :, src], lin_, vv(nxt[0])[:, :, dst], ALU.mult, ALU.add, KL, KL)
                                self.stt('dve', vv(nxt[1])[:, :, dst], vv(cur[0])[:, :, src], li_, vv(cur[1])[:, :, dst], ALU.mult, ALU.add, KL, KL)
                                self.stt('dve', vv(nxt[1])[:, :, dst], vv(cur[1])[:, :, src], lr_, vv(nxt[1])[:, :, dst], ALU.mult, ALU.add, KL, KL)
                            cur, nxt = nxt, cur
                        L = cur
                        E = [v5(L[ri])[:, :, :, 31 if d == 0 else 0] for ri in range(2)]
                        l32r, l32i = PWr[d][:, :, 32], PWi[d][:, :, 32]
                        order = list(range(8)) if d == 0 else list(range(7, -1, -1))
                        s0 = order[0]
                        self.copy('dve', Cy[0][:, :, s0], h0r[d][:], KL, KL)
                        self.copy('dve', Cy[1][:, :, s0], h0i[d][:], KL, KL)
                        for idx in range(1, 8):
                            s, sp_ = order[idx], order[idx - 1]
                            self.cmul(sm2[:, 0, :], sm2[:, 1, :], l32r, l32i, Cy[0][:, :, sp_], Cy[1][:, :, sp_], sm2[:, 2, :], sm2[:, 3, :], KL, KL)
                            for ri in range(2):
                                self.tt('dve', sm2[:, ri, :], sm2[:, ri, :], E[ri][:, :, sp_], ALU.add, KL, KL)
                                self.ts('dve', Cy[ri][:, :, s], sm2[:, ri, :], flag[:, 0:1], None, ALU.mult, None, KL + ['s_flag'], KL)
                        sh4 = [128, 8, 8, 32]
                        if d == 0:
                            pwv = lambda T: T[:, :, 1:33].unsqueeze(2).to_broadcast(sh4)
                        else:
                            pwv = lambda T: T[:, :, 32:0:-1].unsqueeze(2).to_broadcast(sh4)
                        cyv = lambda ri: Cy[ri][:].unsqueeze(3).to_broadcast(sh4)
                        t1v = v5(nxt[0])
                        for (ri, a, b_, op) in ((0, PWr[d], 0, ALU.add), (0, PWi[d], 1, ALU.subtract), (1, PWr[d], 1, ALU.add), (1, PWi[d], 0, ALU.add)):
                            self.tt('dve', t1v, pwv(a), cyv(b_), ALU.mult, KL, ['sc_t1'])
                            self.tt('dve', v5(L[ri]), v5(L[ri]), t1v, op, KL + ['sc_t1'], KL)
                        for ri in range(2):
                            hv = v5(Hp[d][ri])
                            if d == 0:
                                self.copy('act', hv[:, :, :, 1:32], v5(L[ri])[:, :, :, 0:31], KL, [('sc_hp', d)])
                                self.copy('dve', hv[:, :, :, 0], Cy[ri][:], KL, [('sc_hp', d)])
                            else:
                                self.copy('act', hv[:, :, :, 0:31], v5(L[ri])[:, :, :, 1:32], KL, [('sc_hp', d)])
                                self.copy('dve', hv[:, :, :, 31], Cy[ri][:], KL, [('sc_hp', d)])
                        for ri in range(2):
                            self.copy('dve', Fsb[:, ri, :].rearrange("p (a s) -> p a s", a=8), E[ri], KL, ['sc_F'])
                            self.tr(psF[0:64, 128 * ri:128 * ri + 128], Fsb[:, ri, :], self.identf[:], ['sc_F', 'identf'], ['sc_psF'])
                        self.copy('dve', FT[:].rearrange("p a b -> p (a b)"), psF[0:64, 0:256], ['sc_psF'], ['sc_FT'])
                        for ri, nm in enumerate(('ns5re', 'ns5im')):
                            for gh in range(8):
                                self.st(self.o[nm][l, d][:, 128 * gh:128 * gh + 128], FT[8 * gh:8 * gh + 8, ri, :], ['sc_FT'])
                S.barrier()
                if s5stop == 'C':
                    return
                with contextlib.ExitStack() as ed:
                    Ysb = self.sb("sd_Y", [128, 16, 256], BF16, ed)
                    g1 = [self.sb("sd_g%d" % i, [128, 512], F32, ed) for i in range(2)]
                    psY = [self.psum("sd_ps%d" % i, [128, 512], F32, ed) for i in range(3)]
                    for q in range(8):
                        py, pyk = psY[q % 3], ('sd_ps', q % 3)
                        for hh in range(2):
                            g = 2 * q + hh
                            gl, gh = g % 2, g // 2
                            ps_ = slice(64 * gl, 64 * gl + 64)
                            o = py[:, 256 * hh:256 * hh + 256]
                            for d in range(2):
                                self.mm(o, ToepT[d][:, g, :], U[:, g, :], d == 0, False, [('s_toep', d), 's_U'], [pyk])
                                self.mm(o, CCm[d][ps_, gh, 0, :], Hp[d][0][ps_, gh, :], False, False, [('s_ccm', d), ('sc_hp', d)], [pyk])
                                self.mm(o, CCm[d][ps_, gh, 1, :], Hp[d][1][ps_, gh, :], False, d == 1, [('s_ccm', d), ('sc_hp', d)], [pyk])
                        gb, gk = g1[q % 2], ('sd_g', q % 2)
                        self.act(gb[:], py[:], AF.Square, [pyk], [gk])
                        self.ts('dve', gb[:], gb[:], 0.044715, None, ALU.mult, None, [gk], [gk])
                        self.ts('dve', gb[:], gb[:], 1.0, None, ALU.add, None, [gk], [gk])
                        self.tt('dve', gb[:], gb[:], py[:], ALU.mult, [gk, pyk], [gk])
                        self.act(gb[:], gb[:], AF.Sigmoid, [gk], [gk], scale=1.5957691216)
                        self.tt('dve', Ysb[:, 2 * q:2 * q + 2, :], gb[:].rearrange("p (a c) -> p a c", a=2),
                                py[:].rearrange("p (a c) -> p a c", a=2), ALU.mult, [gk, pyk], ['sd_Y'])
                    S.barrier()
                    ytok = self.sb("se_ytok", [128, 2, 8, 256], BF16, ed)
                    ygT = self.sb("se_ygT", [128, 2, NT], BF16, ed)
                    mos = self.sb("se_mo", [128, 2, NT], BF16, ed)
                    wob = [self.sb("se_wo%d" % i, [128, D], BF16, ed) for i in range(2)]
                    wg = self.sb("se_wg", [128, 2, 256], BF16, ed)
                    bg = self.sb("se_bg", [128, 2], F32, ed)
                    sg = [self.sb("se_sg%d" % i, [128, 512], F32, ed) for i in range(2)]
                    pst = [self.psum("se_pst%d" % i, [128, 1024], BF16, ed) for i in range(2)]
                    psg = [self.psum("se_psg%d" % i, [128, 512], F32, ed) for i in range(2)]
                    self.ld(wg[:], dr['s5_w_glu'][l].rearrange("(k p) f -> p k f", p=128), ['se_wg'], eng='pool')
                    self.ld(bg[:], dr['s5_b_glu'][l].rearrange("a p -> p a"), ['se_bg'], allow_slow_non_contiguous=True)
                    n = 0
                    for b in range(2):
                        for q in range(4):
                            pt, ptk = pst[n % 2], ('se_pst', n % 2)
                            n += 1
                            for gg in range(4):
                                g = 4 * q + gg
                                self.tr(pt[:, 128 * gg:128 * gg + 128], Ysb[:, g, 128 * b:128 * b + 128], self.identb[:], ['sd_Y', 'identb'], [ptk])
                            dst = ytok[:, b].rearrange("p t (g c) -> p g t c", c=16)[:, 4 * q:4 * q + 4]
                            self.copy(self.evac_eng(), dst, pt[:, 0:512].rearrange("p (g t c) -> p g t c", g=4, t=8), [ptk], [('se_ytok', b)])
                    for b in range(2):
                        for f in range(2):
                            for h4 in range(2):
                                pt, ptk = pst[n % 2], ('se_pst', n % 2)
                                n += 1
                                for tt_ in range(4):
                                    t = 4 * h4 + tt_
                                    self.tr(pt[:, 128 * tt_:128 * tt_ + 128], ytok[:, b, t, 128 * f:128 * f + 128], self.identb[:],
                                            [('se_ytok', b), 'identb'], [ptk])
                                blk = 2 * b + h4
                                self.copy(self.evac_eng(), ygT[:, f, 512 * blk:512 * blk + 512], pt[:, 0:512], [ptk], [('se_ygT', f, blk)])
                    n = 0
                    for fo in range(2):
                        for blk in range(4):
                            pg, pgk = psg[n % 2], ('se_psg', n % 2)
                            sgb, sgk = sg[n % 2], ('se_sg', n % 2)
                            n += 1
                            for k in range(2):
                                self.mm(pg[:], wg[:, k, 128 * fo:128 * fo + 128], ygT[:, k, 512 * blk:512 * blk + 512], k == 0, k == 1,
                                        ['se_wg', ('se_ygT', k, blk)], [pgk])
                            self.act(sgb[:], pg[:], AF.Sigmoid, [pgk, 'se_bg'], [sgk], bias=bg[:, fo:fo + 1])
                            self.tt('dve', mos[:, fo, 512 * blk:512 * blk + 512], sgb[:], ygT[:, fo, 512 * blk:512 * blk + 512], ALU.mult,
                                    [sgk, ('se_ygT', fo, blk)], [('se_mo', fo, blk)])
                    self.wout_part(l, [6, 7], [(mos[:, fo, :], (lambda blk, fo=fo: ('se_mo', fo, blk))) for fo in range(2)],
                                   [(wob[i][:], ('se_wo', i)) for i in range(2)], psg, [('se_psg', 0), ('se_psg', 1)])

    def rope(self, src5, dsts, tt, nb, tmps, rkey, wkeys):
        b, t = tt // 8, tt % 8
        tk = ['rp_t']
        if nb == 1:
            cosb = self.rc[:, b, t, :].rearrange("p (a f) -> p a f", a=2)
            sinb = self.rsn[:, b, t, :].rearrange("p (a f) -> p a f", a=2)
            x1, x2 = src5[:, 0, :, 0, :], src5[:, 0, :, 1, :]
            t1, t2, t3, t4 = [T[:, 0:16].rearrange("p (a f) -> p a f", a=2) for T in tmps]
        else:
            sh = [128, nb, 2, 8]
            cosb = self.rc[:, b, t, :].rearrange("p (a f) -> p a f", a=2).unsqueeze(1).to_broadcast(sh)
            sinb = self.rsn[:, b, t, :].rearrange("p (a f) -> p a f", a=2).unsqueeze(1).to_broadcast(sh)
            x1, x2 = src5[:, :, :, 0, :], src5[:, :, :, 1, :]
            t1, t2, t3, t4 = [T[:, 0:nb * 16].rearrange("p (n a f) -> p n a f", a=2, f=8) for T in tmps]
        self.tt('dve', t1, x1, cosb, ALU.mult, rkey + ['rope_tab'], tk)
        self.tt('dve', t2, x2, sinb, ALU.mult, rkey + ['rope_tab'], tk)
        self.tt('dve', t3, x1, sinb, ALU.mult, rkey + ['rope_tab'], tk)
        self.tt('dve', t4, x2, cosb, ALU.mult, rkey + ['rope_tab'], tk)
        for (bs, dst5), wk in zip(dsts, wkeys):
            if nb == 1:
                self.tt('dve', dst5[:, 0, :, 0, :], t1, t2, ALU.subtract, tk, [wk])
                self.tt('dve', dst5[:, 0, :, 1, :], t3, t4, ALU.add, tk, [wk])
            else:
                self.tt('dve', dst5[:, :, :, 0, :], t1[:, bs], t2[:, bs], ALU.subtract, tk, [wk])
                self.tt('dve', dst5[:, :, :, 1, :], t3[:, bs], t4[:, bs], ALU.add, tk, [wk])

    def attn(self, l, hm, mo):
        S = self.S
        dr = self._dr_cache
        lam_init = 0.8 - 0.6 * math.exp(-0.3 * l)
        with contextlib.ExitStack() as es:
            sb = lambda n, s, d: self.sb(n, s, d, es)
            self.rc = sb("a_rc", [128, 2, 8, 16], F32)
            self.rsn = sb("a_rs", [128, 2, 8, 16], F32)
            aq = sb("a_aq", [128, 16, 9], F32)
            ak = sb("a_ak", [128, 18, 9], F32)
            QS = sb("a_QS", [128, 16, 128], BF16)
            KS = sb("a_KS", [128, 18, 128], BF16)
            QT = [sb("a_QT%d" % i, [128, NT], BF16) for i in range(2)]
            KT = [sb("a_KT%d" % i, [128, NT + 256], BF16) for i in range(2)]
            Vh = [sb("a_V%d" % i, [128, 18, 72], BF16) for i in range(2)]
            PT = [sb("a_PT%d" % i, [128, 512], BF16) for i in range(4)]
            moh = [sb("a_moh%d" % i, [128, NT], BF16) for i in range(2)]
            wob = [sb("a_wo%d" % i, [128, D], BF16) for i in range(2)]
            rt = [sb("a_rt%d" % i, [128, 64], F32) for i in range(4)]
            o0 = sb("a_o0", [128, 4, 64], F32)
            o1 = sb("a_o1", [128, 4, 64], F32)
            osq = sb("a_osq", [128, 4, 64], F32)
            ost2 = [sb("a_ost%d" % i, [128, 4, 64], BF16) for i in range(2)]
            sml = sb("a_sml", [128, 16], F32)
            dl = sb("a_dl", [128, 128], F32)
            subw = sb("a_subw", [128, 64], F32)
            lamt = sb("a_lam", [128, 4], F32)
            oTs = [sb("a_oTs%d" % i, [128, 512], F32) for i in range(2)]
            edf = contextlib.ExitStack()
            wh = [self.sb("a_wh%d" % i, [128, 8, 192], BF16, edf) for i in range(2)]
            cdk = self.sb("a_cdk", [128, 2, 384], F32, edf)
            cdv = self.sb("a_cdv", [128, 2, 384], F32, edf)
            kvo = [self.sb("a_kvo%d" % i, [128, 128], F32, edf) for i in range(2)]
            psp = [self.psum("a_psp%d" % i, [128, 512], F32, es) for i in range(2)]
            pstr = [self.psum("a_pst%d" % i, [128, 1024], BF16, es) for i in range(1)]
            NPSS = 3
            pss = [self.psum("a_pss%d" % i, [128, 512], F32, es) for i in range(NPSS)]
            pso = [self.psum("a_pso%d" % i, [128, 512], F32, es) for i in range(2)]
            cnt = {'psp': 0, 'pst': 0, 'pss': 0, 'PT': 0, 'kvo': 0}
            pss_l = [(pss[i][:], ('a_pss', i)) for i in range(NPSS)] + [(pstr[0][:].bitcast(F32), ('a_pst', 0))]
            assert list(pss_l[3][0].shape) == [128, 512], pss_l[3][0].shape

            self.ld(self.rc[:], dr['ropec'].rearrange("(b p t) f -> p b t f", b=2, t=8), ['rope_tab'])
            self.ld(self.rsn[:], dr['ropes'].rearrange("(b p t) f -> p b t f", b=2, t=8), ['rope_tab'])
            self.ld(aq[:].rearrange("p (b t) f -> p b t f", b=2), dr['augq'].rearrange("(b p t) f -> p b t f", b=2, t=8), ['a_aq'])
            self.ld(ak[:, 0:16, :].rearrange("p (b t) f -> p b t f", b=2), dr['augk'][0:NT].rearrange("(b p t) f -> p b t f", b=2, t=8), ['a_ak'])
            self.ld(ak[:, 16:18, :], dr['augk'][NT:NT + 256].rearrange("(i p) f -> p i f", p=128), ['a_ak'])
            self.ld(cdk[:], dr['ctx_dk'][l].rearrange("(i p) f -> p i f", p=128), ['a_cdk'])
            self.ld(cdv[:], dr['ctx_dv'][l].rearrange("(i p) f -> p i f", p=128), ['a_cdv'])
            self.ld(dl[:], dr['diff_lambda'][l].partition_broadcast(128), ['a_dl'])
            self.ld(subw[:], dr['diff_subln_w'][l].partition_broadcast(128), ['a_subw'])
            LK = ['a_lam']
            self.tt('dve', o0[:, 0, :].rearrange("p (a f) -> p a f", a=2), dl[:].rearrange("p (a b f) -> p a b f", a=2, b=2)[:, :, 0, :],
                    dl[:].rearrange("p (a b f) -> p a b f", a=2, b=2)[:, :, 1, :], ALU.mult, ['a_dl'], LK)
            S.op('dve', lambda e: e.tensor_reduce(out=lamt[:, 1:3], in_=o0[:, 0, :].rearrange("p (a f) -> p a f", a=2),
                                                  axis=mybir.AxisListType.X, op=ALU.add), LK, LK)
            self.act(lamt[:, 1:3], lamt[:, 1:3], AF.Exp, LK, LK)
            self.tt('dve', lamt[:, 3:4], lamt[:, 2:3], lamt[:, 1:2], ALU.subtract, LK, LK)
            self.ts('dve', lamt[:, 0:1], lamt[:, 3:4], -lam_init, None, ALU.add, None, LK, LK)
            self.ts('dve', subw[:], subw[:], 1.0 - lam_init, None, ALU.mult, None, ['a_subw'], ['a_subw'])
            S.op('dve', lambda e: e.memset(self.epsc[:, 0:1], EPS), (), ['epsc'])

            def init_staging(qcols, kcols):
                S.op('dve', lambda e: e.memset(QS[:], 0.0), ['a_QS'], ['a_QS'])
                S.op('dve', lambda e: e.memset(KS[:], 0.0), ['a_KS'], ['a_KS'])
                for c0 in qcols:
                    self.copy('dve', QS[:, :, c0:c0 + 9], aq[:], ['a_aq'], ['a_QS'])
                for c0 in kcols:
                    self.copy('dve', KS[:, :, c0:c0 + 9], ak[:], ['a_ak'], ['a_KS'])
            for i in range(2):
                S.op('dve', lambda e, i=i: e.memset(Vh[i][:, :, 64:65], 1.0), [('a_V', i)], [('a_V', i)])

            def transposes(src, ntile, dstT, skey, dkey):
                for q in range((ntile + 3) // 4):
                    k = cnt['pst']
                    cnt['pst'] += 1
                    pt, ptk = pstr[0], ('a_pst', 0)
                    m = min(4, ntile - 4 * q)
                    for i in range(m):
                        self.tr(pt[:, 128 * i:128 * i + 128], src[:, 4 * q + i, :], self.identb[:], [skey, 'identb'], [ptk])
                    self.copy(self.evac_eng(), dstT[:, 512 * q:512 * q + 128 * m], pt[:, 0:128 * m], [ptk], [dkey])

            import collections
            pend = collections.deque()
            cur = {'defer': None}

            def emit_chunk(A, B):
                if A is not None:
                    A()
                if pend:
                    pend.popleft()()
                if B is not None:
                    pend.append(B)

            def flush_pend():
                while pend:
                    pend.popleft()()

            def core(hs, comps, scale, post, filler=None):
                qt_, kt_, vh_ = QT[hs], KT[hs], Vh[hs]
                ncmp = len(comps)
                steps = [(qb, ci, kt) for qb in range(4) for kt in range(18) for ci in range(ncmp)]
                n = len(steps)
                slots = {}

                def score(i):
                    qb, ci, kt = steps[i]
                    r0, nr = comps[ci]
                    k = cnt['pss']
                    cnt['pss'] += 1
                    ps_, psk = pss_l[k % 4]
                    slots[i] = (ps_, psk)
                    self.mm(ps_, kt_[r0:r0 + nr, 128 * kt:128 * kt + 128], qt_[r0:r0 + nr, 512 * qb:512 * qb + 512], True, True,
                            [('a_KT', hs), ('a_QT', hs)], [psk])
                filler = list(filler or [])
                stride = max(1, n // (len(filler) + 1)) if filler else 0
                urgent = []
                cur_i = [0]
                cur['defer'] = lambda A, B: urgent.append([cur_i[0] + 2, A, B])
                for j in range(2):
                    score(j)
                for i in range(n):
                    qb, ci, kt = steps[i]
                    cur_i[0] = i
                    if ncmp == 2:
                        if i % 2 == 0:
                            for j in (i + 2, i + 3):
                                if j < n:
                                    score(j)
                    elif i + 2 < n:
                        score(i + 2)
                    ps_, psk = slots.pop(i)
                    po, pok = pso[ci], ('a_pso', ci)
                    k2 = cnt['PT']
                    cnt['PT'] += 1
                    pt, ptk = PT[k2 % 4], ('a_PT', k2 % 4)
                    self.act(pt[:], ps_, AF.Exp, [psk], [ptk], scale=scale)
                    self.mm(po[0:65, :], vh_[:, kt, 0:65], pt[:], kt == 0, kt == 17, [ptk, ('a_V', hs)], [pok])
                    if kt == 17:
                        self.copy('dve', oTs[ci][0:65, :], po[0:65, :], [pok], [('a_oTs', ci)])
                        st = {}

                        def A(ci=ci, st=st):
                            pb, pbk = pbank()
                            st['b'] = (pb, pbk)
                            for qt in range(4):
                                self.tr(pb[:, 128 * qt:128 * qt + 65], oTs[ci][0:65, 128 * qt:128 * qt + 128], self.identf[0:65, 0:65],
                                        [('a_oTs', ci), 'identf'], [pbk])

                        def B(ci=ci, st=st, qb=qb):
                            pb, pbk = st['b']
                            normalize(pb, pbk, ci, o0[:] if ci == 0 else o1[:])
                            if ci == ncmp - 1:
                                post(qb)
                        cur['defer'](A, B)
                    if urgent and i >= urgent[0][0]:
                        _, A_, B_ = urgent.pop(0)
                        emit_chunk(A_, B_)
                    elif filler and (i + 1) % stride == 0:
                        emit_chunk(*filler.pop(0))
                while urgent or filler or pend:
                    if urgent:
                        _, A_, B_ = urgent.pop(0)
                        emit_chunk(A_, B_)
                    elif filler:
                        emit_chunk(*filler.pop(0))
                    else:
                        pend.popleft()()

            def normalize(pb, pbk, ci, dst):
                po = pb[:].rearrange("p (q f) -> p q f", f=128)
                S.op('dve', lambda e: e.reciprocal(out=sml[:, 4 * ci:4 * ci + 4], in_=po[:, :, 64]), [pbk], ['a_sml'])
                self.tt('dve', dst, po[:, :, 0:64], sml[:, 4 * ci:4 * ci + 4].unsqueeze(2).to_broadcast([128, 4, 64]), ALU.mult,
                        [pbk, 'a_sml'], ['a_o'])

            def out_transposes(qb, jt, roff):
                st = {}

                def A():
                    pt, ptk = pbank()
                    st['b'] = (pt, ptk)
                    for qt in range(4):
                        self.mm(pt[roff:roff + 64, 128 * qt:128 * qt + 128], ost2[qb % 2][:, qt, :], self.identb[:], True, True,
                                [('a_ost', qb % 2), 'identb'], [ptk])

                def B():
                    pt, ptk = st['b']
                    self.copy(self.evac_eng(), moh[jt % 2][roff:roff + 64, 512 * qb:512 * qb + 512], pt[roff:roff + 64, :], [ptk],
                              [('a_moh', jt % 2, qb)])
                cur['defer'](A, B)

            def pair_wout(jt):
                self.wout_part(l, [jt], [(moh[jt % 2][:], (lambda blk, jt=jt: ('a_moh', jt % 2, blk)))],
                               [(wob[jt % 2][:], ('a_wo', jt % 2))], psp, [('a_psp', 0), ('a_psp', 1)])

            w_in_v = self._ap('w_in')[l].rearrange("(k p) f -> p k f", p=128)
            import os
            att = int(os.environ.get('ATT', '99'))
            init_staging((32, 96), (32, 96))
            if att <= 1:
                S.barrier()
                edf.close()
                return
            pk = {'n': 0}

            def pbank():
                k = pk['n']
                pk['n'] += 1
                return psp[k % 2], ('a_psp', k % 2)

            def linearize(chunks):
                seq, prevB = [], None
                for A, B in chunks:
                    if A is not None:
                        seq.append(A)
                    if prevB is not None:
                        seq.append(prevB)
                    prevB = B
                if prevB is not None:
                    seq.append(prevB)
                return seq

            def tr_chunks(src, ntile, dstT, skey, dkey):
                out = []
                for q in range((ntile + 3) // 4):
                    m = min(4, ntile - 4 * q)
                    st = {}

                    def A(q=q, m=m, st=st):
                        pb, pbk = pbank()
                        st['b'] = (pb, pbk)
                        ptv = pb[:].bitcast(BF16)
                        for i in range(m):
                            self.tr(ptv[:, 128 * i:128 * i + 128], src[:, 4 * q + i, :], self.identb[:], [skey, 'identb'], [pbk])

                    def B(q=q, m=m, st=st):
                        pb, pbk = st['b']
                        ptv = pb[:].bitcast(BF16)
                        self.copy(self.evac_eng(), dstT[:, 512 * q:512 * q + 128 * m], ptv[:, 0:128 * m], [pbk], [dkey])
                    out.append((A, B))
                return out

            def diff_prologue(h):
                hs = h % 2
                whb, whk = wh[hs], ('a_wh', hs)
                chunks = []

                def LD():
                    for i3 in range(3):
                        self.ld(whb[:, :, 64 * i3:64 * i3 + 64], w_in_v[:, :, 384 * i3 + 64 * h:384 * i3 + 64 * h + 64], [whk], eng='pool')
                chunks.append((LD, None))
                for tt in range(16):
                    st = {}

                    def A(tt=tt, st=st):
                        pos = 128 * tt
                        pp, ppk = pbank()
                        st['b'] = (pp, ppk)
                        for kk in range(8):
                            self.mm(pp[:, 0:192], hm[:, kk, pos:pos + 128], whb[:, kk, :], kk == 0, kk == 7,
                                    [whk, ('m_hm', kk, pos // 512)], [ppk])

                    def B(tt=tt, st=st):
                        pp, ppk = st['b']
                        b, t = tt // 8, tt % 8
                        src5 = pp[:, 0:128].rearrange("p (n a h f) -> p n a h f", n=4, a=2, h=2)
                        qd = QS[:, tt, :].rearrange("p (c x) -> p c x", c=2)[:, :, 0:32].rearrange("p c (a h f) -> p c a h f", a=2, h=2)
                        kd = KS[:, tt, :].rearrange("p (c x) -> p c x", c=2)[:, :, 0:32].rearrange("p c (a h f) -> p c a h f", a=2, h=2)
                        self.rope(src5, [(slice(0, 2), qd), (slice(2, 4), kd)], tt, 4, [r[:] for r in rt], [ppk], ['a_QS', 'a_KS'])
                        kv = cnt['kvo']
                        cnt['kvo'] += 1
                        kvb, kvk = kvo[kv % 2], ('a_kvo', kv % 2)
                        self.copy('act', kvb[:], pp[:, 64:192], [ppk], [kvk])
                        rows_k = self.o['ndk'][l].rearrange("(b p t) f -> b t p f", b=2, t=8)[b, t]
                        rows_v = self.o['ndv'][l].rearrange("(b p t) f -> b t p f", b=2, t=8)[b, t]
                        self.st(rows_k[:, 64 * h:64 * h + 64], kvb[:, 0:64], [kvk])
                        self.st(rows_v[:, 64 * h:64 * h + 64], kvb[:, 64:128], [kvk])
                        self.copy('act', Vh[hs][:, tt, 0:64], pp[:, 128:192], [ppk], [('a_V', hs)])
                    chunks.append((A, B))

                def CTX():
                    for i in range(2):
                        kd = KS[:, 16 + i, :].rearrange("p (c x) -> p c x", c=2)[:, :, 0:32]
                        self.copy('dve', kd, cdk[:, i, 64 * h:64 * h + 64].rearrange("p (c x) -> p c x", c=2), ['a_cdk'], ['a_KS'])
                        self.copy('dve', Vh[hs][:, 16 + i, 0:64], cdv[:, i, 64 * h:64 * h + 64], ['a_cdv'], [('a_V', hs)])
                chunks.append((None, CTX))
                chunks += tr_chunks(QS, 16, QT[hs], 'a_QS', ('a_QT', hs))
                chunks += tr_chunks(KS, 18, KT[hs], 'a_KS', ('a_KT', hs))
                return chunks

            for A_, B_ in diff_prologue(0):
                emit_chunk(A_, B_)
            flush_pend()
            for h in range(6):
                hs = h % 2

                def post(qb, h=h):
                    self.stt('dve', o0[:], o1[:], lamt[:, 0:1], o0[:], ALU.mult, ALU.add, ['a_o', 'a_lam'], ['a_o'])
                    self.tt('dve', osq[:], o0[:], o0[:], ALU.mult, ['a_o'], ['a_osq'])
                    S.op('dve', lambda e: e.tensor_reduce(out=sml[:, 8:12], in_=osq[:], axis=mybir.AxisListType.X, op=ALU.add),
                         ['a_osq'], ['a_sml'])
                    self.act(sml[:, 8:12], sml[:, 8:12], AF.Sqrt, ['a_sml', 'epsc'], ['a_sml'], bias=self.epsc[:, 0:1], scale=1.0 / 64)
                    S.op('dve', lambda e: e.reciprocal(out=sml[:, 8:12], in_=sml[:, 8:12]), ['a_sml'], ['a_sml'])
                    self.tt('dve', o0[:], o0[:], sml[:, 8:12].unsqueeze(2).to_broadcast([128, 4, 64]), ALU.mult, ['a_o', 'a_sml'], ['a_o'])
                    self.tt('dve', ost2[qb % 2][:], o0[:], subw[:].unsqueeze(1).to_broadcast([128, 4, 64]), ALU.mult, ['a_o', 'a_subw'], [('a_ost', qb % 2)])
                    out_transposes(qb, h // 2, 64 * (h % 2))
                nxt = diff_prologue(h + 1) if h + 1 < 6 else []
                core(hs, [(0, 64), (64, 64)], 32 ** -0.5, post, filler=nxt)
                if h % 2 == 1:
                    pair_wout(h // 2)
            S.barrier()
            edf.close()
            if att <= 5:
                return
            with contextlib.ExitStack() as em:
                sbm = lambda n, s, d: self.sb(n, s, d, em)
                wm = sbm("a_wm", [128, 8, 416], BF16)
                cqnT = sbm("a_cqnT", [128, 2, NT], BF16)
                ckvT = sbm("a_ckvT", [128, NT + 256], BF16)
                cqs = sbm("a_cqs", [128, 2, 256], BF16)
                cks = sbm("a_cks", [128, 2, 128], BF16)
                qnw = sbm("a_qnw", [128, 256], F32)
                kvnw = sbm("a_kvnw", [128, 128], F32)
                cckv = sbm("a_cckv", [128, 2, 128], F32)
                ckpe = sbm("a_ckpe", [128, 2, 32], F32)
                tq = sbm("a_tq", [128, 256], F32)
                tk_ = [sbm("a_tk%d" % i, [128, 160], F32) for i in range(2)]
                wq = [sbm("a_wq%d" % i, [128, 2, 96], BF16) for i in range(2)]
                wkv = [sbm("a_wkv%d" % i, [128, 128], BF16) for i in range(2)]
                init_staging((96,), (96,))
                self.ld(wm[:], w_in_v[:, :, 1152:1568], ['a_wm'], eng='pool')
                self.ld(qnw[:], dr['mla_q_norm_w'][l].partition_broadcast(128), ['a_qnw'])
                self.ld(kvnw[:], dr['mla_kv_norm_w'][l].partition_broadcast(128), ['a_kvnw'])
                self.ld(cckv[:], dr['ctx_ckv'][l].rearrange("(i p) f -> p i f", p=128), ['a_cckv'])
                self.ld(ckpe[:], dr['ctx_kpe'][l].rearrange("(i p) f -> p i f", p=128), ['a_ckpe'])
                mla = int(os.environ.get('MLA', '99'))
                if mla <= 1:
                    S.barrier()
                    return
                for tt in range(16):
                    pos = 128 * tt
                    b, t = tt // 8, tt % 8
                    k = cnt['psp']
                    cnt['psp'] += 1
                    pp, ppk = psp[k % 2], ('a_psp', k % 2)
                    for kk in range(8):
                        self.mm(pp[:, 0:416], hm[:, kk, pos:pos + 128], wm[:, kk, :], kk == 0, kk == 7, ['a_wm', ('m_hm', kk, pos // 512)], [ppk])
                    self.act(tq[:], pp[:, 0:256], AF.Square, [ppk], ['a_tq'], accum=sml[:, 12:13])
                    self.act(sml[:, 12:13], sml[:, 12:13], AF.Sqrt, ['a_tq'], ['a_sml2'], bias=self.epsc[:, 0:1], scale=1.0 / 256)
                    S.op('dve', lambda e: e.reciprocal(out=sml[:, 12:13], in_=sml[:, 12:13]), ['a_sml2'], ['a_sml2'])
                    cb = tt % 2
                    self.stt('dve', cqs[:, cb, :], pp[:, 0:256], sml[:, 12:13], qnw[:], ALU.mult, ALU.mult, [ppk, 'a_sml2', 'a_qnw'], [('a_cqs', cb)])
                    kq = cnt['pst']
                    cnt['pst'] += 1
                    ptq, ptqk = pstr[0], ('a_pst', 0)
                    for f in range(2):
                        self.tr(ptq[:, 128 * f:128 * f + 128], cqs[:, cb, 128 * f:128 * f + 128], self.identb[:], [('a_cqs', cb), 'identb'], [ptqk])
                    self.copy(self.evac_eng(), cqnT[:, :, pos:pos + 128], ptq[:, 0:256].rearrange("p (f c) -> p f c", f=2), [ptqk], ['a_cqnT'])
                    mlap = int(os.environ.get('MLAP', '99'))
                    if mlap <= 1:
                        continue
                    kv = cnt['kvo']
                    cnt['kvo'] += 1
                    tkb, tkk = tk_[kv % 2], ('a_tk', kv % 2)
                    self.act(tq[:, 0:128], pp[:, 256:384], AF.Square, [ppk, 'a_tq'], ['a_tq'], accum=sml[:, 13:14])
                    self.act(sml[:, 13:14], sml[:, 13:14], AF.Sqrt, ['a_tq'], ['a_sml3'], bias=self.epsc[:, 0:1], scale=1.0 / 128)
                    S.op('dve', lambda e: e.reciprocal(out=sml[:, 13:14], in_=sml[:, 13:14]), ['a_sml3'], ['a_sml3'])
                    self.stt('dve', tkb[:, 0:128], pp[:, 256:384], sml[:, 13:14], kvnw[:], ALU.mult, ALU.mult, [ppk, 'a_sml3', 'a_kvnw'], [tkk])
                    self.copy('act', tkb[:, 128:160], pp[:, 384:416], [ppk], [tkk])
                    self.copy('act', cks[:, cb, :], tkb[:, 0:128], [tkk], [('a_cks', cb)])
                    self.tr(ptq[:, 256:384], cks[:, cb, :], self.identb[:], [('a_cks', cb), 'identb'], [ptqk])
                    self.copy(self.evac_eng(), ckvT[:, pos:pos + 128], ptq[:, 256:384], [ptqk], ['a_ckvT'])
                    if mlap <= 2:
                        continue
                    rows_c = self.o['nckv'][l].rearrange("(b p t) f -> b t p f", b=2, t=8)[b, t]
                    rows_p = self.o['nkpe'][l].rearrange("(b p t) f -> b t p f", b=2, t=8)[b, t]
                    self.st(rows_c, tkb[:, 0:128], [tkk])
                    self.st(rows_p, tkb[:, 128:160], [tkk])
                    if mlap <= 3:
                        continue
                    src5 = pp[:, 384:416].rearrange("p (n a h f) -> p n a h f", n=1, a=2, h=2)
                    kd = KS[:, tt, 64:96].rearrange("p (n a h f) -> p n a h f", n=1, a=2, h=2)
                    self.rope(src5, [(slice(0, 1), kd)], tt, 1, [r[:] for r in rt], [ppk], ['a_KS'])
                if mla <= 2:
                    S.barrier()
                    return
                for i in range(2):
                    self.copy('dve', cks[:, i, :], cckv[:, i, :], ['a_cckv'], [('a_cks', i)])
                    kq = cnt['pst']
                    cnt['pst'] += 1
                    ptq, ptqk = pstr[0], ('a_pst', 0)
                    self.tr(ptq[:, 0:128], cks[:, i, :], self.identb[:], [('a_cks', i), 'identb'], [ptqk])
                    self.copy(self.evac_eng(), ckvT[:, NT + 128 * i:NT + 128 * i + 128], ptq[:, 0:128], [ptqk], ['a_ckvT'])
                    self.copy('dve', KS[:, 16 + i, 64:96], ckpe[:, i, :], ['a_ckpe'], ['a_KS'])
                if mla <= 3:
                    S.barrier()
                    return
                def mla_prologue(h):
                    hs = h % 2
                    chunks = []

                    def LD():
                        self.ld(wq[hs][:], dr['mla_w_q_up'][l].rearrange("(k p) f -> p k f", p=128)[:, :, 96 * h:96 * h + 96], [('a_wq', hs)], eng='pool')
                        self.ld(wkv[hs][:], dr['mla_w_kv_up'][l][:, 128 * h:128 * h + 128], [('a_wkv', hs)], eng='pool')
                    chunks.append((LD, None))
                    for tt in range(16):
                        st = {}

                        def A(tt=tt, st=st):
                            pos = 128 * tt
                            pp, ppk = pbank()
                            st['b'] = (pp, ppk)
                            for f in range(2):
                                self.mm(pp[:, 0:96], cqnT[:, f, pos:pos + 128], wq[hs][:, f, :], f == 0, f == 1, [('a_wq', hs), 'a_cqnT'], [ppk])

                        def B(tt=tt, st=st):
                            pp, ppk = st['b']
                            kv = cnt['kvo']
                            cnt['kvo'] += 1
                            tkb, tkk = tk_[kv % 2], ('a_tk', kv % 2)
                            self.copy('act', tkb[:, 0:96], pp[:, 0:96], [ppk], [tkk])
                            self.copy('act', QS[:, tt, 0:64], tkb[:, 0:64], [tkk], ['a_QS'])
                            src5 = tkb[:, 64:96].rearrange("p (n a h f) -> p n a h f", n=1, a=2, h=2)
                            qd = QS[:, tt, 64:96].rearrange("p (n a h f) -> p n a h f", n=1, a=2, h=2)
                            self.rope(src5, [(slice(0, 1), qd)], tt, 1, [r[:] for r in rt], [tkk], ['a_QS'])
                        chunks.append((A, B))
                    for kt in range(18):
                        st = {}

                        def A(kt=kt, st=st):
                            pp, ppk = pbank()
                            st['b'] = (pp, ppk)
                            self.mm(pp[:, 0:128], ckvT[:, 128 * kt:128 * kt + 128], wkv[hs][:], True, True, [('a_wkv', hs), 'a_ckvT'], [ppk])

                        def B(kt=kt, st=st):
                            pp, ppk = st['b']
                            self.copy('act', KS[:, kt, 0:64], pp[:, 0:64], [ppk], ['a_KS'])
                            self.copy('dve', Vh[hs][:, kt, 0:64], pp[:, 64:128], [ppk], [('a_V', hs)])
                        chunks.append((A, B))
                    chunks += tr_chunks(QS, 16, QT[hs], 'a_QS', ('a_QT', hs))
                    chunks += tr_chunks(KS, 18, KT[hs], 'a_KS', ('a_KT', hs))
                    return chunks

                for A_, B_ in mla_prologue(0):
                    emit_chunk(A_, B_)
                flush_pend()
                for h in range(6):
                    hs = h % 2

                    def postm(qb, h=h):
                        self.copy('act', ost2[qb % 2][:], o0[:], ['a_o'], [('a_ost', qb % 2)])
                        out_transposes(qb, 3 + h // 2, 64 * (h % 2))
                    nxt = mla_prologue(h + 1) if h + 1 < 6 else []
                    core(hs, [(0, 128)], 96 ** -0.5, postm, filler=nxt)
                    if h % 2 == 1:
                        pair_wout(3 + h // 2)


_PROG = {}


def get_prog(stage=99):
    if stage not in _PROG:
        b = Builder(stage)
        _PROG[stage] = b.build()
    return _PROG[stage]


def rope_tables(n, grid_w=64, theta=10000.0):
    t = np.arange(n)
    row = (t // grid_w).astype(np.float32)
    col = (t % grid_w).astype(np.float32)
    inv = (theta ** (-np.arange(8, dtype=np.float32) / 8)).astype(np.float32)
    ang = np.concatenate([row[:, None] * inv[None], col[:, None] * inv[None]], axis=1).astype(np.float32)
    return np.cos(ang).astype(np.float32), np.sin(ang).astype(np.float32)


def _gsplit(a, axis):
    a = np.asarray(a, dtype=np.float32)
    sh = a.shape
    a = a.reshape(sh[:axis] + (8, 2) + sh[axis + 1:])
    a = np.moveaxis(a, axis + 1, axis)
    return np.ascontiguousarray(a)


def make_in_maps(inp):
    f = lambda a: np.ascontiguousarray(np.asarray(a, dtype=np.float32))
    shared = dict(
        w_ada=f(inp['w_ada']), b_ada=f(inp['b_ada']).reshape(DEPTH * 72, 128),
        norm_w=f(inp['norm_w']).reshape(DEPTH * 3 * 8, 128), final_norm_w=f(inp['final_norm_w']).reshape(8, 128),
        ffn_w_in=f(inp['ffn_w_in']), ffn_w_out=f(inp['ffn_w_out']), w_in=f(inp['w_in']), w_out=f(inp['w_out']),
        diff_lambda=f(inp['diff_lambda']).reshape(DEPTH, 128), diff_subln_w=f(inp['diff_subln_w']),
        mla_q_norm_w=f(inp['mla_q_norm_w']), mla_w_q_up=f(inp['mla_w_q_up']),
        mla_kv_norm_w=f(inp['mla_kv_norm_w']), mla_w_kv_up=f(inp['mla_w_kv_up']),
        s5_a_re=_gsplit(inp['s5_a_re'], 2), s5_a_im=_gsplit(inp['s5_a_im'], 2), s5_log_step=_gsplit(inp['s5_log_step'], 2),
        s5_b_re=_gsplit(inp['s5_b_re'], 2), s5_b_im=_gsplit(inp['s5_b_im'], 2),
        s5_c_re=_gsplit(inp['s5_c_re'], 2).reshape(DEPTH, 2, 2, 128, 64), s5_c_im=_gsplit(inp['s5_c_im'], 2).reshape(DEPTH, 2, 2, 128, 64),
        s5_d=f(inp['s5_d']), s5_w_glu=f(inp['s5_w_glu']), s5_b_glu=f(inp['s5_b_glu']).reshape(DEPTH, 2, 128),
    )
    tq = np.arange(128) // 16
    cm_f = (tq[:, None] <= tq[None, :]).astype(np.float32)
    cm_b = (tq[:, None] >= tq[None, :]).astype(np.float32)
    shared['cmask_f'] = cm_f
    shared['cmask_b'] = cm_b
    rc, rsn = rope_tables(NT)
    maps = []
    for core in range(8):
        m = dict(shared)
        if core < 4:
            b = core
            m['xin'] = f(inp['x_sample'][b])
            m['cvec'] = f(inp['c'][b]).reshape(8, 128)
            m['ctx_dk'] = f(inp['cache_diff_k'][b]).reshape(DEPTH, 256, 384)
            m['ctx_dv'] = f(inp['cache_diff_v'][b]).reshape(DEPTH, 256, 384)
            m['ctx_ckv'] = f(inp['cache_mla_ckv'][b])
            m['ctx_kpe'] = f(inp['cache_mla_kpe'][b])
            m['h0re'] = _gsplit(inp['state_s5_re'][b], 2)
            m['h0im'] = _gsplit(inp['state_s5_im'][b], 2)
            m['ropec'], m['ropes'] = rc, rsn
            augk = np.zeros((NT + 256, 9), np.float32)
            augk[:, 0] = 32.0
            augk[:, 8] = 1.0
            augq = np.zeros((NT, 9), np.float32)
            augq[:, 0] = 32.0
            augq[:, 8] = -1024.0
            m['flag'] = np.ones((128, 1), np.float32)
        else:
            i = core - 4
            m['xin'] = f(inp['x_prompt'][8 * i:8 * i + 8]).reshape(NT, D)
            m['cvec'] = f(inp['c_ctx']).reshape(8, 128)
            m['ctx_dk'] = np.zeros((DEPTH, 256, 384), np.float32)
            m['ctx_dv'] = np.zeros((DEPTH, 256, 384), np.float32)
            m['ctx_ckv'] = np.zeros((DEPTH, 256, 128), np.float32)
            m['ctx_kpe'] = np.zeros((DEPTH, 256, 32), np.float32)
            m['h0re'] = np.zeros((DEPTH, 2, 2, 8, 64), np.float32)
            m['h0im'] = np.zeros((DEPTH, 2, 2, 8, 64), np.float32)
            m['ropec'] = np.ones((NT, 16), np.float32)
            m['ropes'] = np.zeros((NT, 16), np.float32)
            seg = np.arange(NT) // 256
            augk = np.zeros((NT + 256, 9), np.float32)
            augk[np.arange(NT), seg] = 32.0
            augk[:, 8] = 1.0
            augq = np.zeros((NT, 9), np.float32)
            augq[np.arange(NT), seg] = 32.0
            augq[:, 8] = -1024.0
            m['flag'] = np.zeros((128, 1), np.float32)
        m['augk'] = augk
        m['augq'] = augq
        maps.append(m)
    return maps


def kernel(**inputs):
    nc = get_prog(STAGE)
    maps = make_in_maps(inputs)
    res = run_bass_kernel_spmd(nc, maps, core_ids=list(range(8)))
    r = res.results
    B, SEQ = 32, 256
    y_sample = np.stack([r[c]['y'] for c in range(4)], 0).astype(np.float32)
    y_prompt = np.concatenate([r[c]['y'].reshape(8, SEQ, D) for c in range(4, 8)], 0).astype(np.float32)

    def cat(name, tail):
        outs = []
        for c in range(4, 8):
            a = r[c][name].reshape(DEPTH, 8, SEQ, -1).transpose(1, 0, 2, 3)
            outs.append(a)
        a = np.concatenate(outs, 0)
        return np.ascontiguousarray(a.reshape((B, DEPTH, SEQ) + tail)).astype(np.float32)

    def cat5(name):
        outs = []
        for c in range(4, 8):
            a = r[c][name].reshape(DEPTH, 2, 8, 16, 64).transpose(2, 0, 1, 3, 4)
            outs.append(a)
        return np.ascontiguousarray(np.concatenate(outs, 0)).astype(np.float32)
    return (y_prompt, y_sample, cat('ndk', (6, 64)), cat('ndv', (6, 64)), cat('nckv', (128,)), cat('nkpe', (32,)),
            cat5('ns5re'), cat5('ns5im'))
```
